# Optimizing a Trainium2 kernel written in Bass

```python
import math
import jax
import jax.numpy as jnp
from jax import lax
import numpy as np


D_MODEL = 1024
BATCH = 8
SEQ = 2048
DEPTH = 1
DEC_BATCH = 128
DEC_SEQ = 8
PAST_LEN = 16384
PAGE_SIZE = 128

HEAD_DIM = 128
DN_HEADS = D_MODEL // (2 * HEAD_DIM)
RET_HEADS = D_MODEL // (2 * HEAD_DIM)
DN_DK = HEAD_DIM
DN_DV = HEAD_DIM
RET_DK = HEAD_DIM
RET_DV = HEAD_DIM
DN_QK = DN_HEADS * DN_DK
DN_VD = DN_HEADS * DN_DV
RET_QK = RET_HEADS * RET_DK
RET_VD = RET_HEADS * RET_DV
D_IN = 3 * DN_QK + DN_VD + 2 * DN_HEADS + 2 * RET_QK + 2 * RET_VD
D_MIX = DN_VD + RET_VD
DN_CONV = 4
FFN_CONV = 3
D_FF = 2816
PLE_DIM = 256
CHUNK = 64
ROPE_BASE = 10000.0
EPS = 1e-6

kernel_name = 'hymba_gdn_retnet_convffn_step'


def _rmsnorm(x, w):
    xf = x.astype(jnp.float32)
    y = xf * lax.rsqrt(jnp.mean(xf * xf, axis=-1, keepdims=True) + EPS)
    return y * w.astype(jnp.float32)


def _l2norm(x):
    xf = x.astype(jnp.float32)
    return xf * lax.rsqrt(jnp.sum(xf * xf, axis=-1, keepdims=True) + EPS)


def _chunk_len(L):
    return CHUNK if L % CHUNK == 0 else L


def _causal_dwconv(x, buf, w):
    W = w.shape[0]
    L = x.shape[1]
    xp = jnp.concatenate([buf.astype(x.dtype), x], axis=1)
    y = xp[:, 0:L] * w[0]
    for j in range(1, W):
        y = y + xp[:, j:j + L] * w[j]
    return y, xp[:, -(W - 1):]


def _rotary(x, pos):
    d = x.shape[-1]
    inv = ROPE_BASE ** (-jnp.arange(0, d, 2, dtype=jnp.float32) / d)
    ang = pos.astype(jnp.float32)[:, None] * inv[None, :]
    cos = jnp.cos(ang)[None, :, None, :]
    sin = jnp.sin(ang)[None, :, None, :]
    xf = x.astype(jnp.float32)
    x1 = xf[..., 0::2]
    x2 = xf[..., 1::2]
    return jnp.stack([x1 * cos - x2 * sin, x1 * sin + x2 * cos], axis=-1).reshape(x.shape)


def _to_blocks(t, N, C):
    B, L, H, d = t.shape
    return t.reshape(B, N, C, H, d).transpose(0, 3, 1, 2, 4)


def _from_blocks(o):
    N, B, H, C, d = o.shape
    return o.transpose(1, 0, 3, 2, 4).reshape(B, N * C, H, d)


def _gated_delta_chunked(q, k, v, g, beta, S0):
    B, L, H, dk = q.shape
    dv = v.shape[-1]
    C = _chunk_len(L)
    N = L // C
    qb = _to_blocks(q, N, C) * (dk ** -0.5)
    kb = _to_blocks(k, N, C)
    vb = _to_blocks(v.astype(jnp.float32), N, C)
    gb = g.reshape(B, N, C, H).transpose(0, 3, 1, 2)
    bb = beta.reshape(B, N, C, H).transpose(0, 3, 1, 2)
    gc = jnp.cumsum(gb, axis=-1)
    causal = jnp.tril(jnp.ones((C, C), dtype=bool))
    strict = jnp.tril(jnp.ones((C, C), dtype=bool), -1)
    decay = jnp.exp(jnp.where(causal, gc[..., :, None] - gc[..., None, :], -jnp.inf))
    k_beta = kb * bb[..., None]
    A = jnp.where(strict, jnp.einsum('bhncd,bhnmd->bhncm', k_beta, kb) * decay, 0.0)
    T = A + jnp.eye(C, dtype=jnp.float32)
    rhs = jnp.concatenate([vb * bb[..., None], k_beta * jnp.exp(gc)[..., None]], axis=-1)
    sol = lax.linalg.triangular_solve(T, rhs, left_side=True, lower=True, unit_diagonal=True)
    u = sol[..., :dv]
    w = sol[..., dv:]
    qk = jnp.einsum('bhncd,bhnmd->bhncm', qb, kb) * decay
    q_dec = qb * jnp.exp(gc)[..., None]
    g_last = gc[..., -1]
    k_tail = kb * jnp.exp(g_last[..., None] - gc)[..., None]
    mv = lambda t: jnp.moveaxis(t, 2, 0)

    def step(S, xs):
        u_i, w_i, qk_i, qd_i, kt_i, gl_i = xs
        v_new = u_i - jnp.einsum('bhcd,bhde->bhce', w_i, S)
        o = jnp.einsum('bhcd,bhde->bhce', qd_i, S) + jnp.einsum('bhcm,bhme->bhce', qk_i, v_new)
        S = S * jnp.exp(gl_i)[..., None, None] + jnp.einsum('bhcd,bhce->bhde', kt_i, v_new)
        return S, o

    S, o = lax.scan(step, S0.astype(jnp.float32),
                    (mv(u), mv(w), mv(qk), mv(q_dec), mv(k_tail), jnp.moveaxis(g_last, 2, 0)))
    return _from_blocks(o), S


def _retention_chunked(q, k, v, R0):
    B, L, H, dk = q.shape
    C = _chunk_len(L)
    N = L // C
    log_gamma = jnp.log(1.0 - 2.0 ** (-5.0 - jnp.arange(H, dtype=jnp.float32)))
    bpos = (jnp.arange(C, dtype=jnp.float32) + 1.0)[None, :] * log_gamma[:, None]
    causal = jnp.tril(jnp.ones((C, C), dtype=bool))
    D = jnp.exp(jnp.where(causal, bpos[:, :, None] - bpos[:, None, :], -jnp.inf))
    qb = _to_blocks(q, N, C)
    kb = _to_blocks(k, N, C) * (dk ** -0.5)
    vb = _to_blocks(v.astype(jnp.float32), N, C)
    qk = jnp.einsum('bhncd,bhnmd->bhncm', qb, kb) * D[None, :, None]
    o_intra = jnp.einsum('bhncm,bhnme->bhnce', qk, vb)
    q_dec = qb * jnp.exp(bpos)[None, :, None, :, None]
    k_tail = kb * jnp.exp(bpos[:, -1:] - bpos)[None, :, None, :, None]
    chunk_decay = jnp.exp(bpos[:, -1])
    mv = lambda t: jnp.moveaxis(t, 2, 0)

    def step(R, xs):
        oi, qd, kt, vi = xs
        o = oi + jnp.einsum('bhcd,bhde->bhce', qd, R)
        R = R * chunk_decay[None, :, None, None] + jnp.einsum('bhcd,bhce->bhde', kt, vi)
        return R, o

    R, o = lax.scan(step, R0.astype(jnp.float32), (mv(o_intra), mv(q_dec), mv(k_tail), mv(vb)))
    return _from_blocks(o), R


def _split_points():
    sizes = [3 * DN_QK, DN_VD, DN_HEADS, DN_HEADS, RET_QK, RET_QK, RET_VD, RET_VD]
    return [int(s) for s in np.cumsum(sizes)[:-1]]


def _layer(h, p, conv_buf, S_dn, R_ret, ffn_buf, pos0, attn_norm_w, w_in, dn_conv_w,
           dn_A_log, dn_dt_bias, dn_norm_w, ret_norm_w, w_out, ffn_norm_w, w_up,
           ffn_conv_w, ffn_conv_b, w_down, ple_norm_w, w_ple_gate, w_ple):
    B, L, _ = h.shape
    dt = h.dtype
    a = _rmsnorm(h, attn_norm_w).astype(dt)
    proj = a @ w_in
    dn_qkv, dn_z, dn_b, dn_a, r_q, r_k, r_v, r_g = jnp.split(proj, _split_points(), axis=-1)
    qkv, conv_new = _causal_dwconv(dn_qkv, conv_buf, dn_conv_w)
    qkv = jax.nn.silu(qkv.astype(jnp.float32))
    q, k, v = jnp.split(qkv, [DN_QK, 2 * DN_QK], axis=-1)
    q = _l2norm(q.reshape(B, L, DN_HEADS, DN_DK))
    k = _l2norm(k.reshape(B, L, DN_HEADS, DN_DK))
    v = v.reshape(B, L, DN_HEADS, DN_DV)
    beta = jax.nn.sigmoid(dn_b.astype(jnp.float32))
    g = -jnp.exp(dn_A_log.astype(jnp.float32)) * jax.nn.softplus(
        dn_a.astype(jnp.float32) + dn_dt_bias.astype(jnp.float32))
    o_dn, S_new = _gated_delta_chunked(q, k, v, g, beta, S_dn)
    o_dn = _rmsnorm(o_dn, dn_norm_w) * jax.nn.silu(
        dn_z.astype(jnp.float32).reshape(B, L, DN_HEADS, DN_DV))
    pos = pos0 + jnp.arange(L)
    rq = _rotary(r_q.reshape(B, L, RET_HEADS, RET_DK), pos)
    rk = _rotary(r_k.reshape(B, L, RET_HEADS, RET_DK), pos)
    rv = r_v.reshape(B, L, RET_HEADS, RET_DV)
    o_ret, R_new = _retention_chunked(rq, rk, rv, R_ret)
    mu = jnp.mean(o_ret, axis=-1, keepdims=True)
    var = jnp.mean(jnp.square(o_ret - mu), axis=-1, keepdims=True)
    o_ret = (o_ret - mu) * lax.rsqrt(var + EPS) * ret_norm_w.astype(jnp.float32).reshape(RET_HEADS, RET_DV)
    o_ret = o_ret * jax.nn.silu(r_g.astype(jnp.float32).reshape(B, L, RET_HEADS, RET_DV))
    mix = jnp.concatenate([o_dn.reshape(B, L, DN_VD), o_ret.reshape(B, L, RET_VD)], axis=-1).astype(dt)
    h = h + mix @ w_out
    m = _rmsnorm(h, ffn_norm_w).astype(dt)
    u, ffn_new = _causal_dwconv(m @ w_up, ffn_buf, ffn_conv_w)
    u = u + ffn_conv_b
    ug, uv = jnp.split(u, 2, axis=-1)
    h = h + (jax.nn.silu(ug) * uv) @ w_down
    gate = jax.nn.sigmoid((_rmsnorm(h, ple_norm_w).astype(dt) @ w_ple_gate).astype(jnp.float32))
    h = h + (gate * (p @ w_ple).astype(jnp.float32)).astype(dt)
    return (h, conv_new.astype(dt), S_new.astype(dt), R_new.astype(dt), ffn_new.astype(dt))


def setup_inputs(seed: int = 0) -> dict:
    key = jax.random.key(seed)
    ks = jax.random.split(key, 32)
    f32 = jnp.float32
    nrm = lambda k, shape, s: jax.random.normal(k, shape, f32) * s
    gain = lambda k, shape: 1.0 + 0.05 * jax.random.normal(k, shape, f32)
    return {
        'x_prompt': nrm(ks[0], (BATCH, SEQ, D_MODEL), 1.0),
        'x_sample': nrm(ks[1], (DEC_BATCH, DEC_SEQ, D_MODEL), 1.0),
        'p_prompt': nrm(ks[2], (DEPTH, BATCH, SEQ, PLE_DIM), 1.0),
        'p_sample': nrm(ks[3], (DEPTH, DEC_BATCH, DEC_SEQ, PLE_DIM), 1.0),
        'state_dn_conv': nrm(ks[4], (DEPTH, DEC_BATCH, DN_CONV - 1, 3 * DN_QK), 1.0),
        'state_dn': nrm(ks[5], (DEPTH, DEC_BATCH, DN_HEADS, DN_DK, DN_DV), 0.1),
        'state_ret': nrm(ks[6], (DEPTH, DEC_BATCH, RET_HEADS, RET_DK, RET_DV), 0.5),
        'state_ffn_conv': nrm(ks[7], (DEPTH, DEC_BATCH, FFN_CONV - 1, 2 * D_FF), 1.0),
        'attn_norm_w': gain(ks[8], (DEPTH, D_MODEL)),
        'w_in': nrm(ks[9], (DEPTH, D_MODEL, D_IN), D_MODEL ** -0.5),
        'dn_conv_w': nrm(ks[10], (DEPTH, DN_CONV, 3 * DN_QK), DN_CONV ** -0.5),
        'dn_A_log': jnp.log(jax.random.uniform(ks[11], (DEPTH, DN_HEADS), f32, 1.0, 16.0)),
        'dn_dt_bias': nrm(ks[12], (DEPTH, DN_HEADS), 0.1),
        'dn_norm_w': gain(ks[13], (DEPTH, DN_DV)),
        'ret_norm_w': gain(ks[14], (DEPTH, RET_VD)),
        'w_out': nrm(ks[15], (DEPTH, D_MIX, D_MODEL), D_MIX ** -0.5),
        'ffn_norm_w': gain(ks[16], (DEPTH, D_MODEL)),
        'w_up': nrm(ks[17], (DEPTH, D_MODEL, 2 * D_FF), D_MODEL ** -0.5),
        'ffn_conv_w': nrm(ks[18], (DEPTH, FFN_CONV, 2 * D_FF), FFN_CONV ** -0.5),
        'ffn_conv_b': nrm(ks[19], (DEPTH, 2 * D_FF), 0.02),
        'w_down': nrm(ks[20], (DEPTH, D_FF, D_MODEL), D_FF ** -0.5),
        'ple_norm_w': gain(ks[21], (DEPTH, D_MODEL)),
        'w_ple_gate': nrm(ks[22], (DEPTH, D_MODEL, D_MODEL), D_MODEL ** -0.5),
        'w_ple': nrm(ks[23], (DEPTH, PLE_DIM, D_MODEL), PLE_DIM ** -0.5),
        'final_norm_w': gain(ks[24], (D_MODEL,)),
    }


def reference(x_prompt, x_sample, p_prompt, p_sample, state_dn_conv, state_dn, state_ret,
              state_ffn_conv, attn_norm_w, w_in, dn_conv_w, dn_A_log, dn_dt_bias, dn_norm_w,
              ret_norm_w, w_out, ffn_norm_w, w_up, ffn_conv_w, ffn_conv_b, w_down, ple_norm_w,
              w_ple_gate, w_ple, final_norm_w):
    dt = x_prompt.dtype
    Bp = x_prompt.shape[0]
    hp = x_prompt
    hs = x_sample
    outs_p = ([], [], [], [])
    outs_s = ([], [], [], [])
    for i in range(DEPTH):
        wts = (attn_norm_w[i], w_in[i], dn_conv_w[i], dn_A_log[i], dn_dt_bias[i], dn_norm_w[i],
               ret_norm_w[i], w_out[i], ffn_norm_w[i], w_up[i], ffn_conv_w[i], ffn_conv_b[i],
               w_down[i], ple_norm_w[i], w_ple_gate[i], w_ple[i])
        zc = jnp.zeros((Bp, DN_CONV - 1, 3 * DN_QK), dt)
        zs = jnp.zeros((Bp, DN_HEADS, DN_DK, DN_DV), jnp.float32)
        zr = jnp.zeros((Bp, RET_HEADS, RET_DK, RET_DV), jnp.float32)
        zf = jnp.zeros((Bp, FFN_CONV - 1, 2 * D_FF), dt)
        hp, c1, s1, r1, f1 = _layer(hp, p_prompt[i], zc, zs, zr, zf, 0, *wts)
        hs, c2, s2, r2, f2 = _layer(hs, p_sample[i], state_dn_conv[i], state_dn[i], state_ret[i],
                                    state_ffn_conv[i], PAST_LEN, *wts)
        for lst, val in zip(outs_p, (c1, s1, r1, f1)):
            lst.append(val)
        for lst, val in zip(outs_s, (c2, s2, r2, f2)):
            lst.append(val)
    y_prompt = _rmsnorm(hp, final_norm_w).astype(dt)
    y_sample = _rmsnorm(hs, final_norm_w).astype(dt)
    return (y_prompt, y_sample,
            jnp.stack(outs_p[0]), jnp.stack(outs_p[1]), jnp.stack(outs_p[2]), jnp.stack(outs_p[3]),
            jnp.stack(outs_s[0]), jnp.stack(outs_s[1]), jnp.stack(outs_s[2]), jnp.stack(outs_s[3]))
```

```python
import contextlib
import numpy as np
import concourse.bass as bass
import concourse.mybir as mybir
from concourse.bass_utils import run_bass_kernel_spmd

F32 = mybir.dt.float32
BF16 = mybir.dt.bfloat16
AF = mybir.ActivationFunctionType
ALU = mybir.AluOpType
AX = mybir.AxisListType

NCORES = 8
D = 1024
SEQ = 2048
NSAMP = 16
LS = 8
NTOK = SEQ + NSAMP * LS
DIN = 4104
DFF = 2816
EPS = 1e-6
PAST = 16384
NEG = -30000.0
PE2R, PRBF2, PKQT, PRTK, POF = 5, 5, 3, 3, 2
C_QKV, C_Z, C_B, C_A, C_RQ, C_RK, C_RV, C_RG = 0, 1536, 2048, 2052, 2056, 2568, 3080, 3592


class T:
    __slots__ = ("name", "last_writer", "readers")

    def __init__(self, name=""):
        self.name = name
        self.last_writer = None
        self.readers = []


class Op:
    __slots__ = ("eng", "fn", "deps", "users", "ndep", "signaled", "sigval", "sem", "is_dma", "idx", "cost",
                 "aset", "phase", "finish", "pos", "rtime", "tag", "prio")

    def __init__(self, eng, fn, is_dma):
        self.eng = eng
        self.fn = fn
        self.deps = []
        self.users = []
        self.signaled = False
        self.sigval = None
        self.sem = None
        self.is_dma = is_dma
        self.finish = 0.0
        self.pos = -1


class Sched:
    ENGS = ("pe", "act", "dve", "pool", "sp")
    XLAT = 250.0

    def __init__(self, nc, n_dma_sems=14):
        self.nc = nc
        self.ops = []
        self.n_dma_sems = n_dma_sems
        self.nops = 0
        self.phase = 0
        import os as _os
        self.prio_mode = int(_os.environ.get("KS_PRIO", "1"))
        self.prio_w = float(_os.environ.get("KS_PRIOW", "0.0"))

    def op(self, eng, fn, reads=(), writes=(), dma=False, cost=200.0, aset=None):
        o = Op(eng, fn, dma)
        o.idx = self.nops
        self.nops += 1
        o.cost = cost
        o.aset = aset
        o.phase = self.phase
        import sys as _sys
        fr = _sys._getframe(2)
        o.tag = fr.f_lineno if fr.f_code.co_name != "<lambda>" else fr.f_back.f_lineno
        deps = []
        for t in reads:
            if t.last_writer is not None:
                deps.append(t.last_writer)
        for t in writes:
            if t.last_writer is not None:
                deps.append(t.last_writer)
            deps.extend(t.readers)
        seen = set()
        for d in deps:
            if id(d) in seen or d is o or d.phase != o.phase:
                continue
            seen.add(id(d))
            o.deps.append(d)
            d.users.append(o)
        for t in reads:
            t.readers.append(o)
        for t in writes:
            t.last_writer = o
            t.readers = []
        self.ops.append(o)
        return o

    def open(self, stack):
        nc = self.nc
        self.sems = {}
        for e in ("pe", "act", "dve", "pool"):
            self.sems[e] = stack.enter_context(nc.semaphore("s_" + e))
        for e in ("sp", "act", "pool"):
            for k in range(self.n_dma_sems):
                self.sems[(e, k)] = stack.enter_context(nc.semaphore("d_%s_%d" % (e, k)))
        self.cnt = {}
        self.dma_n = {e: 0 for e in self.ENGS}

    def _schedule(self):
        import heapq
        ops = self.ops
        future = {e: [] for e in self.ENGS}
        avail = {e: [] for e in self.ENGS}
        free_at = {e: 0.0 for e in self.ENGS}
        cur_set = {e: None for e in self.ENGS}
        streams = {e: [] for e in self.ENGS}
        self._pipe = 0.0
        bl = {}
        for o in reversed(ops):
            m = 0.0
            for u in o.users:
                lat = self.XLAT if (u.eng != o.eng or o.is_dma) else 60.0
                v = bl[id(u)] + lat
                if v > m:
                    m = v
            bl[id(o)] = m + o.cost
        mode = self.prio_mode
        for o in ops:
            o.ndep = len(o.deps)
            o.rtime = 0.0
            if mode == 0:
                o.prio = o.idx
            else:
                o.prio = -bl[id(o)] + self.prio_w * o.idx
        for o in ops:
            if o.ndep == 0:
                heapq.heappush(future[o.eng], (0.0, o.prio, o.idx, o))
        left = len(ops)
        while left:
            best = None
            for e in self.ENGS:
                f, a = future[e], avail[e]
                while f and f[0][0] <= free_at[e]:
                    _, pr, i, o = heapq.heappop(f)
                    heapq.heappush(a, (pr, i, o))
                if a:
                    cand = (free_at[e], a[0][0], e, True)
                elif f:
                    cand = (f[0][0], f[0][1], e, False)
                else:
                    continue
                if best is None or cand < best:
                    best = cand
            start, _, e, from_avail = best
            if from_avail:
                _, _, o = heapq.heappop(avail[e])
            else:
                _, _, _, o = heapq.heappop(future[e])
            c = o.cost
            if o.aset is not None and o.aset != cur_set[e]:
                if cur_set[e] is not None:
                    c += 1300.0
                cur_set[e] = o.aset
            if o.is_dma:
                xfer = max(0.0, c - 2000.0) * (120.0 / 220.0)
                t0x = max(start + 1000.0, self._pipe)
                self._pipe = t0x + xfer
                o.finish = t0x + xfer + 1000.0
                free_at[e] = start + 60.0
            else:
                o.finish = start + c
                free_at[e] = o.finish
            o.pos = len(streams[e])
            streams[e].append(o)
            left -= 1
            for u in o.users:
                lat = self.XLAT if (u.eng != e or o.is_dma) else 60.0
                t = o.finish + lat
                if t > u.rtime:
                    u.rtime = t
                u.ndep -= 1
                if u.ndep == 0:
                    heapq.heappush(future[u.eng], (u.rtime, u.prio, u.idx, u))
        self.makespan = max(free_at.values())
        return streams

    def emit_phase(self):
        nc = self.nc
        sems = self.sems
        cnt = self.cnt
        streams = self._schedule()
        for e in self.ENGS:
            last_on_sem = {}
            for o in streams[e]:
                if o.is_dma:
                    kk = self.dma_n[e] % self.n_dma_sems
                    self.dma_n[e] += 1
                    o.sem = (e, kk)
                    c = cnt.get(o.sem, 0) + 16
                    cnt[o.sem] = c
                    o.sigval = c
        plan = {}
        for e in self.ENGS:
            wpos = {}
            wl = []
            for o in streams[e]:
                ws = []
                for d in o.deps:
                    if d.is_dma:
                        ws.append(d)
                        continue
                    if d.eng == e and e == "pe":
                        continue
                    if d.pos > wpos.get(d.eng, -1):
                        wpos[d.eng] = d.pos
                        d.signaled = True
                        ws.append(d)
                wl.append(ws)
            plan[e] = wl
        for e in ("pe", "act", "dve", "pool"):
            for o in reversed(streams[e]):
                if not o.is_dma:
                    o.signaled = True
                    break
        for e in self.ENGS:
            for o in streams[e]:
                if (not o.is_dma) and o.signaled:
                    c = cnt.get(e, 0) + 1
                    cnt[e] = c
                    o.sem = e
                    o.sigval = c
        final = dict(cnt)
        with nc.Block() as block:
            engobj = {"pe": block.tensor, "act": block.scalar, "dve": block.vector,
                      "pool": block.gpsimd, "sp": block.sync}

            def run(e, eng):
                waited = {}
                dma_prev = {}

                def wait_sv(sem, val):
                    if waited.get(sem, 0) >= val:
                        return
                    eng.wait_ge(sems[sem], val)
                    waited[sem] = val

                for o, ws in zip(streams[e], plan[e]):
                    for d in ws:
                        wait_sv(d.sem, d.sigval)
                    if o.is_dma:
                        if o.sigval > 16:
                            wait_sv(o.sem, o.sigval - 16)
                    ins = o.fn(eng)
                    if o.is_dma:
                        ins.then_inc(sems[o.sem], 16)
                    elif o.signaled:
                        ins.then_inc(sems[o.sem], 1)
                for sem, val in final.items():
                    wait_sv(sem, val)

            for e in self.ENGS:
                def mk(e):
                    def f(eng):
                        run(e, eng)
                    return f
                engobj[e](mk(e))
        self.ops = []
        self.phase += 1


class StopBuild(Exception):
    pass


def ck(n):
    import os
    lim = float(os.environ.get("KDBG_CK", "1000"))
    if n > lim:
        raise StopBuild()


class Buf:
    def __init__(self, h, name="", excl=False):
        self.h = h
        self.t = T(name)
        self.excl = excl

    def __getitem__(self, k):
        return V(self, self.h[k])


class V:
    def __init__(self, buf, ap):
        self.buf = buf
        self.ap = ap

    def __getitem__(self, k):
        return V(self.buf, self.ap[k])

    def re(self, pat_, **kw):
        return V(self.buf, self.ap.rearrange(pat_, **kw))

    def bc(self, shape):
        return V(self.buf, self.ap.to_broadcast(list(shape)))

    def un(self, ax):
        return V(self.buf, self.ap.unsqueeze(ax))

    def bitcast(self, dt):
        return V(self.buf, self.ap.bitcast(dt))


class Chunked:
    def __init__(self, buf, n, w):
        self.bufs = [Buf(buf.h, "%s_c%d" % (buf.t.name, i)) for i in range(n)]
        self.w = w

    def c(self, i, a=0, b=None):
        b = self.w if b is None else b
        return self.bufs[i][:, i * self.w + a:i * self.w + b]


class Pool:
    def __init__(self, bufs):
        self.bufs = bufs
        self.i = 0

    def next(self):
        b = self.bufs[self.i % len(self.bufs)]
        self.i += 1
        return b


def _tr(*vs):
    return [v.buf.t for v in vs if isinstance(v, V) and v.buf is not None]


def _rw(reads, writes):
    r, w = [], []
    for v in reads:
        if isinstance(v, V) and v.buf is not None:
            (w if v.buf.excl else r).append(v.buf.t)
    for v in writes:
        if isinstance(v, V) and v.buf is not None:
            w.append(v.buf.t)
    return dict(reads=r, writes=w)


def _a(x):
    return x.ap if isinstance(x, V) else x


def _fs(v):
    n = 1
    for d in v.ap.shape[1:]:
        n *= int(d)
    return n


def _is_psum(v):
    return isinstance(v, V) and v.buf is not None and v.buf.excl


_ASET = {AF.Silu: "silu", AF.Exp: "lnexp", AF.Ln: "lnexp", AF.Tanh: "silu"}


class K:
    def __init__(self, nc, S):
        self.nc = nc
        self.S = S

    def mm(self, out, lhsT, rhs, start=True, stop=True):
        n = max(32, _fs(rhs))
        c = n / 2.37 * (4.0 if rhs.ap.dtype == F32 else 1.0) + 48.0
        self.S.op("pe", lambda e: e.matmul(out.ap, lhsT=lhsT.ap, rhs=rhs.ap, start=start, stop=stop),
                  cost=c, **_rw([lhsT, rhs], [out]))

    def tp(self, out, in_, ident):
        c = max(32, _fs(ident)) / 2.37 * (4.0 if in_.ap.dtype == F32 else 1.0) + 48.0
        self.S.op("pe", lambda e: e.transpose(out.ap, in_.ap, ident.ap), cost=c, **_rw([in_, ident], [out]))

    def act(self, out, in_, func, scale=1.0, bias=0.0, accum=None):
        def f(e):
            kw = dict(out=out.ap, in_=in_.ap, func=func, scale=_a(scale), bias=_a(bias))
            if accum is not None:
                kw["accum_out"] = accum.ap
            return e.activation(**kw)
        c = 190.0 + 0.6 * _fs(in_) + (90.0 if accum is not None else 0.0)
        self.S.op("act", f, cost=c, aset=_ASET.get(func), **_rw([in_, scale, bias], [out, accum]))

    def _vc(self, eng, *vs):
        f = max(_fs(v) for v in vs if isinstance(v, V))
        ps = any(_is_psum(v) for v in vs)
        if eng == "pool":
            return 1100.0 + 0.45 * f
        return (130.0 if ps else 90.0) + 1.25 * f

    def tt(self, eng, out, a, b, op):
        self.S.op(eng, lambda e: e.tensor_tensor(out=out.ap, in0=a.ap, in1=b.ap, op=op), cost=self._vc(eng, out, a, b),
                  **_rw([a, b], [out]))

    def ts(self, eng, out, a, s1, op0, s2=None, op1=None):
        def f(e):
            if s2 is None:
                return e.tensor_scalar(out=out.ap, in0=a.ap, scalar1=_a(s1), scalar2=None, op0=op0)
            return e.tensor_scalar(out=out.ap, in0=a.ap, scalar1=_a(s1), scalar2=_a(s2), op0=op0, op1=op1)
        self.S.op(eng, f, cost=self._vc(eng, out, a), **_rw([a, s1, s2], [out]))

    def stt(self, eng, out, a, s, b, op0, op1):
        self.S.op(eng, lambda e: e.scalar_tensor_tensor(out=out.ap, in0=a.ap, scalar=_a(s), in1=b.ap, op0=op0, op1=op1),
                  cost=self._vc(eng, out, a, b), **_rw([a, s, b], [out]))

    def cp(self, eng, out, a):
        if eng == "act":
            self.S.op("act", lambda e: e.copy(out=out.ap, in_=a.ap), cost=190.0 + 0.6 * _fs(a), **_rw([a], [out]))
        else:
            c = (250.0 + 0.3 * _fs(a)) if eng == "pool" else self._vc(eng, out, a)
            self.S.op(eng, lambda e: e.tensor_copy(out=out.ap, in_=a.ap), cost=c, **_rw([a], [out]))

    def recip(self, out, a):
        self.S.op("dve", lambda e: e.reciprocal(out=out.ap, in_=a.ap), cost=self._vc("dve", out, a), **_rw([a], [out]))

    def ms(self, eng, out, val):
        self.S.op(eng, lambda e: e.memset(out.ap, val), cost=250.0 + 0.3 * _fs(out), **_rw([], [out]))

    def dma(self, q, out, in_):
        esz = 2 if (in_.ap.dtype == BF16 and out.ap.dtype == BF16) else 4
        nbytes = int(out.ap.shape[0]) * _fs(out) * esz
        c = 2000.0 + nbytes / 120.0
        return self.S.op(q, lambda e: e.dma_start(out=out.ap, in_=in_.ap), dma=True, cost=c, **_rw([in_], [out]))


def build_program():
    nc = bass.Bass("TRN2", target_bir_lowering=False)

    def din(name, shape):
        return V(None, nc.dram_tensor(name, list(shape), F32, kind="ExternalInput").ap())

    def dout(name, shape):
        return Buf(nc.dram_tensor(name, list(shape), F32, kind="ExternalOutput").ap(), name)

    class RowBufs:
        def __init__(self, ap):
            self.h = ap
            self.b = {}

        def rows(self, r0):
            if r0 not in self.b:
                self.b[r0] = Buf(self.h, "rb%d" % r0)
            return V(self.b[r0], self.h[r0:r0 + 128, :])

    x_d = din("x", [NTOK, D])
    p_d = din("p", [NTOK, 256])
    stdnc_d = din("st_dnc", [48, 1536])
    stdn_d = din("st_dn", [NSAMP, 4, 128, 128])
    stret_d = din("st_ret", [NSAMP, 4, 128, 128])
    stfc_d = din("st_fc", [32, 2 * DFF])
    w_in_d = din("w_in", [D, DIN])
    w_out_d = din("w_out", [D, D])
    w_up_d = din("w_up", [D, 2 * DFF])
    w_down_d = din("w_down", [DFF, D])
    w_gate_d = din("w_gate", [D, D])
    w_ple_d = din("w_ple", [256, D])
    anw8_d = din("anw8", [128, 8])
    fnw8_d = din("fnw8", [128, 8])
    pnw8_d = din("pnw8", [128, 8])
    finw_d = din("finw", [D])
    dncw_d = din("dncw", [128, 48])
    alog_d = din("alog", [4])
    dtb_d = din("dtb", [4])
    dnw4_d = din("dnw4", [512])
    retw_d = din("retw", [512])
    fcw_d = din("fcw", [128, 132])
    fcb_d = din("fcb", [128, 44])
    ident_d = din("ident", [128, 128])
    irep_d = din("irep", [128, 512])
    cos_d = din("cosT", [NTOK, 64])
    sin_d = din("sinT", [NTOK, 64])
    cvar = {}
    for v in ("P", "S"):
        cvar[v] = dict(CM=din("CM" + v, [128, 128]), UM=din("UM" + v, [128, 128]),
                       NEGs=din("NEGs" + v, [128, 128]), NEGT=din("NEGT" + v, [128, 128]),
                       DTr=din("DTr" + v, [128, 512]), rqd=din("rqd" + v, [128, 4]), rkt=din("rkt" + v, [128, 4]))
    e1_d = din("E1", [16 * 128])
    seq2_d = din("seq2", [128, 16])

    y_o = RowBufs(nc.dram_tensor("y", [NTOK, D], F32, kind="ExternalOutput").ap())
    dncp_o = dout("o_dnc_p", [3, 1536])
    dnp_o = dout("o_dn_p", [4, 128, 128])
    retp_o = dout("o_ret_p", [4, 128, 128])
    fcp_o = dout("o_fc_p", [2, 2 * DFF])
    dncs_o = dout("o_dnc_s", [48, 1536])
    dns_o = dout("o_dn_s", [NSAMP, 4, 128, 128])
    rets_o = dout("o_ret_s", [NSAMP, 4, 128, 128])
    fcs_o = dout("o_fc_s", [32, 2 * DFF])
    import os as _os
    _dbg = _os.environ.get("KDBG_OUT", "") == "1"
    _kind = dict(kind="ExternalOutput") if _dbg else {}
    h1_s = RowBufs(nc.dram_tensor("h1_scr", [NTOK, D], F32, **_kind).ap())
    h2_s = RowBufs(nc.dram_tensor("h2_scr", [NTOK, D], F32, **_kind).ap())

    lg = [float(np.log(1.0 - 2.0 ** (-5.0 - h))) for h in range(4)]
    wb_up = nc.dram_tensor("wb_up", [D, 2 * DFF], BF16).ap()
    wb_down = nc.dram_tensor("wb_down", [DFF, D], BF16).ap()
    wb_gate = nc.dram_tensor("wb_gate", [D, D], BF16).ap()
    wb_ple = nc.dram_tensor("wb_ple", [256, D], BF16).ap()

    with contextlib.ExitStack() as top:
        S = Sched(nc)
        S.open(top)
        k = K(nc, S)

        gcnt = [0]

        def mk_alloc(st):
            cnt = gcnt

            def sb(shape, dt=F32, n=0, name="t"):
                def one():
                    cnt[0] += 1
                    nm = "%s_%d" % (name, cnt[0])
                    return Buf(st.enter_context(nc.sbuf_tensor(nm, list(shape), dt)), nm)
                if n == 0:
                    return one()
                return Pool([one() for _ in range(n)])

            def psum_pool(**roles):
                assert sum(roles.values()) <= 8
                out = {}
                for role, n in roles.items():
                    bufs = []
                    for i in range(n):
                        cnt[0] += 1
                        nm = "ps_%d" % cnt[0]
                        bufs.append(Buf(st.enter_context(nc.psum_tensor(nm, [128, 512], F32)), nm, excl=True))
                    out[role] = Pool(bufs)
                return out
            return sb, psum_pool

        def load_w(st_sb, wd, kchunks, ncols, name):
            out = []
            for kk in range(kchunks):
                b = st_sb([128, ncols], BF16, name=name)
                k.dma("pool", b[:], wd[kk * 128:(kk + 1) * 128, :])
                out.append(b)
            return out

        def rstd_act(ss_in, n_inv, small, ncol):
            a = small.next()
            k.act(a[:, 0:ncol], ss_in, AF.Ln, scale=n_inv, bias=epsc[:, 0:1])
            r = small.next()
            k.act(r[:, 0:ncol], a[:, 0:ncol], AF.Exp, scale=-0.5)
            return r[:, 0:ncol]

        def rstd_of(ss_in, n_inv, small, mhalf, ncol):
            if USE_ACT_RSTD[0]:
                return rstd_act(ss_in, n_inv, small, ncol)
            a = small.next()
            k.ts("dve", a[:, 0:ncol], ss_in, n_inv, ALU.mult, EPS, ALU.add)
            r = small.next()
            k.tt("pool", r[:, 0:ncol], a[:, 0:ncol], mhalf[:, 0:ncol], ALU.pow)
            return r[:, 0:ncol]

        USE_ACT_RSTD = [False]
        epsc = None

        def phase_a(samp):
            USE_ACT_RSTD[0] = True
            try:
                phase_a_body(samp)
            finally:
                USE_ACT_RSTD[0] = False

        def phase_a_body(samp):
            nonlocal epsc
            with contextlib.ExitStack() as st:
                sb, psum_pool = mk_alloc(st)
                pp = psum_pool(E=3, M=2, R=2, L=1)
                cv = cvar["S" if samp else "P"]
                nst = NSAMP if samp else 1
                nlev = 3 if samp else 7
                Cc = LS if samp else 128
                cdec = [float(np.exp(Cc * lg[h])) for h in range(4)]
                nb = 1 if samp else 1
                w_inq, w_inr, w_out = WA
                identf = sb([128, 128]); k.dma("sp", identf[:], ident_d)
                identb = sb([128, 128], BF16); k.dma("pool", identb[:], ident_d)
                irep = sb([128, 512], BF16); k.dma("pool", irep[:], irep_d)
                onesf = sb([128, 128]); k.ms("pool", onesf[:], 1.0)
                nonesf = sb([128, 128]); k.ms("pool", nonesf[:], -1.0)
                mhalf = sb([128, 8]); k.ms("pool", mhalf[:], -0.5)
                epsc = sb([128, 1]); k.ms("pool", epsc[:], EPS)
                ecvp = sb([128, 256], n=2, name="ecv")
                CM = sb([128, 128]); k.dma("sp", CM[:], cv["CM"])
                UM = sb([128, 128]); k.dma("sp", UM[:], cv["UM"])
                NEGs = sb([128, 128], BF16); k.dma("pool", NEGs[:], cv["NEGs"])
                NEGT = sb([128, 128], BF16); k.dma("pool", NEGT[:], cv["NEGT"])
                DTr = sb([128, 512]); k.dma("sp", DTr[:], cv["DTr"])
                rqd = sb([128, 4]); k.dma("sp", rqd[:], cv["rqd"])
                rkt = sb([128, 4]); k.dma("sp", rkt[:], cv["rkt"])
                anw8 = sb([128, 8]); k.dma("sp", anw8[:], anw8_d)
                cw = sb([128, 48]); k.dma("sp", cw[:], dncw_d)
                alogb = sb([128, 4]); k.dma("sp", alogb[:], V(None, alog_d.ap.partition_broadcast(128)))
                dtbb = sb([128, 4]); k.dma("sp", dtbb[:], V(None, dtb_d.ap.partition_broadcast(128)))
                negA = sb([128, 4])
                k.act(negA[:], alogb[:], AF.Exp)
                k.ts("dve", negA[:], negA[:], -1.0, ALU.mult)
                dnw = sb([128, 512]); k.dma("sp", dnw[:], V(None, dnw4_d.ap.partition_broadcast(128)))
                retw = sb([128, 512]); k.dma("sp", retw[:], V(None, retw_d.ap.partition_broadcast(128)))
                if samp:
                    E1 = sb([128, 16 * 128], BF16)
                    k.dma("pool", E1[:], V(None, e1_d.ap.partition_broadcast(128)))
                    seq2 = sb([128, 16]); k.dma("sp", seq2[:], seq2_d)
                NT = 128 if samp else 256
                nbuf = 1 if samp else 2
                xtp = sb([128, D], n=1, name="xt")
                xrp = None if samp else sb([128, D], n=1, name="xr")
                abfp = sb([128, D], BF16, n=1, name="abf")
                aTp = sb([128, 8 * NT], BF16, n=1, name="aT")
                qkvT = sb([128, 12 * NT], BF16, name="qkvT")
                xprep = sb([128, 264], n=1 if samp else 2, name="xpre")
                ycvp = sb([128, 256], n=1 if samp else 2, name="ycv")
                cx_all = sb([128, 12 * 48], name="cx"); cx = Chunked(cx_all, 12, 48)
                junk = sb([128, D], BF16, name="junk"); junk = V(None, junk.h[:])
                small = sb([128, 24], n=24 if samp else 32, name="small")
                zsp = sb([128, 512], BF16, n=1 if samp else 2, name="zs")
                rqkf = sb([128, 1024], name="rqkf")
                qkf = sb([128, 1024], BF16, name="qkf")
                rgsp = sb([128, 512], BF16, n=1 if samp else 2, name="rgs")
                tmpp = sb([128, 512], n=2, name="tmp")
                ebf = sb([128, 512], BF16, n=4, name="ebf")
                e2r = sb([128, 512], BF16, n=3 if samp else PE2R, name="e2r")
                rbf = sb([128, 512], BF16, n=2 if samp else 4, name="rbf")
                rbf2 = sb([128, 512], BF16, n=5 if samp else PRBF2, name="rbf2")
                kqT = sb([128, 1024], BF16, n=2 if samp else PKQT, name="kqT")
                rtk = sb([128, 1024], BF16, n=2 if samp else PRTK, name="rtk")
                MG = sb([128, 512], name="MG")
                Xp = sb([128, 512], BF16, n=2, name="X")
                XTp = sb([128, 512], BF16, n=2, name="XT")
                IXp = sb([128, 512], BF16, n=2, name="IX")
                PTp = sb([128, 512], BF16, n=2 if samp else 3, name="PT")
                dexp = sb([128, 512], BF16, n=3, name="dexp")
                ofp = sb([128, 512], n=POF, name="of")
                mix = sb([128, D], BF16, name="mix")
                mixT = sb([128, D], BF16, name="mixT")
                h1p = None if samp else sb([128, D], n=1, name="h1")
                csp = sb([128, 128], n=nbuf, name="cs")
                if samp:
                    sbigp = sb([128, 16 * 128], n=2, name="sbig")
                    sbigbp = sb([128, 16 * 128], BF16, n=2, name="sbigb")
                    expp = sb([128, 16 * 128], BF16, n=2, name="exp")
                    dncT = sb([128, 12 * 48], name="dncT")
                else:
                    Sdn = sb([128, 512], name="Sdn"); k.ms("pool", Sdn[:], 0.0)
                    Sdnb = sb([128, 512], BF16, name="Sdnb"); k.ms("pool", Sdnb[:], 0.0)
                    Srt = sb([128, 512], name="Srt"); k.ms("pool", Srt[:], 0.0)
                    Srtb = sb([128, 512], BF16, name="Srtb"); k.ms("pool", Srtb[:], 0.0)
                    k.ms("pool", cx_all[:], 0.0)

                if samp:
                    blocks = [(SEQ, 128, NSAMP, LS)]
                else:
                    blocks = [(b * 256, 256, 1, 256) for b in range(SEQ // 256)]

                hsl = lambda h: slice(h * 128, (h + 1) * 128)
                b3 = lambda v: v.un(2).bc([128, 4, 128])
                r3 = lambda v: v.re("p (h d) -> p h d", h=4)

                if samp:
                    for gI in range(3):
                        tmp = tmpp.next()
                        k.dma("sp", tmp[0:48, :], stdnc_d[:, gI * 512:(gI + 1) * 512])
                        ps = pp["M"].next()
                        for i in range(4):
                            k.tp(ps[:, i * 48:(i + 1) * 48], tmp[0:48, i * 128:(i + 1) * 128], identf[0:48, 0:48])
                        k.cp("act", dncT[:, gI * 192:(gI + 1) * 192], ps[:, 0:192])

                last_xt = [None]

                def stage_a0(t0, NT, aT):
                    nsub = NT // 128
                    for j in range(nsub):
                        xt = xtp.next()
                        last_xt[0] = xt
                        k.dma("sp", xt[:], x_d[t0 + 128 * j:t0 + 128 * (j + 1), :])
                        ss = small.next()
                        k.act(junk, xt[:], AF.Square, accum=ss[:, 0:1])
                        rs = rstd_of(ss[:, 0:1], 1.0 / D, small, mhalf, 1)
                        abf = abfp.next()
                        k.ts("dve", abf[:], xt[:], rs, ALU.mult)
                        ps = pp["M"].next()
                        pb = ps[:].bitcast(BF16)
                        for kk in range(8):
                            k.tp(pb[:, hsl(kk)], abf[:, hsl(kk)], identb[:])
                        k.tt("dve", aT[:].re("p (k t) -> p k t", k=8)[:, :, 128 * j:128 * (j + 1)],
                             pb.re("p (k t) -> p k t", k=8), anw8[:].un(2).bc([128, 8, 128]), ALU.mult)

                try:
                  ck(0)
                  for bi, (t0, NT, nseq, L) in enumerate(blocks):
                    nsub = NT // 128
                    aT = aTp.next()
                    stage_a0(t0, NT, aT)
                    ck(1)
                    aT3 = aT[:].re("p (k t) -> p k t", k=8)
                    qkvT3 = qkvT[:].re("p (c t) -> p c t", c=12)
                    if not samp:
                        for _i in range(5):
                            if PRECAST:
                                dst_ap, src_v = PRECAST.pop(0)
                                S.op("pool", (lambda d_, s_: lambda e: e.dma_start(out=d_, in_=s_.ap))(dst_ap, src_v),
                                     reads=[aT.t], writes=[], dma=True, cost=2000.0 + 128 * int(dst_ap.shape[1]) * 4 / 120.0)
                    for fc in range(12):
                        ps = pp["M"].next()
                        for kk in range(8):
                            k.mm(ps[:, 0:NT], w_inq[kk][:, fc * 128:(fc + 1) * 128], aT3[:, kk, :], start=kk == 0, stop=kk == 7)
                        ck(1.1)
                        xp = xprep.next()
                        xv = xp[:, 0:nseq * (3 + L)].re("p (s l) -> p s l", s=nseq)
                        psv = ps[:, 0:NT].re("p (s l) -> p s l", s=nseq)
                        k.cp("act", xv[:, :, 3:3 + L], psv)
                        ck(1.2)
                        if samp:
                            k.cp("pool", xv[:, :, 0:3], dncT[:, fc * 48:(fc + 1) * 48].re("p (s j) -> p s j", s=nseq))
                        else:
                            k.cp("pool", xv[:, :, 0:3], cx.c(fc, 0, 3).un(1))
                        k.cp("pool", cx.c(fc, 0, 3 * nseq).re("p (s j) -> p s j", s=nseq), xv[:, :, L:L + 3])
                        ck(1.3)
                        y = ycvp.next()
                        yv = y[:, 0:NT].re("p (s l) -> p s l", s=nseq)
                        k.act(yv, psv, AF.Copy, scale=cw[:, fc * 4 + 3:fc * 4 + 4])
                        ck(1.4)
                        k.stt("dve", yv, xv[:, :, 2:2 + L], cw[:, fc * 4 + 2:fc * 4 + 3], yv, ALU.mult, ALU.add)
                        k.stt("dve", yv, xv[:, :, 1:1 + L], cw[:, fc * 4 + 1:fc * 4 + 2], yv, ALU.mult, ALU.add)
                        k.stt("dve", yv, xv[:, :, 0:L], cw[:, fc * 4 + 0:fc * 4 + 1], yv, ALU.mult, ALU.add)
                        ck(1.5)
                        ecv = ecvp.next()
                        k.act(ecv[:, 0:NT], y[:, 0:NT], AF.Exp, scale=-1.0)
                        k.act(ecv[:, 0:NT], ecv[:, 0:NT], AF.Ln, bias=1.0)
                        k.act(ecv[:, 0:NT], ecv[:, 0:NT], AF.Exp, scale=-1.0)
                        k.tt("dve", qkvT3[:, fc, :], y[:, 0:NT], ecv[:, 0:NT], ALU.mult)
                        ck(1.6)

                    ck(2)
                    for j in range(nsub):
                        js = slice(128 * j, 128 * (j + 1))
                        r0 = t0 + 128 * j
                        if samp:
                            xr = last_xt[0]
                        else:
                            xr = xrp.next()
                            k.dma("sp", xr[:], x_d[r0:r0 + 128, :])
                        cs = csp.next()
                        k.dma("sp", cs[:, 0:64], cos_d[r0:r0 + 128, :])
                        k.dma("sp", cs[:, 64:128], sin_d[r0:r0 + 128, :])

                        def proj(c0, n, role="M"):
                            ps = pp[role].next()
                            for kk in range(8):
                                k.mm(ps[:, 0:n], aT3[:, kk, js], w_inr[kk][:, c0 - 1536:c0 - 1536 + n], start=kk == 0, stop=kk == 7)
                            return ps

                        ck(2.1)
                        psqk = pp["E"].next(); pqk = psqk[:].bitcast(BF16)
                        for i in range(8):
                            k.tp(pqk[:, hsl(i)], qkvT3[:, i, js], identb[:])
                        psv_ = pp["E"].next(); pv = psv_[:].bitcast(BF16)
                        for h in range(4):
                            k.tp(pv[:, hsl(h)], qkvT3[:, 8 + h, js], identb[:])
                        ck(2.2)
                        k.cp("act", qkf[:], pqk)
                        ck(2.3)
                        st = small.next()
                        for i in range(8):
                            k.act(junk[:, 0:128], qkf[:, hsl(i)], AF.Square, accum=st[:, i:i + 1])
                        ck(2.4)
                        rs = rstd_of(st[:, 0:8], 1.0, small, mhalf, 8)
                        ck(3)
                        psba = proj(C_B, 8, "E")
                        sm = small.next()
                        k.act(sm[:, 0:4], psba[:, 0:4], AF.Exp, scale=-1.0)
                        k.ts("dve", sm[:, 0:4], sm[:, 0:4], 1.0, ALU.add)
                        beta_t = small.next(); beta = beta_t[:, 0:4]
                        k.recip(beta, sm[:, 0:4])
                        k.tt("dve", sm[:, 4:8], psba[:, 4:8], dtbb[:], ALU.add)
                        k.act(sm[:, 8:12], sm[:, 4:8], AF.Exp)
                        k.act(sm[:, 12:16], sm[:, 8:12], AF.Ln, bias=1.0)
                        g_t = small.next(); g = g_t[:, 0:4]
                        k.tt("dve", g, sm[:, 12:16], negA[:], ALU.mult)
                        ck(4)
                        psg = pp["E"].next()
                        k.mm(psg[:, 0:4], CM[:], g)
                        k.mm(psg[:, 4:8], UM[:], g)
                        if samp:
                            Gs = small.next() if False else tmpp.next()
                            k.tt("pool", Gs[:, 0:64].re("p (h s) -> p h s", h=4), g.un(2).bc([128, 4, 16]),
                                 seq2[:].un(1).bc([128, 4, 16]), ALU.mult)
                            k.mm(psg[:, 8:72], onesf[:], Gs[:, 0:64])
                        else:
                            k.mm(psg[:, 8:12], onesf[:], g)
                        nex = 8 + 4 * nst
                        ex_t = sb_ex.next()
                        ex = ex_t[:, 0:nex]
                        k.act(ex, psg[:, 0:nex], AF.Exp)
                        sc_t = small.next(); sc = sc_t
                        k.tt("dve", sc[:, 0:4], rs[:, 4:8], beta, ALU.mult)
                        k.tt("dve", sc[:, 4:8], rs[:, 4:8], ex[:, 4:8], ALU.mult)
                        k.ts("dve", sc[:, 8:12], rs[:, 0:4], 128.0 ** -0.5, ALU.mult)
                        k.tt("dve", sc[:, 12:16], sc[:, 8:12], ex[:, 0:4], ALU.mult)
                        k.stt("dve", sc[:, 16:20], beta, -1.0, ex[:, 0:4], ALU.mult, ALU.mult)
                        kf = r3(qkf[:, 512:1024]); qf = r3(qkf[:, 0:512])
                        Kn = ebf.next(); KB = ebf.next(); Qs = ebf.next(); Qd = ebf.next(); Kt = e2r.next(); Vb = e2r.next()
                        k.tt("pool", r3(Kn[:]), kf, b3(rs[:, 4:8]), ALU.mult)
                        k.tt("dve", r3(KB[:]), kf, b3(sc[:, 0:4]), ALU.mult)
                        k.tt("pool", r3(Kt[:]), kf, b3(sc[:, 4:8]), ALU.mult)
                        k.tt("dve", r3(Qs[:]), qf, b3(sc[:, 8:12]), ALU.mult)
                        k.tt("pool", r3(Qd[:]), qf, b3(sc[:, 12:16]), ALU.mult)
                        k.tt("dve", r3(Vb[:]), r3(pv[:, 0:512]), b3(beta), ALU.mult)
                        ck(5)
                        KKT = kqT.next(); QQT = kqT.next()
                        psA = pp["E"].next(); pA = psA[:].bitcast(BF16)
                        for h in range(4):
                            k.tp(pA[:, hsl(h)], Kn[:, hsl(h)], identb[:])
                        for h in range(4):
                            k.tp(pA[:, hsl(4 + h)], KB[:, hsl(h)], identb[:])
                        k.cp("act", KKT[:], pA)
                        psB = pp["E"].next(); pB = psB[:].bitcast(BF16)
                        for h in range(4):
                            k.tp(pB[:, hsl(h)], Qs[:, hsl(h)], identb[:])
                        for h in range(4):
                            k.tp(pB[:, hsl(4 + h)], Qd[:, hsl(h)], identb[:])
                        k.cp("dve", QQT[:], pB)
                        KnT = lambda h: KKT[:, hsl(h)]
                        KBT = lambda h: KKT[:, hsl(4 + h)]
                        QsT = lambda h: QQT[:, hsl(h)]
                        QdT = lambda h: QQT[:, hsl(4 + h)]
                        ck(6)
                        k.tt("pool", r3(MG[:]), CM[:].un(1).bc([128, 4, 128]), g.un(2).bc([128, 4, 128]), ALU.mult)
                        psD = pp["E"].next()
                        for h in range(4):
                            k.mm(psD[:, hsl(h)], MG[:, hsl(h)], onesf[:], True, False)
                            k.mm(psD[:, hsl(h)], nonesf[:], MG[:, hsl(h)], False, False)
                            k.mm(psD[:, hsl(h)], identb[:], NEGs[:], False, True)
                        Ds = dexp.next(); DT = dexp.next(); DTs = dexp.next()
                        k.act(Ds[:], psD[:], AF.Exp)
                        psDT = pp["E"].next(); pDT = psDT[:].bitcast(BF16)
                        for h in range(4):
                            k.tp(pDT[:, hsl(h)], Ds[:, hsl(h)], identb[:])
                        k.cp("act", DTs[:], pDT[:, 0:512])
                        k.tt("dve", DT[:], pDT[:, 0:512], irep[:], ALU.add)
                        ck(7)
                        psA_ = pp["E"].next(); psAT = pp["E"].next(); psKQ = pp["E"].next()
                        for h in range(4):
                            k.mm(psA_[:, hsl(h)], KBT(h), KnT(h))
                        for h in range(4):
                            k.mm(psAT[:, hsl(h)], KnT(h), KBT(h))
                        for h in range(4):
                            k.mm(psKQ[:, hsl(h)], KnT(h), QsT(h))
                        X = Xp.next(); XT = XTp.next(); PT = PTp.next(); QKDT = e2r.next()
                        k.stt("dve", X[:], psA_[:], -1.0, Ds[:], ALU.mult, ALU.mult)
                        k.stt("dve", XT[:], psAT[:], -1.0, DTs[:], ALU.mult, ALU.mult)
                        k.tt("dve", QKDT[:], psKQ[:], DT[:], ALU.mult)
                        k.tt("pool", PT[:], XT[:], irep[:], ALU.add)
                        ck(8)
                        for lv in range(1, nlev):
                            psX = pp["E"].next()
                            for h in range(4):
                                k.mm(psX[:, hsl(h)], XT[:, hsl(h)], X[:, hsl(h)])
                            last = lv == nlev - 1
                            IX = IXp.next()
                            k.tt("dve", IX[:], psX[:], irep[:], ALU.add)
                            if not last:
                                Xn = Xp.next()
                                k.cp("act", Xn[:], psX[:])
                                psXT = pp["E"].next()
                                for h in range(4):
                                    k.mm(psXT[:, hsl(h)], X[:, hsl(h)], XT[:, hsl(h)])
                                XTn = XTp.next()
                                k.cp("act", XTn[:], psXT[:])
                            psP = pp["E"].next()
                            for h in range(4):
                                k.mm(psP[:, hsl(h)], IX[:, hsl(h)], PT[:, hsl(h)])
                            PTn = PTp.next()
                            k.cp("act", PTn[:], psP[:])
                            PT = PTn
                            if not last:
                                X, XT = Xn, XTn
                        ck(9)
                        zs = zsp.next(); rgs = rgsp.next()
                        psz = proj(C_Z, 512)
                        sgt = tmpp.next()
                        k.act(sgt[:], psz[:], AF.Exp, scale=-1.0)
                        k.act(sgt[:], sgt[:], AF.Ln, bias=1.0)
                        k.act(sgt[:], sgt[:], AF.Exp, scale=-1.0)
                        k.tt("dve", zs[:], psz[:], sgt[:], ALU.mult)
                        psrq = proj(C_RQ, 512)
                        k.cp("act", rqkf[:, 0:512], psrq[:])
                        psrk = proj(C_RK, 512)
                        k.cp("act", rqkf[:, 512:1024], psrk[:])
                        RV = rbf2.next()
                        psrv = proj(C_RV, 512)
                        k.cp("act", RV[:], psrv[:])
                        psrg = proj(C_RG, 512)
                        sgt = tmpp.next()
                        k.act(sgt[:], psrg[:], AF.Exp, scale=-1.0)
                        k.act(sgt[:], sgt[:], AF.Ln, bias=1.0)
                        k.act(sgt[:], sgt[:], AF.Exp, scale=-1.0)
                        k.tt("dve", rgs[:], psrg[:], sgt[:], ALU.mult)

                        ck(10)
                        of = ofp.next()
                        R = rbf.next(); vn = rbf.next()
                        headsets = [[h] for h in range(4)] if samp else [[0, 1, 2, 3]]
                        for hs in headsets:
                            cols = slice(hs[0] * 128, (hs[-1] + 1) * 128)
                            if samp:
                                h = hs[0]
                                sbig = sbigp.next(); sbigb = sbigbp.next()
                                for _q in range(4):
                                    k.dma("sp", sbig[:, _q * 512:(_q + 1) * 512].re("p (s v) -> p s v", s=4), V(None, stdn_d.ap[4 * _q:4 * _q + 4, h].rearrange("s k v -> k s v")))
                                k.cp("pool", sbigb[:], sbig[:])
                                KnE = expp.next(); QdE = expp.next()
                                e3 = lambda v: v.re("p (s c) -> p s c", s=16)
                                k.tt("pool", e3(KnE[:]), KnT(h).un(1).bc([128, 16, 128]), e3(E1[:]), ALU.mult)
                                k.tt("dve", e3(QdE[:]), QdT(h).un(1).bc([128, 16, 128]), e3(E1[:]), ALU.mult)
                                Sf = lambda hh, s, sbig=sbig: sbig[:, hsl(s)]
                                Sb = lambda hh, s, sbigb=sbigb: sbigb[:, hsl(s)]
                                lK = lambda hh, s: KnE[:, hsl(s)]
                                lQ = lambda hh, s: QdE[:, hsl(s)]
                            else:
                                Sf = lambda hh, s: Sdn[:, hsl(hh)]
                                Sb = lambda hh, s: Sdnb[:, hsl(hh)]
                                lK = lambda hh, s: KnT(hh)
                                lQ = lambda hh, s: QdT(hh)
                                lT = lambda hh, s: Kt[:, hsl(hh)]
                            psKS = pp["R"].next()
                            for h in hs:
                                for s in range(nst):
                                    k.mm(psKS[:, hsl(h)], lK(h, s), Sb(h, s), s == 0, s == nst - 1)
                            if samp:
                                KtE = expp.next()
                                k.tt("pool", e3(KtE[:]), Kt[:, hsl(hs[0])].un(1).bc([128, 16, 128]), seq2[:].un(2).bc([128, 16, 128]), ALU.mult)
                                lT = lambda hh, s: KtE[:, hsl(s)]
                            for h in hs:
                                k.stt("dve", R[:, hsl(h)], psKS[:, hsl(h)], sc[:, 16 + h:17 + h], Vb[:, hsl(h)], ALU.mult, ALU.add)
                            psV = pp["R"].next()
                            for h in hs:
                                k.mm(psV[:, hsl(h)], PT[:, hsl(h)], R[:, hsl(h)])
                            k.cp("act", vn[:, cols], psV[:, cols])
                            psO = pp["R"].next()
                            for h in hs:
                                for s in range(nst):
                                    k.mm(psO[:, hsl(h)], lQ(h, s), Sb(h, s), s == 0, False)
                                k.mm(psO[:, hsl(h)], QKDT[:, hsl(h)], vn[:, hsl(h)], False, True)
                            k.cp("act", of[:, cols], psO[:, cols])
                            pairs = [(h, s) for h in hs for s in range(nst)]
                            for g0 in range(0, len(pairs), 4):
                                grp = pairs[g0:g0 + 4]
                                psS = pp["R"].next()
                                for i, (h, s) in enumerate(grp):
                                    k.mm(psS[:, hsl(i)], lT(h, s), vn[:, hsl(h)])
                                for i, (h, s) in enumerate(grp):
                                    k.stt("dve", Sf(h, s), Sf(h, s), ex[:, 8 + h * nst + s:9 + h * nst + s], psS[:, hsl(i)], ALU.mult, ALU.add)
                            if samp:
                                for _q in range(4):
                                    k.dma("sp", V(Buf(dns_o.h, "dns%d_%d" % (hs[0], _q)), dns_o.h[4 * _q:4 * _q + 4, hs[0]].rearrange("s k v -> k s v")), sbig[:, _q * 512:(_q + 1) * 512].re("p (s v) -> p s v", s=4))
                            else:
                                k.cp("act", Sdnb[:], Sdn[:])
                        ck(11)
                        st = small.next()
                        for h in range(4):
                            k.act(junk[:, 0:128], of[:, hsl(h)], AF.Square, accum=st[:, h:h + 1])
                        rso = rstd_of(st[:, 0:4], 1.0 / 128, small, mhalf, 4)
                        k.tt("dve", r3(of[:]), r3(of[:]), b3(rso), ALU.mult)
                        k.tt("pool", of[:], of[:], dnw[:], ALU.mult)
                        k.tt("dve", mix[:, 0:512], of[:], zs[:], ALU.mult)

                        ck(12)
                        rqkb = rtk.next()
                        g4 = lambda v: v.re("p (g i two) -> p g i two", g=8, two=2)
                        x1 = g4(rqkf[:])[:, :, :, 0]; x2 = g4(rqkf[:])[:, :, :, 1]
                        o1 = g4(rqkb[:])[:, :, :, 0]; o2 = g4(rqkb[:])[:, :, :, 1]
                        cosb = cs[:, 0:64].un(1).bc([128, 8, 64]); sinb = cs[:, 64:128].un(1).bc([128, 8, 64])
                        t8 = lambda v: v.re("p (g i) -> p g i", g=8)
                        ta = tmpp.next(); tb = tmpp.next()
                        k.tt("dve", t8(ta[:]), x1, cosb, ALU.mult)
                        k.tt("pool", t8(tb[:]), x2, sinb, ALU.mult)
                        k.tt("dve", o1, t8(ta[:]), t8(tb[:]), ALU.subtract)
                        ta = tmpp.next(); tb = tmpp.next()
                        k.tt("pool", t8(ta[:]), x1, sinb, ALU.mult)
                        k.tt("dve", t8(tb[:]), x2, cosb, ALU.mult)
                        k.tt("pool", o2, t8(ta[:]), t8(tb[:]), ALU.add)
                        RQd = rbf2.next(); RKt = rbf2.next()
                        k.tt("dve", r3(RQd[:]), r3(rqkb[:, 0:512]), b3(rqd[:]), ALU.mult)
                        k.tt("pool", r3(RKt[:]), r3(rqkb[:, 512:1024]), b3(rkt[:]), ALU.mult)
                        RQKT = rtk.next(); RQdT_t = rbf2.next()
                        psR1 = pp["M"].next(); pR1 = psR1[:].bitcast(BF16)
                        for i in range(8):
                            k.tp(pR1[:, hsl(i)], rqkb[:, hsl(i)], identb[:])
                        k.cp("act", RQKT[:], pR1)
                        psR2 = pp["M"].next(); pR2 = psR2[:].bitcast(BF16)
                        for h in range(4):
                            k.tp(pR2[:, hsl(h)], RQd[:, hsl(h)], identb[:])
                        k.cp("dve", RQdT_t[:], pR2[:, 0:512])
                        psKQr = pp["M"].next()
                        for h in range(4):
                            k.mm(psKQr[:, hsl(h)], RQKT[:, hsl(4 + h)], RQKT[:, hsl(h)])
                        QKDTr = rbf2.next()
                        k.tt("dve", QKDTr[:], psKQr[:], DTr[:], ALU.mult)
                        orf = ofp.next()
                        for hs in headsets:
                            cols = slice(hs[0] * 128, (hs[-1] + 1) * 128)
                            if samp:
                                h = hs[0]
                                sbig = sbigp.next(); sbigb = sbigbp.next()
                                for _q in range(4):
                                    k.dma("sp", sbig[:, _q * 512:(_q + 1) * 512].re("p (s v) -> p s v", s=4), V(None, stret_d.ap[4 * _q:4 * _q + 4, h].rearrange("s k v -> k s v")))
                                k.cp("pool", sbigb[:], sbig[:])
                                QdE = expp.next(); KtE = expp.next()
                                e3 = lambda v: v.re("p (s c) -> p s c", s=16)
                                k.tt("dve", e3(QdE[:]), RQdT_t[:, hsl(h)].un(1).bc([128, 16, 128]), e3(E1[:]), ALU.mult)
                                k.tt("pool", e3(KtE[:]), RKt[:, hsl(h)].un(1).bc([128, 16, 128]), seq2[:].un(2).bc([128, 16, 128]), ALU.mult)
                                Sf = lambda hh, s, sbig=sbig: sbig[:, hsl(s)]
                                Sb = lambda hh, s, sbigb=sbigb: sbigb[:, hsl(s)]
                                lQ = lambda hh, s: QdE[:, hsl(s)]
                                lT = lambda hh, s: KtE[:, hsl(s)]
                            else:
                                Sf = lambda hh, s: Srt[:, hsl(hh)]
                                Sb = lambda hh, s: Srtb[:, hsl(hh)]
                                lQ = lambda hh, s: RQdT_t[:, hsl(hh)]
                                lT = lambda hh, s: RKt[:, hsl(hh)]
                            psO = pp["R"].next()
                            for h in hs:
                                for s in range(nst):
                                    k.mm(psO[:, hsl(h)], lQ(h, s), Sb(h, s), s == 0, False)
                                k.mm(psO[:, hsl(h)], QKDTr[:, hsl(h)], RV[:, hsl(h)], False, True)
                            k.cp("act", orf[:, cols], psO[:, cols])
                            pairs = [(h, s) for h in hs for s in range(nst)]
                            for g0 in range(0, len(pairs), 4):
                                grp = pairs[g0:g0 + 4]
                                psS = pp["R"].next()
                                for i, (h, s) in enumerate(grp):
                                    k.mm(psS[:, hsl(i)], lT(h, s), RV[:, hsl(h)])
                                for i, (h, s) in enumerate(grp):
                                    k.stt("dve", Sf(h, s), Sf(h, s), cdec[h], psS[:, hsl(i)], ALU.mult, ALU.add)
                            if samp:
                                for _q in range(4):
                                    k.dma("sp", V(Buf(rets_o.h, "rets%d_%d" % (hs[0], _q)), rets_o.h[4 * _q:4 * _q + 4, hs[0]].rearrange("s k v -> k s v")), sbig[:, _q * 512:(_q + 1) * 512].re("p (s v) -> p s v", s=4))
                            else:
                                k.cp("act", Srtb[:], Srt[:])
                        ck(13)
                        st = small.next()
                        for h in range(4):
                            k.act(junk[:, 0:128], orf[:, hsl(h)], AF.Copy, accum=st[:, h:h + 1])
                        for h in range(4):
                            k.act(junk[:, 128:256], orf[:, hsl(h)], AF.Square, accum=st[:, 4 + h:5 + h])
                        s2 = small.next()
                        k.ts("dve", s2[:, 0:4], st[:, 0:4], 1.0 / 128, ALU.mult)
                        k.tt("dve", s2[:, 4:8], s2[:, 0:4], s2[:, 0:4], ALU.mult)
                        k.stt("dve", s2[:, 8:12], st[:, 4:8], 1.0 / 128, s2[:, 4:8], ALU.mult, ALU.subtract)
                        rsr = rstd_of(s2[:, 8:12], 1.0, small, mhalf, 4)
                        k.tt("dve", r3(orf[:]), r3(orf[:]), b3(s2[:, 0:4]), ALU.subtract)
                        k.tt("pool", r3(orf[:]), r3(orf[:]), b3(rsr), ALU.mult)
                        k.tt("dve", orf[:], orf[:], retw[:], ALU.mult)
                        k.tt("pool", mix[:, 512:1024], orf[:], rgs[:], ALU.mult)
                        ck(14)
                        psM = pp["L"].next(); pM = psM[:].bitcast(BF16)
                        for kk in range(8):
                            k.tp(pM[:, hsl(kk)], mix[:, hsl(kk)], identb[:])
                        k.cp("act", mixT[:], pM)
                        h1 = xr if samp else h1p.next()
                        for half in range(2):
                            psH = pp["L"].next()
                            for kk in range(8):
                                k.mm(psH[:], mixT[:, hsl(kk)], w_out[kk][:, half * 512:(half + 1) * 512], kk == 0, kk == 7)
                            k.tt("dve", h1[:, half * 512:(half + 1) * 512], psH[:], xr[:, half * 512:(half + 1) * 512], ALU.add)
                        k.dma("sp", h1_s.rows(r0), h1[:])

                except StopBuild:
                    pass
                n3 = 3 * (NSAMP if samp else 1)
                dnc_o = dncs_o if samp else dncp_o
                for gI in range(3):
                    ps = pp["L"].next()
                    for i in range(4):
                        fc = gI * 4 + i
                        k.tp(ps[0:n3, hsl(i)], cx.c(fc, 0, n3), identf[:])
                    tmp = tmpp.next()
                    k.cp("act", tmp[0:n3, :], ps[0:n3, :])
                    k.dma("sp", V(Buf(dnc_o.h, "dnc%d" % gI), dnc_o.h[:, gI * 512:(gI + 1) * 512]), tmp[0:n3, :])
                if not samp:
                    k.dma("sp", V(dnp_o, dnp_o.h.rearrange("h k v -> k h v")), Sdn[:].re("p (h v) -> p h v", h=4))
                    k.dma("sp", V(retp_o, retp_o.h.rearrange("h k v -> k h v")), Srt[:].re("p (h v) -> p h v", h=4))
                S.emit_phase()

        sb_ex = None

        WA = None

        def phase_a_wrap(samp):
            nonlocal sb_ex
            with contextlib.ExitStack() as st0:
                sb0, _ = mk_alloc(st0)
                sb_ex = sb0([128, 72], n=3, name="ex")
                phase_a(samp)

        def load_cols(st_sb, wd, kchunks, c0, c1, name, q="pool"):
            out = []
            for kk in range(kchunks):
                b = st_sb([128, c1 - c0], BF16, name=name)
                k.dma(q, b[:], wd[kk * 128:(kk + 1) * 128, c0:c1])
                out.append(b)
            return out

        import os
        _ph = os.environ.get("KDBG_PH", "ABCD")
        with contextlib.ExitStack() as stw:
            sbw, _ = mk_alloc(stw)
            if "A" in _ph or "B" in _ph:
                WA = (load_cols(sbw, w_in_d, 8, 0, 1536, "w_inq"), load_cols(sbw, w_in_d, 8, 1536, DIN, "w_inr"),
                      load_cols(sbw, w_out_d, 8, 0, D, "w_out"))
            PRECAST = []
            if "A" in _ph:
                for (src, dst, nk) in ((w_up_d, wb_up, 8), (w_down_d, wb_down, 22), (w_gate_d, wb_gate, 8), (w_ple_d, wb_ple, 2)):
                    for kk in range(nk):
                        PRECAST.append((dst[kk * 128:(kk + 1) * 128, :], src[kk * 128:(kk + 1) * 128, :]))
            if "A" in _ph:
                phase_a_wrap(False)
            if "B" in _ph:
                phase_a_wrap(True)

        def phase_b():
            with contextlib.ExitStack() as st:
                sb, psum_pool = mk_alloc(st)
                pp = psum_pool(tr=2, up=4, down=2)
                GP = [(0, 11), (11, 22)]
                w_up_g = []
                for (p0, p1) in GP:
                    ug = load_cols(sb, V(None, wb_up), 8, p0 * 128, p1 * 128, "w_upg", q="sp")
                    uv = load_cols(sb, V(None, wb_up), 8, DFF + p0 * 128, DFF + p1 * 128, "w_upv", q="act")
                    w_up_g.append((p0, p1, ug, uv))

                def w_up_sl(kk, fc):
                    part, i = (0, fc) if fc < 22 else (1, fc - 22)
                    for (p0, p1, ug, uv) in w_up_g:
                        if p0 <= i < p1:
                            return (ug, uv)[part][kk][:, (i - p0) * 128:(i - p0 + 1) * 128]
                w_down = load_cols(sb, V(None, wb_down), 22, 0, D, "w_down", q="sp")
                identf = sb([128, 128]); k.dma("sp", identf[:], ident_d)
                identb = sb([128, 128], BF16); k.dma("pool", identb[:], ident_d)
                mhalf = sb([128, 8]); k.ms("pool", mhalf[:], -0.5)
                fnw8 = sb([128, 8]); k.dma("sp", fnw8[:], fnw8_d)
                fcw = sb([128, 132]); k.dma("sp", fcw[:], fcw_d)
                fcb = sb([128, 44]); k.dma("sp", fcb[:], fcb_d)
                NTm = 256
                import os as _o
                htp = sb([128, D], n=int(_o.environ.get("KB_HT", "3")), name="ht")
                mbfp = sb([128, D], BF16, n=2, name="mbf")
                import os as _o
                mTp = sb([128, 8 * NTm], BF16, n=int(_o.environ.get("KB_MT", "1")), name="mT")
                actTp = sb([128, 22 * NTm], BF16, n=int(_o.environ.get("KB_ACTT", "1")), name="actT")
                uprep = sb([128, 264], n=3, name="upre")
                ycp = sb([128, 256], n=3, name="yc")
                sgp = sb([128, 256], n=2, name="sg")
                cf_all = sb([128, 44 * 32], name="cf"); k.ms("pool", cf_all[:], 0.0); cf = Chunked(cf_all, 44, 32)
                fcT = sb([128, 44 * 32], name="fcT")
                junk = sb([128, D], BF16, name="junk"); junk = V(None, junk.h[:])
                small = sb([128, 8], n=8, name="small")
                h2p = sb([128, D], n=int(_o.environ.get("KB_H2", "2")), name="h2")
                tmpp = sb([128, 512], n=int(_o.environ.get("KB_TMP", "2")), name="tmp")
                hsl = lambda h: slice(h * 128, (h + 1) * 128)

                for gI in range(11):
                    tmp = tmpp.next()
                    k.dma("sp", tmp[0:32, :], stfc_d[:, gI * 512:(gI + 1) * 512])
                    ps = pp["tr"].next()
                    for i in range(4):
                        k.tp(ps[:, i * 32:(i + 1) * 32], tmp[0:32, hsl(i)], identf[0:32, 0:32])
                    k.cp("act", fcT[:, gI * 128:(gI + 1) * 128], ps[:, 0:128])

                blocks = [(b * 256, 256, 1, 256, False) for b in range(SEQ // 256)] + [(SEQ, 128, NSAMP, LS, True)]
                for (t0, NT, nseq, L, samp) in blocks:
                    nsub = NT // 128
                    mT = mTp.next(); actT = actTp.next()
                    mT3 = mT[:].re("p (k t) -> p k t", k=8)
                    hts = []
                    for j in range(nsub):
                        r0 = t0 + 128 * j
                        ht = htp.next(); hts.append(ht)
                        k.dma("sp", ht[:], h1_s.rows(r0))
                        ss = small.next()
                        k.act(junk, ht[:], AF.Square, accum=ss[:, 0:1])
                        rs = rstd_of(ss[:, 0:1], 1.0 / D, small, mhalf, 1)
                        mbf = mbfp.next()
                        k.ts("dve", mbf[:], ht[:], rs, ALU.mult)
                        ps = pp["tr"].next(); pb = ps[:].bitcast(BF16)
                        for kk in range(8):
                            k.tp(pb[:, hsl(kk)], mbf[:, hsl(kk)], identb[:])
                        k.tt("dve", mT3[:, :, 128 * j:128 * (j + 1)], pb.re("p (k t) -> p k t", k=8),
                             fnw8[:].un(2).bc([128, 8, 128]), ALU.mult)
                    actT3 = actT[:].re("p (c t) -> p c t", c=22)
                    for i in range(22):
                        ys = []
                        for fc in (i, 22 + i):
                            ps = pp["up"].next()
                            for kk in range(8):
                                k.mm(ps[:, 0:NT], w_up_sl(kk, fc), mT3[:, kk, 0:NT], kk == 0, kk == 7)
                            up = uprep.next()
                            uv = up[:, 0:nseq * (2 + L)].re("p (s l) -> p s l", s=nseq)
                            psv = ps[:, 0:NT].re("p (s l) -> p s l", s=nseq)
                            k.cp("act", uv[:, :, 2:2 + L], psv)
                            if samp:
                                k.cp("pool", uv[:, :, 0:2], fcT[:, fc * 32:(fc + 1) * 32].re("p (s j) -> p s j", s=nseq))
                            else:
                                k.cp("pool", uv[:, :, 0:2], cf.c(fc, 0, 2).un(1))
                            k.cp("pool", cf.c(fc, 0, 2 * nseq).re("p (s j) -> p s j", s=nseq), uv[:, :, L:L + 2])
                            y = ycp.next(); ys.append(y)
                            yv = y[:, 0:NT].re("p (s l) -> p s l", s=nseq)
                            k.act(yv, psv, AF.Identity, scale=fcw[:, fc * 3 + 2:fc * 3 + 3], bias=fcb[:, fc:fc + 1])
                            k.stt("dve", yv, uv[:, :, 1:1 + L], fcw[:, fc * 3 + 1:fc * 3 + 2], yv, ALU.mult, ALU.add)
                            k.stt("dve", yv, uv[:, :, 0:L], fcw[:, fc * 3 + 0:fc * 3 + 1], yv, ALU.mult, ALU.add)
                        sg = sgp.next()
                        k.act(sg[:, 0:NT], ys[0][:, 0:NT], AF.Silu)
                        k.tt("dve", actT3[:, i, 0:NT], sg[:, 0:NT], ys[1][:, 0:NT], ALU.mult)
                    for j in range(nsub):
                        r0 = t0 + 128 * j
                        h2 = h2p.next()
                        for half in range(2):
                            psH = pp["down"].next()
                            for c in range(22):
                                k.mm(psH[:], actT3[:, c, 128 * j:128 * (j + 1)], w_down[c][:, half * 512:(half + 1) * 512], c == 0, c == 21)
                            k.tt("dve", h2[:, half * 512:(half + 1) * 512], psH[:], hts[j][:, half * 512:(half + 1) * 512], ALU.add)
                        k.dma("sp", h2_s.rows(r0), h2[:])
                    last_prompt = (not samp) and t0 + NT == SEQ
                    if last_prompt or samp:
                        n2 = 2 * nseq
                        fo = fcs_o if samp else fcp_o
                        for gI in range(11):
                            ps = pp["tr"].next()
                            for i in range(4):
                                fc = gI * 4 + i
                                k.tp(ps[0:n2, hsl(i)], cf.c(fc, 0, n2), identf[:])
                            tmp = tmpp.next()
                            k.cp("act", tmp[0:n2, :], ps[0:n2, :])
                            k.dma("sp", V(Buf(fo.h, "fo%d" % gI), fo.h[:, gI * 512:(gI + 1) * 512]), tmp[0:n2, :])
                S.emit_phase()

        if "C" in _ph:
            phase_b()

        def phase_c():
            with contextlib.ExitStack() as st:
                sb, psum_pool = mk_alloc(st)
                pp = psum_pool(tr=2, mm=6)
                w_gate = load_cols(sb, V(None, wb_gate), 8, 0, D, "w_gate", q="sp")
                w_ple = load_cols(sb, V(None, wb_ple), 2, 0, D, "w_ple", q="act")
                identb = sb([128, 128], BF16); k.dma("pool", identb[:], ident_d)
                mhalf = sb([128, 8]); k.ms("pool", mhalf[:], -0.5)
                pnw8 = sb([128, 8]); k.dma("sp", pnw8[:], pnw8_d)
                finw = sb([128, D]); k.dma("sp", finw[:], V(None, finw_d.ap.partition_broadcast(128)))
                h2p = sb([128, D], n=3, name="h2")
                ptp = sb([128, 256], n=3, name="pt")
                pbp = sb([128, 256], BF16, n=2, name="pb")
                nbfp = sb([128, D], BF16, n=2, name="nbf")
                nTp = sb([128, D], BF16, n=2, name="nT")
                pTp = sb([128, 256], BF16, n=2, name="pT")
                tgp = sb([128, D], n=2, name="tg")
                h3p = sb([128, D], n=2, name="h3")
                yp = sb([128, D], n=2, name="y")
                junk = sb([128, D], BF16, name="junk"); junk = V(None, junk.h[:])
                small = sb([128, 8], n=12, name="small")
                hsl = lambda h: slice(h * 128, (h + 1) * 128)
                for it in range(NTOK // 128):
                    r0 = it * 128
                    h2 = h2p.next()
                    k.dma("sp", h2[:], h2_s.rows(r0))
                    pt = ptp.next()
                    k.dma("sp", pt[:], p_d[r0:r0 + 128, :])
                    ss = small.next()
                    k.act(junk, h2[:], AF.Square, accum=ss[:, 0:1])
                    rs = rstd_of(ss[:, 0:1], 1.0 / D, small, mhalf, 1)
                    nbf = nbfp.next()
                    k.ts("dve", nbf[:], h2[:], rs, ALU.mult)
                    ps = pp["tr"].next(); pb = ps[:].bitcast(BF16)
                    for kk in range(8):
                        k.tp(pb[:, hsl(kk)], nbf[:, hsl(kk)], identb[:])
                    nT = nTp.next()
                    k.tt("dve", nT[:].re("p (k t) -> p k t", k=8), pb.re("p (k t) -> p k t", k=8),
                         pnw8[:].un(2).bc([128, 8, 128]), ALU.mult)
                    pbf = pbp.next()
                    k.cp("pool", pbf[:], pt[:])
                    ps2 = pp["tr"].next(); pb2 = ps2[:].bitcast(BF16)
                    for kk in range(2):
                        k.tp(pb2[:, hsl(kk)], pbf[:, hsl(kk)], identb[:])
                    pT = pTp.next()
                    k.cp("act", pT[:], pb2[:, 0:256])
                    tg = tgp.next(); h3 = h3p.next()
                    for half in range(2):
                        hs_ = slice(half * 512, (half + 1) * 512)
                        psG = pp["mm"].next()
                        for kk in range(8):
                            k.mm(psG[:], nT[:, hsl(kk)], w_gate[kk][:, hs_], kk == 0, kk == 7)
                        psP = pp["mm"].next()
                        for kk in range(2):
                            k.mm(psP[:], pT[:, hsl(kk)], w_ple[kk][:, hs_], kk == 0, kk == 1)
                        k.act(tg[:, hs_], psG[:], AF.Tanh, scale=0.5)
                        k.stt("dve", tg[:, hs_], tg[:, hs_], 1.0, psP[:], ALU.add, ALU.mult)
                        k.stt("dve", h3[:, hs_], tg[:, hs_], 0.5, h2[:, hs_], ALU.mult, ALU.add)
                    ss = small.next()
                    k.act(junk, h3[:], AF.Square, accum=ss[:, 0:1])
                    rs = rstd_of(ss[:, 0:1], 1.0 / D, small, mhalf, 1)
                    y = yp.next()
                    k.stt("dve", y[:], h3[:], rs, finw[:], ALU.mult, ALU.mult)
                    k.dma("sp", y_o.rows(r0), y[:])
                S.emit_phase()

        if "D" in _ph:
            phase_c()
    return nc


def _consts():
    c = {}
    idx = np.arange(128)
    c["ident"] = np.eye(128, dtype=np.float32)
    c["irep"] = np.tile(np.eye(128, dtype=np.float32), (1, 4))
    lg = np.log(1.0 - 2.0 ** (-5.0 - np.arange(4, dtype=np.float64)))
    sc = 128.0 ** -0.5
    for v, C in (("P", 128), ("S", LS)):
        seq = idx // C
        pos = idx % C
        same = seq[:, None] == seq[None, :]
        a = idx[:, None]
        b = idx[None, :]
        c["CM" + v] = (same & (a <= b)).astype(np.float32)
        c["UM" + v] = (same & (a > b)).astype(np.float32)
        c["NEGs" + v] = np.where(same & (a > b), 0.0, NEG).astype(np.float32)
        c["NEGT" + v] = np.where(same & (b >= a), 0.0, NEG).astype(np.float32)
        dtr = np.zeros((128, 4, 128), np.float64)
        for h in range(4):
            dtr[:, h, :] = np.where(same & (b >= a), sc * np.exp((b - a) * lg[h]), 0.0)
        c["DTr" + v] = dtr.reshape(128, 512).astype(np.float32)
        c["rqd" + v] = np.exp((pos[:, None] + 1.0) * lg[None, :]).astype(np.float32)
        c["rkt" + v] = (sc * np.exp((C - 1.0 - pos[:, None]) * lg[None, :])).astype(np.float32)
    s16 = np.arange(16)
    c["E1"] = (s16[:, None] == (idx[None, :] // LS)).astype(np.float32).reshape(-1)
    c["seq2"] = ((idx[:, None] // LS) == s16[None, :]).astype(np.float32)
    pos = np.concatenate([np.arange(SEQ), PAST + (np.arange(NSAMP * LS) % LS)]).astype(np.float32)
    inv = (np.float32(10000.0) ** (-np.arange(0, 128, 2, dtype=np.float32) / np.float32(128))).astype(np.float32)
    ang = (pos[:, None] * inv[None, :]).astype(np.float32)
    c["cosT"] = np.cos(ang.astype(np.float64)).astype(np.float32)
    c["sinT"] = np.sin(ang.astype(np.float64)).astype(np.float32)
    return c


_NC_CACHE = {}


def kernel(x_prompt, x_sample, p_prompt, p_sample, state_dn_conv, state_dn, state_ret,
           state_ffn_conv, attn_norm_w, w_in, dn_conv_w, dn_A_log, dn_dt_bias, dn_norm_w,
           ret_norm_w, w_out, ffn_norm_w, w_up, ffn_conv_w, ffn_conv_b, w_down, ple_norm_w,
           w_ple_gate, w_ple, final_norm_w):
    f = lambda a: np.ascontiguousarray(np.asarray(a), dtype=np.float32)
    x_prompt, x_sample, p_prompt, p_sample = f(x_prompt), f(x_sample), f(p_prompt), f(p_sample)
    state_dn_conv, state_dn, state_ret, state_ffn_conv = f(state_dn_conv), f(state_dn), f(state_ret), f(state_ffn_conv)
    col8 = lambda w: f(np.asarray(w).reshape(8, 128).T)
    shared = dict(
        w_in=f(w_in)[0], w_out=f(w_out)[0], w_up=f(w_up)[0], w_down=f(w_down)[0],
        w_gate=f(w_ple_gate)[0], w_ple=f(w_ple)[0],
        anw8=col8(f(attn_norm_w)[0]), fnw8=col8(f(ffn_norm_w)[0]), pnw8=col8(f(ple_norm_w)[0]),
        finw=f(final_norm_w),
        dncw=f(f(dn_conv_w)[0].T.reshape(12, 128, 4).transpose(1, 0, 2).reshape(128, 48)),
        alog=f(dn_A_log)[0], dtb=f(dn_dt_bias)[0],
        dnw4=f(np.tile(f(dn_norm_w)[0], 4)), retw=f(ret_norm_w)[0],
        fcw=f(f(ffn_conv_w)[0].T.reshape(44, 128, 3).transpose(1, 0, 2).reshape(128, 132)),
        fcb=f(f(ffn_conv_b)[0].reshape(44, 128).T),
    )
    shared.update(_consts())
    in_maps = []
    for i in range(NCORES):
        sl = slice(NSAMP * i, NSAMP * (i + 1))
        m = dict(shared)
        m["x"] = f(np.concatenate([x_prompt[i], x_sample[sl].reshape(NSAMP * LS, D)], axis=0))
        m["p"] = f(np.concatenate([p_prompt[0, i], p_sample[0, sl].reshape(NSAMP * LS, 256)], axis=0))
        m["st_dnc"] = f(state_dn_conv[0, sl].reshape(48, 1536))
        m["st_dn"] = f(state_dn[0, sl])
        m["st_ret"] = f(state_ret[0, sl])
        m["st_fc"] = f(state_ffn_conv[0, sl].reshape(32, 2 * DFF))
        in_maps.append(m)
    if "nc" not in _NC_CACHE:
        _NC_CACHE["nc"] = build_program()
    nc = _NC_CACHE["nc"]
    res = run_bass_kernel_spmd(nc, in_maps, core_ids=list(range(NCORES)))
    R = res.results
    _NC_CACHE["last"] = R
    g = lambda name: [np.asarray(R[i][name], dtype=np.float32) for i in range(NCORES)]
    y = g("y")
    y_prompt = np.stack([a[:SEQ] for a in y], axis=0)
    y_sample = np.concatenate([a[SEQ:].reshape(NSAMP, LS, D) for a in y], axis=0)
    dncp = np.stack(g("o_dnc_p"), axis=0)[None]
    dnp = np.stack(g("o_dn_p"), axis=0)[None]
    retp = np.stack(g("o_ret_p"), axis=0)[None]
    fcp = np.stack(g("o_fc_p"), axis=0)[None]
    dncs = np.concatenate([a.reshape(NSAMP, 3, 1536) for a in g("o_dnc_s")], axis=0)[None]
    dns = np.concatenate(g("o_dn_s"), axis=0)[None]
    rets = np.concatenate(g("o_ret_s"), axis=0)[None]
    fcs = np.concatenate([a.reshape(NSAMP, 2, 2 * DFF) for a in g("o_fc_s")], axis=0)[None]
    return (y_prompt, y_sample, dncp, dnp, retp, fcp, dncs, dns, rets, fcs)
```

```python
import contextlib
import numpy as np
import concourse.bass as bass
import concourse.mybir as mybir
from concourse.bass_utils import run_bass_kernel_spmd

F32 = mybir.dt.float32
BF16 = mybir.dt.bfloat16
AF = mybir.ActivationFunctionType
ALU = mybir.AluOpType
AX = mybir.AxisListType

NCORES = 8
D = 1024
SEQ = 2048
NSAMP = 16
LS = 8
NTOK = SEQ + NSAMP * LS
DIN = 4104
DFF = 2816
EPS = 1e-6
PAST = 16384
NEG = -30000.0
PE2R, PRBF2, PKQT, PRTK, POF = 5, 5, 3, 3, 2
import os as _osb
BAL = _osb.environ.get('K_BAL', '')
PCN = int(_osb.environ.get('K_PCN', '6'))
C_QKV, C_Z, C_B, C_A, C_RQ, C_RK, C_RV, C_RG = 0, 1536, 2048, 2052, 2056, 2568, 3080, 3592


class T:
    __slots__ = ("name", "last_writer", "readers")

    def __init__(self, name=""):
        self.name = name
        self.last_writer = None
        self.readers = []


class Op:
    __slots__ = ("eng", "fn", "deps", "users", "ndep", "signaled", "sigval", "sem", "is_dma", "idx", "cost",
                 "aset", "phase", "finish", "pos", "rtime", "tag", "prio")

    def __init__(self, eng, fn, is_dma):
        self.eng = eng
        self.fn = fn
        self.deps = []
        self.users = []
        self.signaled = False
        self.sigval = None
        self.sem = None
        self.is_dma = is_dma
        self.finish = 0.0
        self.pos = -1


class Sched:
    ENGS = ("pe", "act", "dve", "pool", "sp")
    XLAT = 250.0

    def __init__(self, nc, n_dma_sems=14):
        self.nc = nc
        self.ops = []
        self.n_dma_sems = n_dma_sems
        self.nops = 0
        self.phase = 0
        import os as _os
        self.prio_mode = int(_os.environ.get("KS_PRIO", "1"))
        self.prio_w = float(_os.environ.get("KS_PRIOW", "0.0"))

    def op(self, eng, fn, reads=(), writes=(), dma=False, cost=200.0, aset=None):
        o = Op(eng, fn, dma)
        o.idx = self.nops
        self.nops += 1
        o.cost = cost
        o.aset = aset
        o.phase = self.phase
        import sys as _sys
        fr = _sys._getframe(2)
        o.tag = fr.f_lineno if fr.f_code.co_name != "<lambda>" else fr.f_back.f_lineno
        deps = []
        for t in reads:
            if t.last_writer is not None:
                deps.append(t.last_writer)
        for t in writes:
            if t.last_writer is not None:
                deps.append(t.last_writer)
            deps.extend(t.readers)
        seen = set()
        for d in deps:
            if id(d) in seen or d is o or d.phase != o.phase:
                continue
            seen.add(id(d))
            o.deps.append(d)
            d.users.append(o)
        for t in reads:
            t.readers.append(o)
        for t in writes:
            t.last_writer = o
            t.readers = []
        self.ops.append(o)
        return o

    def open(self, stack):
        nc = self.nc
        self.sems = {}
        for e in ("pe", "act", "dve", "pool"):
            self.sems[e] = stack.enter_context(nc.semaphore("s_" + e))
        for e in ("sp", "act", "pool"):
            for k in range(self.n_dma_sems):
                self.sems[(e, k)] = stack.enter_context(nc.semaphore("d_%s_%d" % (e, k)))
        self.cnt = {}
        self.dma_n = {e: 0 for e in self.ENGS}

    def _schedule(self):
        import heapq
        ops = self.ops
        future = {e: [] for e in self.ENGS}
        avail = {e: [] for e in self.ENGS}
        free_at = {e: 0.0 for e in self.ENGS}
        cur_set = {e: None for e in self.ENGS}
        streams = {e: [] for e in self.ENGS}
        self._pipe = 0.0
        bl = {}
        for o in reversed(ops):
            m = 0.0
            for u in o.users:
                lat = self.XLAT if (u.eng != o.eng or o.is_dma) else 60.0
                v = bl[id(u)] + lat
                if v > m:
                    m = v
            bl[id(o)] = m + o.cost
        mode = self.prio_mode
        for o in ops:
            o.ndep = len(o.deps)
            o.rtime = 0.0
            if mode == 0:
                o.prio = o.idx
            else:
                o.prio = -bl[id(o)] + self.prio_w * o.idx
        for o in ops:
            if o.ndep == 0:
                heapq.heappush(future[o.eng], (0.0, o.prio, o.idx, o))
        left = len(ops)
        while left:
            best = None
            for e in self.ENGS:
                f, a = future[e], avail[e]
                while f and f[0][0] <= free_at[e]:
                    _, pr, i, o = heapq.heappop(f)
                    heapq.heappush(a, (pr, i, o))
                if a:
                    cand = (free_at[e], a[0][0], e, True)
                elif f:
                    cand = (f[0][0], f[0][1], e, False)
                else:
                    continue
                if best is None or cand < best:
                    best = cand
            start, _, e, from_avail = best
            if from_avail:
                _, _, o = heapq.heappop(avail[e])
            else:
                _, _, _, o = heapq.heappop(future[e])
            c = o.cost
            if o.aset is not None and o.aset != cur_set[e]:
                if cur_set[e] is not None:
                    c += 1300.0
                cur_set[e] = o.aset
            if o.is_dma:
                xfer = max(0.0, c - 2000.0) * (120.0 / 220.0)
                t0x = max(start + 1000.0, self._pipe)
                self._pipe = t0x + xfer
                o.finish = t0x + xfer + 1000.0
                free_at[e] = start + 60.0
            else:
                o.finish = start + c
                free_at[e] = o.finish
            o.pos = len(streams[e])
            streams[e].append(o)
            left -= 1
            for u in o.users:
                lat = self.XLAT if (u.eng != e or o.is_dma) else 60.0
                t = o.finish + lat
                if t > u.rtime:
                    u.rtime = t
                u.ndep -= 1
                if u.ndep == 0:
                    heapq.heappush(future[u.eng], (u.rtime, u.prio, u.idx, u))
        self.makespan = max(free_at.values())
        return streams

    def emit_phase(self):
        nc = self.nc
        sems = self.sems
        cnt = self.cnt
        streams = self._schedule()
        for e in self.ENGS:
            last_on_sem = {}
            for o in streams[e]:
                if o.is_dma:
                    kk = self.dma_n[e] % self.n_dma_sems
                    self.dma_n[e] += 1
                    o.sem = (e, kk)
                    c = cnt.get(o.sem, 0) + 16
                    cnt[o.sem] = c
                    o.sigval = c
        plan = {}
        for e in self.ENGS:
            wpos = {}
            wl = []
            for o in streams[e]:
                ws = []
                for d in o.deps:
                    if d.is_dma:
                        ws.append(d)
                        continue
                    if d.eng == e and e == "pe":
                        continue
                    if d.pos > wpos.get(d.eng, -1):
                        wpos[d.eng] = d.pos
                        d.signaled = True
                        ws.append(d)
                wl.append(ws)
            plan[e] = wl
        for e in ("pe", "act", "dve", "pool"):
            for o in reversed(streams[e]):
                if not o.is_dma:
                    o.signaled = True
                    break
        for e in self.ENGS:
            for o in streams[e]:
                if (not o.is_dma) and o.signaled:
                    c = cnt.get(e, 0) + 1
                    cnt[e] = c
                    o.sem = e
                    o.sigval = c
        final = dict(cnt)
        with nc.Block() as block:
            engobj = {"pe": block.tensor, "act": block.scalar, "dve": block.vector,
                      "pool": block.gpsimd, "sp": block.sync}

            def run(e, eng):
                waited = {}
                dma_prev = {}

                def wait_sv(sem, val):
                    if waited.get(sem, 0) >= val:
                        return
                    eng.wait_ge(sems[sem], val)
                    waited[sem] = val

                for o, ws in zip(streams[e], plan[e]):
                    for d in ws:
                        wait_sv(d.sem, d.sigval)
                    if o.is_dma:
                        if o.sigval > 16:
                            wait_sv(o.sem, o.sigval - 16)
                    ins = o.fn(eng)
                    if o.is_dma:
                        ins.then_inc(sems[o.sem], 16)
                    elif o.signaled:
                        ins.then_inc(sems[o.sem], 1)
                for sem, val in final.items():
                    wait_sv(sem, val)

            for e in self.ENGS:
                def mk(e):
                    def f(eng):
                        run(e, eng)
                    return f
                engobj[e](mk(e))
        self.ops = []
        self.phase += 1


class StopBuild(Exception):
    pass


def ck(n):
    import os
    lim = float(os.environ.get("KDBG_CK", "1000"))
    if n > lim:
        raise StopBuild()


class Buf:
    def __init__(self, h, name="", excl=False):
        self.h = h
        self.t = T(name)
        self.excl = excl

    def __getitem__(self, k):
        return V(self, self.h[k])


class V:
    def __init__(self, buf, ap):
        self.buf = buf
        self.ap = ap

    def __getitem__(self, k):
        return V(self.buf, self.ap[k])

    def re(self, pat_, **kw):
        return V(self.buf, self.ap.rearrange(pat_, **kw))

    def bc(self, shape):
        return V(self.buf, self.ap.to_broadcast(list(shape)))

    def un(self, ax):
        return V(self.buf, self.ap.unsqueeze(ax))

    def bitcast(self, dt):
        return V(self.buf, self.ap.bitcast(dt))


class Chunked:
    def __init__(self, buf, n, w):
        self.bufs = [Buf(buf.h, "%s_c%d" % (buf.t.name, i)) for i in range(n)]
        self.w = w

    def c(self, i, a=0, b=None):
        b = self.w if b is None else b
        return self.bufs[i][:, i * self.w + a:i * self.w + b]


class Pool:
    def __init__(self, bufs):
        self.bufs = bufs
        self.i = 0

    def next(self):
        b = self.bufs[self.i % len(self.bufs)]
        self.i += 1
        return b


def _tr(*vs):
    return [v.buf.t for v in vs if isinstance(v, V) and v.buf is not None]


def _rw(reads, writes):
    r, w = [], []
    for v in reads:
        if isinstance(v, V) and v.buf is not None:
            (w if v.buf.excl else r).append(v.buf.t)
    for v in writes:
        if isinstance(v, V) and v.buf is not None:
            w.append(v.buf.t)
    return dict(reads=r, writes=w)


def _a(x):
    return x.ap if isinstance(x, V) else x


def _fs(v):
    n = 1
    for d in v.ap.shape[1:]:
        n *= int(d)
    return n


def _is_psum(v):
    return isinstance(v, V) and v.buf is not None and v.buf.excl


_ASET = {AF.Silu: "silu", AF.Exp: "lnexp", AF.Ln: "lnexp", AF.Tanh: "silu"}


class K:
    def __init__(self, nc, S):
        self.nc = nc
        self.S = S

    def mm(self, out, lhsT, rhs, start=True, stop=True):
        n = max(32, _fs(rhs))
        c = n / 2.37 * (4.0 if rhs.ap.dtype == F32 else 1.0) + 48.0
        self.S.op("pe", lambda e: e.matmul(out.ap, lhsT=lhsT.ap, rhs=rhs.ap, start=start, stop=stop),
                  cost=c, **_rw([lhsT, rhs], [out]))

    def tp(self, out, in_, ident):
        c = max(32, _fs(ident)) / 2.37 * (4.0 if in_.ap.dtype == F32 else 1.0) + 48.0
        self.S.op("pe", lambda e: e.transpose(out.ap, in_.ap, ident.ap), cost=c, **_rw([in_, ident], [out]))

    def act(self, out, in_, func, scale=1.0, bias=0.0, accum=None):
        def f(e):
            kw = dict(out=out.ap, in_=in_.ap, func=func, scale=_a(scale), bias=_a(bias))
            if accum is not None:
                kw["accum_out"] = accum.ap
            return e.activation(**kw)
        c = 190.0 + 0.6 * _fs(in_) + (90.0 if accum is not None else 0.0)
        self.S.op("act", f, cost=c, aset=_ASET.get(func), **_rw([in_, scale, bias], [out, accum]))

    def _vc(self, eng, *vs):
        f = max(_fs(v) for v in vs if isinstance(v, V))
        ps = any(_is_psum(v) for v in vs)
        if eng == "pool":
            return 1100.0 + 0.45 * f
        return (130.0 if ps else 90.0) + 1.25 * f

    def tt(self, eng, out, a, b, op):
        self.S.op(eng, lambda e: e.tensor_tensor(out=out.ap, in0=a.ap, in1=b.ap, op=op), cost=self._vc(eng, out, a, b),
                  **_rw([a, b], [out]))

    def ts(self, eng, out, a, s1, op0, s2=None, op1=None):
        def f(e):
            if s2 is None:
                return e.tensor_scalar(out=out.ap, in0=a.ap, scalar1=_a(s1), scalar2=None, op0=op0)
            return e.tensor_scalar(out=out.ap, in0=a.ap, scalar1=_a(s1), scalar2=_a(s2), op0=op0, op1=op1)
        self.S.op(eng, f, cost=self._vc(eng, out, a), **_rw([a, s1, s2], [out]))

    def stt(self, eng, out, a, s, b, op0, op1):
        self.S.op(eng, lambda e: e.scalar_tensor_tensor(out=out.ap, in0=a.ap, scalar=_a(s), in1=b.ap, op0=op0, op1=op1),
                  cost=self._vc(eng, out, a, b), **_rw([a, s, b], [out]))

    def cp(self, eng, out, a):
        if eng == "act":
            self.S.op("act", lambda e: e.copy(out=out.ap, in_=a.ap), cost=190.0 + 0.6 * _fs(a), **_rw([a], [out]))
        else:
            c = (250.0 + 0.3 * _fs(a)) if eng == "pool" else self._vc(eng, out, a)
            self.S.op(eng, lambda e: e.tensor_copy(out=out.ap, in_=a.ap), cost=c, **_rw([a], [out]))

    def recip(self, out, a):
        self.S.op("dve", lambda e: e.reciprocal(out=out.ap, in_=a.ap), cost=self._vc("dve", out, a), **_rw([a], [out]))

    def ms(self, eng, out, val):
        self.S.op(eng, lambda e: e.memset(out.ap, val), cost=250.0 + 0.3 * _fs(out), **_rw([], [out]))

    def dma(self, q, out, in_):
        nbytes = int(out.ap.shape[0]) * _fs(out) * 4
        c = 2000.0 + nbytes / 120.0
        return self.S.op(q, lambda e: e.dma_start(out=out.ap, in_=in_.ap), dma=True, cost=c, **_rw([in_], [out]))


def build_program():
    nc = bass.Bass("TRN2", target_bir_lowering=False)

    def din(name, shape):
        return V(None, nc.dram_tensor(name, list(shape), F32, kind="ExternalInput").ap())

    def dout(name, shape):
        return Buf(nc.dram_tensor(name, list(shape), F32, kind="ExternalOutput").ap(), name)

    class RowBufs:
        def __init__(self, ap):
            self.h = ap
            self.b = {}

        def rows(self, r0):
            if r0 not in self.b:
                self.b[r0] = Buf(self.h, "rb%d" % r0)
            return V(self.b[r0], self.h[r0:r0 + 128, :])

    x_d = din("x", [NTOK, D])
    p_d = din("p", [NTOK, 256])
    stdnc_d = din("st_dnc", [48, 1536])
    stdn_d = din("st_dn", [NSAMP, 4, 128, 128])
    stret_d = din("st_ret", [NSAMP, 4, 128, 128])
    stfc_d = din("st_fc", [32, 2 * DFF])
    w_in_d = din("w_in", [D, DIN])
    w_out_d = din("w_out", [D, D])
    w_up_d = din("w_up", [D, 2 * DFF])
    w_down_d = din("w_down", [DFF, D])
    w_gate_d = din("w_gate", [D, D])
    w_ple_d = din("w_ple", [256, D])
    anw8_d = din("anw8", [128, 8])
    fnw8_d = din("fnw8", [128, 8])
    pnw8_d = din("pnw8", [128, 8])
    finw_d = din("finw", [D])
    dncw_d = din("dncw", [128, 48])
    alog_d = din("alog", [4])
    dtb_d = din("dtb", [4])
    dnw4_d = din("dnw4", [512])
    retw_d = din("retw", [512])
    fcw_d = din("fcw", [128, 132])
    fcb_d = din("fcb", [128, 44])
    ident_d = din("ident", [128, 128])
    irep_d = din("irep", [128, 512])
    cos_d = din("cosT", [NTOK, 64])
    sin_d = din("sinT", [NTOK, 64])
    cvar = {}
    for v in ("P", "S"):
        cvar[v] = dict(CM=din("CM" + v, [128, 128]), UM=din("UM" + v, [128, 128]),
                       NEGs=din("NEGs" + v, [128, 128]), NEGT=din("NEGT" + v, [128, 128]),
                       DTr=din("DTr" + v, [128, 512]), rqd=din("rqd" + v, [128, 4]), rkt=din("rkt" + v, [128, 4]))
    e1_d = din("E1", [16 * 128])
    seq2_d = din("seq2", [128, 16])

    y_o = RowBufs(nc.dram_tensor("y", [NTOK, D], F32, kind="ExternalOutput").ap())
    dncp_o = dout("o_dnc_p", [3, 1536])
    dnp_o = dout("o_dn_p", [4, 128, 128])
    retp_o = dout("o_ret_p", [4, 128, 128])
    fcp_o = dout("o_fc_p", [2, 2 * DFF])
    dncs_o = dout("o_dnc_s", [48, 1536])
    dns_o = dout("o_dn_s", [NSAMP, 4, 128, 128])
    rets_o = dout("o_ret_s", [NSAMP, 4, 128, 128])
    fcs_o = dout("o_fc_s", [32, 2 * DFF])
    import os as _os
    _dbg = _os.environ.get("KDBG_OUT", "") == "1"
    _kind = dict(kind="ExternalOutput") if _dbg else {}
    h1_s = RowBufs(nc.dram_tensor("h1_scr", [NTOK, D], F32, **_kind).ap())
    h2_s = RowBufs(nc.dram_tensor("h2_scr", [NTOK, D], F32, **_kind).ap())

    lg = [float(np.log(1.0 - 2.0 ** (-5.0 - h))) for h in range(4)]

    with contextlib.ExitStack() as top:
        S = Sched(nc)
        S.open(top)
        k = K(nc, S)

        gcnt = [0]

        def mk_alloc(st):
            cnt = gcnt

            def sb(shape, dt=F32, n=0, name="t"):
                def one():
                    cnt[0] += 1
                    nm = "%s_%d" % (name, cnt[0])
                    return Buf(st.enter_context(nc.sbuf_tensor(nm, list(shape), dt)), nm)
                if n == 0:
                    return one()
                return Pool([one() for _ in range(n)])

            def psum_pool(**roles):
                assert sum(roles.values()) <= 8
                out = {}
                for role, n in roles.items():
                    bufs = []
                    for i in range(n):
                        cnt[0] += 1
                        nm = "ps_%d" % cnt[0]
                        bufs.append(Buf(st.enter_context(nc.psum_tensor(nm, [128, 512], F32)), nm, excl=True))
                    out[role] = Pool(bufs)
                return out
            return sb, psum_pool

        def load_w(st_sb, wd, kchunks, ncols, name):
            out = []
            for kk in range(kchunks):
                b = st_sb([128, ncols], BF16, name=name)
                k.dma("pool", b[:], wd[kk * 128:(kk + 1) * 128, :])
                out.append(b)
            return out

        def rstd_act(ss_in, n_inv, small, ncol):
            a = small.next()
            k.act(a[:, 0:ncol], ss_in, AF.Ln, scale=n_inv, bias=epsc[:, 0:1])
            r = small.next()
            k.act(r[:, 0:ncol], a[:, 0:ncol], AF.Exp, scale=-0.5)
            return r[:, 0:ncol]

        def rstd_of(ss_in, n_inv, small, mhalf, ncol):
            if USE_ACT_RSTD[0]:
                return rstd_act(ss_in, n_inv, small, ncol)
            a = small.next()
            k.ts("dve", a[:, 0:ncol], ss_in, n_inv, ALU.mult, EPS, ALU.add)
            r = small.next()
            k.tt("pool", r[:, 0:ncol], a[:, 0:ncol], mhalf[:, 0:ncol], ALU.pow)
            return r[:, 0:ncol]

        USE_ACT_RSTD = [False]
        epsc = None

        def phase_a(samp):
            USE_ACT_RSTD[0] = True
            try:
                phase_a_body(samp)
            finally:
                USE_ACT_RSTD[0] = False

        def phase_a_body(samp):
            nonlocal epsc
            with contextlib.ExitStack() as st:
                sb, psum_pool = mk_alloc(st)
                pp = psum_pool(E=3, M=2, R=2, L=1)
                cv = cvar["S" if samp else "P"]
                nst = NSAMP if samp else 1
                nlev = 3 if samp else 7
                Cc = LS if samp else 128
                cdec = [float(np.exp(Cc * lg[h])) for h in range(4)]
                nb = 1 if samp else 1
                w_inq, w_inr, w_out = WA
                identf = sb([128, 128]); k.dma("sp", identf[:], ident_d)
                identb = sb([128, 128], BF16); k.dma("pool", identb[:], ident_d)
                irep = sb([128, 512], BF16); k.dma("pool", irep[:], irep_d)
                onesf = sb([128, 128]); k.ms("pool", onesf[:], 1.0)
                nonesf = sb([128, 128]); k.ms("pool", nonesf[:], -1.0)
                mhalf = sb([128, 8]); k.ms("pool", mhalf[:], -0.5)
                epsc = sb([128, 1]); k.ms("pool", epsc[:], EPS)
                ecvp = sb([128, 256], n=2, name="ecv")
                CM = sb([128, 128]); k.dma("sp", CM[:], cv["CM"])
                UM = sb([128, 128]); k.dma("sp", UM[:], cv["UM"])
                NEGs = sb([128, 128], BF16); k.dma("pool", NEGs[:], cv["NEGs"])
                NEGT = sb([128, 128], BF16); k.dma("pool", NEGT[:], cv["NEGT"])
                DTr = sb([128, 512]); k.dma("sp", DTr[:], cv["DTr"])
                rqd = sb([128, 4]); k.dma("sp", rqd[:], cv["rqd"])
                rkt = sb([128, 4]); k.dma("sp", rkt[:], cv["rkt"])
                anw8 = sb([128, 8]); k.dma("sp", anw8[:], anw8_d)
                cw = sb([128, 48]); k.dma("sp", cw[:], dncw_d)
                alogb = sb([128, 4]); k.dma("sp", alogb[:], V(None, alog_d.ap.partition_broadcast(128)))
                dtbb = sb([128, 4]); k.dma("sp", dtbb[:], V(None, dtb_d.ap.partition_broadcast(128)))
                negA = sb([128, 4])
                k.act(negA[:], alogb[:], AF.Exp)
                k.ts("dve", negA[:], negA[:], -1.0, ALU.mult)
                dnw = sb([128, 512]); k.dma("sp", dnw[:], V(None, dnw4_d.ap.partition_broadcast(128)))
                retw = sb([128, 512]); k.dma("sp", retw[:], V(None, retw_d.ap.partition_broadcast(128)))
                if samp:
                    E1 = sb([128, 16 * 128], BF16)
                    k.dma("pool", E1[:], V(None, e1_d.ap.partition_broadcast(128)))
                    seq2 = sb([128, 16]); k.dma("sp", seq2[:], seq2_d)
                NT = 128 if samp else 256
                nbuf = 1 if samp else 2
                xtp = sb([128, D], n=1, name="xt")
                xrp = None if samp else sb([128, D], n=1, name="xr")
                abfp = sb([128, D], BF16, n=1, name="abf")
                aTp = sb([128, 8 * NT], BF16, n=1, name="aT")
                qkvT = sb([128, 12 * NT], BF16, name="qkvT")
                xprep = sb([128, 264], n=1 if samp else 2, name="xpre")
                ycvp = sb([128, 256], n=1 if samp else 2, name="ycv")
                cx_all = sb([128, 12 * 48], name="cx"); cx = Chunked(cx_all, 12, 48)
                junk = sb([128, D], BF16, name="junk"); junk = V(None, junk.h[:])
                small = sb([128, 24], n=24 if samp else 32, name="small")
                zsp = sb([128, 512], BF16, n=1 if samp else 2, name="zs")
                rqkf = sb([128, 1024], name="rqkf")
                qkf = sb([128, 1024], BF16, name="qkf")
                rgsp = sb([128, 512], BF16, n=1 if samp else 2, name="rgs")
                tmpp = sb([128, 512], n=2, name="tmp")
                ebf = sb([128, 512], BF16, n=4, name="ebf")
                e2r = sb([128, 512], BF16, n=3 if samp else PE2R, name="e2r")
                rbf = sb([128, 512], BF16, n=2 if samp else 4, name="rbf")
                rbf2 = sb([128, 512], BF16, n=5 if samp else PRBF2, name="rbf2")
                kqT = sb([128, 1024], BF16, n=2 if samp else PKQT, name="kqT")
                rtk = sb([128, 1024], BF16, n=2 if samp else PRTK, name="rtk")
                MG = sb([128, 512], name="MG")
                Xp = sb([128, 512], BF16, n=2, name="X")
                XTp = sb([128, 512], BF16, n=2, name="XT")
                IXp = sb([128, 512], BF16, n=2, name="IX")
                PTp = sb([128, 512], BF16, n=2 if samp else 3, name="PT")
                dexp = sb([128, 512], BF16, n=3, name="dexp")
                ofp = sb([128, 512], n=POF, name="of")
                mix = sb([128, D], BF16, name="mix")
                mixT = sb([128, D], BF16, name="mixT")
                h1p = None if samp else sb([128, D], n=1, name="h1")
                csp = sb([128, 128], n=nbuf, name="cs")
                if samp:
                    sbigp = sb([128, 16 * 128], n=2, name="sbig")
                    sbigbp = sb([128, 16 * 128], BF16, n=2, name="sbigb")
                    expp = sb([128, 16 * 128], BF16, n=2, name="exp")
                    dncT = sb([128, 12 * 48], name="dncT")
                else:
                    Sdn = sb([128, 512], name="Sdn"); k.ms("pool", Sdn[:], 0.0)
                    Sdnb = sb([128, 512], BF16, name="Sdnb"); k.ms("pool", Sdnb[:], 0.0)
                    Srt = sb([128, 512], name="Srt"); k.ms("pool", Srt[:], 0.0)
                    Srtb = sb([128, 512], BF16, name="Srtb"); k.ms("pool", Srtb[:], 0.0)
                    k.ms("pool", cx_all[:], 0.0)

                if samp:
                    blocks = [(SEQ, 128, NSAMP, LS)]
                else:
                    blocks = [(b * 256, 256, 1, 256) for b in range(SEQ // 256)]

                hsl = lambda h: slice(h * 128, (h + 1) * 128)
                b3 = lambda v: v.un(2).bc([128, 4, 128])
                r3 = lambda v: v.re("p (h d) -> p h d", h=4)

                if samp:
                    for gI in range(3):
                        tmp = tmpp.next()
                        k.dma("sp", tmp[0:48, :], stdnc_d[:, gI * 512:(gI + 1) * 512])
                        ps = pp["M"].next()
                        for i in range(4):
                            k.tp(ps[:, i * 48:(i + 1) * 48], tmp[0:48, i * 128:(i + 1) * 128], identf[0:48, 0:48])
                        k.cp("act", dncT[:, gI * 192:(gI + 1) * 192], ps[:, 0:192])

                last_xt = [None]

                def stage_a0(t0, NT, aT):
                    nsub = NT // 128
                    for j in range(nsub):
                        xt = xtp.next()
                        last_xt[0] = xt
                        k.dma("sp", xt[:], x_d[t0 + 128 * j:t0 + 128 * (j + 1), :])
                        ss = small.next()
                        k.act(junk, xt[:], AF.Square, accum=ss[:, 0:1])
                        rs = rstd_of(ss[:, 0:1], 1.0 / D, small, mhalf, 1)
                        abf = abfp.next()
                        k.ts("dve", abf[:], xt[:], rs, ALU.mult)
                        ps = pp["M"].next()
                        pb = ps[:].bitcast(BF16)
                        for kk in range(8):
                            k.tp(pb[:, hsl(kk)], abf[:, hsl(kk)], identb[:])
                        k.tt("dve", aT[:].re("p (k t) -> p k t", k=8)[:, :, 128 * j:128 * (j + 1)],
                             pb.re("p (k t) -> p k t", k=8), anw8[:].un(2).bc([128, 8, 128]), ALU.mult)

                try:
                  ck(0)
                  for bi, (t0, NT, nseq, L) in enumerate(blocks):
                    nsub = NT // 128
                    aT = aTp.next()
                    stage_a0(t0, NT, aT)
                    ck(1)
                    aT3 = aT[:].re("p (k t) -> p k t", k=8)
                    qkvT3 = qkvT[:].re("p (c t) -> p c t", c=12)
                    for fc in range(12):
                        ps = pp["M"].next()
                        for kk in range(8):
                            k.mm(ps[:, 0:NT], w_inq[kk][:, fc * 128:(fc + 1) * 128], aT3[:, kk, :], start=kk == 0, stop=kk == 7)
                        ck(1.1)
                        xp = xprep.next()
                        xv = xp[:, 0:nseq * (3 + L)].re("p (s l) -> p s l", s=nseq)
                        psv = ps[:, 0:NT].re("p (s l) -> p s l", s=nseq)
                        k.cp("act", xv[:, :, 3:3 + L], psv)
                        ck(1.2)
                        if samp:
                            k.cp("pool", xv[:, :, 0:3], dncT[:, fc * 48:(fc + 1) * 48].re("p (s j) -> p s j", s=nseq))
                        else:
                            k.cp("pool", xv[:, :, 0:3], cx.c(fc, 0, 3).un(1))
                        k.cp("pool", cx.c(fc, 0, 3 * nseq).re("p (s j) -> p s j", s=nseq), xv[:, :, L:L + 3])
                        ck(1.3)
                        y = ycvp.next()
                        yv = y[:, 0:NT].re("p (s l) -> p s l", s=nseq)
                        k.act(yv, psv, AF.Copy, scale=cw[:, fc * 4 + 3:fc * 4 + 4])
                        ck(1.4)
                        k.stt("dve", yv, xv[:, :, 2:2 + L], cw[:, fc * 4 + 2:fc * 4 + 3], yv, ALU.mult, ALU.add)
                        k.stt("dve", yv, xv[:, :, 1:1 + L], cw[:, fc * 4 + 1:fc * 4 + 2], yv, ALU.mult, ALU.add)
                        k.stt("dve", yv, xv[:, :, 0:L], cw[:, fc * 4 + 0:fc * 4 + 1], yv, ALU.mult, ALU.add)
                        ck(1.5)
                        ecv = ecvp.next()
                        k.act(ecv[:, 0:NT], y[:, 0:NT], AF.Exp, scale=-1.0)
                        k.act(ecv[:, 0:NT], ecv[:, 0:NT], AF.Ln, bias=1.0)
                        k.act(ecv[:, 0:NT], ecv[:, 0:NT], AF.Exp, scale=-1.0)
                        k.tt("dve", qkvT3[:, fc, :], y[:, 0:NT], ecv[:, 0:NT], ALU.mult)
                        ck(1.6)

                    ck(2)
                    for j in range(nsub):
                        js = slice(128 * j, 128 * (j + 1))
                        r0 = t0 + 128 * j
                        if samp:
                            xr = last_xt[0]
                        else:
                            xr = xrp.next()
                            k.dma("sp", xr[:], x_d[r0:r0 + 128, :])
                        cs = csp.next()
                        k.dma("sp", cs[:, 0:64], cos_d[r0:r0 + 128, :])
                        k.dma("sp", cs[:, 64:128], sin_d[r0:r0 + 128, :])

                        def proj(c0, n, role="M"):
                            ps = pp[role].next()
                            for kk in range(8):
                                k.mm(ps[:, 0:n], aT3[:, kk, js], w_inr[kk][:, c0 - 1536:c0 - 1536 + n], start=kk == 0, stop=kk == 7)
                            return ps

                        ck(2.1)
                        psqk = pp["E"].next(); pqk = psqk[:].bitcast(BF16)
                        for i in range(8):
                            k.tp(pqk[:, hsl(i)], qkvT3[:, i, js], identb[:])
                        psv_ = pp["E"].next(); pv = psv_[:].bitcast(BF16)
                        for h in range(4):
                            k.tp(pv[:, hsl(h)], qkvT3[:, 8 + h, js], identb[:])
                        ck(2.2)
                        k.cp("act", qkf[:], pqk)
                        ck(2.3)
                        st = small.next()
                        for i in range(8):
                            k.act(junk[:, 0:128], qkf[:, hsl(i)], AF.Square, accum=st[:, i:i + 1])
                        ck(2.4)
                        rs = rstd_of(st[:, 0:8], 1.0, small, mhalf, 8)
                        ck(3)
                        psba = proj(C_B, 8, "E")
                        sm = small.next()
                        k.act(sm[:, 0:4], psba[:, 0:4], AF.Exp, scale=-1.0)
                        k.ts("dve", sm[:, 0:4], sm[:, 0:4], 1.0, ALU.add)
                        beta_t = small.next(); beta = beta_t[:, 0:4]
                        k.recip(beta, sm[:, 0:4])
                        k.tt("dve", sm[:, 4:8], psba[:, 4:8], dtbb[:], ALU.add)
                        k.act(sm[:, 8:12], sm[:, 4:8], AF.Exp)
                        k.act(sm[:, 12:16], sm[:, 8:12], AF.Ln, bias=1.0)
                        g_t = small.next(); g = g_t[:, 0:4]
                        k.tt("dve", g, sm[:, 12:16], negA[:], ALU.mult)
                        ck(4)
                        psg = pp["E"].next()
                        k.mm(psg[:, 0:4], CM[:], g)
                        k.mm(psg[:, 4:8], UM[:], g)
                        if samp:
                            Gs = small.next() if False else tmpp.next()
                            k.tt("pool", Gs[:, 0:64].re("p (h s) -> p h s", h=4), g.un(2).bc([128, 4, 16]),
                                 seq2[:].un(1).bc([128, 4, 16]), ALU.mult)
                            k.mm(psg[:, 8:72], onesf[:], Gs[:, 0:64])
                        else:
                            k.mm(psg[:, 8:12], onesf[:], g)
                        nex = 8 + 4 * nst
                        ex_t = sb_ex.next()
                        ex = ex_t[:, 0:nex]
                        k.act(ex, psg[:, 0:nex], AF.Exp)
                        sc_t = small.next(); sc = sc_t
                        k.tt("dve", sc[:, 0:4], rs[:, 4:8], beta, ALU.mult)
                        k.tt("dve", sc[:, 4:8], rs[:, 4:8], ex[:, 4:8], ALU.mult)
                        k.ts("dve", sc[:, 8:12], rs[:, 0:4], 128.0 ** -0.5, ALU.mult)
                        k.tt("dve", sc[:, 12:16], sc[:, 8:12], ex[:, 0:4], ALU.mult)
                        k.stt("dve", sc[:, 16:20], beta, -1.0, ex[:, 0:4], ALU.mult, ALU.mult)
                        kf = r3(qkf[:, 512:1024]); qf = r3(qkf[:, 0:512])
                        Kn = ebf.next(); KB = ebf.next(); Qs = ebf.next(); Qd = ebf.next(); Kt = e2r.next(); Vb = e2r.next()
                        k.tt("pool", r3(Kn[:]), kf, b3(rs[:, 4:8]), ALU.mult)
                        k.tt("dve", r3(KB[:]), kf, b3(sc[:, 0:4]), ALU.mult)
                        k.tt("pool", r3(Kt[:]), kf, b3(sc[:, 4:8]), ALU.mult)
                        k.tt("dve", r3(Qs[:]), qf, b3(sc[:, 8:12]), ALU.mult)
                        k.tt("pool", r3(Qd[:]), qf, b3(sc[:, 12:16]), ALU.mult)
                        k.tt("dve", r3(Vb[:]), r3(pv[:, 0:512]), b3(beta), ALU.mult)
                        ck(5)
                        KKT = kqT.next(); QQT = kqT.next()
                        psA = pp["E"].next(); pA = psA[:].bitcast(BF16)
                        for h in range(4):
                            k.tp(pA[:, hsl(h)], Kn[:, hsl(h)], identb[:])
                        for h in range(4):
                            k.tp(pA[:, hsl(4 + h)], KB[:, hsl(h)], identb[:])
                        k.cp("dve" if "h" in BAL else "act", KKT[:], pA)
                        psB = pp["E"].next(); pB = psB[:].bitcast(BF16)
                        for h in range(4):
                            k.tp(pB[:, hsl(h)], Qs[:, hsl(h)], identb[:])
                        for h in range(4):
                            k.tp(pB[:, hsl(4 + h)], Qd[:, hsl(h)], identb[:])
                        k.cp("dve", QQT[:], pB)
                        KnT = lambda h: KKT[:, hsl(h)]
                        KBT = lambda h: KKT[:, hsl(4 + h)]
                        QsT = lambda h: QQT[:, hsl(h)]
                        QdT = lambda h: QQT[:, hsl(4 + h)]
                        ck(6)
                        k.tt("pool", r3(MG[:]), CM[:].un(1).bc([128, 4, 128]), g.un(2).bc([128, 4, 128]), ALU.mult)
                        psD = pp["E"].next()
                        for h in range(4):
                            k.mm(psD[:, hsl(h)], MG[:, hsl(h)], onesf[:], True, False)
                            k.mm(psD[:, hsl(h)], nonesf[:], MG[:, hsl(h)], False, False)
                            k.mm(psD[:, hsl(h)], identb[:], NEGs[:], False, True)
                        Ds = dexp.next(); DT = dexp.next(); DTs = dexp.next()
                        k.act(Ds[:], psD[:], AF.Exp)
                        psDT = pp["E"].next(); pDT = psDT[:].bitcast(BF16)
                        for h in range(4):
                            k.tp(pDT[:, hsl(h)], Ds[:, hsl(h)], identb[:])
                        k.cp("act", DTs[:], pDT[:, 0:512])
                        k.tt("dve", DT[:], pDT[:, 0:512], irep[:], ALU.add)
                        ck(7)
                        psA_ = pp["E"].next(); psAT = pp["E"].next(); psKQ = pp["E"].next()
                        for h in range(4):
                            k.mm(psA_[:, hsl(h)], KBT(h), KnT(h))
                        for h in range(4):
                            k.mm(psAT[:, hsl(h)], KnT(h), KBT(h))
                        for h in range(4):
                            k.mm(psKQ[:, hsl(h)], KnT(h), QsT(h))
                        X = Xp.next(); XT = XTp.next(); PT = PTp.next(); QKDT = e2r.next()
                        k.stt("dve", X[:], psA_[:], -1.0, Ds[:], ALU.mult, ALU.mult)
                        k.stt("dve", XT[:], psAT[:], -1.0, DTs[:], ALU.mult, ALU.mult)
                        k.tt("dve", QKDT[:], psKQ[:], DT[:], ALU.mult)
                        k.tt("pool", PT[:], XT[:], irep[:], ALU.add)
                        ck(8)
                        for lv in range(1, nlev):
                            psX = pp["E"].next()
                            for h in range(4):
                                k.mm(psX[:, hsl(h)], XT[:, hsl(h)], X[:, hsl(h)])
                            last = lv == nlev - 1
                            IX = IXp.next()
                            k.tt("dve", IX[:], psX[:], irep[:], ALU.add)
                            if not last:
                                Xn = Xp.next()
                                k.cp("dve" if "e" in BAL else "act", Xn[:], psX[:])
                                psXT = pp["E"].next()
                                for h in range(4):
                                    k.mm(psXT[:, hsl(h)], X[:, hsl(h)], XT[:, hsl(h)])
                                XTn = XTp.next()
                                k.cp("dve" if "f" in BAL else "act", XTn[:], psXT[:])
                            psP = pp["E"].next()
                            for h in range(4):
                                k.mm(psP[:, hsl(h)], IX[:, hsl(h)], PT[:, hsl(h)])
                            PTn = PTp.next()
                            k.cp("dve" if "g" in BAL else "act", PTn[:], psP[:])
                            PT = PTn
                            if not last:
                                X, XT = Xn, XTn
                        ck(9)
                        zs = zsp.next(); rgs = rgsp.next()
                        psz = proj(C_Z, 512)
                        sgt = tmpp.next()
                        k.act(sgt[:], psz[:], AF.Exp, scale=-1.0)
                        k.act(sgt[:], sgt[:], AF.Ln, bias=1.0)
                        k.act(sgt[:], sgt[:], AF.Exp, scale=-1.0)
                        k.tt("dve", zs[:], psz[:], sgt[:], ALU.mult)
                        psrq = proj(C_RQ, 512)
                        k.cp("act", rqkf[:, 0:512], psrq[:])
                        psrk = proj(C_RK, 512)
                        k.cp("act", rqkf[:, 512:1024], psrk[:])
                        RV = rbf2.next()
                        psrv = proj(C_RV, 512)
                        k.cp("act", RV[:], psrv[:])
                        psrg = proj(C_RG, 512)
                        sgt = tmpp.next()
                        k.act(sgt[:], psrg[:], AF.Exp, scale=-1.0)
                        k.act(sgt[:], sgt[:], AF.Ln, bias=1.0)
                        k.act(sgt[:], sgt[:], AF.Exp, scale=-1.0)
                        k.tt("dve", rgs[:], psrg[:], sgt[:], ALU.mult)

                        ck(10)
                        of = ofp.next()
                        R = rbf.next(); vn = rbf.next()
                        headsets = [[h] for h in range(4)] if samp else [[0, 1, 2, 3]]
                        for hs in headsets:
                            cols = slice(hs[0] * 128, (hs[-1] + 1) * 128)
                            if samp:
                                h = hs[0]
                                sbig = sbigp.next(); sbigb = sbigbp.next()
                                for _q in range(4):
                                    k.dma("sp", sbig[:, _q * 512:(_q + 1) * 512].re("p (s v) -> p s v", s=4), V(None, stdn_d.ap[4 * _q:4 * _q + 4, h].rearrange("s k v -> k s v")))
                                k.cp("pool", sbigb[:], sbig[:])
                                KnE = expp.next(); QdE = expp.next()
                                e3 = lambda v: v.re("p (s c) -> p s c", s=16)
                                k.tt("pool", e3(KnE[:]), KnT(h).un(1).bc([128, 16, 128]), e3(E1[:]), ALU.mult)
                                k.tt("dve", e3(QdE[:]), QdT(h).un(1).bc([128, 16, 128]), e3(E1[:]), ALU.mult)
                                Sf = lambda hh, s, sbig=sbig: sbig[:, hsl(s)]
                                Sb = lambda hh, s, sbigb=sbigb: sbigb[:, hsl(s)]
                                lK = lambda hh, s: KnE[:, hsl(s)]
                                lQ = lambda hh, s: QdE[:, hsl(s)]
                            else:
                                Sf = lambda hh, s: Sdn[:, hsl(hh)]
                                Sb = lambda hh, s: Sdnb[:, hsl(hh)]
                                lK = lambda hh, s: KnT(hh)
                                lQ = lambda hh, s: QdT(hh)
                                lT = lambda hh, s: Kt[:, hsl(hh)]
                            psKS = pp["R"].next()
                            for h in hs:
                                for s in range(nst):
                                    k.mm(psKS[:, hsl(h)], lK(h, s), Sb(h, s), s == 0, s == nst - 1)
                            if samp:
                                KtE = expp.next()
                                k.tt("pool", e3(KtE[:]), Kt[:, hsl(hs[0])].un(1).bc([128, 16, 128]), seq2[:].un(2).bc([128, 16, 128]), ALU.mult)
                                lT = lambda hh, s: KtE[:, hsl(s)]
                            for h in hs:
                                k.stt("dve", R[:, hsl(h)], psKS[:, hsl(h)], sc[:, 16 + h:17 + h], Vb[:, hsl(h)], ALU.mult, ALU.add)
                            psV = pp["R"].next()
                            for h in hs:
                                k.mm(psV[:, hsl(h)], PT[:, hsl(h)], R[:, hsl(h)])
                            k.cp("act", vn[:, cols], psV[:, cols])
                            psO = pp["R"].next()
                            for h in hs:
                                for s in range(nst):
                                    k.mm(psO[:, hsl(h)], lQ(h, s), Sb(h, s), s == 0, False)
                                k.mm(psO[:, hsl(h)], QKDT[:, hsl(h)], vn[:, hsl(h)], False, True)
                            k.cp("act", of[:, cols], psO[:, cols])
                            pairs = [(h, s) for h in hs for s in range(nst)]
                            for g0 in range(0, len(pairs), 4):
                                grp = pairs[g0:g0 + 4]
                                psS = pp["R"].next()
                                for i, (h, s) in enumerate(grp):
                                    k.mm(psS[:, hsl(i)], lT(h, s), vn[:, hsl(h)])
                                for i, (h, s) in enumerate(grp):
                                    k.stt("dve", Sf(h, s), Sf(h, s), ex[:, 8 + h * nst + s:9 + h * nst + s], psS[:, hsl(i)], ALU.mult, ALU.add)
                            if samp:
                                for _q in range(4):
                                    k.dma("sp", V(Buf(dns_o.h, "dns%d_%d" % (hs[0], _q)), dns_o.h[4 * _q:4 * _q + 4, hs[0]].rearrange("s k v -> k s v")), sbig[:, _q * 512:(_q + 1) * 512].re("p (s v) -> p s v", s=4))
                            else:
                                k.cp("act", Sdnb[:], Sdn[:])
                        ck(11)
                        st = small.next()
                        for h in range(4):
                            k.act(junk[:, 0:128], of[:, hsl(h)], AF.Square, accum=st[:, h:h + 1])
                        rso = rstd_of(st[:, 0:4], 1.0 / 128, small, mhalf, 4)
                        k.tt("pool" if "a" in BAL else "dve", r3(of[:]), r3(of[:]), b3(rso), ALU.mult)
                        k.tt("pool", of[:], of[:], dnw[:], ALU.mult)
                        k.tt("pool" if "b" in BAL else "dve", mix[:, 0:512], of[:], zs[:], ALU.mult)

                        ck(12)
                        rqkb = rtk.next()
                        g4 = lambda v: v.re("p (g i two) -> p g i two", g=8, two=2)
                        x1 = g4(rqkf[:])[:, :, :, 0]; x2 = g4(rqkf[:])[:, :, :, 1]
                        o1 = g4(rqkb[:])[:, :, :, 0]; o2 = g4(rqkb[:])[:, :, :, 1]
                        cosb = cs[:, 0:64].un(1).bc([128, 8, 64]); sinb = cs[:, 64:128].un(1).bc([128, 8, 64])
                        t8 = lambda v: v.re("p (g i) -> p g i", g=8)
                        ta = tmpp.next(); tb = tmpp.next()
                        k.tt("dve", t8(ta[:]), x1, cosb, ALU.mult)
                        k.tt("pool", t8(tb[:]), x2, sinb, ALU.mult)
                        k.tt("dve", o1, t8(ta[:]), t8(tb[:]), ALU.subtract)
                        ta = tmpp.next(); tb = tmpp.next()
                        k.tt("pool", t8(ta[:]), x1, sinb, ALU.mult)
                        k.tt("dve", t8(tb[:]), x2, cosb, ALU.mult)
                        k.tt("pool", o2, t8(ta[:]), t8(tb[:]), ALU.add)
                        RQd = rbf2.next(); RKt = rbf2.next()
                        k.tt("dve", r3(RQd[:]), r3(rqkb[:, 0:512]), b3(rqd[:]), ALU.mult)
                        k.tt("pool", r3(RKt[:]), r3(rqkb[:, 512:1024]), b3(rkt[:]), ALU.mult)
                        RQKT = rtk.next(); RQdT_t = rbf2.next()
                        psR1 = pp["M"].next(); pR1 = psR1[:].bitcast(BF16)
                        for i in range(8):
                            k.tp(pR1[:, hsl(i)], rqkb[:, hsl(i)], identb[:])
                        k.cp("act", RQKT[:], pR1)
                        psR2 = pp["M"].next(); pR2 = psR2[:].bitcast(BF16)
                        for h in range(4):
                            k.tp(pR2[:, hsl(h)], RQd[:, hsl(h)], identb[:])
                        k.cp("dve", RQdT_t[:], pR2[:, 0:512])
                        psKQr = pp["M"].next()
                        for h in range(4):
                            k.mm(psKQr[:, hsl(h)], RQKT[:, hsl(4 + h)], RQKT[:, hsl(h)])
                        QKDTr = rbf2.next()
                        k.tt("dve", QKDTr[:], psKQr[:], DTr[:], ALU.mult)
                        orf = ofp.next()
                        for hs in headsets:
                            cols = slice(hs[0] * 128, (hs[-1] + 1) * 128)
                            if samp:
                                h = hs[0]
                                sbig = sbigp.next(); sbigb = sbigbp.next()
                                for _q in range(4):
                                    k.dma("sp", sbig[:, _q * 512:(_q + 1) * 512].re("p (s v) -> p s v", s=4), V(None, stret_d.ap[4 * _q:4 * _q + 4, h].rearrange("s k v -> k s v")))
                                k.cp("pool", sbigb[:], sbig[:])
                                QdE = expp.next(); KtE = expp.next()
                                e3 = lambda v: v.re("p (s c) -> p s c", s=16)
                                k.tt("dve", e3(QdE[:]), RQdT_t[:, hsl(h)].un(1).bc([128, 16, 128]), e3(E1[:]), ALU.mult)
                                k.tt("pool", e3(KtE[:]), RKt[:, hsl(h)].un(1).bc([128, 16, 128]), seq2[:].un(2).bc([128, 16, 128]), ALU.mult)
                                Sf = lambda hh, s, sbig=sbig: sbig[:, hsl(s)]
                                Sb = lambda hh, s, sbigb=sbigb: sbigb[:, hsl(s)]
                                lQ = lambda hh, s: QdE[:, hsl(s)]
                                lT = lambda hh, s: KtE[:, hsl(s)]
                            else:
                                Sf = lambda hh, s: Srt[:, hsl(hh)]
                                Sb = lambda hh, s: Srtb[:, hsl(hh)]
                                lQ = lambda hh, s: RQdT_t[:, hsl(hh)]
                                lT = lambda hh, s: RKt[:, hsl(hh)]
                            psO = pp["R"].next()
                            for h in hs:
                                for s in range(nst):
                                    k.mm(psO[:, hsl(h)], lQ(h, s), Sb(h, s), s == 0, False)
                                k.mm(psO[:, hsl(h)], QKDTr[:, hsl(h)], RV[:, hsl(h)], False, True)
                            k.cp("act", orf[:, cols], psO[:, cols])
                            pairs = [(h, s) for h in hs for s in range(nst)]
                            for g0 in range(0, len(pairs), 4):
                                grp = pairs[g0:g0 + 4]
                                psS = pp["R"].next()
                                for i, (h, s) in enumerate(grp):
                                    k.mm(psS[:, hsl(i)], lT(h, s), RV[:, hsl(h)])
                                for i, (h, s) in enumerate(grp):
                                    k.stt("dve", Sf(h, s), Sf(h, s), cdec[h], psS[:, hsl(i)], ALU.mult, ALU.add)
                            if samp:
                                for _q in range(4):
                                    k.dma("sp", V(Buf(rets_o.h, "rets%d_%d" % (hs[0], _q)), rets_o.h[4 * _q:4 * _q + 4, hs[0]].rearrange("s k v -> k s v")), sbig[:, _q * 512:(_q + 1) * 512].re("p (s v) -> p s v", s=4))
                            else:
                                k.cp("act", Srtb[:], Srt[:])
                        ck(13)
                        st = small.next()
                        for h in range(4):
                            k.act(junk[:, 0:128], orf[:, hsl(h)], AF.Copy, accum=st[:, h:h + 1])
                        for h in range(4):
                            k.act(junk[:, 128:256], orf[:, hsl(h)], AF.Square, accum=st[:, 4 + h:5 + h])
                        s2 = small.next()
                        k.ts("dve", s2[:, 0:4], st[:, 0:4], 1.0 / 128, ALU.mult)
                        k.tt("dve", s2[:, 4:8], s2[:, 0:4], s2[:, 0:4], ALU.mult)
                        k.stt("dve", s2[:, 8:12], st[:, 4:8], 1.0 / 128, s2[:, 4:8], ALU.mult, ALU.subtract)
                        rsr = rstd_of(s2[:, 8:12], 1.0, small, mhalf, 4)
                        k.tt("pool" if "d" in BAL else "dve", r3(orf[:]), r3(orf[:]), b3(s2[:, 0:4]), ALU.subtract)
                        k.tt("pool", r3(orf[:]), r3(orf[:]), b3(rsr), ALU.mult)
                        k.tt("pool" if "c" in BAL else "dve", orf[:], orf[:], retw[:], ALU.mult)
                        k.tt("pool", mix[:, 512:1024], orf[:], rgs[:], ALU.mult)
                        ck(14)
                        psM = pp["L"].next(); pM = psM[:].bitcast(BF16)
                        for kk in range(8):
                            k.tp(pM[:, hsl(kk)], mix[:, hsl(kk)], identb[:])
                        k.cp("act", mixT[:], pM)
                        h1 = xr if samp else h1p.next()
                        for half in range(2):
                            psH = pp["L"].next()
                            for kk in range(8):
                                k.mm(psH[:], mixT[:, hsl(kk)], w_out[kk][:, half * 512:(half + 1) * 512], kk == 0, kk == 7)
                            k.tt("dve", h1[:, half * 512:(half + 1) * 512], psH[:], xr[:, half * 512:(half + 1) * 512], ALU.add)
                        k.dma("sp", h1_s.rows(r0), h1[:])

                except StopBuild:
                    pass
                n3 = 3 * (NSAMP if samp else 1)
                dnc_o = dncs_o if samp else dncp_o
                for gI in range(3):
                    ps = pp["L"].next()
                    for i in range(4):
                        fc = gI * 4 + i
                        k.tp(ps[0:n3, hsl(i)], cx.c(fc, 0, n3), identf[:])
                    tmp = tmpp.next()
                    k.cp("act", tmp[0:n3, :], ps[0:n3, :])
                    k.dma("sp", V(Buf(dnc_o.h, "dnc%d" % gI), dnc_o.h[:, gI * 512:(gI + 1) * 512]), tmp[0:n3, :])
                if not samp:
                    k.dma("sp", V(dnp_o, dnp_o.h.rearrange("h k v -> k h v")), Sdn[:].re("p (h v) -> p h v", h=4))
                    k.dma("sp", V(retp_o, retp_o.h.rearrange("h k v -> k h v")), Srt[:].re("p (h v) -> p h v", h=4))
                S.emit_phase()

        sb_ex = None

        WA = None

        def phase_a_wrap(samp):
            nonlocal sb_ex
            with contextlib.ExitStack() as st0:
                sb0, _ = mk_alloc(st0)
                sb_ex = sb0([128, 72], n=3, name="ex")
                phase_a(samp)

        def load_cols(st_sb, wd, kchunks, c0, c1, name):
            out = []
            for kk in range(kchunks):
                b = st_sb([128, c1 - c0], BF16, name=name)
                k.dma("pool", b[:], wd[kk * 128:(kk + 1) * 128, c0:c1])
                out.append(b)
            return out

        import os
        _ph = os.environ.get("KDBG_PH", "ABCD")
        with contextlib.ExitStack() as stw:
            sbw, _ = mk_alloc(stw)
            if "A" in _ph or "B" in _ph:
                WA = (load_cols(sbw, w_in_d, 8, 0, 1536, "w_inq"), load_cols(sbw, w_in_d, 8, 1536, DIN, "w_inr"),
                      load_cols(sbw, w_out_d, 8, 0, D, "w_out"))
            if "A" in _ph:
                phase_a_wrap(False)
            if "B" in _ph:
                phase_a_wrap(True)

        def phase_b():
            with contextlib.ExitStack() as st:
                sb, psum_pool = mk_alloc(st)
                pp = psum_pool(tr=2, up=4, down=2)
                GP = [(0, 6), (6, 12), (12, 17), (17, 22)]
                w_up_g = []
                for (p0, p1) in GP:
                    ug = load_cols(sb, w_up_d, 8, p0 * 128, p1 * 128, "w_upg")
                    uv = load_cols(sb, w_up_d, 8, DFF + p0 * 128, DFF + p1 * 128, "w_upv")
                    w_up_g.append((p0, p1, ug, uv))

                def w_up_sl(kk, fc):
                    part, i = (0, fc) if fc < 22 else (1, fc - 22)
                    for (p0, p1, ug, uv) in w_up_g:
                        if p0 <= i < p1:
                            return (ug, uv)[part][kk][:, (i - p0) * 128:(i - p0 + 1) * 128]
                w_down = load_w(sb, w_down_d, 22, D, "w_down")
                identf = sb([128, 128]); k.dma("sp", identf[:], ident_d)
                identb = sb([128, 128], BF16); k.dma("pool", identb[:], ident_d)
                mhalf = sb([128, 8]); k.ms("pool", mhalf[:], -0.5)
                fnw8 = sb([128, 8]); k.dma("sp", fnw8[:], fnw8_d)
                fcw = sb([128, 132]); k.dma("sp", fcw[:], fcw_d)
                fcb = sb([128, 44]); k.dma("sp", fcb[:], fcb_d)
                NTm = 256
                import os as _o
                htp = sb([128, D], n=int(_o.environ.get("KB_HT", "4")), name="ht")
                mbfp = sb([128, D], BF16, n=2, name="mbf")
                import os as _o
                mTp = sb([128, 8 * NTm], BF16, n=int(_o.environ.get("KB_MT", "1")), name="mT")
                actTp = sb([128, 22 * NTm], BF16, n=int(_o.environ.get("KB_ACTT", "1")), name="actT")
                uprep = sb([128, 264], n=int(_o.environ.get("KB_UP", "4")), name="upre")
                ycp = sb([128, 256], n=int(_o.environ.get("KB_YC", "4")), name="yc")
                sgp = sb([128, 256], n=int(_o.environ.get("KB_SG", "2")), name="sg")
                cf_all = sb([128, 44 * 32], name="cf"); k.ms("pool", cf_all[:], 0.0); cf = Chunked(cf_all, 44, 32)
                fcT = sb([128, 44 * 32], name="fcT")
                junk = sb([128, D], BF16, name="junk"); junk = V(None, junk.h[:])
                small = sb([128, 8], n=8, name="small")
                h2p = sb([128, D], n=int(_o.environ.get("KB_H2", "2")), name="h2")
                tmpp = sb([128, 512], n=int(_o.environ.get("KB_TMP", "2")), name="tmp")
                hsl = lambda h: slice(h * 128, (h + 1) * 128)

                for gI in range(11):
                    tmp = tmpp.next()
                    k.dma("sp", tmp[0:32, :], stfc_d[:, gI * 512:(gI + 1) * 512])
                    ps = pp["tr"].next()
                    for i in range(4):
                        k.tp(ps[:, i * 32:(i + 1) * 32], tmp[0:32, hsl(i)], identf[0:32, 0:32])
                    k.cp("act", fcT[:, gI * 128:(gI + 1) * 128], ps[:, 0:128])

                blocks = [(b * 256, 256, 1, 256, False) for b in range(SEQ // 256)] + [(SEQ, 128, NSAMP, LS, True)]
                for (t0, NT, nseq, L, samp) in blocks:
                    nsub = NT // 128
                    mT = mTp.next(); actT = actTp.next()
                    mT3 = mT[:].re("p (k t) -> p k t", k=8)
                    hts = []
                    for j in range(nsub):
                        r0 = t0 + 128 * j
                        ht = htp.next(); hts.append(ht)
                        k.dma("sp", ht[:], h1_s.rows(r0))
                        ss = small.next()
                        k.act(junk, ht[:], AF.Square, accum=ss[:, 0:1])
                        rs = rstd_of(ss[:, 0:1], 1.0 / D, small, mhalf, 1)
                        mbf = mbfp.next()
                        k.ts("dve", mbf[:], ht[:], rs, ALU.mult)
                        ps = pp["tr"].next(); pb = ps[:].bitcast(BF16)
                        for kk in range(8):
                            k.tp(pb[:, hsl(kk)], mbf[:, hsl(kk)], identb[:])
                        k.tt("dve", mT3[:, :, 128 * j:128 * (j + 1)], pb.re("p (k t) -> p k t", k=8),
                             fnw8[:].un(2).bc([128, 8, 128]), ALU.mult)
                    actT3 = actT[:].re("p (c t) -> p c t", c=22)
                    for i in range(22):
                        ys = []
                        for fc in (i, 22 + i):
                            ps = pp["up"].next()
                            for kk in range(8):
                                k.mm(ps[:, 0:NT], w_up_sl(kk, fc), mT3[:, kk, 0:NT], kk == 0, kk == 7)
                            up = uprep.next()
                            uv = up[:, 0:nseq * (2 + L)].re("p (s l) -> p s l", s=nseq)
                            psv = ps[:, 0:NT].re("p (s l) -> p s l", s=nseq)
                            k.cp("act", uv[:, :, 2:2 + L], psv)
                            if samp:
                                k.cp("pool", uv[:, :, 0:2], fcT[:, fc * 32:(fc + 1) * 32].re("p (s j) -> p s j", s=nseq))
                            else:
                                k.cp("pool", uv[:, :, 0:2], cf.c(fc, 0, 2).un(1))
                            k.cp("pool", cf.c(fc, 0, 2 * nseq).re("p (s j) -> p s j", s=nseq), uv[:, :, L:L + 2])
                            y = ycp.next(); ys.append(y)
                            yv = y[:, 0:NT].re("p (s l) -> p s l", s=nseq)
                            k.act(yv, psv, AF.Identity, scale=fcw[:, fc * 3 + 2:fc * 3 + 3], bias=fcb[:, fc:fc + 1])
                            k.stt("dve", yv, uv[:, :, 1:1 + L], fcw[:, fc * 3 + 1:fc * 3 + 2], yv, ALU.mult, ALU.add)
                            k.stt("dve", yv, uv[:, :, 0:L], fcw[:, fc * 3 + 0:fc * 3 + 1], yv, ALU.mult, ALU.add)
                        sg = sgp.next()
                        k.act(sg[:, 0:NT], ys[0][:, 0:NT], AF.Silu)
                        k.tt("dve", actT3[:, i, 0:NT], sg[:, 0:NT], ys[1][:, 0:NT], ALU.mult)
                    for j in range(nsub):
                        r0 = t0 + 128 * j
                        h2 = h2p.next()
                        for half in range(2):
                            psH = pp["down"].next()
                            for c in range(22):
                                k.mm(psH[:], actT3[:, c, 128 * j:128 * (j + 1)], w_down[c][:, half * 512:(half + 1) * 512], c == 0, c == 21)
                            k.tt("dve", h2[:, half * 512:(half + 1) * 512], psH[:], hts[j][:, half * 512:(half + 1) * 512], ALU.add)
                        k.dma("sp", h2_s.rows(r0), h2[:])
                    last_prompt = (not samp) and t0 + NT == SEQ
                    if last_prompt or samp:
                        n2 = 2 * nseq
                        fo = fcs_o if samp else fcp_o
                        for gI in range(11):
                            ps = pp["tr"].next()
                            for i in range(4):
                                fc = gI * 4 + i
                                k.tp(ps[0:n2, hsl(i)], cf.c(fc, 0, n2), identf[:])
                            tmp = tmpp.next()
                            k.cp("act", tmp[0:n2, :], ps[0:n2, :])
                            k.dma("sp", V(Buf(fo.h, "fo%d" % gI), fo.h[:, gI * 512:(gI + 1) * 512]), tmp[0:n2, :])
                S.emit_phase()

        if "C" in _ph:
            phase_b()

        def phase_c():
            with contextlib.ExitStack() as st:
                sb, psum_pool = mk_alloc(st)
                pp = psum_pool(tr=2, mm=6)
                w_gate = load_w(sb, w_gate_d, 8, D, "w_gate")
                w_ple = load_w(sb, w_ple_d, 2, D, "w_ple")
                identb = sb([128, 128], BF16); k.dma("pool", identb[:], ident_d)
                mhalf = sb([128, 8]); k.ms("pool", mhalf[:], -0.5)
                pnw8 = sb([128, 8]); k.dma("sp", pnw8[:], pnw8_d)
                finw = sb([128, D]); k.dma("sp", finw[:], V(None, finw_d.ap.partition_broadcast(128)))
                h2p = sb([128, D], n=PCN, name="h2")
                ptp = sb([128, 256], n=PCN, name="pt")
                pbp = sb([128, 256], BF16, n=PCN, name="pb")
                nbfp = sb([128, D], BF16, n=PCN, name="nbf")
                nTp = sb([128, D], BF16, n=PCN, name="nT")
                pTp = sb([128, 256], BF16, n=PCN, name="pT")
                tgp = sb([128, D], n=PCN, name="tg")
                h3p = sb([128, D], n=PCN, name="h3")
                yp = sb([128, D], n=PCN, name="y")
                junk = sb([128, D], BF16, name="junk"); junk = V(None, junk.h[:])
                small = sb([128, 8], n=4 * PCN, name="small")
                hsl = lambda h: slice(h * 128, (h + 1) * 128)
                for it in range(NTOK // 128):
                    r0 = it * 128
                    h2 = h2p.next()
                    k.dma("sp", h2[:], h2_s.rows(r0))
                    pt = ptp.next()
                    k.dma("sp", pt[:], p_d[r0:r0 + 128, :])
                    ss = small.next()
                    k.act(junk, h2[:], AF.Square, accum=ss[:, 0:1])
                    rs = rstd_of(ss[:, 0:1], 1.0 / D, small, mhalf, 1)
                    nbf = nbfp.next()
                    k.act(nbf[:], h2[:], AF.Copy, scale=rs)
                    ps = pp["tr"].next(); pb = ps[:].bitcast(BF16)
                    for kk in range(8):
                        k.tp(pb[:, hsl(kk)], nbf[:, hsl(kk)], identb[:])
                    nT = nTp.next()
                    k.tt("dve", nT[:].re("p (k t) -> p k t", k=8), pb.re("p (k t) -> p k t", k=8),
                         pnw8[:].un(2).bc([128, 8, 128]), ALU.mult)
                    pbf = pbp.next()
                    k.cp("pool", pbf[:], pt[:])
                    ps2 = pp["tr"].next(); pb2 = ps2[:].bitcast(BF16)
                    for kk in range(2):
                        k.tp(pb2[:, hsl(kk)], pbf[:, hsl(kk)], identb[:])
                    pT = pTp.next()
                    k.cp("act", pT[:], pb2[:, 0:256])
                    tg = tgp.next(); h3 = h3p.next()
                    for half in range(2):
                        hs_ = slice(half * 512, (half + 1) * 512)
                        psG = pp["mm"].next()
                        for kk in range(8):
                            k.mm(psG[:], nT[:, hsl(kk)], w_gate[kk][:, hs_], kk == 0, kk == 7)
                        psP = pp["mm"].next()
                        for kk in range(2):
                            k.mm(psP[:], pT[:, hsl(kk)], w_ple[kk][:, hs_], kk == 0, kk == 1)
                        k.act(tg[:, hs_], psG[:], AF.Tanh, scale=0.5)
                        k.stt("dve", tg[:, hs_], tg[:, hs_], 1.0, psP[:], ALU.add, ALU.mult)
                        k.stt("dve", h3[:, hs_], tg[:, hs_], 0.5, h2[:, hs_], ALU.mult, ALU.add)
                    ss = small.next()
                    k.act(junk, h3[:], AF.Square, accum=ss[:, 0:1])
                    rs = rstd_of(ss[:, 0:1], 1.0 / D, small, mhalf, 1)
                    y = yp.next()
                    k.stt("dve", y[:], h3[:], rs, finw[:], ALU.mult, ALU.mult)
                    k.dma("sp", y_o.rows(r0), y[:])
                S.emit_phase()

        if "D" in _ph:
            phase_c()
    return nc


def _consts():
    c = {}
    idx = np.arange(128)
    c["ident"] = np.eye(128, dtype=np.float32)
    c["irep"] = np.tile(np.eye(128, dtype=np.float32), (1, 4))
    lg = np.log(1.0 - 2.0 ** (-5.0 - np.arange(4, dtype=np.float64)))
    sc = 128.0 ** -0.5
    for v, C in (("P", 128), ("S", LS)):
        seq = idx // C
        pos = idx % C
        same = seq[:, None] == seq[None, :]
        a = idx[:, None]
        b = idx[None, :]
        c["CM" + v] = (same & (a <= b)).astype(np.float32)
        c["UM" + v] = (same & (a > b)).astype(np.float32)
        c["NEGs" + v] = np.where(same & (a > b), 0.0, NEG).astype(np.float32)
        c["NEGT" + v] = np.where(same & (b >= a), 0.0, NEG).astype(np.float32)
        dtr = np.zeros((128, 4, 128), np.float64)
        for h in range(4):
            dtr[:, h, :] = np.where(same & (b >= a), sc * np.exp((b - a) * lg[h]), 0.0)
        c["DTr" + v] = dtr.reshape(128, 512).astype(np.float32)
        c["rqd" + v] = np.exp((pos[:, None] + 1.0) * lg[None, :]).astype(np.float32)
        c["rkt" + v] = (sc * np.exp((C - 1.0 - pos[:, None]) * lg[None, :])).astype(np.float32)
    s16 = np.arange(16)
    c["E1"] = (s16[:, None] == (idx[None, :] // LS)).astype(np.float32).reshape(-1)
    c["seq2"] = ((idx[:, None] // LS) == s16[None, :]).astype(np.float32)
    pos = np.concatenate([np.arange(SEQ), PAST + (np.arange(NSAMP * LS) % LS)]).astype(np.float32)
    inv = (np.float32(10000.0) ** (-np.arange(0, 128, 2, dtype=np.float32) / np.float32(128))).astype(np.float32)
    ang = (pos[:, None] * inv[None, :]).astype(np.float32)
    c["cosT"] = np.cos(ang.astype(np.float64)).astype(np.float32)
    c["sinT"] = np.sin(ang.astype(np.float64)).astype(np.float32)
    return c


_NC_CACHE = {}


def kernel(x_prompt, x_sample, p_prompt, p_sample, state_dn_conv, state_dn, state_ret,
           state_ffn_conv, attn_norm_w, w_in, dn_conv_w, dn_A_log, dn_dt_bias, dn_norm_w,
           ret_norm_w, w_out, ffn_norm_w, w_up, ffn_conv_w, ffn_conv_b, w_down, ple_norm_w,
           w_ple_gate, w_ple, final_norm_w):
    f = lambda a: np.ascontiguousarray(np.asarray(a), dtype=np.float32)
    x_prompt, x_sample, p_prompt, p_sample = f(x_prompt), f(x_sample), f(p_prompt), f(p_sample)
    state_dn_conv, state_dn, state_ret, state_ffn_conv = f(state_dn_conv), f(state_dn), f(state_ret), f(state_ffn_conv)
    col8 = lambda w: f(np.asarray(w).reshape(8, 128).T)
    shared = dict(
        w_in=f(w_in)[0], w_out=f(w_out)[0], w_up=f(w_up)[0], w_down=f(w_down)[0],
        w_gate=f(w_ple_gate)[0], w_ple=f(w_ple)[0],
        anw8=col8(f(attn_norm_w)[0]), fnw8=col8(f(ffn_norm_w)[0]), pnw8=col8(f(ple_norm_w)[0]),
        finw=f(final_norm_w),
        dncw=f(f(dn_conv_w)[0].T.reshape(12, 128, 4).transpose(1, 0, 2).reshape(128, 48)),
        alog=f(dn_A_log)[0], dtb=f(dn_dt_bias)[0],
        dnw4=f(np.tile(f(dn_norm_w)[0], 4)), retw=f(ret_norm_w)[0],
        fcw=f(f(ffn_conv_w)[0].T.reshape(44, 128, 3).transpose(1, 0, 2).reshape(128, 132)),
        fcb=f(f(ffn_conv_b)[0].reshape(44, 128).T),
    )
    shared.update(_consts())
    in_maps = []
    for i in range(NCORES):
        sl = slice(NSAMP * i, NSAMP * (i + 1))
        m = dict(shared)
        m["x"] = f(np.concatenate([x_prompt[i], x_sample[sl].reshape(NSAMP * LS, D)], axis=0))
        m["p"] = f(np.concatenate([p_prompt[0, i], p_sample[0, sl].reshape(NSAMP * LS, 256)], axis=0))
        m["st_dnc"] = f(state_dn_conv[0, sl].reshape(48, 1536))
        m["st_dn"] = f(state_dn[0, sl])
        m["st_ret"] = f(state_ret[0, sl])
        m["st_fc"] = f(state_ffn_conv[0, sl].reshape(32, 2 * DFF))
        in_maps.append(m)
    if "nc" not in _NC_CACHE:
        _NC_CACHE["nc"] = build_program()
    nc = _NC_CACHE["nc"]
    res = run_bass_kernel_spmd(nc, in_maps, core_ids=list(range(NCORES)))
    R = res.results
    _NC_CACHE["last"] = R
    g = lambda name: [np.asarray(R[i][name], dtype=np.float32) for i in range(NCORES)]
    y = g("y")
    y_prompt = np.stack([a[:SEQ] for a in y], axis=0)
    y_sample = np.concatenate([a[SEQ:].reshape(NSAMP, LS, D) for a in y], axis=0)
    dncp = np.stack(g("o_dnc_p"), axis=0)[None]
    dnp = np.stack(g("o_dn_p"), axis=0)[None]
    retp = np.stack(g("o_ret_p"), axis=0)[None]
    fcp = np.stack(g("o_fc_p"), axis=0)[None]
    dncs = np.concatenate([a.reshape(NSAMP, 3, 1536) for a in g("o_dnc_s")], axis=0)[None]
    dns = np.concatenate(g("o_dn_s"), axis=0)[None]
    rets = np.concatenate(g("o_ret_s"), axis=0)[None]
    fcs = np.concatenate([a.reshape(NSAMP, 2, 2 * DFF) for a in g("o_fc_s")], axis=0)[None]
    return (y_prompt, y_sample, dncp, dnp, retp, fcp, dncs, dns, rets, fcs)
```

```python
import contextlib
import os as _osb
import numpy as np
import concourse.bass as bass
import concourse.mybir as mybir
from concourse.bass_utils import run_bass_kernel_spmd

F32 = mybir.dt.float32
BF16 = mybir.dt.bfloat16
AF = mybir.ActivationFunctionType
ALU = mybir.AluOpType
AX = mybir.AxisListType

NCORES = 8
D = 1024
SEQ = 2048
NSAMP = 16
LS = 8
NTOK = SEQ + NSAMP * LS
DIN = 4104
DFF = 2816
EPS = 1e-6
PAST = 16384
NEG = -30000.0
PE2R, PRBF2, PKQT, PRTK, POF = 5, 5, 3, 3, 2
import os as _osb
BAL = _osb.environ.get('K_BAL', '')
PCN = int(_osb.environ.get('K_PCN', '6'))
C_QKV, C_Z, C_B, C_A, C_RQ, C_RK, C_RV, C_RG = 0, 1536, 2048, 2052, 2056, 2568, 3080, 3592


class T:
    __slots__ = ("name", "last_writer", "readers")

    def __init__(self, name=""):
        self.name = name
        self.last_writer = None
        self.readers = []


class Op:
    __slots__ = ("eng", "fn", "deps", "users", "ndep", "signaled", "sigval", "sem", "is_dma", "idx", "cost",
                 "aset", "phase", "finish", "pos", "rtime", "tag", "prio")

    def __init__(self, eng, fn, is_dma):
        self.eng = eng
        self.fn = fn
        self.deps = []
        self.users = []
        self.signaled = False
        self.sigval = None
        self.sem = None
        self.is_dma = is_dma
        self.finish = 0.0
        self.pos = -1


class Sched:
    ENGS = ("pe", "act", "dve", "pool", "sp")
    XLAT = float(_osb.environ.get('K_XLAT', '500'))
    SLAT = float(_osb.environ.get('K_SLAT', '60'))

    def __init__(self, nc, n_dma_sems=14):
        self.nc = nc
        self.ops = []
        self.n_dma_sems = n_dma_sems
        self.nops = 0
        self.phase = 0
        import os as _os
        self.prio_mode = int(_os.environ.get("KS_PRIO", "1"))
        self.prio_w = float(_os.environ.get("KS_PRIOW", "0.0"))

    def op(self, eng, fn, reads=(), writes=(), dma=False, cost=200.0, aset=None):
        o = Op(eng, fn, dma)
        o.idx = self.nops
        self.nops += 1
        o.cost = cost
        o.aset = aset
        o.phase = self.phase
        import sys as _sys
        fr = _sys._getframe(2)
        o.tag = fr.f_lineno if fr.f_code.co_name != "<lambda>" else fr.f_back.f_lineno
        deps = []
        for t in reads:
            if t.last_writer is not None:
                deps.append(t.last_writer)
        for t in writes:
            if t.last_writer is not None:
                deps.append(t.last_writer)
            deps.extend(t.readers)
        seen = set()
        for d in deps:
            if id(d) in seen or d is o or d.phase != o.phase:
                continue
            seen.add(id(d))
            o.deps.append(d)
            d.users.append(o)
        for t in reads:
            t.readers.append(o)
        for t in writes:
            t.last_writer = o
            t.readers = []
        self.ops.append(o)
        return o

    def open(self, stack):
        nc = self.nc
        self.sems = {}
        for e in ("pe", "act", "dve", "pool"):
            self.sems[e] = stack.enter_context(nc.semaphore("s_" + e))
        for e in ("sp", "act", "pool"):
            for k in range(self.n_dma_sems):
                self.sems[(e, k)] = stack.enter_context(nc.semaphore("d_%s_%d" % (e, k)))
        self.cnt = {}
        self.dma_n = {e: 0 for e in self.ENGS}

    def _schedule(self):
        import heapq
        ops = self.ops
        future = {e: [] for e in self.ENGS}
        avail = {e: [] for e in self.ENGS}
        free_at = {e: 0.0 for e in self.ENGS}
        cur_set = {e: None for e in self.ENGS}
        streams = {e: [] for e in self.ENGS}
        self._pipe = 0.0
        bl = {}
        for o in reversed(ops):
            m = 0.0
            for u in o.users:
                lat = self.XLAT if (u.eng != o.eng or o.is_dma) else self.SLAT
                v = bl[id(u)] + lat
                if v > m:
                    m = v
            bl[id(o)] = m + o.cost
        mode = self.prio_mode
        for o in ops:
            o.ndep = len(o.deps)
            o.rtime = 0.0
            if mode == 0:
                o.prio = o.idx
            else:
                o.prio = -bl[id(o)] + self.prio_w * o.idx
        for o in ops:
            if o.ndep == 0:
                heapq.heappush(future[o.eng], (0.0, o.prio, o.idx, o))
        left = len(ops)
        while left:
            best = None
            for e in self.ENGS:
                f, a = future[e], avail[e]
                while f and f[0][0] <= free_at[e]:
                    _, pr, i, o = heapq.heappop(f)
                    heapq.heappush(a, (pr, i, o))
                if a:
                    cand = (free_at[e], a[0][0], e, True)
                elif f:
                    cand = (f[0][0], f[0][1], e, False)
                else:
                    continue
                if best is None or cand < best:
                    best = cand
            start, _, e, from_avail = best
            if from_avail:
                _, _, o = heapq.heappop(avail[e])
            else:
                _, _, _, o = heapq.heappop(future[e])
            c = o.cost
            if o.aset is not None and o.aset != cur_set[e]:
                if cur_set[e] is not None:
                    c += 1300.0
                cur_set[e] = o.aset
            if o.is_dma:
                xfer = max(0.0, c - 2000.0) * (120.0 / 220.0)
                t0x = max(start + 1000.0, self._pipe)
                self._pipe = t0x + xfer
                o.finish = t0x + xfer + 1000.0
                free_at[e] = start + 60.0
            else:
                o.finish = start + c
                free_at[e] = o.finish
            o.pos = len(streams[e])
            streams[e].append(o)
            left -= 1
            for u in o.users:
                lat = self.XLAT if (u.eng != e or o.is_dma) else self.SLAT
                t = o.finish + lat
                if t > u.rtime:
                    u.rtime = t
                u.ndep -= 1
                if u.ndep == 0:
                    heapq.heappush(future[u.eng], (u.rtime, u.prio, u.idx, u))
        self.makespan = max(free_at.values())
        return streams

    def emit_phase(self):
        nc = self.nc
        sems = self.sems
        cnt = self.cnt
        streams = self._schedule()
        for e in self.ENGS:
            last_on_sem = {}
            for o in streams[e]:
                if o.is_dma:
                    kk = self.dma_n[e] % self.n_dma_sems
                    self.dma_n[e] += 1
                    o.sem = (e, kk)
                    c = cnt.get(o.sem, 0) + 16
                    cnt[o.sem] = c
                    o.sigval = c
        plan = {}
        for e in self.ENGS:
            wpos = {}
            wl = []
            for o in streams[e]:
                ws = []
                for d in o.deps:
                    if d.is_dma:
                        ws.append(d)
                        continue
                    if d.eng == e and e == "pe":
                        continue
                    if d.pos > wpos.get(d.eng, -1):
                        wpos[d.eng] = d.pos
                        d.signaled = True
                        ws.append(d)
                wl.append(ws)
            plan[e] = wl
        for e in ("pe", "act", "dve", "pool"):
            for o in reversed(streams[e]):
                if not o.is_dma:
                    o.signaled = True
                    break
        for e in self.ENGS:
            for o in streams[e]:
                if (not o.is_dma) and o.signaled:
                    c = cnt.get(e, 0) + 1
                    cnt[e] = c
                    o.sem = e
                    o.sigval = c
        final = dict(cnt)
        with nc.Block() as block:
            engobj = {"pe": block.tensor, "act": block.scalar, "dve": block.vector,
                      "pool": block.gpsimd, "sp": block.sync}

            def run(e, eng):
                waited = {}
                dma_prev = {}

                def wait_sv(sem, val):
                    if waited.get(sem, 0) >= val:
                        return
                    eng.wait_ge(sems[sem], val)
                    waited[sem] = val

                for o, ws in zip(streams[e], plan[e]):
                    for d in ws:
                        wait_sv(d.sem, d.sigval)
                    if o.is_dma:
                        if o.sigval > 16:
                            wait_sv(o.sem, o.sigval - 16)
                    ins = o.fn(eng)
                    if o.is_dma:
                        ins.then_inc(sems[o.sem], 16)
                    elif o.signaled:
                        ins.then_inc(sems[o.sem], 1)
                for sem, val in final.items():
                    wait_sv(sem, val)

            for e in self.ENGS:
                def mk(e):
                    def f(eng):
                        run(e, eng)
                    return f
                engobj[e](mk(e))
        self.ops = []
        self.phase += 1


class StopBuild(Exception):
    pass


def ck(n):
    import os
    lim = float(os.environ.get("KDBG_CK", "1000"))
    if n > lim:
        raise StopBuild()


class Buf:
    def __init__(self, h, name="", excl=False):
        self.h = h
        self.t = T(name)
        self.excl = excl

    def __getitem__(self, k):
        return V(self, self.h[k])


class V:
    def __init__(self, buf, ap):
        self.buf = buf
        self.ap = ap

    def __getitem__(self, k):
        return V(self.buf, self.ap[k])

    def re(self, pat_, **kw):
        return V(self.buf, self.ap.rearrange(pat_, **kw))

    def bc(self, shape):
        return V(self.buf, self.ap.to_broadcast(list(shape)))

    def un(self, ax):
        return V(self.buf, self.ap.unsqueeze(ax))

    def bitcast(self, dt):
        return V(self.buf, self.ap.bitcast(dt))


class Chunked:
    def __init__(self, buf, n, w):
        self.bufs = [Buf(buf.h, "%s_c%d" % (buf.t.name, i)) for i in range(n)]
        self.w = w

    def c(self, i, a=0, b=None):
        b = self.w if b is None else b
        return self.bufs[i][:, i * self.w + a:i * self.w + b]


class Pool:
    def __init__(self, bufs):
        self.bufs = bufs
        self.i = 0

    def next(self):
        b = self.bufs[self.i % len(self.bufs)]
        self.i += 1
        return b


def _tr(*vs):
    return [v.buf.t for v in vs if isinstance(v, V) and v.buf is not None]


def _rw(reads, writes):
    r, w = [], []
    for v in reads:
        if isinstance(v, V) and v.buf is not None:
            (w if v.buf.excl else r).append(v.buf.t)
    for v in writes:
        if isinstance(v, V) and v.buf is not None:
            w.append(v.buf.t)
    return dict(reads=r, writes=w)


def _a(x):
    return x.ap if isinstance(x, V) else x


def _fs(v):
    n = 1
    for d in v.ap.shape[1:]:
        n *= int(d)
    return n


def _is_psum(v):
    return isinstance(v, V) and v.buf is not None and v.buf.excl


_ASET = {AF.Silu: "silu", AF.Exp: "lnexp", AF.Ln: "lnexp", AF.Tanh: "silu"}


class K:
    def __init__(self, nc, S):
        self.nc = nc
        self.S = S

    def mm(self, out, lhsT, rhs, start=True, stop=True):
        n = max(32, _fs(rhs))
        c = n / 2.37 * (4.0 if rhs.ap.dtype == F32 else 1.0) + 48.0
        self.S.op("pe", lambda e: e.matmul(out.ap, lhsT=lhsT.ap, rhs=rhs.ap, start=start, stop=stop),
                  cost=c, **_rw([lhsT, rhs], [out]))

    def tp(self, out, in_, ident):
        c = max(32, _fs(ident)) / 2.37 * (4.0 if in_.ap.dtype == F32 else 1.0) + 48.0
        self.S.op("pe", lambda e: e.transpose(out.ap, in_.ap, ident.ap), cost=c, **_rw([in_, ident], [out]))

    def act(self, out, in_, func, scale=1.0, bias=0.0, accum=None):
        def f(e):
            kw = dict(out=out.ap, in_=in_.ap, func=func, scale=_a(scale), bias=_a(bias))
            if accum is not None:
                kw["accum_out"] = accum.ap
            return e.activation(**kw)
        c = 190.0 + 0.6 * _fs(in_) + (90.0 if accum is not None else 0.0)
        self.S.op("act", f, cost=c, aset=_ASET.get(func), **_rw([in_, scale, bias], [out, accum]))

    def _vc(self, eng, *vs):
        f = max(_fs(v) for v in vs if isinstance(v, V))
        ps = any(_is_psum(v) for v in vs)
        if eng == "pool":
            return 1100.0 + 0.45 * f
        return (130.0 if ps else 90.0) + 1.25 * f

    def tt(self, eng, out, a, b, op):
        self.S.op(eng, lambda e: e.tensor_tensor(out=out.ap, in0=a.ap, in1=b.ap, op=op), cost=self._vc(eng, out, a, b),
                  **_rw([a, b], [out]))

    def ts(self, eng, out, a, s1, op0, s2=None, op1=None):
        def f(e):
            if s2 is None:
                return e.tensor_scalar(out=out.ap, in0=a.ap, scalar1=_a(s1), scalar2=None, op0=op0)
            return e.tensor_scalar(out=out.ap, in0=a.ap, scalar1=_a(s1), scalar2=_a(s2), op0=op0, op1=op1)
        self.S.op(eng, f, cost=self._vc(eng, out, a), **_rw([a, s1, s2], [out]))

    def stt(self, eng, out, a, s, b, op0, op1):
        self.S.op(eng, lambda e: e.scalar_tensor_tensor(out=out.ap, in0=a.ap, scalar=_a(s), in1=b.ap, op0=op0, op1=op1),
                  cost=self._vc(eng, out, a, b), **_rw([a, s, b], [out]))

    def cp(self, eng, out, a):
        if eng == "act":
            self.S.op("act", lambda e: e.copy(out=out.ap, in_=a.ap), cost=190.0 + 0.6 * _fs(a), **_rw([a], [out]))
        else:
            c = (250.0 + 0.3 * _fs(a)) if eng == "pool" else self._vc(eng, out, a)
            self.S.op(eng, lambda e: e.tensor_copy(out=out.ap, in_=a.ap), cost=c, **_rw([a], [out]))

    def recip(self, out, a):
        self.S.op("dve", lambda e: e.reciprocal(out=out.ap, in_=a.ap), cost=self._vc("dve", out, a), **_rw([a], [out]))

    def ms(self, eng, out, val):
        self.S.op(eng, lambda e: e.memset(out.ap, val), cost=250.0 + 0.3 * _fs(out), **_rw([], [out]))

    def dma(self, q, out, in_):
        nbytes = int(out.ap.shape[0]) * _fs(out) * 4
        c = 2000.0 + nbytes / 120.0
        return self.S.op(q, lambda e: e.dma_start(out=out.ap, in_=in_.ap), dma=True, cost=c, **_rw([in_], [out]))


def build_program():
    nc = bass.Bass("TRN2", target_bir_lowering=False)

    def din(name, shape):
        return V(None, nc.dram_tensor(name, list(shape), F32, kind="ExternalInput").ap())

    def dout(name, shape):
        return Buf(nc.dram_tensor(name, list(shape), F32, kind="ExternalOutput").ap(), name)

    class RowBufs:
        def __init__(self, ap):
            self.h = ap
            self.b = {}

        def rows(self, r0):
            if r0 not in self.b:
                self.b[r0] = Buf(self.h, "rb%d" % r0)
            return V(self.b[r0], self.h[r0:r0 + 128, :])

    x_d = din("x", [NTOK, D])
    p_d = din("p", [NTOK, 256])
    stdnc_d = din("st_dnc", [48, 1536])
    stdn_d = din("st_dn", [NSAMP, 4, 128, 128])
    stret_d = din("st_ret", [NSAMP, 4, 128, 128])
    stfc_d = din("st_fc", [32, 2 * DFF])
    w_in_d = din("w_in", [D, DIN])
    w_out_d = din("w_out", [D, D])
    w_up_d = din("w_up", [D, 2 * DFF])
    w_down_d = din("w_down", [DFF, D])
    w_gate_d = din("w_gate", [D, D])
    w_ple_d = din("w_ple", [256, D])
    anw8_d = din("anw8", [128, 8])
    fnw8_d = din("fnw8", [128, 8])
    pnw8_d = din("pnw8", [128, 8])
    finw_d = din("finw", [D])
    dncw_d = din("dncw", [128, 48])
    alog_d = din("alog", [4])
    dtb_d = din("dtb", [4])
    dnw4_d = din("dnw4", [512])
    retw_d = din("retw", [512])
    fcw_d = din("fcw", [128, 132])
    fcb_d = din("fcb", [128, 44])
    ident_d = din("ident", [128, 128])
    irep_d = din("irep", [128, 512])
    cos_d = din("cosT", [NTOK, 64])
    sin_d = din("sinT", [NTOK, 64])
    cvar = {}
    for v in ("P", "S"):
        cvar[v] = dict(CM=din("CM" + v, [128, 128]), UM=din("UM" + v, [128, 128]),
                       NEGs=din("NEGs" + v, [128, 128]), NEGT=din("NEGT" + v, [128, 128]),
                       DTr=din("DTr" + v, [128, 512]), rqd=din("rqd" + v, [128, 4]), rkt=din("rkt" + v, [128, 4]))
    e1_d = din("E1", [16 * 128])
    seq2_d = din("seq2", [128, 16])

    y_o = RowBufs(nc.dram_tensor("y", [NTOK, D], F32, kind="ExternalOutput").ap())
    dncp_o = dout("o_dnc_p", [3, 1536])
    dnp_o = dout("o_dn_p", [4, 128, 128])
    retp_o = dout("o_ret_p", [4, 128, 128])
    fcp_o = dout("o_fc_p", [2, 2 * DFF])
    dncs_o = dout("o_dnc_s", [48, 1536])
    dns_o = dout("o_dn_s", [NSAMP, 4, 128, 128])
    rets_o = dout("o_ret_s", [NSAMP, 4, 128, 128])
    fcs_o = dout("o_fc_s", [32, 2 * DFF])
    import os as _os
    _dbg = _os.environ.get("KDBG_OUT", "") == "1"
    _kind = dict(kind="ExternalOutput") if _dbg else {}
    h1_s = RowBufs(nc.dram_tensor("h1_scr", [NTOK, D], F32, **_kind).ap())
    h2_s = RowBufs(nc.dram_tensor("h2_scr", [NTOK, D], F32, **_kind).ap())

    lg = [float(np.log(1.0 - 2.0 ** (-5.0 - h))) for h in range(4)]

    with contextlib.ExitStack() as top:
        S = Sched(nc)
        S.open(top)
        k = K(nc, S)

        gcnt = [0]

        def mk_alloc(st):
            cnt = gcnt

            def sb(shape, dt=F32, n=0, name="t"):
                def one():
                    cnt[0] += 1
                    nm = "%s_%d" % (name, cnt[0])
                    return Buf(st.enter_context(nc.sbuf_tensor(nm, list(shape), dt)), nm)
                if n == 0:
                    return one()
                return Pool([one() for _ in range(n)])

            def psum_pool(**roles):
                assert sum(roles.values()) <= 8
                out = {}
                for role, n in roles.items():
                    bufs = []
                    for i in range(n):
                        cnt[0] += 1
                        nm = "ps_%d" % cnt[0]
                        bufs.append(Buf(st.enter_context(nc.psum_tensor(nm, [128, 512], F32)), nm, excl=True))
                    out[role] = Pool(bufs)
                return out
            return sb, psum_pool

        def load_w(st_sb, wd, kchunks, ncols, name):
            out = []
            for kk in range(kchunks):
                b = st_sb([128, ncols], BF16, name=name)
                k.dma("pool", b[:], wd[kk * 128:(kk + 1) * 128, :])
                out.append(b)
            return out

        def rstd_act(ss_in, n_inv, small, ncol):
            a = small.next()
            k.act(a[:, 0:ncol], ss_in, AF.Ln, scale=n_inv, bias=epsc[:, 0:1])
            r = small.next()
            k.act(r[:, 0:ncol], a[:, 0:ncol], AF.Exp, scale=-0.5)
            return r[:, 0:ncol]

        def rstd_of(ss_in, n_inv, small, mhalf, ncol):
            if USE_ACT_RSTD[0]:
                return rstd_act(ss_in, n_inv, small, ncol)
            a = small.next()
            k.ts("dve", a[:, 0:ncol], ss_in, n_inv, ALU.mult, EPS, ALU.add)
            r = small.next()
            k.tt("pool", r[:, 0:ncol], a[:, 0:ncol], mhalf[:, 0:ncol], ALU.pow)
            return r[:, 0:ncol]

        USE_ACT_RSTD = [False]
        epsc = None

        def phase_a(samp):
            USE_ACT_RSTD[0] = True
            try:
                phase_a_body(samp)
            finally:
                USE_ACT_RSTD[0] = False

        def phase_a_body(samp):
            nonlocal epsc
            with contextlib.ExitStack() as st:
                sb, psum_pool = mk_alloc(st)
                pp = psum_pool(E=3, M=2, R=2, L=1)
                cv = cvar["S" if samp else "P"]
                nst = NSAMP if samp else 1
                nlev = 3 if samp else 7
                Cc = LS if samp else 128
                cdec = [float(np.exp(Cc * lg[h])) for h in range(4)]
                nb = 1 if samp else 1
                w_inq, w_inr, w_out = WA
                identf = sb([128, 128]); k.dma("sp", identf[:], ident_d)
                identb = sb([128, 128], BF16); k.dma("pool", identb[:], ident_d)
                irep = sb([128, 512], BF16); k.dma("pool", irep[:], irep_d)
                onesf = sb([128, 128]); k.ms("pool", onesf[:], 1.0)
                nonesf = sb([128, 128]); k.ms("pool", nonesf[:], -1.0)
                mhalf = sb([128, 8]); k.ms("pool", mhalf[:], -0.5)
                epsc = sb([128, 1]); k.ms("pool", epsc[:], EPS)
                ecvp = sb([128, 256], n=2, name="ecv")
                CM = sb([128, 128]); k.dma("sp", CM[:], cv["CM"])
                UM = sb([128, 128]); k.dma("sp", UM[:], cv["UM"])
                NEGs = sb([128, 128], BF16); k.dma("pool", NEGs[:], cv["NEGs"])
                NEGT = sb([128, 128], BF16); k.dma("pool", NEGT[:], cv["NEGT"])
                DTr = sb([128, 512]); k.dma("sp", DTr[:], cv["DTr"])
                rqd = sb([128, 4]); k.dma("sp", rqd[:], cv["rqd"])
                rkt = sb([128, 4]); k.dma("sp", rkt[:], cv["rkt"])
                anw8 = sb([128, 8]); k.dma("sp", anw8[:], anw8_d)
                cw = sb([128, 48]); k.dma("sp", cw[:], dncw_d)
                alogb = sb([128, 4]); k.dma("sp", alogb[:], V(None, alog_d.ap.partition_broadcast(128)))
                dtbb = sb([128, 4]); k.dma("sp", dtbb[:], V(None, dtb_d.ap.partition_broadcast(128)))
                negA = sb([128, 4])
                k.act(negA[:], alogb[:], AF.Exp)
                k.ts("dve", negA[:], negA[:], -1.0, ALU.mult)
                dnw = sb([128, 512]); k.dma("sp", dnw[:], V(None, dnw4_d.ap.partition_broadcast(128)))
                retw = sb([128, 512]); k.dma("sp", retw[:], V(None, retw_d.ap.partition_broadcast(128)))
                if samp:
                    E1 = sb([128, 16 * 128], BF16)
                    k.dma("pool", E1[:], V(None, e1_d.ap.partition_broadcast(128)))
                    seq2 = sb([128, 16]); k.dma("sp", seq2[:], seq2_d)
                NT = 128 if samp else 256
                nbuf = 1 if samp else 2
                xtp = sb([128, D], n=1, name="xt")
                xrp = None if samp else sb([128, D], n=1, name="xr")
                abfp = sb([128, D], BF16, n=1, name="abf")
                aTp = sb([128, 8 * NT], BF16, n=1, name="aT")
                qkvT = sb([128, 12 * NT], BF16, name="qkvT")
                xprep = sb([128, 264], n=1 if samp else 2, name="xpre")
                ycvp = sb([128, 256], n=1 if samp else 2, name="ycv")
                cx_all = sb([128, 12 * 48], name="cx"); cx = Chunked(cx_all, 12, 48)
                junk = sb([128, D], BF16, name="junk"); junk = V(None, junk.h[:])
                small = sb([128, 24], n=24 if samp else 32, name="small")
                zsp = sb([128, 512], BF16, n=1 if samp else 2, name="zs")
                rqkf = sb([128, 1024], name="rqkf")
                qkf = sb([128, 1024], BF16, name="qkf")
                rgsp = sb([128, 512], BF16, n=1 if samp else 2, name="rgs")
                tmpp = sb([128, 512], n=2, name="tmp")
                ebf = sb([128, 512], BF16, n=4, name="ebf")
                e2r = sb([128, 512], BF16, n=3 if samp else PE2R, name="e2r")
                rbf = sb([128, 512], BF16, n=2 if samp else 4, name="rbf")
                rbf2 = sb([128, 512], BF16, n=5 if samp else PRBF2, name="rbf2")
                kqT = sb([128, 1024], BF16, n=2 if samp else PKQT, name="kqT")
                rtk = sb([128, 1024], BF16, n=2 if samp else PRTK, name="rtk")
                MG = sb([128, 512], name="MG")
                Xp = sb([128, 512], BF16, n=2, name="X")
                XTp = sb([128, 512], BF16, n=2, name="XT")
                IXp = sb([128, 512], BF16, n=2, name="IX")
                PTp = sb([128, 512], BF16, n=2 if samp else 3, name="PT")
                dexp = sb([128, 512], BF16, n=3, name="dexp")
                ofp = sb([128, 512], n=POF, name="of")
                mix = sb([128, D], BF16, name="mix")
                mixT = sb([128, D], BF16, name="mixT")
                h1p = None if samp else sb([128, D], n=1, name="h1")
                csp = sb([128, 128], n=nbuf, name="cs")
                if samp:
                    sbigp = sb([128, 16 * 128], n=2, name="sbig")
                    sbigbp = sb([128, 16 * 128], BF16, n=2, name="sbigb")
                    expp = sb([128, 16 * 128], BF16, n=2, name="exp")
                    dncT = sb([128, 12 * 48], name="dncT")
                else:
                    Sdn = sb([128, 512], name="Sdn"); k.ms("pool", Sdn[:], 0.0)
                    Sdnb = sb([128, 512], BF16, name="Sdnb"); k.ms("pool", Sdnb[:], 0.0)
                    Srt = sb([128, 512], name="Srt"); k.ms("pool", Srt[:], 0.0)
                    Srtb = sb([128, 512], BF16, name="Srtb"); k.ms("pool", Srtb[:], 0.0)
                    k.ms("pool", cx_all[:], 0.0)

                if samp:
                    blocks = [(SEQ, 128, NSAMP, LS)]
                else:
                    blocks = [(b * 256, 256, 1, 256) for b in range(SEQ // 256)]

                hsl = lambda h: slice(h * 128, (h + 1) * 128)
                b3 = lambda v: v.un(2).bc([128, 4, 128])
                r3 = lambda v: v.re("p (h d) -> p h d", h=4)

                if samp:
                    for gI in range(3):
                        tmp = tmpp.next()
                        k.dma("sp", tmp[0:48, :], stdnc_d[:, gI * 512:(gI + 1) * 512])
                        ps = pp["M"].next()
                        for i in range(4):
                            k.tp(ps[:, i * 48:(i + 1) * 48], tmp[0:48, i * 128:(i + 1) * 128], identf[0:48, 0:48])
                        k.cp("act", dncT[:, gI * 192:(gI + 1) * 192], ps[:, 0:192])

                last_xt = [None]

                def stage_a0(t0, NT, aT):
                    nsub = NT // 128
                    for j in range(nsub):
                        xt = xtp.next()
                        last_xt[0] = xt
                        k.dma("sp", xt[:], x_d[t0 + 128 * j:t0 + 128 * (j + 1), :])
                        ss = small.next()
                        k.act(junk, xt[:], AF.Square, accum=ss[:, 0:1])
                        rs = rstd_of(ss[:, 0:1], 1.0 / D, small, mhalf, 1)
                        abf = abfp.next()
                        k.ts("dve", abf[:], xt[:], rs, ALU.mult)
                        ps = pp["M"].next()
                        pb = ps[:].bitcast(BF16)
                        for kk in range(8):
                            k.tp(pb[:, hsl(kk)], abf[:, hsl(kk)], identb[:])
                        k.tt("dve", aT[:].re("p (k t) -> p k t", k=8)[:, :, 128 * j:128 * (j + 1)],
                             pb.re("p (k t) -> p k t", k=8), anw8[:].un(2).bc([128, 8, 128]), ALU.mult)

                try:
                  ck(0)
                  for bi, (t0, NT, nseq, L) in enumerate(blocks):
                    nsub = NT // 128
                    aT = aTp.next()
                    stage_a0(t0, NT, aT)
                    ck(1)
                    aT3 = aT[:].re("p (k t) -> p k t", k=8)
                    qkvT3 = qkvT[:].re("p (c t) -> p c t", c=12)
                    for fc in range(12):
                        ps = pp["M"].next()
                        for kk in range(8):
                            k.mm(ps[:, 0:NT], w_inq[kk][:, fc * 128:(fc + 1) * 128], aT3[:, kk, :], start=kk == 0, stop=kk == 7)
                        ck(1.1)
                        xp = xprep.next()
                        xv = xp[:, 0:nseq * (3 + L)].re("p (s l) -> p s l", s=nseq)
                        psv = ps[:, 0:NT].re("p (s l) -> p s l", s=nseq)
                        k.cp("act", xv[:, :, 3:3 + L], psv)
                        ck(1.2)
                        if samp:
                            k.cp("pool", xv[:, :, 0:3], dncT[:, fc * 48:(fc + 1) * 48].re("p (s j) -> p s j", s=nseq))
                        else:
                            k.cp("pool", xv[:, :, 0:3], cx.c(fc, 0, 3).un(1))
                        k.cp("pool", cx.c(fc, 0, 3 * nseq).re("p (s j) -> p s j", s=nseq), xv[:, :, L:L + 3])
                        ck(1.3)
                        y = ycvp.next()
                        yv = y[:, 0:NT].re("p (s l) -> p s l", s=nseq)
                        k.act(yv, psv, AF.Copy, scale=cw[:, fc * 4 + 3:fc * 4 + 4])
                        ck(1.4)
                        k.stt("dve", yv, xv[:, :, 2:2 + L], cw[:, fc * 4 + 2:fc * 4 + 3], yv, ALU.mult, ALU.add)
                        k.stt("dve", yv, xv[:, :, 1:1 + L], cw[:, fc * 4 + 1:fc * 4 + 2], yv, ALU.mult, ALU.add)
                        k.stt("dve", yv, xv[:, :, 0:L], cw[:, fc * 4 + 0:fc * 4 + 1], yv, ALU.mult, ALU.add)
                        ck(1.5)
                        ecv = ecvp.next()
                        k.act(ecv[:, 0:NT], y[:, 0:NT], AF.Exp, scale=-1.0)
                        k.act(ecv[:, 0:NT], ecv[:, 0:NT], AF.Ln, bias=1.0)
                        k.act(ecv[:, 0:NT], ecv[:, 0:NT], AF.Exp, scale=-1.0)
                        k.tt("dve", qkvT3[:, fc, :], y[:, 0:NT], ecv[:, 0:NT], ALU.mult)
                        ck(1.6)

                    ck(2)
                    for j in range(nsub):
                        js = slice(128 * j, 128 * (j + 1))
                        r0 = t0 + 128 * j
                        if samp:
                            xr = last_xt[0]
                        else:
                            xr = xrp.next()
                            k.dma("sp", xr[:], x_d[r0:r0 + 128, :])
                        cs = csp.next()
                        k.dma("sp", cs[:, 0:64], cos_d[r0:r0 + 128, :])
                        k.dma("sp", cs[:, 64:128], sin_d[r0:r0 + 128, :])

                        def proj(c0, n, role="M"):
                            ps = pp[role].next()
                            for kk in range(8):
                                k.mm(ps[:, 0:n], aT3[:, kk, js], w_inr[kk][:, c0 - 1536:c0 - 1536 + n], start=kk == 0, stop=kk == 7)
                            return ps

                        ck(2.1)
                        psqk = pp["E"].next(); pqk = psqk[:].bitcast(BF16)
                        for i in range(8):
                            k.tp(pqk[:, hsl(i)], qkvT3[:, i, js], identb[:])
                        psv_ = pp["E"].next(); pv = psv_[:].bitcast(BF16)
                        for h in range(4):
                            k.tp(pv[:, hsl(h)], qkvT3[:, 8 + h, js], identb[:])
                        ck(2.2)
                        k.cp("act", qkf[:], pqk)
                        ck(2.3)
                        st = small.next()
                        for i in range(8):
                            k.act(junk[:, 0:128], qkf[:, hsl(i)], AF.Square, accum=st[:, i:i + 1])
                        ck(2.4)
                        rs = rstd_of(st[:, 0:8], 1.0, small, mhalf, 8)
                        ck(3)
                        psba = proj(C_B, 8, "E")
                        sm = small.next()
                        k.act(sm[:, 0:4], psba[:, 0:4], AF.Exp, scale=-1.0)
                        k.ts("dve", sm[:, 0:4], sm[:, 0:4], 1.0, ALU.add)
                        beta_t = small.next(); beta = beta_t[:, 0:4]
                        k.recip(beta, sm[:, 0:4])
                        k.tt("dve", sm[:, 4:8], psba[:, 4:8], dtbb[:], ALU.add)
                        k.act(sm[:, 8:12], sm[:, 4:8], AF.Exp)
                        k.act(sm[:, 12:16], sm[:, 8:12], AF.Ln, bias=1.0)
                        g_t = small.next(); g = g_t[:, 0:4]
                        k.tt("dve", g, sm[:, 12:16], negA[:], ALU.mult)
                        ck(4)
                        psg = pp["E"].next()
                        k.mm(psg[:, 0:4], CM[:], g)
                        k.mm(psg[:, 4:8], UM[:], g)
                        if samp:
                            Gs = small.next() if False else tmpp.next()
                            k.tt("pool", Gs[:, 0:64].re("p (h s) -> p h s", h=4), g.un(2).bc([128, 4, 16]),
                                 seq2[:].un(1).bc([128, 4, 16]), ALU.mult)
                            k.mm(psg[:, 8:72], onesf[:], Gs[:, 0:64])
                        else:
                            k.mm(psg[:, 8:12], onesf[:], g)
                        nex = 8 + 4 * nst
                        ex_t = sb_ex.next()
                        ex = ex_t[:, 0:nex]
                        k.act(ex, psg[:, 0:nex], AF.Exp)
                        sc_t = small.next(); sc = sc_t
                        k.tt("dve", sc[:, 0:4], rs[:, 4:8], beta, ALU.mult)
                        k.tt("dve", sc[:, 4:8], rs[:, 4:8], ex[:, 4:8], ALU.mult)
                        k.ts("dve", sc[:, 8:12], rs[:, 0:4], 128.0 ** -0.5, ALU.mult)
                        k.tt("dve", sc[:, 12:16], sc[:, 8:12], ex[:, 0:4], ALU.mult)
                        k.stt("dve", sc[:, 16:20], beta, -1.0, ex[:, 0:4], ALU.mult, ALU.mult)
                        kf = r3(qkf[:, 512:1024]); qf = r3(qkf[:, 0:512])
                        Kn = ebf.next(); KB = ebf.next(); Qs = ebf.next(); Qd = ebf.next(); Kt = e2r.next(); Vb = e2r.next()
                        k.tt("pool", r3(Kn[:]), kf, b3(rs[:, 4:8]), ALU.mult)
                        k.tt("dve", r3(KB[:]), kf, b3(sc[:, 0:4]), ALU.mult)
                        k.tt("pool", r3(Kt[:]), kf, b3(sc[:, 4:8]), ALU.mult)
                        k.tt("dve", r3(Qs[:]), qf, b3(sc[:, 8:12]), ALU.mult)
                        k.tt("pool", r3(Qd[:]), qf, b3(sc[:, 12:16]), ALU.mult)
                        k.tt("dve", r3(Vb[:]), r3(pv[:, 0:512]), b3(beta), ALU.mult)
                        ck(5)
                        KKT = kqT.next(); QQT = kqT.next()
                        psA = pp["E"].next(); pA = psA[:].bitcast(BF16)
                        for h in range(4):
                            k.tp(pA[:, hsl(h)], Kn[:, hsl(h)], identb[:])
                        for h in range(4):
                            k.tp(pA[:, hsl(4 + h)], KB[:, hsl(h)], identb[:])
                        k.cp("dve" if "h" in BAL else "act", KKT[:], pA)
                        psB = pp["E"].next(); pB = psB[:].bitcast(BF16)
                        for h in range(4):
                            k.tp(pB[:, hsl(h)], Qs[:, hsl(h)], identb[:])
                        for h in range(4):
                            k.tp(pB[:, hsl(4 + h)], Qd[:, hsl(h)], identb[:])
                        k.cp("dve", QQT[:], pB)
                        KnT = lambda h: KKT[:, hsl(h)]
                        KBT = lambda h: KKT[:, hsl(4 + h)]
                        QsT = lambda h: QQT[:, hsl(h)]
                        QdT = lambda h: QQT[:, hsl(4 + h)]
                        ck(6)
                        k.tt("pool", r3(MG[:]), CM[:].un(1).bc([128, 4, 128]), g.un(2).bc([128, 4, 128]), ALU.mult)
                        psD = pp["E"].next()
                        for h in range(4):
                            k.mm(psD[:, hsl(h)], MG[:, hsl(h)], onesf[:], True, False)
                            k.mm(psD[:, hsl(h)], nonesf[:], MG[:, hsl(h)], False, False)
                            k.mm(psD[:, hsl(h)], identb[:], NEGs[:], False, True)
                        Ds = dexp.next(); DT = dexp.next(); DTs = dexp.next()
                        k.act(Ds[:], psD[:], AF.Exp)
                        psDT = pp["E"].next(); pDT = psDT[:].bitcast(BF16)
                        for h in range(4):
                            k.tp(pDT[:, hsl(h)], Ds[:, hsl(h)], identb[:])
                        k.cp("act", DTs[:], pDT[:, 0:512])
                        k.tt("dve", DT[:], pDT[:, 0:512], irep[:], ALU.add)
                        ck(7)
                        psA_ = pp["E"].next(); psAT = pp["E"].next(); psKQ = pp["E"].next()
                        for h in range(4):
                            k.mm(psA_[:, hsl(h)], KBT(h), KnT(h))
                        for h in range(4):
                            k.mm(psAT[:, hsl(h)], KnT(h), KBT(h))
                        for h in range(4):
                            k.mm(psKQ[:, hsl(h)], KnT(h), QsT(h))
                        X = Xp.next(); XT = XTp.next(); PT = PTp.next(); QKDT = e2r.next()
                        k.stt("dve", X[:], psA_[:], -1.0, Ds[:], ALU.mult, ALU.mult)
                        k.stt("dve", XT[:], psAT[:], -1.0, DTs[:], ALU.mult, ALU.mult)
                        k.tt("dve", QKDT[:], psKQ[:], DT[:], ALU.mult)
                        k.tt("pool", PT[:], XT[:], irep[:], ALU.add)
                        ck(8)
                        for lv in range(1, nlev):
                            psX = pp["E"].next()
                            for h in range(4):
                                k.mm(psX[:, hsl(h)], XT[:, hsl(h)], X[:, hsl(h)])
                            last = lv == nlev - 1
                            IX = IXp.next()
                            k.tt("dve", IX[:], psX[:], irep[:], ALU.add)
                            if not last:
                                Xn = Xp.next()
                                k.cp("dve" if "e" in BAL else "act", Xn[:], psX[:])
                                psXT = pp["E"].next()
                                for h in range(4):
                                    k.mm(psXT[:, hsl(h)], X[:, hsl(h)], XT[:, hsl(h)])
                                XTn = XTp.next()
                                k.cp("dve" if "f" in BAL else "act", XTn[:], psXT[:])
                            psP = pp["E"].next()
                            for h in range(4):
                                k.mm(psP[:, hsl(h)], IX[:, hsl(h)], PT[:, hsl(h)])
                            PTn = PTp.next()
                            k.cp("dve" if "g" in BAL else "act", PTn[:], psP[:])
                            PT = PTn
                            if not last:
                                X, XT = Xn, XTn
                        ck(9)
                        zs = zsp.next(); rgs = rgsp.next()
                        psz = proj(C_Z, 512)
                        sgt = tmpp.next()
                        k.act(sgt[:], psz[:], AF.Exp, scale=-1.0)
                        k.act(sgt[:], sgt[:], AF.Ln, bias=1.0)
                        k.act(sgt[:], sgt[:], AF.Exp, scale=-1.0)
                        k.tt("dve", zs[:], psz[:], sgt[:], ALU.mult)
                        psrq = proj(C_RQ, 512)
                        k.cp("act", rqkf[:, 0:512], psrq[:])
                        psrk = proj(C_RK, 512)
                        k.cp("act", rqkf[:, 512:1024], psrk[:])
                        RV = rbf2.next()
                        psrv = proj(C_RV, 512)
                        k.cp("act", RV[:], psrv[:])
                        psrg = proj(C_RG, 512)
                        sgt = tmpp.next()
                        k.act(sgt[:], psrg[:], AF.Exp, scale=-1.0)
                        k.act(sgt[:], sgt[:], AF.Ln, bias=1.0)
                        k.act(sgt[:], sgt[:], AF.Exp, scale=-1.0)
                        k.tt("dve", rgs[:], psrg[:], sgt[:], ALU.mult)

                        ck(10)
                        of = ofp.next()
                        R = rbf.next(); vn = rbf.next()
                        headsets = [[h] for h in range(4)] if samp else [[0, 1, 2, 3]]
                        for hs in headsets:
                            cols = slice(hs[0] * 128, (hs[-1] + 1) * 128)
                            if samp:
                                h = hs[0]
                                sbig = sbigp.next(); sbigb = sbigbp.next()
                                for _q in range(4):
                                    k.dma("sp", sbig[:, _q * 512:(_q + 1) * 512].re("p (s v) -> p s v", s=4), V(None, stdn_d.ap[4 * _q:4 * _q + 4, h].rearrange("s k v -> k s v")))
                                k.cp("pool", sbigb[:], sbig[:])
                                KnE = expp.next(); QdE = expp.next()
                                e3 = lambda v: v.re("p (s c) -> p s c", s=16)
                                k.tt("pool", e3(KnE[:]), KnT(h).un(1).bc([128, 16, 128]), e3(E1[:]), ALU.mult)
                                k.tt("dve", e3(QdE[:]), QdT(h).un(1).bc([128, 16, 128]), e3(E1[:]), ALU.mult)
                                Sf = lambda hh, s, sbig=sbig: sbig[:, hsl(s)]
                                Sb = lambda hh, s, sbigb=sbigb: sbigb[:, hsl(s)]
                                lK = lambda hh, s: KnE[:, hsl(s)]
                                lQ = lambda hh, s: QdE[:, hsl(s)]
                            else:
                                Sf = lambda hh, s: Sdn[:, hsl(hh)]
                                Sb = lambda hh, s: Sdnb[:, hsl(hh)]
                                lK = lambda hh, s: KnT(hh)
                                lQ = lambda hh, s: QdT(hh)
                                lT = lambda hh, s: Kt[:, hsl(hh)]
                            psKS = pp["R"].next()
                            for h in hs:
                                for s in range(nst):
                                    k.mm(psKS[:, hsl(h)], lK(h, s), Sb(h, s), s == 0, s == nst - 1)
                            if samp:
                                KtE = expp.next()
                                k.tt("pool", e3(KtE[:]), Kt[:, hsl(hs[0])].un(1).bc([128, 16, 128]), seq2[:].un(2).bc([128, 16, 128]), ALU.mult)
                                lT = lambda hh, s: KtE[:, hsl(s)]
                            for h in hs:
                                k.stt("dve", R[:, hsl(h)], psKS[:, hsl(h)], sc[:, 16 + h:17 + h], Vb[:, hsl(h)], ALU.mult, ALU.add)
                            psV = pp["R"].next()
                            for h in hs:
                                k.mm(psV[:, hsl(h)], PT[:, hsl(h)], R[:, hsl(h)])
                            k.cp("act", vn[:, cols], psV[:, cols])
                            psO = pp["R"].next()
                            for h in hs:
                                for s in range(nst):
                                    k.mm(psO[:, hsl(h)], lQ(h, s), Sb(h, s), s == 0, False)
                                k.mm(psO[:, hsl(h)], QKDT[:, hsl(h)], vn[:, hsl(h)], False, True)
                            k.cp("act", of[:, cols], psO[:, cols])
                            pairs = [(h, s) for h in hs for s in range(nst)]
                            for g0 in range(0, len(pairs), 4):
                                grp = pairs[g0:g0 + 4]
                                psS = pp["R"].next()
                                for i, (h, s) in enumerate(grp):
                                    k.mm(psS[:, hsl(i)], lT(h, s), vn[:, hsl(h)])
                                for i, (h, s) in enumerate(grp):
                                    k.stt("dve", Sf(h, s), Sf(h, s), ex[:, 8 + h * nst + s:9 + h * nst + s], psS[:, hsl(i)], ALU.mult, ALU.add)
                            if samp:
                                for _q in range(4):
                                    k.dma("sp", V(Buf(dns_o.h, "dns%d_%d" % (hs[0], _q)), dns_o.h[4 * _q:4 * _q + 4, hs[0]].rearrange("s k v -> k s v")), sbig[:, _q * 512:(_q + 1) * 512].re("p (s v) -> p s v", s=4))
                            else:
                                k.cp("act", Sdnb[:], Sdn[:])
                        ck(11)
                        st = small.next()
                        for h in range(4):
                            k.act(junk[:, 0:128], of[:, hsl(h)], AF.Square, accum=st[:, h:h + 1])
                        rso = rstd_of(st[:, 0:4], 1.0 / 128, small, mhalf, 4)
                        k.tt("pool" if "a" in BAL else "dve", r3(of[:]), r3(of[:]), b3(rso), ALU.mult)
                        k.tt("pool", of[:], of[:], dnw[:], ALU.mult)
                        k.tt("pool" if "b" in BAL else "dve", mix[:, 0:512], of[:], zs[:], ALU.mult)

                        ck(12)
                        rqkb = rtk.next()
                        g4 = lambda v: v.re("p (g i two) -> p g i two", g=8, two=2)
                        x1 = g4(rqkf[:])[:, :, :, 0]; x2 = g4(rqkf[:])[:, :, :, 1]
                        o1 = g4(rqkb[:])[:, :, :, 0]; o2 = g4(rqkb[:])[:, :, :, 1]
                        cosb = cs[:, 0:64].un(1).bc([128, 8, 64]); sinb = cs[:, 64:128].un(1).bc([128, 8, 64])
                        t8 = lambda v: v.re("p (g i) -> p g i", g=8)
                        ta = tmpp.next(); tb = tmpp.next()
                        k.tt("dve", t8(ta[:]), x1, cosb, ALU.mult)
                        k.tt("pool", t8(tb[:]), x2, sinb, ALU.mult)
                        k.tt("dve", o1, t8(ta[:]), t8(tb[:]), ALU.subtract)
                        ta = tmpp.next(); tb = tmpp.next()
                        k.tt("pool", t8(ta[:]), x1, sinb, ALU.mult)
                        k.tt("dve", t8(tb[:]), x2, cosb, ALU.mult)
                        k.tt("pool", o2, t8(ta[:]), t8(tb[:]), ALU.add)
                        RQd = rbf2.next(); RKt = rbf2.next()
                        k.tt("dve", r3(RQd[:]), r3(rqkb[:, 0:512]), b3(rqd[:]), ALU.mult)
                        k.tt("pool", r3(RKt[:]), r3(rqkb[:, 512:1024]), b3(rkt[:]), ALU.mult)
                        RQKT = rtk.next(); RQdT_t = rbf2.next()
                        psR1 = pp["M"].next(); pR1 = psR1[:].bitcast(BF16)
                        for i in range(8):
                            k.tp(pR1[:, hsl(i)], rqkb[:, hsl(i)], identb[:])
                        k.cp("act", RQKT[:], pR1)
                        psR2 = pp["M"].next(); pR2 = psR2[:].bitcast(BF16)
                        for h in range(4):
                            k.tp(pR2[:, hsl(h)], RQd[:, hsl(h)], identb[:])
                        k.cp("dve", RQdT_t[:], pR2[:, 0:512])
                        psKQr = pp["M"].next()
                        for h in range(4):
                            k.mm(psKQr[:, hsl(h)], RQKT[:, hsl(4 + h)], RQKT[:, hsl(h)])
                        QKDTr = rbf2.next()
                        k.tt("dve", QKDTr[:], psKQr[:], DTr[:], ALU.mult)
                        orf = ofp.next()
                        for hs in headsets:
                            cols = slice(hs[0] * 128, (hs[-1] + 1) * 128)
                            if samp:
                                h = hs[0]
                                sbig = sbigp.next(); sbigb = sbigbp.next()
                                for _q in range(4):
                                    k.dma("sp", sbig[:, _q * 512:(_q + 1) * 512].re("p (s v) -> p s v", s=4), V(None, stret_d.ap[4 * _q:4 * _q + 4, h].rearrange("s k v -> k s v")))
                                k.cp("pool", sbigb[:], sbig[:])
                                QdE = expp.next(); KtE = expp.next()
                                e3 = lambda v: v.re("p (s c) -> p s c", s=16)
                                k.tt("dve", e3(QdE[:]), RQdT_t[:, hsl(h)].un(1).bc([128, 16, 128]), e3(E1[:]), ALU.mult)
                                k.tt("pool", e3(KtE[:]), RKt[:, hsl(h)].un(1).bc([128, 16, 128]), seq2[:].un(2).bc([128, 16, 128]), ALU.mult)
                                Sf = lambda hh, s, sbig=sbig: sbig[:, hsl(s)]
                                Sb = lambda hh, s, sbigb=sbigb: sbigb[:, hsl(s)]
                                lQ = lambda hh, s: QdE[:, hsl(s)]
                                lT = lambda hh, s: KtE[:, hsl(s)]
                            else:
                                Sf = lambda hh, s: Srt[:, hsl(hh)]
                                Sb = lambda hh, s: Srtb[:, hsl(hh)]
                                lQ = lambda hh, s: RQdT_t[:, hsl(hh)]
                                lT = lambda hh, s: RKt[:, hsl(hh)]
                            psO = pp["R"].next()
                            for h in hs:
                                for s in range(nst):
                                    k.mm(psO[:, hsl(h)], lQ(h, s), Sb(h, s), s == 0, False)
                                k.mm(psO[:, hsl(h)], QKDTr[:, hsl(h)], RV[:, hsl(h)], False, True)
                            k.cp("act", orf[:, cols], psO[:, cols])
                            pairs = [(h, s) for h in hs for s in range(nst)]
                            for g0 in range(0, len(pairs), 4):
                                grp = pairs[g0:g0 + 4]
                                psS = pp["R"].next()
                                for i, (h, s) in enumerate(grp):
                                    k.mm(psS[:, hsl(i)], lT(h, s), RV[:, hsl(h)])
                                for i, (h, s) in enumerate(grp):
                                    k.stt("dve", Sf(h, s), Sf(h, s), cdec[h], psS[:, hsl(i)], ALU.mult, ALU.add)
                            if samp:
                                for _q in range(4):
                                    k.dma("sp", V(Buf(rets_o.h, "rets%d_%d" % (hs[0], _q)), rets_o.h[4 * _q:4 * _q + 4, hs[0]].rearrange("s k v -> k s v")), sbig[:, _q * 512:(_q + 1) * 512].re("p (s v) -> p s v", s=4))
                            else:
                                k.cp("act", Srtb[:], Srt[:])
                        ck(13)
                        st = small.next()
                        for h in range(4):
                            k.act(junk[:, 0:128], orf[:, hsl(h)], AF.Copy, accum=st[:, h:h + 1])
                        for h in range(4):
                            k.act(junk[:, 128:256], orf[:, hsl(h)], AF.Square, accum=st[:, 4 + h:5 + h])
                        s2 = small.next()
                        k.ts("dve", s2[:, 0:4], st[:, 0:4], 1.0 / 128, ALU.mult)
                        k.tt("dve", s2[:, 4:8], s2[:, 0:4], s2[:, 0:4], ALU.mult)
                        k.stt("dve", s2[:, 8:12], st[:, 4:8], 1.0 / 128, s2[:, 4:8], ALU.mult, ALU.subtract)
                        rsr = rstd_of(s2[:, 8:12], 1.0, small, mhalf, 4)
                        k.tt("pool" if "d" in BAL else "dve", r3(orf[:]), r3(orf[:]), b3(s2[:, 0:4]), ALU.subtract)
                        k.tt("pool", r3(orf[:]), r3(orf[:]), b3(rsr), ALU.mult)
                        k.tt("pool" if "c" in BAL else "dve", orf[:], orf[:], retw[:], ALU.mult)
                        k.tt("pool", mix[:, 512:1024], orf[:], rgs[:], ALU.mult)
                        ck(14)
                        psM = pp["L"].next(); pM = psM[:].bitcast(BF16)
                        for kk in range(8):
                            k.tp(pM[:, hsl(kk)], mix[:, hsl(kk)], identb[:])
                        k.cp("act", mixT[:], pM)
                        h1 = xr if samp else h1p.next()
                        for half in range(2):
                            psH = pp["L"].next()
                            for kk in range(8):
                                k.mm(psH[:], mixT[:, hsl(kk)], w_out[kk][:, half * 512:(half + 1) * 512], kk == 0, kk == 7)
                            k.tt("dve", h1[:, half * 512:(half + 1) * 512], psH[:], xr[:, half * 512:(half + 1) * 512], ALU.add)
                        k.dma("sp", h1_s.rows(r0), h1[:])

                except StopBuild:
                    pass
                n3 = 3 * (NSAMP if samp else 1)
                dnc_o = dncs_o if samp else dncp_o
                for gI in range(3):
                    ps = pp["L"].next()
                    for i in range(4):
                        fc = gI * 4 + i
                        k.tp(ps[0:n3, hsl(i)], cx.c(fc, 0, n3), identf[:])
                    tmp = tmpp.next()
                    k.cp("act", tmp[0:n3, :], ps[0:n3, :])
                    k.dma("sp", V(Buf(dnc_o.h, "dnc%d" % gI), dnc_o.h[:, gI * 512:(gI + 1) * 512]), tmp[0:n3, :])
                if not samp:
                    k.dma("sp", V(dnp_o, dnp_o.h.rearrange("h k v -> k h v")), Sdn[:].re("p (h v) -> p h v", h=4))
                    k.dma("sp", V(retp_o, retp_o.h.rearrange("h k v -> k h v")), Srt[:].re("p (h v) -> p h v", h=4))
                S.emit_phase()

        sb_ex = None

        WA = None

        def phase_a_wrap(samp):
            nonlocal sb_ex
            with contextlib.ExitStack() as st0:
                sb0, _ = mk_alloc(st0)
                sb_ex = sb0([128, 72], n=3, name="ex")
                phase_a(samp)

        def load_cols(st_sb, wd, kchunks, c0, c1, name):
            out = []
            for kk in range(kchunks):
                b = st_sb([128, c1 - c0], BF16, name=name)
                k.dma("pool", b[:], wd[kk * 128:(kk + 1) * 128, c0:c1])
                out.append(b)
            return out

        import os
        _ph = os.environ.get("KDBG_PH", "ABCD")
        with contextlib.ExitStack() as stw:
            sbw, _ = mk_alloc(stw)
            if "A" in _ph or "B" in _ph:
                WA = (load_cols(sbw, w_in_d, 8, 0, 1536, "w_inq"), load_cols(sbw, w_in_d, 8, 1536, DIN, "w_inr"),
                      load_cols(sbw, w_out_d, 8, 0, D, "w_out"))
            if "A" in _ph:
                phase_a_wrap(False)
            if "B" in _ph:
                phase_a_wrap(True)

        def phase_b():
            with contextlib.ExitStack() as st:
                sb, psum_pool = mk_alloc(st)
                pp = psum_pool(tr=2, up=4, down=2)
                GP = [(0, 6), (6, 12), (12, 17), (17, 22)]
                w_up_g = []
                for (p0, p1) in GP:
                    ug = load_cols(sb, w_up_d, 8, p0 * 128, p1 * 128, "w_upg")
                    uv = load_cols(sb, w_up_d, 8, DFF + p0 * 128, DFF + p1 * 128, "w_upv")
                    w_up_g.append((p0, p1, ug, uv))

                def w_up_sl(kk, fc):
                    part, i = (0, fc) if fc < 22 else (1, fc - 22)
                    for (p0, p1, ug, uv) in w_up_g:
                        if p0 <= i < p1:
                            return (ug, uv)[part][kk][:, (i - p0) * 128:(i - p0 + 1) * 128]
                w_down = load_w(sb, w_down_d, 22, D, "w_down")
                identf = sb([128, 128]); k.dma("sp", identf[:], ident_d)
                identb = sb([128, 128], BF16); k.dma("pool", identb[:], ident_d)
                mhalf = sb([128, 8]); k.ms("pool", mhalf[:], -0.5)
                fnw8 = sb([128, 8]); k.dma("sp", fnw8[:], fnw8_d)
                fcw = sb([128, 132]); k.dma("sp", fcw[:], fcw_d)
                fcb = sb([128, 44]); k.dma("sp", fcb[:], fcb_d)
                NTm = 256
                import os as _o
                htp = sb([128, D], n=int(_o.environ.get("KB_HT", "4")), name="ht")
                mbfp = sb([128, D], BF16, n=2, name="mbf")
                import os as _o
                mTp = sb([128, 8 * NTm], BF16, n=int(_o.environ.get("KB_MT", "1")), name="mT")
                actTp = sb([128, 22 * NTm], BF16, n=int(_o.environ.get("KB_ACTT", "1")), name="actT")
                uprep = sb([128, 264], n=int(_o.environ.get("KB_UP", "4")), name="upre")
                ycp = sb([128, 256], n=int(_o.environ.get("KB_YC", "4")), name="yc")
                sgp = sb([128, 256], n=int(_o.environ.get("KB_SG", "2")), name="sg")
                cf_all = sb([128, 44 * 32], name="cf"); k.ms("pool", cf_all[:], 0.0); cf = Chunked(cf_all, 44, 32)
                fcT = sb([128, 44 * 32], name="fcT")
                junk = sb([128, D], BF16, name="junk"); junk = V(None, junk.h[:])
                small = sb([128, 8], n=8, name="small")
                h2p = sb([128, D], n=int(_o.environ.get("KB_H2", "2")), name="h2")
                tmpp = sb([128, 512], n=int(_o.environ.get("KB_TMP", "2")), name="tmp")
                hsl = lambda h: slice(h * 128, (h + 1) * 128)

                for gI in range(11):
                    tmp = tmpp.next()
                    k.dma("sp", tmp[0:32, :], stfc_d[:, gI * 512:(gI + 1) * 512])
                    ps = pp["tr"].next()
                    for i in range(4):
                        k.tp(ps[:, i * 32:(i + 1) * 32], tmp[0:32, hsl(i)], identf[0:32, 0:32])
                    k.cp("act", fcT[:, gI * 128:(gI + 1) * 128], ps[:, 0:128])

                blocks = [(b * 256, 256, 1, 256, False) for b in range(SEQ // 256)] + [(SEQ, 128, NSAMP, LS, True)]
                for (t0, NT, nseq, L, samp) in blocks:
                    nsub = NT // 128
                    mT = mTp.next(); actT = actTp.next()
                    mT3 = mT[:].re("p (k t) -> p k t", k=8)
                    hts = []
                    for j in range(nsub):
                        r0 = t0 + 128 * j
                        ht = htp.next(); hts.append(ht)
                        k.dma("sp", ht[:], h1_s.rows(r0))
                        ss = small.next()
                        k.act(junk, ht[:], AF.Square, accum=ss[:, 0:1])
                        rs = rstd_of(ss[:, 0:1], 1.0 / D, small, mhalf, 1)
                        mbf = mbfp.next()
                        k.ts("dve", mbf[:], ht[:], rs, ALU.mult)
                        ps = pp["tr"].next(); pb = ps[:].bitcast(BF16)
                        for kk in range(8):
                            k.tp(pb[:, hsl(kk)], mbf[:, hsl(kk)], identb[:])
                        k.tt("dve", mT3[:, :, 128 * j:128 * (j + 1)], pb.re("p (k t) -> p k t", k=8),
                             fnw8[:].un(2).bc([128, 8, 128]), ALU.mult)
                    actT3 = actT[:].re("p (c t) -> p c t", c=22)
                    for i in range(22):
                        ys = []
                        for fc in (i, 22 + i):
                            ps = pp["up"].next()
                            for kk in range(8):
                                k.mm(ps[:, 0:NT], w_up_sl(kk, fc), mT3[:, kk, 0:NT], kk == 0, kk == 7)
                            up = uprep.next()
                            uv = up[:, 0:nseq * (2 + L)].re("p (s l) -> p s l", s=nseq)
                            psv = ps[:, 0:NT].re("p (s l) -> p s l", s=nseq)
                            k.cp("act", uv[:, :, 2:2 + L], psv)
                            if samp:
                                k.cp("pool", uv[:, :, 0:2], fcT[:, fc * 32:(fc + 1) * 32].re("p (s j) -> p s j", s=nseq))
                            else:
                                k.cp("pool", uv[:, :, 0:2], cf.c(fc, 0, 2).un(1))
                            k.cp("pool", cf.c(fc, 0, 2 * nseq).re("p (s j) -> p s j", s=nseq), uv[:, :, L:L + 2])
                            y = ycp.next(); ys.append(y)
                            yv = y[:, 0:NT].re("p (s l) -> p s l", s=nseq)
                            k.act(yv, psv, AF.Identity, scale=fcw[:, fc * 3 + 2:fc * 3 + 3], bias=fcb[:, fc:fc + 1])
                            k.stt("dve", yv, uv[:, :, 1:1 + L], fcw[:, fc * 3 + 1:fc * 3 + 2], yv, ALU.mult, ALU.add)
                            k.stt("dve", yv, uv[:, :, 0:L], fcw[:, fc * 3 + 0:fc * 3 + 1], yv, ALU.mult, ALU.add)
                        sg = sgp.next()
                        k.act(sg[:, 0:NT], ys[0][:, 0:NT], AF.Silu)
                        k.tt("dve", actT3[:, i, 0:NT], sg[:, 0:NT], ys[1][:, 0:NT], ALU.mult)
                    for j in range(nsub):
                        r0 = t0 + 128 * j
                        h2 = h2p.next()
                        for half in range(2):
                            psH = pp["down"].next()
                            for c in range(22):
                                k.mm(psH[:], actT3[:, c, 128 * j:128 * (j + 1)], w_down[c][:, half * 512:(half + 1) * 512], c == 0, c == 21)
                            k.tt("dve", h2[:, half * 512:(half + 1) * 512], psH[:], hts[j][:, half * 512:(half + 1) * 512], ALU.add)
                        k.dma("sp", h2_s.rows(r0), h2[:])
                    last_prompt = (not samp) and t0 + NT == SEQ
                    if last_prompt or samp:
                        n2 = 2 * nseq
                        fo = fcs_o if samp else fcp_o
                        for gI in range(11):
                            ps = pp["tr"].next()
                            for i in range(4):
                                fc = gI * 4 + i
                                k.tp(ps[0:n2, hsl(i)], cf.c(fc, 0, n2), identf[:])
                            tmp = tmpp.next()
                            k.cp("act", tmp[0:n2, :], ps[0:n2, :])
                            k.dma("sp", V(Buf(fo.h, "fo%d" % gI), fo.h[:, gI * 512:(gI + 1) * 512]), tmp[0:n2, :])
                S.emit_phase()

        if "C" in _ph:
            phase_b()

        def phase_c():
            with contextlib.ExitStack() as st:
                sb, psum_pool = mk_alloc(st)
                pp = psum_pool(tr=2, mm=6)
                w_gate = load_w(sb, w_gate_d, 8, D, "w_gate")
                w_ple = load_w(sb, w_ple_d, 2, D, "w_ple")
                identb = sb([128, 128], BF16); k.dma("pool", identb[:], ident_d)
                mhalf = sb([128, 8]); k.ms("pool", mhalf[:], -0.5)
                pnw8 = sb([128, 8]); k.dma("sp", pnw8[:], pnw8_d)
                finw = sb([128, D]); k.dma("sp", finw[:], V(None, finw_d.ap.partition_broadcast(128)))
                h2p = sb([128, D], n=PCN, name="h2")
                ptp = sb([128, 256], n=PCN, name="pt")
                pbp = sb([128, 256], BF16, n=PCN, name="pb")
                nbfp = sb([128, D], BF16, n=PCN, name="nbf")
                nTp = sb([128, D], BF16, n=PCN, name="nT")
                pTp = sb([128, 256], BF16, n=PCN, name="pT")
                tgp = sb([128, D], n=PCN, name="tg")
                h3p = sb([128, D], n=PCN, name="h3")
                yp = sb([128, D], n=PCN, name="y")
                junk = sb([128, D], BF16, name="junk"); junk = V(None, junk.h[:])
                small = sb([128, 8], n=4 * PCN, name="small")
                hsl = lambda h: slice(h * 128, (h + 1) * 128)
                for it in range(NTOK // 128):
                    r0 = it * 128
                    h2 = h2p.next()
                    k.dma("sp", h2[:], h2_s.rows(r0))
                    pt = ptp.next()
                    k.dma("sp", pt[:], p_d[r0:r0 + 128, :])
                    ss = small.next()
                    k.act(junk, h2[:], AF.Square, accum=ss[:, 0:1])
                    rs = rstd_of(ss[:, 0:1], 1.0 / D, small, mhalf, 1)
                    nbf = nbfp.next()
                    k.act(nbf[:], h2[:], AF.Copy, scale=rs)
                    ps = pp["tr"].next(); pb = ps[:].bitcast(BF16)
                    for kk in range(8):
                        k.tp(pb[:, hsl(kk)], nbf[:, hsl(kk)], identb[:])
                    nT = nTp.next()
                    k.tt("dve", nT[:].re("p (k t) -> p k t", k=8), pb.re("p (k t) -> p k t", k=8),
                         pnw8[:].un(2).bc([128, 8, 128]), ALU.mult)
                    pbf = pbp.next()
                    k.cp("pool", pbf[:], pt[:])
                    ps2 = pp["tr"].next(); pb2 = ps2[:].bitcast(BF16)
                    for kk in range(2):
                        k.tp(pb2[:, hsl(kk)], pbf[:, hsl(kk)], identb[:])
                    pT = pTp.next()
                    k.cp("act", pT[:], pb2[:, 0:256])
                    tg = tgp.next(); h3 = h3p.next()
                    for half in range(2):
                        hs_ = slice(half * 512, (half + 1) * 512)
                        psG = pp["mm"].next()
                        for kk in range(8):
                            k.mm(psG[:], nT[:, hsl(kk)], w_gate[kk][:, hs_], kk == 0, kk == 7)
                        psP = pp["mm"].next()
                        for kk in range(2):
                            k.mm(psP[:], pT[:, hsl(kk)], w_ple[kk][:, hs_], kk == 0, kk == 1)
                        k.act(tg[:, hs_], psG[:], AF.Tanh, scale=0.5)
                        k.stt("dve", tg[:, hs_], tg[:, hs_], 1.0, psP[:], ALU.add, ALU.mult)
                        k.stt("dve", h3[:, hs_], tg[:, hs_], 0.5, h2[:, hs_], ALU.mult, ALU.add)
                    ss = small.next()
                    k.act(junk, h3[:], AF.Square, accum=ss[:, 0:1])
                    rs = rstd_of(ss[:, 0:1], 1.0 / D, small, mhalf, 1)
                    y = yp.next()
                    k.stt("dve", y[:], h3[:], rs, finw[:], ALU.mult, ALU.mult)
                    k.dma("sp", y_o.rows(r0), y[:])
                S.emit_phase()

        if "D" in _ph:
            phase_c()
    return nc


def _consts():
    c = {}
    idx = np.arange(128)
    c["ident"] = np.eye(128, dtype=np.float32)
    c["irep"] = np.tile(np.eye(128, dtype=np.float32), (1, 4))
    lg = np.log(1.0 - 2.0 ** (-5.0 - np.arange(4, dtype=np.float64)))
    sc = 128.0 ** -0.5
    for v, C in (("P", 128), ("S", LS)):
        seq = idx // C
        pos = idx % C
        same = seq[:, None] == seq[None, :]
        a = idx[:, None]
        b = idx[None, :]
        c["CM" + v] = (same & (a <= b)).astype(np.float32)
        c["UM" + v] = (same & (a > b)).astype(np.float32)
        c["NEGs" + v] = np.where(same & (a > b), 0.0, NEG).astype(np.float32)
        c["NEGT" + v] = np.where(same & (b >= a), 0.0, NEG).astype(np.float32)
        dtr = np.zeros((128, 4, 128), np.float64)
        for h in range(4):
            dtr[:, h, :] = np.where(same & (b >= a), sc * np.exp((b - a) * lg[h]), 0.0)
        c["DTr" + v] = dtr.reshape(128, 512).astype(np.float32)
        c["rqd" + v] = np.exp((pos[:, None] + 1.0) * lg[None, :]).astype(np.float32)
        c["rkt" + v] = (sc * np.exp((C - 1.0 - pos[:, None]) * lg[None, :])).astype(np.float32)
    s16 = np.arange(16)
    c["E1"] = (s16[:, None] == (idx[None, :] // LS)).astype(np.float32).reshape(-1)
    c["seq2"] = ((idx[:, None] // LS) == s16[None, :]).astype(np.float32)
    pos = np.concatenate([np.arange(SEQ), PAST + (np.arange(NSAMP * LS) % LS)]).astype(np.float32)
    inv = (np.float32(10000.0) ** (-np.arange(0, 128, 2, dtype=np.float32) / np.float32(128))).astype(np.float32)
    ang = (pos[:, None] * inv[None, :]).astype(np.float32)
    c["cosT"] = np.cos(ang.astype(np.float64)).astype(np.float32)
    c["sinT"] = np.sin(ang.astype(np.float64)).astype(np.float32)
    return c


_NC_CACHE = {}


def kernel(x_prompt, x_sample, p_prompt, p_sample, state_dn_conv, state_dn, state_ret,
           state_ffn_conv, attn_norm_w, w_in, dn_conv_w, dn_A_log, dn_dt_bias, dn_norm_w,
           ret_norm_w, w_out, ffn_norm_w, w_up, ffn_conv_w, ffn_conv_b, w_down, ple_norm_w,
           w_ple_gate, w_ple, final_norm_w):
    f = lambda a: np.ascontiguousarray(np.asarray(a), dtype=np.float32)
    x_prompt, x_sample, p_prompt, p_sample = f(x_prompt), f(x_sample), f(p_prompt), f(p_sample)
    state_dn_conv, state_dn, state_ret, state_ffn_conv = f(state_dn_conv), f(state_dn), f(state_ret), f(state_ffn_conv)
    col8 = lambda w: f(np.asarray(w).reshape(8, 128).T)
    shared = dict(
        w_in=f(w_in)[0], w_out=f(w_out)[0], w_up=f(w_up)[0], w_down=f(w_down)[0],
        w_gate=f(w_ple_gate)[0], w_ple=f(w_ple)[0],
        anw8=col8(f(attn_norm_w)[0]), fnw8=col8(f(ffn_norm_w)[0]), pnw8=col8(f(ple_norm_w)[0]),
        finw=f(final_norm_w),
        dncw=f(f(dn_conv_w)[0].T.reshape(12, 128, 4).transpose(1, 0, 2).reshape(128, 48)),
        alog=f(dn_A_log)[0], dtb=f(dn_dt_bias)[0],
        dnw4=f(np.tile(f(dn_norm_w)[0], 4)), retw=f(ret_norm_w)[0],
        fcw=f(f(ffn_conv_w)[0].T.reshape(44, 128, 3).transpose(1, 0, 2).reshape(128, 132)),
        fcb=f(f(ffn_conv_b)[0].reshape(44, 128).T),
    )
    shared.update(_consts())
    in_maps = []
    for i in range(NCORES):
        sl = slice(NSAMP * i, NSAMP * (i + 1))
        m = dict(shared)
        m["x"] = f(np.concatenate([x_prompt[i], x_sample[sl].reshape(NSAMP * LS, D)], axis=0))
        m["p"] = f(np.concatenate([p_prompt[0, i], p_sample[0, sl].reshape(NSAMP * LS, 256)], axis=0))
        m["st_dnc"] = f(state_dn_conv[0, sl].reshape(48, 1536))
        m["st_dn"] = f(state_dn[0, sl])
        m["st_ret"] = f(state_ret[0, sl])
        m["st_fc"] = f(state_ffn_conv[0, sl].reshape(32, 2 * DFF))
        in_maps.append(m)
    if "nc" not in _NC_CACHE:
        _NC_CACHE["nc"] = build_program()
    nc = _NC_CACHE["nc"]
    res = run_bass_kernel_spmd(nc, in_maps, core_ids=list(range(NCORES)))
    R = res.results
    _NC_CACHE["last"] = R
    g = lambda name: [np.asarray(R[i][name], dtype=np.float32) for i in range(NCORES)]
    y = g("y")
    y_prompt = np.stack([a[:SEQ] for a in y], axis=0)
    y_sample = np.concatenate([a[SEQ:].reshape(NSAMP, LS, D) for a in y], axis=0)
    dncp = np.stack(g("o_dnc_p"), axis=0)[None]
    dnp = np.stack(g("o_dn_p"), axis=0)[None]
    retp = np.stack(g("o_ret_p"), axis=0)[None]
    fcp = np.stack(g("o_fc_p"), axis=0)[None]
    dncs = np.concatenate([a.reshape(NSAMP, 3, 1536) for a in g("o_dnc_s")], axis=0)[None]
    dns = np.concatenate(g("o_dn_s"), axis=0)[None]
    rets = np.concatenate(g("o_ret_s"), axis=0)[None]
    fcs = np.concatenate([a.reshape(NSAMP, 2, 2 * DFF) for a in g("o_fc_s")], axis=0)[None]
    return (y_prompt, y_sample, dncp, dnp, retp, fcp, dncs, dns, rets, fcs)
```

```python
import contextlib
import os as _osb
import numpy as np
import concourse.bass as bass
import concourse.mybir as mybir
from concourse.bass_utils import run_bass_kernel_spmd

F32 = mybir.dt.float32
BF16 = mybir.dt.bfloat16
AF = mybir.ActivationFunctionType
ALU = mybir.AluOpType
AX = mybir.AxisListType

NCORES = 8
D = 1024
SEQ = 2048
NSAMP = 16
LS = 8
NTOK = SEQ + NSAMP * LS
DIN = 4104
DFF = 2816
EPS = 1e-6
PAST = 16384
NEG = -30000.0
PE2R, PRBF2, PKQT, PRTK, POF = 5, 5, 3, 3, 2
import os as _osb
BAL = _osb.environ.get('K_BAL', '')
PCN = int(_osb.environ.get('K_PCN', '6'))
C_QKV, C_Z, C_B, C_A, C_RQ, C_RK, C_RV, C_RG = 0, 1536, 2048, 2052, 2056, 2568, 3080, 3592


class T:
    __slots__ = ("name", "last_writer", "readers")

    def __init__(self, name=""):
        self.name = name
        self.last_writer = None
        self.readers = []


class Op:
    __slots__ = ("eng", "fn", "deps", "users", "ndep", "signaled", "sigval", "sem", "is_dma", "idx", "cost",
                 "aset", "phase", "finish", "pos", "rtime", "tag", "prio")

    def __init__(self, eng, fn, is_dma):
        self.eng = eng
        self.fn = fn
        self.deps = []
        self.users = []
        self.signaled = False
        self.sigval = None
        self.sem = None
        self.is_dma = is_dma
        self.finish = 0.0
        self.pos = -1


class Sched:
    ENGS = ("pe", "act", "dve", "pool", "sp")
    XLAT = float(_osb.environ.get('K_XLAT', '500'))
    SLAT = float(_osb.environ.get('K_SLAT', '60'))

    def __init__(self, nc, n_dma_sems=14):
        self.nc = nc
        self.ops = []
        self.n_dma_sems = n_dma_sems
        self.nops = 0
        self.phase = 0
        import os as _os
        self.prio_mode = int(_os.environ.get("KS_PRIO", "1"))
        self.prio_w = float(_os.environ.get("KS_PRIOW", "0.0"))

    def op(self, eng, fn, reads=(), writes=(), dma=False, cost=200.0, aset=None):
        o = Op(eng, fn, dma)
        o.idx = self.nops
        self.nops += 1
        o.cost = cost
        o.aset = aset
        o.phase = self.phase
        import sys as _sys
        fr = _sys._getframe(2)
        o.tag = fr.f_lineno if fr.f_code.co_name != "<lambda>" else fr.f_back.f_lineno
        deps = []
        for t in reads:
            if t.last_writer is not None:
                deps.append(t.last_writer)
        for t in writes:
            if t.last_writer is not None:
                deps.append(t.last_writer)
            deps.extend(t.readers)
        seen = set()
        for d in deps:
            if id(d) in seen or d is o or d.phase != o.phase:
                continue
            seen.add(id(d))
            o.deps.append(d)
            d.users.append(o)
        for t in reads:
            t.readers.append(o)
        for t in writes:
            t.last_writer = o
            t.readers = []
        self.ops.append(o)
        return o

    def open(self, stack):
        nc = self.nc
        self.sems = {}
        for e in ("pe", "act", "dve", "pool"):
            self.sems[e] = stack.enter_context(nc.semaphore("s_" + e))
        for e in ("sp", "act", "pool"):
            for k in range(self.n_dma_sems):
                self.sems[(e, k)] = stack.enter_context(nc.semaphore("d_%s_%d" % (e, k)))
        self.cnt = {}
        self.dma_n = {e: 0 for e in self.ENGS}

    def _schedule(self):
        import heapq
        ops = self.ops
        future = {e: [] for e in self.ENGS}
        avail = {e: [] for e in self.ENGS}
        free_at = {e: 0.0 for e in self.ENGS}
        cur_set = {e: None for e in self.ENGS}
        streams = {e: [] for e in self.ENGS}
        self._pipe = 0.0
        bl = {}
        for o in reversed(ops):
            m = 0.0
            for u in o.users:
                lat = self.XLAT if (u.eng != o.eng or o.is_dma) else self.SLAT
                v = bl[id(u)] + lat
                if v > m:
                    m = v
            bl[id(o)] = m + o.cost
        mode = self.prio_mode
        for o in ops:
            o.ndep = len(o.deps)
            o.rtime = 0.0
            if mode == 0:
                o.prio = o.idx
            else:
                o.prio = -bl[id(o)] + self.prio_w * o.idx
        for o in ops:
            if o.ndep == 0:
                heapq.heappush(future[o.eng], (0.0, o.prio, o.idx, o))
        left = len(ops)
        while left:
            best = None
            for e in self.ENGS:
                f, a = future[e], avail[e]
                while f and f[0][0] <= free_at[e]:
                    _, pr, i, o = heapq.heappop(f)
                    heapq.heappush(a, (pr, i, o))
                if a:
                    cand = (free_at[e], a[0][0], e, True)
                elif f:
                    cand = (f[0][0], f[0][1], e, False)
                else:
                    continue
                if best is None or cand < best:
                    best = cand
            start, _, e, from_avail = best
            if from_avail:
                _, _, o = heapq.heappop(avail[e])
            else:
                _, _, _, o = heapq.heappop(future[e])
            c = o.cost
            if o.aset is not None and o.aset != cur_set[e]:
                if cur_set[e] is not None:
                    c += 1300.0
                cur_set[e] = o.aset
            if o.is_dma:
                xfer = max(0.0, c - 2000.0) * (120.0 / 220.0)
                t0x = max(start + 1000.0, self._pipe)
                self._pipe = t0x + xfer
                o.finish = t0x + xfer + 1000.0
                free_at[e] = start + 60.0
            else:
                o.finish = start + c
                free_at[e] = o.finish
            o.pos = len(streams[e])
            streams[e].append(o)
            left -= 1
            for u in o.users:
                lat = self.XLAT if (u.eng != e or o.is_dma) else self.SLAT
                t = o.finish + lat
                if t > u.rtime:
                    u.rtime = t
                u.ndep -= 1
                if u.ndep == 0:
                    heapq.heappush(future[u.eng], (u.rtime, u.prio, u.idx, u))
        self.makespan = max(free_at.values())
        return streams

    def emit_phase(self):
        nc = self.nc
        sems = self.sems
        cnt = self.cnt
        streams = self._schedule()
        for e in self.ENGS:
            last_on_sem = {}
            for o in streams[e]:
                if o.is_dma:
                    kk = self.dma_n[e] % self.n_dma_sems
                    self.dma_n[e] += 1
                    o.sem = (e, kk)
                    c = cnt.get(o.sem, 0) + 16
                    cnt[o.sem] = c
                    o.sigval = c
        plan = {}
        for e in self.ENGS:
            wpos = {}
            wl = []
            for o in streams[e]:
                ws = []
                for d in o.deps:
                    if d.is_dma:
                        ws.append(d)
                        continue
                    if d.eng == e and e == "pe":
                        continue
                    if d.pos > wpos.get(d.eng, -1):
                        wpos[d.eng] = d.pos
                        d.signaled = True
                        ws.append(d)
                wl.append(ws)
            plan[e] = wl
        for e in ("pe", "act", "dve", "pool"):
            for o in reversed(streams[e]):
                if not o.is_dma:
                    o.signaled = True
                    break
        for e in self.ENGS:
            for o in streams[e]:
                if (not o.is_dma) and o.signaled:
                    c = cnt.get(e, 0) + 1
                    cnt[e] = c
                    o.sem = e
                    o.sigval = c
        final = dict(cnt)
        with nc.Block() as block:
            engobj = {"pe": block.tensor, "act": block.scalar, "dve": block.vector,
                      "pool": block.gpsimd, "sp": block.sync}

            def run(e, eng):
                waited = {}
                dma_prev = {}

                def wait_sv(sem, val):
                    if waited.get(sem, 0) >= val:
                        return
                    eng.wait_ge(sems[sem], val)
                    waited[sem] = val

                for o, ws in zip(streams[e], plan[e]):
                    for d in ws:
                        wait_sv(d.sem, d.sigval)
                    if o.is_dma:
                        if o.sigval > 16:
                            wait_sv(o.sem, o.sigval - 16)
                    ins = o.fn(eng)
                    if o.is_dma:
                        ins.then_inc(sems[o.sem], 16)
                    elif o.signaled:
                        ins.then_inc(sems[o.sem], 1)
                for sem, val in final.items():
                    wait_sv(sem, val)

            for e in self.ENGS:
                def mk(e):
                    def f(eng):
                        run(e, eng)
                    return f
                engobj[e](mk(e))
        self.ops = []
        self.phase += 1


class StopBuild(Exception):
    pass


def ck(n):
    import os
    lim = float(os.environ.get("KDBG_CK", "1000"))
    if n > lim:
        raise StopBuild()


class Buf:
    def __init__(self, h, name="", excl=False):
        self.h = h
        self.t = T(name)
        self.excl = excl

    def __getitem__(self, k):
        return V(self, self.h[k])


class V:
    def __init__(self, buf, ap):
        self.buf = buf
        self.ap = ap

    def __getitem__(self, k):
        return V(self.buf, self.ap[k])

    def re(self, pat_, **kw):
        return V(self.buf, self.ap.rearrange(pat_, **kw))

    def bc(self, shape):
        return V(self.buf, self.ap.to_broadcast(list(shape)))

    def un(self, ax):
        return V(self.buf, self.ap.unsqueeze(ax))

    def bitcast(self, dt):
        return V(self.buf, self.ap.bitcast(dt))


class Chunked:
    def __init__(self, buf, n, w):
        self.bufs = [Buf(buf.h, "%s_c%d" % (buf.t.name, i)) for i in range(n)]
        self.w = w

    def c(self, i, a=0, b=None):
        b = self.w if b is None else b
        return self.bufs[i][:, i * self.w + a:i * self.w + b]


class Pool:
    def __init__(self, bufs):
        self.bufs = bufs
        self.i = 0

    def next(self):
        b = self.bufs[self.i % len(self.bufs)]
        self.i += 1
        return b


def _tr(*vs):
    return [v.buf.t for v in vs if isinstance(v, V) and v.buf is not None]


def _rw(reads, writes):
    r, w = [], []
    for v in reads:
        if isinstance(v, V) and v.buf is not None:
            (w if v.buf.excl else r).append(v.buf.t)
    for v in writes:
        if isinstance(v, V) and v.buf is not None:
            w.append(v.buf.t)
    return dict(reads=r, writes=w)


def _a(x):
    return x.ap if isinstance(x, V) else x


def _fs(v):
    n = 1
    for d in v.ap.shape[1:]:
        n *= int(d)
    return n


def _is_psum(v):
    return isinstance(v, V) and v.buf is not None and v.buf.excl


_ASET = {AF.Silu: "silu", AF.Exp: "lnexp", AF.Ln: "lnexp", AF.Tanh: "silu"}


class K:
    def __init__(self, nc, S):
        self.nc = nc
        self.S = S

    def mm(self, out, lhsT, rhs, start=True, stop=True):
        n = max(32, _fs(rhs))
        c = n / 2.37 * (4.0 if rhs.ap.dtype == F32 else 1.0) + 48.0
        self.S.op("pe", lambda e: e.matmul(out.ap, lhsT=lhsT.ap, rhs=rhs.ap, start=start, stop=stop),
                  cost=c, **_rw([lhsT, rhs], [out]))

    def tp(self, out, in_, ident):
        c = max(32, _fs(ident)) / 2.37 * (4.0 if in_.ap.dtype == F32 else 1.0) + 48.0
        self.S.op("pe", lambda e: e.transpose(out.ap, in_.ap, ident.ap), cost=c, **_rw([in_, ident], [out]))

    def act(self, out, in_, func, scale=1.0, bias=0.0, accum=None):
        def f(e):
            kw = dict(out=out.ap, in_=in_.ap, func=func, scale=_a(scale), bias=_a(bias))
            if accum is not None:
                kw["accum_out"] = accum.ap
            return e.activation(**kw)
        c = 190.0 + 0.6 * _fs(in_) + (90.0 if accum is not None else 0.0)
        self.S.op("act", f, cost=c, aset=_ASET.get(func), **_rw([in_, scale, bias], [out, accum]))

    def _vc(self, eng, *vs):
        f = max(_fs(v) for v in vs if isinstance(v, V))
        ps = any(_is_psum(v) for v in vs)
        if eng == "pool":
            return 1100.0 + 0.45 * f
        return (130.0 if ps else 90.0) + 1.25 * f

    def tt(self, eng, out, a, b, op):
        self.S.op(eng, lambda e: e.tensor_tensor(out=out.ap, in0=a.ap, in1=b.ap, op=op), cost=self._vc(eng, out, a, b),
                  **_rw([a, b], [out]))

    def ts(self, eng, out, a, s1, op0, s2=None, op1=None):
        def f(e):
            if s2 is None:
                return e.tensor_scalar(out=out.ap, in0=a.ap, scalar1=_a(s1), scalar2=None, op0=op0)
            return e.tensor_scalar(out=out.ap, in0=a.ap, scalar1=_a(s1), scalar2=_a(s2), op0=op0, op1=op1)
        self.S.op(eng, f, cost=self._vc(eng, out, a), **_rw([a, s1, s2], [out]))

    def stt(self, eng, out, a, s, b, op0, op1):
        self.S.op(eng, lambda e: e.scalar_tensor_tensor(out=out.ap, in0=a.ap, scalar=_a(s), in1=b.ap, op0=op0, op1=op1),
                  cost=self._vc(eng, out, a, b), **_rw([a, s, b], [out]))

    def cp(self, eng, out, a):
        if eng == "act":
            self.S.op("act", lambda e: e.copy(out=out.ap, in_=a.ap), cost=190.0 + 0.6 * _fs(a), **_rw([a], [out]))
        else:
            c = (250.0 + 0.3 * _fs(a)) if eng == "pool" else self._vc(eng, out, a)
            self.S.op(eng, lambda e: e.tensor_copy(out=out.ap, in_=a.ap), cost=c, **_rw([a], [out]))

    def recip(self, out, a):
        self.S.op("dve", lambda e: e.reciprocal(out=out.ap, in_=a.ap), cost=self._vc("dve", out, a), **_rw([a], [out]))

    def ms(self, eng, out, val):
        self.S.op(eng, lambda e: e.memset(out.ap, val), cost=250.0 + 0.3 * _fs(out), **_rw([], [out]))

    def dma(self, q, out, in_):
        nbytes = int(out.ap.shape[0]) * _fs(out) * 4
        c = 2000.0 + nbytes / 120.0
        return self.S.op(q, lambda e: e.dma_start(out=out.ap, in_=in_.ap), dma=True, cost=c, **_rw([in_], [out]))


def build_program():
    nc = bass.Bass("TRN2", target_bir_lowering=False)

    def din(name, shape):
        return V(None, nc.dram_tensor(name, list(shape), F32, kind="ExternalInput").ap())

    def dout(name, shape):
        return Buf(nc.dram_tensor(name, list(shape), F32, kind="ExternalOutput").ap(), name)

    class RowBufs:
        def __init__(self, ap):
            self.h = ap
            self.b = {}

        def rows(self, r0):
            if r0 not in self.b:
                self.b[r0] = Buf(self.h, "rb%d" % r0)
            return V(self.b[r0], self.h[r0:r0 + 128, :])

    x_d = din("x", [NTOK, D])
    p_d = din("p", [NTOK, 256])
    stdnc_d = din("st_dnc", [48, 1536])
    stdn_d = din("st_dn", [NSAMP, 4, 128, 128])
    stret_d = din("st_ret", [NSAMP, 4, 128, 128])
    stfc_d = din("st_fc", [32, 2 * DFF])
    w_in_d = din("w_in", [D, DIN])
    w_out_d = din("w_out", [D, D])
    w_up_d = din("w_up", [D, 2 * DFF])
    w_down_d = din("w_down", [DFF, D])
    w_gate_d = din("w_gate", [D, D])
    w_ple_d = din("w_ple", [256, D])
    anw8_d = din("anw8", [128, 8])
    fnw8_d = din("fnw8", [128, 8])
    pnw8_d = din("pnw8", [128, 8])
    finw_d = din("finw", [D])
    dncw_d = din("dncw", [128, 48])
    alog_d = din("alog", [4])
    dtb_d = din("dtb", [4])
    dnw4_d = din("dnw4", [512])
    retw_d = din("retw", [512])
    fcw_d = din("fcw", [128, 132])
    fcb_d = din("fcb", [128, 44])
    ident_d = din("ident", [128, 128])
    irep_d = din("irep", [128, 512])
    cos_d = din("cosT", [NTOK, 64])
    sin_d = din("sinT", [NTOK, 64])
    cvar = {}
    for v in ("P", "S"):
        cvar[v] = dict(CM=din("CM" + v, [128, 128]), UM=din("UM" + v, [128, 128]),
                       NEGs=din("NEGs" + v, [128, 128]), NEGT=din("NEGT" + v, [128, 128]),
                       DTr=din("DTr" + v, [128, 512]), rqd=din("rqd" + v, [128, 4]), rkt=din("rkt" + v, [128, 4]))
    e1_d = din("E1", [16 * 128])
    seq2_d = din("seq2", [128, 16])

    y_o = RowBufs(nc.dram_tensor("y", [NTOK, D], F32, kind="ExternalOutput").ap())
    dncp_o = dout("o_dnc_p", [3, 1536])
    dnp_o = dout("o_dn_p", [4, 128, 128])
    retp_o = dout("o_ret_p", [4, 128, 128])
    fcp_o = dout("o_fc_p", [2, 2 * DFF])
    dncs_o = dout("o_dnc_s", [48, 1536])
    dns_o = dout("o_dn_s", [NSAMP, 4, 128, 128])
    rets_o = dout("o_ret_s", [NSAMP, 4, 128, 128])
    fcs_o = dout("o_fc_s", [32, 2 * DFF])
    import os as _os
    _dbg = _os.environ.get("KDBG_OUT", "") == "1"
    _kind = dict(kind="ExternalOutput") if _dbg else {}
    h1_s = RowBufs(nc.dram_tensor("h1_scr", [NTOK, D], F32, **_kind).ap())
    h2_s = RowBufs(nc.dram_tensor("h2_scr", [NTOK, D], F32, **_kind).ap())

    lg = [float(np.log(1.0 - 2.0 ** (-5.0 - h))) for h in range(4)]

    with contextlib.ExitStack() as top:
        S = Sched(nc)
        S.open(top)
        k = K(nc, S)

        gcnt = [0]

        def mk_alloc(st):
            cnt = gcnt

            def sb(shape, dt=F32, n=0, name="t"):
                def one():
                    cnt[0] += 1
                    nm = "%s_%d" % (name, cnt[0])
                    return Buf(st.enter_context(nc.sbuf_tensor(nm, list(shape), dt)), nm)
                if n == 0:
                    return one()
                return Pool([one() for _ in range(n)])

            def psum_pool(**roles):
                assert sum(roles.values()) <= 8
                out = {}
                for role, n in roles.items():
                    bufs = []
                    for i in range(n):
                        cnt[0] += 1
                        nm = "ps_%d" % cnt[0]
                        bufs.append(Buf(st.enter_context(nc.psum_tensor(nm, [128, 512], F32)), nm, excl=True))
                    out[role] = Pool(bufs)
                return out
            return sb, psum_pool

        def load_w(st_sb, wd, kchunks, ncols, name):
            return load_cols(st_sb, wd, kchunks, 0, ncols, name)

        def rstd_act(ss_in, n_inv, small, ncol):
            a = small.next()
            k.act(a[:, 0:ncol], ss_in, AF.Ln, scale=n_inv, bias=epsc[:, 0:1])
            r = small.next()
            k.act(r[:, 0:ncol], a[:, 0:ncol], AF.Exp, scale=-0.5)
            return r[:, 0:ncol]

        def rstd_of(ss_in, n_inv, small, mhalf, ncol):
            if USE_ACT_RSTD[0]:
                return rstd_act(ss_in, n_inv, small, ncol)
            a = small.next()
            k.ts("dve", a[:, 0:ncol], ss_in, n_inv, ALU.mult, EPS, ALU.add)
            r = small.next()
            k.tt("pool", r[:, 0:ncol], a[:, 0:ncol], mhalf[:, 0:ncol], ALU.pow)
            return r[:, 0:ncol]

        USE_ACT_RSTD = [False]
        epsc = None

        def phase_a(samp):
            USE_ACT_RSTD[0] = True
            try:
                phase_a_body(samp)
            finally:
                USE_ACT_RSTD[0] = False

        def phase_a_body(samp):
            nonlocal epsc
            with contextlib.ExitStack() as st:
                sb, psum_pool = mk_alloc(st)
                pp = psum_pool(E=3, M=2, R=2, L=1)
                cv = cvar["S" if samp else "P"]
                nst = NSAMP if samp else 1
                nlev = 3 if samp else 7
                Cc = LS if samp else 128
                cdec = [float(np.exp(Cc * lg[h])) for h in range(4)]
                nb = 1 if samp else 1
                w_inq, w_inr, w_out = WA
                identf = sb([128, 128]); k.dma("sp", identf[:], ident_d)
                identb = sb([128, 128], BF16); k.dma("pool", identb[:], ident_d)
                irep = sb([128, 512], BF16); k.dma("pool", irep[:], irep_d)
                onesf = sb([128, 128]); k.ms("pool", onesf[:], 1.0)
                nonesf = sb([128, 128]); k.ms("pool", nonesf[:], -1.0)
                mhalf = sb([128, 8]); k.ms("pool", mhalf[:], -0.5)
                epsc = sb([128, 1]); k.ms("pool", epsc[:], EPS)
                ecvp = sb([128, 256], n=2, name="ecv")
                CM = sb([128, 128]); k.dma("sp", CM[:], cv["CM"])
                UM = sb([128, 128]); k.dma("sp", UM[:], cv["UM"])
                NEGs = sb([128, 128], BF16); k.dma("pool", NEGs[:], cv["NEGs"])
                NEGT = sb([128, 128], BF16); k.dma("pool", NEGT[:], cv["NEGT"])
                DTr = sb([128, 512]); k.dma("sp", DTr[:], cv["DTr"])
                rqd = sb([128, 4]); k.dma("sp", rqd[:], cv["rqd"])
                rkt = sb([128, 4]); k.dma("sp", rkt[:], cv["rkt"])
                anw8 = sb([128, 8]); k.dma("sp", anw8[:], anw8_d)
                cw = sb([128, 48]); k.dma("sp", cw[:], dncw_d)
                alogb = sb([128, 4]); k.dma("sp", alogb[:], V(None, alog_d.ap.partition_broadcast(128)))
                dtbb = sb([128, 4]); k.dma("sp", dtbb[:], V(None, dtb_d.ap.partition_broadcast(128)))
                negA = sb([128, 4])
                k.act(negA[:], alogb[:], AF.Exp)
                k.ts("dve", negA[:], negA[:], -1.0, ALU.mult)
                dnw = sb([128, 512]); k.dma("sp", dnw[:], V(None, dnw4_d.ap.partition_broadcast(128)))
                retw = sb([128, 512]); k.dma("sp", retw[:], V(None, retw_d.ap.partition_broadcast(128)))
                if samp:
                    E1 = sb([128, 16 * 128], BF16)
                    k.dma("pool", E1[:], V(None, e1_d.ap.partition_broadcast(128)))
                    seq2 = sb([128, 16]); k.dma("sp", seq2[:], seq2_d)
                NT = 128 if samp else 256
                nbuf = 1 if samp else 2
                xtp = sb([128, D], n=1, name="xt")
                xrp = None if samp else sb([128, D], n=1, name="xr")
                abfp = sb([128, D], BF16, n=1, name="abf")
                aTp = sb([128, 8 * NT], BF16, n=1, name="aT")
                qkvT = sb([128, 12 * NT], BF16, name="qkvT")
                xprep = sb([128, 264], n=1 if samp else 2, name="xpre")
                ycvp = sb([128, 256], n=1 if samp else 2, name="ycv")
                cx_all = sb([128, 12 * 48], name="cx"); cx = Chunked(cx_all, 12, 48)
                junk = sb([128, D], BF16, name="junk"); junk = V(None, junk.h[:])
                small = sb([128, 24], n=24 if samp else 32, name="small")
                zsp = sb([128, 512], BF16, n=1 if samp else 2, name="zs")
                rqkf = sb([128, 1024], name="rqkf")
                qkf = sb([128, 1024], BF16, name="qkf")
                rgsp = sb([128, 512], BF16, n=1 if samp else 2, name="rgs")
                tmpp = sb([128, 512], n=2, name="tmp")
                ebf = sb([128, 512], BF16, n=4, name="ebf")
                e2r = sb([128, 512], BF16, n=3 if samp else PE2R, name="e2r")
                rbf = sb([128, 512], BF16, n=2 if samp else 4, name="rbf")
                rbf2 = sb([128, 512], BF16, n=5 if samp else PRBF2, name="rbf2")
                kqT = sb([128, 1024], BF16, n=2 if samp else PKQT, name="kqT")
                rtk = sb([128, 1024], BF16, n=2 if samp else PRTK, name="rtk")
                MG = sb([128, 512], name="MG")
                Xp = sb([128, 512], BF16, n=2, name="X")
                XTp = sb([128, 512], BF16, n=2, name="XT")
                IXp = sb([128, 512], BF16, n=2, name="IX")
                PTp = sb([128, 512], BF16, n=2 if samp else 3, name="PT")
                dexp = sb([128, 512], BF16, n=3, name="dexp")
                ofp = sb([128, 512], n=POF, name="of")
                mix = sb([128, D], BF16, name="mix")
                mixT = sb([128, D], BF16, name="mixT")
                h1p = None if samp else sb([128, D], n=1, name="h1")
                csp = sb([128, 128], n=nbuf, name="cs")
                if samp:
                    sbigp = sb([128, 16 * 128], n=2, name="sbig")
                    sbigbp = sb([128, 16 * 128], BF16, n=2, name="sbigb")
                    expp = sb([128, 16 * 128], BF16, n=2, name="exp")
                    dncT = sb([128, 12 * 48], name="dncT")
                else:
                    Sdn = sb([128, 512], name="Sdn"); k.ms("pool", Sdn[:], 0.0)
                    Sdnb = sb([128, 512], BF16, name="Sdnb"); k.ms("pool", Sdnb[:], 0.0)
                    Srt = sb([128, 512], name="Srt"); k.ms("pool", Srt[:], 0.0)
                    Srtb = sb([128, 512], BF16, name="Srtb"); k.ms("pool", Srtb[:], 0.0)
                    k.ms("pool", cx_all[:], 0.0)

                if samp:
                    blocks = [(SEQ, 128, NSAMP, LS)]
                else:
                    blocks = [(b * 256, 256, 1, 256) for b in range(SEQ // 256)]

                hsl = lambda h: slice(h * 128, (h + 1) * 128)
                b3 = lambda v: v.un(2).bc([128, 4, 128])
                r3 = lambda v: v.re("p (h d) -> p h d", h=4)

                if samp:
                    for gI in range(3):
                        tmp = tmpp.next()
                        k.dma("sp", tmp[0:48, :], stdnc_d[:, gI * 512:(gI + 1) * 512])
                        ps = pp["M"].next()
                        for i in range(4):
                            k.tp(ps[:, i * 48:(i + 1) * 48], tmp[0:48, i * 128:(i + 1) * 128], identf[0:48, 0:48])
                        k.cp("act", dncT[:, gI * 192:(gI + 1) * 192], ps[:, 0:192])

                last_xt = [None]

                def stage_a0(t0, NT, aT):
                    nsub = NT // 128
                    for j in range(nsub):
                        xt = xtp.next()
                        last_xt[0] = xt
                        k.dma("sp", xt[:], x_d[t0 + 128 * j:t0 + 128 * (j + 1), :])
                        ss = small.next()
                        k.act(junk, xt[:], AF.Square, accum=ss[:, 0:1])
                        rs = rstd_of(ss[:, 0:1], 1.0 / D, small, mhalf, 1)
                        abf = abfp.next()
                        k.ts("dve", abf[:], xt[:], rs, ALU.mult)
                        ps = pp["M"].next()
                        pb = ps[:].bitcast(BF16)
                        for kk in range(8):
                            k.tp(pb[:, hsl(kk)], abf[:, hsl(kk)], identb[:])
                        k.tt("dve", aT[:].re("p (k t) -> p k t", k=8)[:, :, 128 * j:128 * (j + 1)],
                             pb.re("p (k t) -> p k t", k=8), anw8[:].un(2).bc([128, 8, 128]), ALU.mult)

                try:
                  ck(0)
                  for bi, (t0, NT, nseq, L) in enumerate(blocks):
                    nsub = NT // 128
                    aT = aTp.next()
                    stage_a0(t0, NT, aT)
                    ck(1)
                    aT3 = aT[:].re("p (k t) -> p k t", k=8)
                    qkvT3 = qkvT[:].re("p (c t) -> p c t", c=12)
                    for fc in range(12):
                        ps = pp["M"].next()
                        for kk in range(8):
                            k.mm(ps[:, 0:NT], w_inq[kk][:, fc * 128:(fc + 1) * 128], aT3[:, kk, :], start=kk == 0, stop=kk == 7)
                        ck(1.1)
                        xp = xprep.next()
                        xv = xp[:, 0:nseq * (3 + L)].re("p (s l) -> p s l", s=nseq)
                        psv = ps[:, 0:NT].re("p (s l) -> p s l", s=nseq)
                        k.cp("act", xv[:, :, 3:3 + L], psv)
                        ck(1.2)
                        if samp:
                            k.cp("pool", xv[:, :, 0:3], dncT[:, fc * 48:(fc + 1) * 48].re("p (s j) -> p s j", s=nseq))
                        else:
                            k.cp("pool", xv[:, :, 0:3], cx.c(fc, 0, 3).un(1))
                        k.cp("pool", cx.c(fc, 0, 3 * nseq).re("p (s j) -> p s j", s=nseq), xv[:, :, L:L + 3])
                        ck(1.3)
                        y = ycvp.next()
                        yv = y[:, 0:NT].re("p (s l) -> p s l", s=nseq)
                        k.act(yv, psv, AF.Copy, scale=cw[:, fc * 4 + 3:fc * 4 + 4])
                        ck(1.4)
                        k.stt("dve", yv, xv[:, :, 2:2 + L], cw[:, fc * 4 + 2:fc * 4 + 3], yv, ALU.mult, ALU.add)
                        k.stt("dve", yv, xv[:, :, 1:1 + L], cw[:, fc * 4 + 1:fc * 4 + 2], yv, ALU.mult, ALU.add)
                        k.stt("dve", yv, xv[:, :, 0:L], cw[:, fc * 4 + 0:fc * 4 + 1], yv, ALU.mult, ALU.add)
                        ck(1.5)
                        ecv = ecvp.next()
                        k.act(ecv[:, 0:NT], y[:, 0:NT], AF.Exp, scale=-1.0)
                        k.act(ecv[:, 0:NT], ecv[:, 0:NT], AF.Ln, bias=1.0)
                        k.act(ecv[:, 0:NT], ecv[:, 0:NT], AF.Exp, scale=-1.0)
                        k.tt("dve", qkvT3[:, fc, :], y[:, 0:NT], ecv[:, 0:NT], ALU.mult)
                        ck(1.6)

                    ck(2)
                    for j in range(nsub):
                        js = slice(128 * j, 128 * (j + 1))
                        r0 = t0 + 128 * j
                        if samp:
                            xr = last_xt[0]
                        else:
                            xr = xrp.next()
                            k.dma("sp", xr[:], x_d[r0:r0 + 128, :])
                        cs = csp.next()
                        k.dma("sp", cs[:, 0:64], cos_d[r0:r0 + 128, :])
                        k.dma("sp", cs[:, 64:128], sin_d[r0:r0 + 128, :])

                        def proj(c0, n, role="M"):
                            ps = pp[role].next()
                            for kk in range(8):
                                k.mm(ps[:, 0:n], aT3[:, kk, js], w_inr[kk][:, c0 - 1536:c0 - 1536 + n], start=kk == 0, stop=kk == 7)
                            return ps

                        ck(2.1)
                        psqk = pp["E"].next(); pqk = psqk[:].bitcast(BF16)
                        for i in range(8):
                            k.tp(pqk[:, hsl(i)], qkvT3[:, i, js], identb[:])
                        psv_ = pp["E"].next(); pv = psv_[:].bitcast(BF16)
                        for h in range(4):
                            k.tp(pv[:, hsl(h)], qkvT3[:, 8 + h, js], identb[:])
                        ck(2.2)
                        k.cp("act", qkf[:], pqk)
                        ck(2.3)
                        st = small.next()
                        for i in range(8):
                            k.act(junk[:, 0:128], qkf[:, hsl(i)], AF.Square, accum=st[:, i:i + 1])
                        ck(2.4)
                        rs = rstd_of(st[:, 0:8], 1.0, small, mhalf, 8)
                        ck(3)
                        psba = proj(C_B, 8, "E")
                        sm = small.next()
                        k.act(sm[:, 0:4], psba[:, 0:4], AF.Exp, scale=-1.0)
                        k.ts("dve", sm[:, 0:4], sm[:, 0:4], 1.0, ALU.add)
                        beta_t = small.next(); beta = beta_t[:, 0:4]
                        k.recip(beta, sm[:, 0:4])
                        k.tt("dve", sm[:, 4:8], psba[:, 4:8], dtbb[:], ALU.add)
                        k.act(sm[:, 8:12], sm[:, 4:8], AF.Exp)
                        k.act(sm[:, 12:16], sm[:, 8:12], AF.Ln, bias=1.0)
                        g_t = small.next(); g = g_t[:, 0:4]
                        k.tt("dve", g, sm[:, 12:16], negA[:], ALU.mult)
                        ck(4)
                        psg = pp["E"].next()
                        k.mm(psg[:, 0:4], CM[:], g)
                        k.mm(psg[:, 4:8], UM[:], g)
                        if samp:
                            Gs = small.next() if False else tmpp.next()
                            k.tt("pool", Gs[:, 0:64].re("p (h s) -> p h s", h=4), g.un(2).bc([128, 4, 16]),
                                 seq2[:].un(1).bc([128, 4, 16]), ALU.mult)
                            k.mm(psg[:, 8:72], onesf[:], Gs[:, 0:64])
                        else:
                            k.mm(psg[:, 8:12], onesf[:], g)
                        nex = 8 + 4 * nst
                        ex_t = sb_ex.next()
                        ex = ex_t[:, 0:nex]
                        k.act(ex, psg[:, 0:nex], AF.Exp)
                        sc_t = small.next(); sc = sc_t
                        k.tt("dve", sc[:, 0:4], rs[:, 4:8], beta, ALU.mult)
                        k.tt("dve", sc[:, 4:8], rs[:, 4:8], ex[:, 4:8], ALU.mult)
                        k.ts("dve", sc[:, 8:12], rs[:, 0:4], 128.0 ** -0.5, ALU.mult)
                        k.tt("dve", sc[:, 12:16], sc[:, 8:12], ex[:, 0:4], ALU.mult)
                        k.stt("dve", sc[:, 16:20], beta, -1.0, ex[:, 0:4], ALU.mult, ALU.mult)
                        kf = r3(qkf[:, 512:1024]); qf = r3(qkf[:, 0:512])
                        Kn = ebf.next(); KB = ebf.next(); Qs = ebf.next(); Qd = ebf.next(); Kt = e2r.next(); Vb = e2r.next()
                        k.tt("pool", r3(Kn[:]), kf, b3(rs[:, 4:8]), ALU.mult)
                        k.tt("dve", r3(KB[:]), kf, b3(sc[:, 0:4]), ALU.mult)
                        k.tt("pool", r3(Kt[:]), kf, b3(sc[:, 4:8]), ALU.mult)
                        k.tt("dve", r3(Qs[:]), qf, b3(sc[:, 8:12]), ALU.mult)
                        k.tt("pool", r3(Qd[:]), qf, b3(sc[:, 12:16]), ALU.mult)
                        k.tt("dve", r3(Vb[:]), r3(pv[:, 0:512]), b3(beta), ALU.mult)
                        ck(5)
                        KKT = kqT.next(); QQT = kqT.next()
                        psA = pp["E"].next(); pA = psA[:].bitcast(BF16)
                        for h in range(4):
                            k.tp(pA[:, hsl(h)], Kn[:, hsl(h)], identb[:])
                        for h in range(4):
                            k.tp(pA[:, hsl(4 + h)], KB[:, hsl(h)], identb[:])
                        k.cp("dve" if "h" in BAL else "act", KKT[:], pA)
                        psB = pp["E"].next(); pB = psB[:].bitcast(BF16)
                        for h in range(4):
                            k.tp(pB[:, hsl(h)], Qs[:, hsl(h)], identb[:])
                        for h in range(4):
                            k.tp(pB[:, hsl(4 + h)], Qd[:, hsl(h)], identb[:])
                        k.cp("dve", QQT[:], pB)
                        KnT = lambda h: KKT[:, hsl(h)]
                        KBT = lambda h: KKT[:, hsl(4 + h)]
                        QsT = lambda h: QQT[:, hsl(h)]
                        QdT = lambda h: QQT[:, hsl(4 + h)]
                        ck(6)
                        k.tt("pool", r3(MG[:]), CM[:].un(1).bc([128, 4, 128]), g.un(2).bc([128, 4, 128]), ALU.mult)
                        psD = pp["E"].next()
                        for h in range(4):
                            k.mm(psD[:, hsl(h)], MG[:, hsl(h)], onesf[:], True, False)
                            k.mm(psD[:, hsl(h)], nonesf[:], MG[:, hsl(h)], False, False)
                            k.mm(psD[:, hsl(h)], identb[:], NEGs[:], False, True)
                        Ds = dexp.next(); DT = dexp.next(); DTs = dexp.next()
                        k.act(Ds[:], psD[:], AF.Exp)
                        psDT = pp["E"].next(); pDT = psDT[:].bitcast(BF16)
                        for h in range(4):
                            k.tp(pDT[:, hsl(h)], Ds[:, hsl(h)], identb[:])
                        k.cp("act", DTs[:], pDT[:, 0:512])
                        k.tt("dve", DT[:], pDT[:, 0:512], irep[:], ALU.add)
                        ck(7)
                        psA_ = pp["E"].next(); psAT = pp["E"].next(); psKQ = pp["E"].next()
                        for h in range(4):
                            k.mm(psA_[:, hsl(h)], KBT(h), KnT(h))
                        for h in range(4):
                            k.mm(psAT[:, hsl(h)], KnT(h), KBT(h))
                        for h in range(4):
                            k.mm(psKQ[:, hsl(h)], KnT(h), QsT(h))
                        X = Xp.next(); XT = XTp.next(); PT = PTp.next(); QKDT = e2r.next()
                        k.stt("dve", X[:], psA_[:], -1.0, Ds[:], ALU.mult, ALU.mult)
                        k.stt("dve", XT[:], psAT[:], -1.0, DTs[:], ALU.mult, ALU.mult)
                        k.tt("dve", QKDT[:], psKQ[:], DT[:], ALU.mult)
                        k.tt("pool", PT[:], XT[:], irep[:], ALU.add)
                        ck(8)
                        for lv in range(1, nlev):
                            psX = pp["E"].next()
                            for h in range(4):
                                k.mm(psX[:, hsl(h)], XT[:, hsl(h)], X[:, hsl(h)])
                            last = lv == nlev - 1
                            IX = IXp.next()
                            k.tt("dve", IX[:], psX[:], irep[:], ALU.add)
                            if not last:
                                Xn = Xp.next()
                                k.cp("dve" if "e" in BAL else "act", Xn[:], psX[:])
                                psXT = pp["E"].next()
                                for h in range(4):
                                    k.mm(psXT[:, hsl(h)], X[:, hsl(h)], XT[:, hsl(h)])
                                XTn = XTp.next()
                                k.cp("dve" if "f" in BAL else "act", XTn[:], psXT[:])
                            psP = pp["E"].next()
                            for h in range(4):
                                k.mm(psP[:, hsl(h)], IX[:, hsl(h)], PT[:, hsl(h)])
                            PTn = PTp.next()
                            k.cp("dve" if "g" in BAL else "act", PTn[:], psP[:])
                            PT = PTn
                            if not last:
                                X, XT = Xn, XTn
                        ck(9)
                        zs = zsp.next(); rgs = rgsp.next()
                        psz = proj(C_Z, 512)
                        sgt = tmpp.next()
                        k.act(sgt[:], psz[:], AF.Exp, scale=-1.0)
                        k.act(sgt[:], sgt[:], AF.Ln, bias=1.0)
                        k.act(sgt[:], sgt[:], AF.Exp, scale=-1.0)
                        k.tt("dve", zs[:], psz[:], sgt[:], ALU.mult)
                        psrq = proj(C_RQ, 512)
                        k.cp("act", rqkf[:, 0:512], psrq[:])
                        psrk = proj(C_RK, 512)
                        k.cp("act", rqkf[:, 512:1024], psrk[:])
                        RV = rbf2.next()
                        psrv = proj(C_RV, 512)
                        k.cp("act", RV[:], psrv[:])
                        psrg = proj(C_RG, 512)
                        sgt = tmpp.next()
                        k.act(sgt[:], psrg[:], AF.Exp, scale=-1.0)
                        k.act(sgt[:], sgt[:], AF.Ln, bias=1.0)
                        k.act(sgt[:], sgt[:], AF.Exp, scale=-1.0)
                        k.tt("dve", rgs[:], psrg[:], sgt[:], ALU.mult)

                        ck(10)
                        of = ofp.next()
                        R = rbf.next(); vn = rbf.next()
                        headsets = [[h] for h in range(4)] if samp else [[0, 1, 2, 3]]
                        for hs in headsets:
                            cols = slice(hs[0] * 128, (hs[-1] + 1) * 128)
                            if samp:
                                h = hs[0]
                                sbig = sbigp.next(); sbigb = sbigbp.next()
                                for _q in range(4):
                                    k.dma("sp", sbig[:, _q * 512:(_q + 1) * 512].re("p (s v) -> p s v", s=4), V(None, stdn_d.ap[4 * _q:4 * _q + 4, h].rearrange("s k v -> k s v")))
                                k.cp("pool", sbigb[:], sbig[:])
                                KnE = expp.next(); QdE = expp.next()
                                e3 = lambda v: v.re("p (s c) -> p s c", s=16)
                                k.tt("pool", e3(KnE[:]), KnT(h).un(1).bc([128, 16, 128]), e3(E1[:]), ALU.mult)
                                k.tt("dve", e3(QdE[:]), QdT(h).un(1).bc([128, 16, 128]), e3(E1[:]), ALU.mult)
                                Sf = lambda hh, s, sbig=sbig: sbig[:, hsl(s)]
                                Sb = lambda hh, s, sbigb=sbigb: sbigb[:, hsl(s)]
                                lK = lambda hh, s: KnE[:, hsl(s)]
                                lQ = lambda hh, s: QdE[:, hsl(s)]
                            else:
                                Sf = lambda hh, s: Sdn[:, hsl(hh)]
                                Sb = lambda hh, s: Sdnb[:, hsl(hh)]
                                lK = lambda hh, s: KnT(hh)
                                lQ = lambda hh, s: QdT(hh)
                                lT = lambda hh, s: Kt[:, hsl(hh)]
                            psKS = pp["R"].next()
                            for h in hs:
                                for s in range(nst):
                                    k.mm(psKS[:, hsl(h)], lK(h, s), Sb(h, s), s == 0, s == nst - 1)
                            if samp:
                                KtE = expp.next()
                                k.tt("pool", e3(KtE[:]), Kt[:, hsl(hs[0])].un(1).bc([128, 16, 128]), seq2[:].un(2).bc([128, 16, 128]), ALU.mult)
                                lT = lambda hh, s: KtE[:, hsl(s)]
                            for h in hs:
                                k.stt("dve", R[:, hsl(h)], psKS[:, hsl(h)], sc[:, 16 + h:17 + h], Vb[:, hsl(h)], ALU.mult, ALU.add)
                            psV = pp["R"].next()
                            for h in hs:
                                k.mm(psV[:, hsl(h)], PT[:, hsl(h)], R[:, hsl(h)])
                            k.cp("act", vn[:, cols], psV[:, cols])
                            psO = pp["R"].next()
                            for h in hs:
                                for s in range(nst):
                                    k.mm(psO[:, hsl(h)], lQ(h, s), Sb(h, s), s == 0, False)
                                k.mm(psO[:, hsl(h)], QKDT[:, hsl(h)], vn[:, hsl(h)], False, True)
                            k.cp("act", of[:, cols], psO[:, cols])
                            pairs = [(h, s) for h in hs for s in range(nst)]
                            for g0 in range(0, len(pairs), 4):
                                grp = pairs[g0:g0 + 4]
                                psS = pp["R"].next()
                                for i, (h, s) in enumerate(grp):
                                    k.mm(psS[:, hsl(i)], lT(h, s), vn[:, hsl(h)])
                                for i, (h, s) in enumerate(grp):
                                    k.stt("dve", Sf(h, s), Sf(h, s), ex[:, 8 + h * nst + s:9 + h * nst + s], psS[:, hsl(i)], ALU.mult, ALU.add)
                            if samp:
                                for _q in range(4):
                                    k.dma("sp", V(Buf(dns_o.h, "dns%d_%d" % (hs[0], _q)), dns_o.h[4 * _q:4 * _q + 4, hs[0]].rearrange("s k v -> k s v")), sbig[:, _q * 512:(_q + 1) * 512].re("p (s v) -> p s v", s=4))
                            else:
                                k.cp("act", Sdnb[:], Sdn[:])
                        ck(11)
                        st = small.next()
                        for h in range(4):
                            k.act(junk[:, 0:128], of[:, hsl(h)], AF.Square, accum=st[:, h:h + 1])
                        rso = rstd_of(st[:, 0:4], 1.0 / 128, small, mhalf, 4)
                        k.tt("pool" if "a" in BAL else "dve", r3(of[:]), r3(of[:]), b3(rso), ALU.mult)
                        k.tt("pool", of[:], of[:], dnw[:], ALU.mult)
                        k.tt("pool" if "b" in BAL else "dve", mix[:, 0:512], of[:], zs[:], ALU.mult)

                        ck(12)
                        rqkb = rtk.next()
                        g4 = lambda v: v.re("p (g i two) -> p g i two", g=8, two=2)
                        x1 = g4(rqkf[:])[:, :, :, 0]; x2 = g4(rqkf[:])[:, :, :, 1]
                        o1 = g4(rqkb[:])[:, :, :, 0]; o2 = g4(rqkb[:])[:, :, :, 1]
                        cosb = cs[:, 0:64].un(1).bc([128, 8, 64]); sinb = cs[:, 64:128].un(1).bc([128, 8, 64])
                        t8 = lambda v: v.re("p (g i) -> p g i", g=8)
                        ta = tmpp.next(); tb = tmpp.next()
                        k.tt("dve", t8(ta[:]), x1, cosb, ALU.mult)
                        k.tt("pool", t8(tb[:]), x2, sinb, ALU.mult)
                        k.tt("dve", o1, t8(ta[:]), t8(tb[:]), ALU.subtract)
                        ta = tmpp.next(); tb = tmpp.next()
                        k.tt("pool", t8(ta[:]), x1, sinb, ALU.mult)
                        k.tt("dve", t8(tb[:]), x2, cosb, ALU.mult)
                        k.tt("pool", o2, t8(ta[:]), t8(tb[:]), ALU.add)
                        RQd = rbf2.next(); RKt = rbf2.next()
                        k.tt("dve", r3(RQd[:]), r3(rqkb[:, 0:512]), b3(rqd[:]), ALU.mult)
                        k.tt("pool", r3(RKt[:]), r3(rqkb[:, 512:1024]), b3(rkt[:]), ALU.mult)
                        RQKT = rtk.next(); RQdT_t = rbf2.next()
                        psR1 = pp["M"].next(); pR1 = psR1[:].bitcast(BF16)
                        for i in range(8):
                            k.tp(pR1[:, hsl(i)], rqkb[:, hsl(i)], identb[:])
                        k.cp("act", RQKT[:], pR1)
                        psR2 = pp["M"].next(); pR2 = psR2[:].bitcast(BF16)
                        for h in range(4):
                            k.tp(pR2[:, hsl(h)], RQd[:, hsl(h)], identb[:])
                        k.cp("dve", RQdT_t[:], pR2[:, 0:512])
                        psKQr = pp["M"].next()
                        for h in range(4):
                            k.mm(psKQr[:, hsl(h)], RQKT[:, hsl(4 + h)], RQKT[:, hsl(h)])
                        QKDTr = rbf2.next()
                        k.tt("dve", QKDTr[:], psKQr[:], DTr[:], ALU.mult)
                        orf = ofp.next()
                        for hs in headsets:
                            cols = slice(hs[0] * 128, (hs[-1] + 1) * 128)
                            if samp:
                                h = hs[0]
                                sbig = sbigp.next(); sbigb = sbigbp.next()
                                for _q in range(4):
                                    k.dma("sp", sbig[:, _q * 512:(_q + 1) * 512].re("p (s v) -> p s v", s=4), V(None, stret_d.ap[4 * _q:4 * _q + 4, h].rearrange("s k v -> k s v")))
                                k.cp("pool", sbigb[:], sbig[:])
                                QdE = expp.next(); KtE = expp.next()
                                e3 = lambda v: v.re("p (s c) -> p s c", s=16)
                                k.tt("dve", e3(QdE[:]), RQdT_t[:, hsl(h)].un(1).bc([128, 16, 128]), e3(E1[:]), ALU.mult)
                                k.tt("pool", e3(KtE[:]), RKt[:, hsl(h)].un(1).bc([128, 16, 128]), seq2[:].un(2).bc([128, 16, 128]), ALU.mult)
                                Sf = lambda hh, s, sbig=sbig: sbig[:, hsl(s)]
                                Sb = lambda hh, s, sbigb=sbigb: sbigb[:, hsl(s)]
                                lQ = lambda hh, s: QdE[:, hsl(s)]
                                lT = lambda hh, s: KtE[:, hsl(s)]
                            else:
                                Sf = lambda hh, s: Srt[:, hsl(hh)]
                                Sb = lambda hh, s: Srtb[:, hsl(hh)]
                                lQ = lambda hh, s: RQdT_t[:, hsl(hh)]
                                lT = lambda hh, s: RKt[:, hsl(hh)]
                            psO = pp["R"].next()
                            for h in hs:
                                for s in range(nst):
                                    k.mm(psO[:, hsl(h)], lQ(h, s), Sb(h, s), s == 0, False)
                                k.mm(psO[:, hsl(h)], QKDTr[:, hsl(h)], RV[:, hsl(h)], False, True)
                            k.cp("act", orf[:, cols], psO[:, cols])
                            pairs = [(h, s) for h in hs for s in range(nst)]
                            for g0 in range(0, len(pairs), 4):
                                grp = pairs[g0:g0 + 4]
                                psS = pp["R"].next()
                                for i, (h, s) in enumerate(grp):
                                    k.mm(psS[:, hsl(i)], lT(h, s), RV[:, hsl(h)])
                                for i, (h, s) in enumerate(grp):
                                    k.stt("dve", Sf(h, s), Sf(h, s), cdec[h], psS[:, hsl(i)], ALU.mult, ALU.add)
                            if samp:
                                for _q in range(4):
                                    k.dma("sp", V(Buf(rets_o.h, "rets%d_%d" % (hs[0], _q)), rets_o.h[4 * _q:4 * _q + 4, hs[0]].rearrange("s k v -> k s v")), sbig[:, _q * 512:(_q + 1) * 512].re("p (s v) -> p s v", s=4))
                            else:
                                k.cp("act", Srtb[:], Srt[:])
                        ck(13)
                        st = small.next()
                        for h in range(4):
                            k.act(junk[:, 0:128], orf[:, hsl(h)], AF.Copy, accum=st[:, h:h + 1])
                        for h in range(4):
                            k.act(junk[:, 128:256], orf[:, hsl(h)], AF.Square, accum=st[:, 4 + h:5 + h])
                        s2 = small.next()
                        k.ts("dve", s2[:, 0:4], st[:, 0:4], 1.0 / 128, ALU.mult)
                        k.tt("dve", s2[:, 4:8], s2[:, 0:4], s2[:, 0:4], ALU.mult)
                        k.stt("dve", s2[:, 8:12], st[:, 4:8], 1.0 / 128, s2[:, 4:8], ALU.mult, ALU.subtract)
                        rsr = rstd_of(s2[:, 8:12], 1.0, small, mhalf, 4)
                        k.tt("pool" if "d" in BAL else "dve", r3(orf[:]), r3(orf[:]), b3(s2[:, 0:4]), ALU.subtract)
                        k.tt("pool", r3(orf[:]), r3(orf[:]), b3(rsr), ALU.mult)
                        k.tt("pool" if "c" in BAL else "dve", orf[:], orf[:], retw[:], ALU.mult)
                        k.tt("pool", mix[:, 512:1024], orf[:], rgs[:], ALU.mult)
                        ck(14)
                        psM = pp["L"].next(); pM = psM[:].bitcast(BF16)
                        for kk in range(8):
                            k.tp(pM[:, hsl(kk)], mix[:, hsl(kk)], identb[:])
                        k.cp("act", mixT[:], pM)
                        h1 = xr if samp else h1p.next()
                        for half in range(2):
                            psH = pp["L"].next()
                            for kk in range(8):
                                k.mm(psH[:], mixT[:, hsl(kk)], w_out[kk][:, half * 512:(half + 1) * 512], kk == 0, kk == 7)
                            k.tt("dve", h1[:, half * 512:(half + 1) * 512], psH[:], xr[:, half * 512:(half + 1) * 512], ALU.add)
                        k.dma("sp", h1_s.rows(r0), h1[:])

                except StopBuild:
                    pass
                n3 = 3 * (NSAMP if samp else 1)
                dnc_o = dncs_o if samp else dncp_o
                for gI in range(3):
                    ps = pp["L"].next()
                    for i in range(4):
                        fc = gI * 4 + i
                        k.tp(ps[0:n3, hsl(i)], cx.c(fc, 0, n3), identf[:])
                    tmp = tmpp.next()
                    k.cp("act", tmp[0:n3, :], ps[0:n3, :])
                    k.dma("sp", V(Buf(dnc_o.h, "dnc%d" % gI), dnc_o.h[:, gI * 512:(gI + 1) * 512]), tmp[0:n3, :])
                if not samp:
                    k.dma("sp", V(dnp_o, dnp_o.h.rearrange("h k v -> k h v")), Sdn[:].re("p (h v) -> p h v", h=4))
                    k.dma("sp", V(retp_o, retp_o.h.rearrange("h k v -> k h v")), Srt[:].re("p (h v) -> p h v", h=4))
                S.emit_phase()

        sb_ex = None

        WA = None

        def phase_a_wrap(samp):
            nonlocal sb_ex
            with contextlib.ExitStack() as st0:
                sb0, _ = mk_alloc(st0)
                sb_ex = sb0([128, 72], n=3, name="ex")
                phase_a(samp)

        class Sub:
            def __init__(self, buf, off, w):
                self.buf, self.off, self.w = buf, off, w

            def __getitem__(self, key):
                sl = key[1]
                a0 = 0 if sl.start is None else sl.start
                a1 = self.w if sl.stop is None else sl.stop
                return V(self.buf, self.buf.h[:, self.off + a0:self.off + a1])

        def load_cols(st_sb, wd, kchunks, c0, c1, name, kmax=11):
            out = []
            W = c1 - c0
            k0 = 0
            while k0 < kchunks:
                nk = min(kmax, kchunks - k0)
                b = st_sb([128, nk * W], BF16, name=name)
                src = V(None, wd.ap[k0 * 128:(k0 + nk) * 128, c0:c1].rearrange("(k p) c -> p k c", p=128))
                k.dma("pool", b[:].re("p (k c) -> p k c", k=nk), src)
                for j in range(nk):
                    out.append(Sub(b, j * W, W))
                k0 += nk
            return out

        import os
        _ph = os.environ.get("KDBG_PH", "ABCD")
        with contextlib.ExitStack() as stw:
            sbw, _ = mk_alloc(stw)
            if "A" in _ph or "B" in _ph:
                WA = (load_cols(sbw, w_in_d, 8, 0, 1536, "w_inq"), load_cols(sbw, w_in_d, 8, 1536, DIN, "w_inr"),
                      load_cols(sbw, w_out_d, 8, 0, D, "w_out"))
            if "A" in _ph:
                phase_a_wrap(False)
            if "B" in _ph:
                phase_a_wrap(True)

        def phase_b():
            with contextlib.ExitStack() as st:
                sb, psum_pool = mk_alloc(st)
                pp = psum_pool(tr=2, up=4, down=2)
                GP = [(0, 6), (6, 12), (12, 17), (17, 22)]
                w_up_g = []
                for (p0, p1) in GP:
                    ug = load_cols(sb, w_up_d, 8, p0 * 128, p1 * 128, "w_upg")
                    uv = load_cols(sb, w_up_d, 8, DFF + p0 * 128, DFF + p1 * 128, "w_upv")
                    w_up_g.append((p0, p1, ug, uv))

                def w_up_sl(kk, fc):
                    part, i = (0, fc) if fc < 22 else (1, fc - 22)
                    for (p0, p1, ug, uv) in w_up_g:
                        if p0 <= i < p1:
                            return (ug, uv)[part][kk][:, (i - p0) * 128:(i - p0 + 1) * 128]
                w_down = load_w(sb, w_down_d, 22, D, "w_down")
                identf = sb([128, 128]); k.dma("sp", identf[:], ident_d)
                identb = sb([128, 128], BF16); k.dma("pool", identb[:], ident_d)
                mhalf = sb([128, 8]); k.ms("pool", mhalf[:], -0.5)
                fnw8 = sb([128, 8]); k.dma("sp", fnw8[:], fnw8_d)
                fcw = sb([128, 132]); k.dma("sp", fcw[:], fcw_d)
                fcb = sb([128, 44]); k.dma("sp", fcb[:], fcb_d)
                NTm = 256
                import os as _o
                htp = sb([128, D], n=int(_o.environ.get("KB_HT", "4")), name="ht")
                mbfp = sb([128, D], BF16, n=2, name="mbf")
                import os as _o
                mTp = sb([128, 8 * NTm], BF16, n=int(_o.environ.get("KB_MT", "1")), name="mT")
                actTp = sb([128, 22 * NTm], BF16, n=int(_o.environ.get("KB_ACTT", "1")), name="actT")
                uprep = sb([128, 264], n=int(_o.environ.get("KB_UP", "4")), name="upre")
                ycp = sb([128, 256], n=int(_o.environ.get("KB_YC", "4")), name="yc")
                sgp = sb([128, 256], n=int(_o.environ.get("KB_SG", "2")), name="sg")
                cf_all = sb([128, 44 * 32], name="cf"); k.ms("pool", cf_all[:], 0.0); cf = Chunked(cf_all, 44, 32)
                fcT = sb([128, 44 * 32], name="fcT")
                junk = sb([128, D], BF16, name="junk"); junk = V(None, junk.h[:])
                small = sb([128, 8], n=8, name="small")
                h2p = sb([128, D], n=int(_o.environ.get("KB_H2", "2")), name="h2")
                tmpp = sb([128, 512], n=int(_o.environ.get("KB_TMP", "2")), name="tmp")
                hsl = lambda h: slice(h * 128, (h + 1) * 128)

                for gI in range(11):
                    tmp = tmpp.next()
                    k.dma("sp", tmp[0:32, :], stfc_d[:, gI * 512:(gI + 1) * 512])
                    ps = pp["tr"].next()
                    for i in range(4):
                        k.tp(ps[:, i * 32:(i + 1) * 32], tmp[0:32, hsl(i)], identf[0:32, 0:32])
                    k.cp("act", fcT[:, gI * 128:(gI + 1) * 128], ps[:, 0:128])

                blocks = [(b * 256, 256, 1, 256, False) for b in range(SEQ // 256)] + [(SEQ, 128, NSAMP, LS, True)]
                for (t0, NT, nseq, L, samp) in blocks:
                    nsub = NT // 128
                    mT = mTp.next(); actT = actTp.next()
                    mT3 = mT[:].re("p (k t) -> p k t", k=8)
                    hts = []
                    for j in range(nsub):
                        r0 = t0 + 128 * j
                        ht = htp.next(); hts.append(ht)
                        k.dma("sp", ht[:], h1_s.rows(r0))
                        ss = small.next()
                        k.act(junk, ht[:], AF.Square, accum=ss[:, 0:1])
                        rs = rstd_of(ss[:, 0:1], 1.0 / D, small, mhalf, 1)
                        mbf = mbfp.next()
                        k.ts("dve", mbf[:], ht[:], rs, ALU.mult)
                        ps = pp["tr"].next(); pb = ps[:].bitcast(BF16)
                        for kk in range(8):
                            k.tp(pb[:, hsl(kk)], mbf[:, hsl(kk)], identb[:])
                        k.tt("dve", mT3[:, :, 128 * j:128 * (j + 1)], pb.re("p (k t) -> p k t", k=8),
                             fnw8[:].un(2).bc([128, 8, 128]), ALU.mult)
                    actT3 = actT[:].re("p (c t) -> p c t", c=22)
                    for i in range(22):
                        ys = []
                        for fc in (i, 22 + i):
                            ps = pp["up"].next()
                            for kk in range(8):
                                k.mm(ps[:, 0:NT], w_up_sl(kk, fc), mT3[:, kk, 0:NT], kk == 0, kk == 7)
                            up = uprep.next()
                            uv = up[:, 0:nseq * (2 + L)].re("p (s l) -> p s l", s=nseq)
                            psv = ps[:, 0:NT].re("p (s l) -> p s l", s=nseq)
                            k.cp("act", uv[:, :, 2:2 + L], psv)
                            if samp:
                                k.cp("pool", uv[:, :, 0:2], fcT[:, fc * 32:(fc + 1) * 32].re("p (s j) -> p s j", s=nseq))
                            else:
                                k.cp("pool", uv[:, :, 0:2], cf.c(fc, 0, 2).un(1))
                            k.cp("pool", cf.c(fc, 0, 2 * nseq).re("p (s j) -> p s j", s=nseq), uv[:, :, L:L + 2])
                            y = ycp.next(); ys.append(y)
                            yv = y[:, 0:NT].re("p (s l) -> p s l", s=nseq)
                            k.act(yv, psv, AF.Identity, scale=fcw[:, fc * 3 + 2:fc * 3 + 3], bias=fcb[:, fc:fc + 1])
                            k.stt("dve", yv, uv[:, :, 1:1 + L], fcw[:, fc * 3 + 1:fc * 3 + 2], yv, ALU.mult, ALU.add)
                            k.stt("dve", yv, uv[:, :, 0:L], fcw[:, fc * 3 + 0:fc * 3 + 1], yv, ALU.mult, ALU.add)
                        sg = sgp.next()
                        k.act(sg[:, 0:NT], ys[0][:, 0:NT], AF.Silu)
                        k.tt("dve", actT3[:, i, 0:NT], sg[:, 0:NT], ys[1][:, 0:NT], ALU.mult)
                    for j in range(nsub):
                        r0 = t0 + 128 * j
                        h2 = h2p.next()
                        for half in range(2):
                            psH = pp["down"].next()
                            for c in range(22):
                                k.mm(psH[:], actT3[:, c, 128 * j:128 * (j + 1)], w_down[c][:, half * 512:(half + 1) * 512], c == 0, c == 21)
                            k.tt("dve", h2[:, half * 512:(half + 1) * 512], psH[:], hts[j][:, half * 512:(half + 1) * 512], ALU.add)
                        k.dma("sp", h2_s.rows(r0), h2[:])
                    last_prompt = (not samp) and t0 + NT == SEQ
                    if last_prompt or samp:
                        n2 = 2 * nseq
                        fo = fcs_o if samp else fcp_o
                        for gI in range(11):
                            ps = pp["tr"].next()
                            for i in range(4):
                                fc = gI * 4 + i
                                k.tp(ps[0:n2, hsl(i)], cf.c(fc, 0, n2), identf[:])
                            tmp = tmpp.next()
                            k.cp("act", tmp[0:n2, :], ps[0:n2, :])
                            k.dma("sp", V(Buf(fo.h, "fo%d" % gI), fo.h[:, gI * 512:(gI + 1) * 512]), tmp[0:n2, :])
                S.emit_phase()

        if "C" in _ph:
            phase_b()

        def phase_c():
            with contextlib.ExitStack() as st:
                sb, psum_pool = mk_alloc(st)
                pp = psum_pool(tr=2, mm=6)
                w_gate = load_w(sb, w_gate_d, 8, D, "w_gate")
                w_ple = load_w(sb, w_ple_d, 2, D, "w_ple")
                identb = sb([128, 128], BF16); k.dma("pool", identb[:], ident_d)
                mhalf = sb([128, 8]); k.ms("pool", mhalf[:], -0.5)
                pnw8 = sb([128, 8]); k.dma("sp", pnw8[:], pnw8_d)
                finw = sb([128, D]); k.dma("sp", finw[:], V(None, finw_d.ap.partition_broadcast(128)))
                h2p = sb([128, D], n=PCN, name="h2")
                ptp = sb([128, 256], n=PCN, name="pt")
                pbp = sb([128, 256], BF16, n=PCN, name="pb")
                nbfp = sb([128, D], BF16, n=PCN, name="nbf")
                nTp = sb([128, D], BF16, n=PCN, name="nT")
                pTp = sb([128, 256], BF16, n=PCN, name="pT")
                tgp = sb([128, D], n=PCN, name="tg")
                h3p = sb([128, D], n=PCN, name="h3")
                yp = sb([128, D], n=PCN, name="y")
                junk = sb([128, D], BF16, name="junk"); junk = V(None, junk.h[:])
                small = sb([128, 8], n=4 * PCN, name="small")
                hsl = lambda h: slice(h * 128, (h + 1) * 128)
                for it in range(NTOK // 128):
                    r0 = it * 128
                    h2 = h2p.next()
                    k.dma("sp", h2[:], h2_s.rows(r0))
                    pt = ptp.next()
                    k.dma("sp", pt[:], p_d[r0:r0 + 128, :])
                    ss = small.next()
                    k.act(junk, h2[:], AF.Square, accum=ss[:, 0:1])
                    rs = rstd_of(ss[:, 0:1], 1.0 / D, small, mhalf, 1)
                    nbf = nbfp.next()
                    k.act(nbf[:], h2[:], AF.Copy, scale=rs)
                    ps = pp["tr"].next(); pb = ps[:].bitcast(BF16)
                    for kk in range(8):
                        k.tp(pb[:, hsl(kk)], nbf[:, hsl(kk)], identb[:])
                    nT = nTp.next()
                    k.tt("dve", nT[:].re("p (k t) -> p k t", k=8), pb.re("p (k t) -> p k t", k=8),
                         pnw8[:].un(2).bc([128, 8, 128]), ALU.mult)
                    pbf = pbp.next()
                    k.cp("pool", pbf[:], pt[:])
                    ps2 = pp["tr"].next(); pb2 = ps2[:].bitcast(BF16)
                    for kk in range(2):
                        k.tp(pb2[:, hsl(kk)], pbf[:, hsl(kk)], identb[:])
                    pT = pTp.next()
                    k.cp("act", pT[:], pb2[:, 0:256])
                    tg = tgp.next(); h3 = h3p.next()
                    for half in range(2):
                        hs_ = slice(half * 512, (half + 1) * 512)
                        psG = pp["mm"].next()
                        for kk in range(8):
                            k.mm(psG[:], nT[:, hsl(kk)], w_gate[kk][:, hs_], kk == 0, kk == 7)
                        psP = pp["mm"].next()
                        for kk in range(2):
                            k.mm(psP[:], pT[:, hsl(kk)], w_ple[kk][:, hs_], kk == 0, kk == 1)
                        k.act(tg[:, hs_], psG[:], AF.Tanh, scale=0.5)
                        k.stt("dve", tg[:, hs_], tg[:, hs_], 1.0, psP[:], ALU.add, ALU.mult)
                        k.stt("dve", h3[:, hs_], tg[:, hs_], 0.5, h2[:, hs_], ALU.mult, ALU.add)
                    ss = small.next()
                    k.act(junk, h3[:], AF.Square, accum=ss[:, 0:1])
                    rs = rstd_of(ss[:, 0:1], 1.0 / D, small, mhalf, 1)
                    y = yp.next()
                    k.stt("dve", y[:], h3[:], rs, finw[:], ALU.mult, ALU.mult)
                    k.dma("sp", y_o.rows(r0), y[:])
                S.emit_phase()

        if "D" in _ph:
            phase_c()
    return nc


def _consts():
    c = {}
    idx = np.arange(128)
    c["ident"] = np.eye(128, dtype=np.float32)
    c["irep"] = np.tile(np.eye(128, dtype=np.float32), (1, 4))
    lg = np.log(1.0 - 2.0 ** (-5.0 - np.arange(4, dtype=np.float64)))
    sc = 128.0 ** -0.5
    for v, C in (("P", 128), ("S", LS)):
        seq = idx // C
        pos = idx % C
        same = seq[:, None] == seq[None, :]
        a = idx[:, None]
        b = idx[None, :]
        c["CM" + v] = (same & (a <= b)).astype(np.float32)
        c["UM" + v] = (same & (a > b)).astype(np.float32)
        c["NEGs" + v] = np.where(same & (a > b), 0.0, NEG).astype(np.float32)
        c["NEGT" + v] = np.where(same & (b >= a), 0.0, NEG).astype(np.float32)
        dtr = np.zeros((128, 4, 128), np.float64)
        for h in range(4):
            dtr[:, h, :] = np.where(same & (b >= a), sc * np.exp((b - a) * lg[h]), 0.0)
        c["DTr" + v] = dtr.reshape(128, 512).astype(np.float32)
        c["rqd" + v] = np.exp((pos[:, None] + 1.0) * lg[None, :]).astype(np.float32)
        c["rkt" + v] = (sc * np.exp((C - 1.0 - pos[:, None]) * lg[None, :])).astype(np.float32)
    s16 = np.arange(16)
    c["E1"] = (s16[:, None] == (idx[None, :] // LS)).astype(np.float32).reshape(-1)
    c["seq2"] = ((idx[:, None] // LS) == s16[None, :]).astype(np.float32)
    pos = np.concatenate([np.arange(SEQ), PAST + (np.arange(NSAMP * LS) % LS)]).astype(np.float32)
    inv = (np.float32(10000.0) ** (-np.arange(0, 128, 2, dtype=np.float32) / np.float32(128))).astype(np.float32)
    ang = (pos[:, None] * inv[None, :]).astype(np.float32)
    c["cosT"] = np.cos(ang.astype(np.float64)).astype(np.float32)
    c["sinT"] = np.sin(ang.astype(np.float64)).astype(np.float32)
    return c


_NC_CACHE = {}


def kernel(x_prompt, x_sample, p_prompt, p_sample, state_dn_conv, state_dn, state_ret,
           state_ffn_conv, attn_norm_w, w_in, dn_conv_w, dn_A_log, dn_dt_bias, dn_norm_w,
           ret_norm_w, w_out, ffn_norm_w, w_up, ffn_conv_w, ffn_conv_b, w_down, ple_norm_w,
           w_ple_gate, w_ple, final_norm_w):
    f = lambda a: np.ascontiguousarray(np.asarray(a), dtype=np.float32)
    x_prompt, x_sample, p_prompt, p_sample = f(x_prompt), f(x_sample), f(p_prompt), f(p_sample)
    state_dn_conv, state_dn, state_ret, state_ffn_conv = f(state_dn_conv), f(state_dn), f(state_ret), f(state_ffn_conv)
    col8 = lambda w: f(np.asarray(w).reshape(8, 128).T)
    shared = dict(
        w_in=f(w_in)[0], w_out=f(w_out)[0], w_up=f(w_up)[0], w_down=f(w_down)[0],
        w_gate=f(w_ple_gate)[0], w_ple=f(w_ple)[0],
        anw8=col8(f(attn_norm_w)[0]), fnw8=col8(f(ffn_norm_w)[0]), pnw8=col8(f(ple_norm_w)[0]),
        finw=f(final_norm_w),
        dncw=f(f(dn_conv_w)[0].T.reshape(12, 128, 4).transpose(1, 0, 2).reshape(128, 48)),
        alog=f(dn_A_log)[0], dtb=f(dn_dt_bias)[0],
        dnw4=f(np.tile(f(dn_norm_w)[0], 4)), retw=f(ret_norm_w)[0],
        fcw=f(f(ffn_conv_w)[0].T.reshape(44, 128, 3).transpose(1, 0, 2).reshape(128, 132)),
        fcb=f(f(ffn_conv_b)[0].reshape(44, 128).T),
    )
    shared.update(_consts())
    in_maps = []
    for i in range(NCORES):
        sl = slice(NSAMP * i, NSAMP * (i + 1))
        m = dict(shared)
        m["x"] = f(np.concatenate([x_prompt[i], x_sample[sl].reshape(NSAMP * LS, D)], axis=0))
        m["p"] = f(np.concatenate([p_prompt[0, i], p_sample[0, sl].reshape(NSAMP * LS, 256)], axis=0))
        m["st_dnc"] = f(state_dn_conv[0, sl].reshape(48, 1536))
        m["st_dn"] = f(state_dn[0, sl])
        m["st_ret"] = f(state_ret[0, sl])
        m["st_fc"] = f(state_ffn_conv[0, sl].reshape(32, 2 * DFF))
        in_maps.append(m)
    if "nc" not in _NC_CACHE:
        _NC_CACHE["nc"] = build_program()
    nc = _NC_CACHE["nc"]
    res = run_bass_kernel_spmd(nc, in_maps, core_ids=list(range(NCORES)))
    R = res.results
    _NC_CACHE["last"] = R
    g = lambda name: [np.asarray(R[i][name], dtype=np.float32) for i in range(NCORES)]
    y = g("y")
    y_prompt = np.stack([a[:SEQ] for a in y], axis=0)
    y_sample = np.concatenate([a[SEQ:].reshape(NSAMP, LS, D) for a in y], axis=0)
    dncp = np.stack(g("o_dnc_p"), axis=0)[None]
    dnp = np.stack(g("o_dn_p"), axis=0)[None]
    retp = np.stack(g("o_ret_p"), axis=0)[None]
    fcp = np.stack(g("o_fc_p"), axis=0)[None]
    dncs = np.concatenate([a.reshape(NSAMP, 3, 1536) for a in g("o_dnc_s")], axis=0)[None]
    dns = np.concatenate(g("o_dn_s"), axis=0)[None]
    rets = np.concatenate(g("o_ret_s"), axis=0)[None]
    fcs = np.concatenate([a.reshape(NSAMP, 2, 2 * DFF) for a in g("o_fc_s")], axis=0)[None]
    return (y_prompt, y_sample, dncp, dnp, retp, fcp, dncs, dns, rets, fcs)
```

```python
import contextlib
import os as _osb
import numpy as np
import concourse.bass as bass
import concourse.mybir as mybir
from concourse.bass_utils import run_bass_kernel_spmd

F32 = mybir.dt.float32
BF16 = mybir.dt.bfloat16
AF = mybir.ActivationFunctionType
ALU = mybir.AluOpType
AX = mybir.AxisListType

NCORES = 8
D = 1024
SEQ = 2048
NSAMP = 16
LS = 8
NTOK = SEQ + NSAMP * LS
DIN = 4104
DFF = 2816
EPS = 1e-6
PAST = 16384
NEG = -30000.0
PE2R, PRBF2, PKQT, PRTK, POF = 5, 5, 3, 3, 2
import os as _osb
BAL = _osb.environ.get('K_BAL', '')
PCN = int(_osb.environ.get('K_PCN', '6'))
C_QKV, C_Z, C_B, C_A, C_RQ, C_RK, C_RV, C_RG = 0, 1536, 2048, 2052, 2056, 2568, 3080, 3592


class T:
    __slots__ = ("name", "last_writer", "readers")

    def __init__(self, name=""):
        self.name = name
        self.last_writer = None
        self.readers = []


class Op:
    __slots__ = ("eng", "fn", "deps", "users", "ndep", "signaled", "sigval", "sem", "is_dma", "idx", "cost",
                 "aset", "phase", "finish", "pos", "rtime", "tag", "prio")

    def __init__(self, eng, fn, is_dma):
        self.eng = eng
        self.fn = fn
        self.deps = []
        self.users = []
        self.signaled = False
        self.sigval = None
        self.sem = None
        self.is_dma = is_dma
        self.finish = 0.0
        self.pos = -1


class Sched:
    ENGS = ("pe", "act", "dve", "pool", "sp")
    XLAT = float(_osb.environ.get('K_XLAT', '500'))
    SLAT = float(_osb.environ.get('K_SLAT', '60'))

    def __init__(self, nc, n_dma_sems=14):
        self.nc = nc
        self.ops = []
        self.n_dma_sems = n_dma_sems
        self.nops = 0
        self.phase = 0
        import os as _os
        self.prio_mode = int(_os.environ.get("KS_PRIO", "1"))
        self.prio_w = float(_os.environ.get("KS_PRIOW", "0.0"))

    def op(self, eng, fn, reads=(), writes=(), dma=False, cost=200.0, aset=None):
        o = Op(eng, fn, dma)
        o.idx = self.nops
        self.nops += 1
        o.cost = cost
        o.aset = aset
        o.phase = self.phase
        import sys as _sys
        fr = _sys._getframe(2)
        o.tag = fr.f_lineno if fr.f_code.co_name != "<lambda>" else fr.f_back.f_lineno
        deps = []
        for t in reads:
            if t.last_writer is not None:
                deps.append(t.last_writer)
        for t in writes:
            if t.last_writer is not None:
                deps.append(t.last_writer)
            deps.extend(t.readers)
        seen = set()
        for d in deps:
            if id(d) in seen or d is o or d.phase != o.phase:
                continue
            seen.add(id(d))
            o.deps.append(d)
            d.users.append(o)
        for t in reads:
            t.readers.append(o)
        for t in writes:
            t.last_writer = o
            t.readers = []
        self.ops.append(o)
        return o

    def open(self, stack):
        nc = self.nc
        self.sems = {}
        for e in ("pe", "act", "dve", "pool"):
            self.sems[e] = stack.enter_context(nc.semaphore("s_" + e))
        for e in ("sp", "act", "pool"):
            for k in range(self.n_dma_sems):
                self.sems[(e, k)] = stack.enter_context(nc.semaphore("d_%s_%d" % (e, k)))
        self.cnt = {}
        self.dma_n = {e: 0 for e in self.ENGS}

    def _schedule(self):
        import heapq
        ops = self.ops
        future = {e: [] for e in self.ENGS}
        avail = {e: [] for e in self.ENGS}
        free_at = {e: 0.0 for e in self.ENGS}
        cur_set = {e: None for e in self.ENGS}
        streams = {e: [] for e in self.ENGS}
        self._pipe = 0.0
        bl = {}
        for o in reversed(ops):
            m = 0.0
            for u in o.users:
                lat = self.XLAT if (u.eng != o.eng or o.is_dma) else self.SLAT
                v = bl[id(u)] + lat
                if v > m:
                    m = v
            bl[id(o)] = m + o.cost
        mode = self.prio_mode
        for o in ops:
            o.ndep = len(o.deps)
            o.rtime = 0.0
            if mode == 0:
                o.prio = o.idx
            else:
                o.prio = -bl[id(o)] + self.prio_w * o.idx
        for o in ops:
            if o.ndep == 0:
                heapq.heappush(future[o.eng], (0.0, o.prio, o.idx, o))
        left = len(ops)
        while left:
            best = None
            for e in self.ENGS:
                f, a = future[e], avail[e]
                while f and f[0][0] <= free_at[e]:
                    _, pr, i, o = heapq.heappop(f)
                    heapq.heappush(a, (pr, i, o))
                if a:
                    cand = (free_at[e], a[0][0], e, True)
                elif f:
                    cand = (f[0][0], f[0][1], e, False)
                else:
                    continue
                if best is None or cand < best:
                    best = cand
            start, _, e, from_avail = best
            if from_avail:
                _, _, o = heapq.heappop(avail[e])
            else:
                _, _, _, o = heapq.heappop(future[e])
            c = o.cost
            if o.aset is not None and o.aset != cur_set[e]:
                if cur_set[e] is not None:
                    c += 1300.0
                cur_set[e] = o.aset
            if o.is_dma:
                xfer = max(0.0, c - 2000.0) * (120.0 / 220.0)
                t0x = max(start + 1000.0, self._pipe)
                self._pipe = t0x + xfer
                o.finish = t0x + xfer + 1000.0
                free_at[e] = start + 60.0
            else:
                o.finish = start + c
                free_at[e] = o.finish
            o.pos = len(streams[e])
            streams[e].append(o)
            left -= 1
            for u in o.users:
                lat = self.XLAT if (u.eng != e or o.is_dma) else self.SLAT
                t = o.finish + lat
                if t > u.rtime:
                    u.rtime = t
                u.ndep -= 1
                if u.ndep == 0:
                    heapq.heappush(future[u.eng], (u.rtime, u.prio, u.idx, u))
        self.makespan = max(free_at.values())
        return streams

    def emit_phase(self):
        nc = self.nc
        sems = self.sems
        cnt = self.cnt
        streams = self._schedule()
        for e in self.ENGS:
            last_on_sem = {}
            for o in streams[e]:
                if o.is_dma:
                    kk = self.dma_n[e] % self.n_dma_sems
                    self.dma_n[e] += 1
                    o.sem = (e, kk)
                    c = cnt.get(o.sem, 0) + 16
                    cnt[o.sem] = c
                    o.sigval = c
        plan = {}
        for e in self.ENGS:
            wpos = {}
            wl = []
            for o in streams[e]:
                ws = []
                for d in o.deps:
                    if d.is_dma:
                        ws.append(d)
                        continue
                    if d.eng == e and e == "pe":
                        continue
                    if d.pos > wpos.get(d.eng, -1):
                        wpos[d.eng] = d.pos
                        d.signaled = True
                        ws.append(d)
                wl.append(ws)
            plan[e] = wl
        for e in ("pe", "act", "dve", "pool"):
            for o in reversed(streams[e]):
                if not o.is_dma:
                    o.signaled = True
                    break
        for e in self.ENGS:
            for o in streams[e]:
                if (not o.is_dma) and o.signaled:
                    c = cnt.get(e, 0) + 1
                    cnt[e] = c
                    o.sem = e
                    o.sigval = c
        final = dict(cnt)
        with nc.Block() as block:
            engobj = {"pe": block.tensor, "act": block.scalar, "dve": block.vector,
                      "pool": block.gpsimd, "sp": block.sync}

            def run(e, eng):
                waited = {}
                dma_prev = {}

                def wait_sv(sem, val):
                    if waited.get(sem, 0) >= val:
                        return
                    eng.wait_ge(sems[sem], val)
                    waited[sem] = val

                for o, ws in zip(streams[e], plan[e]):
                    for d in ws:
                        wait_sv(d.sem, d.sigval)
                    if o.is_dma:
                        if o.sigval > 16:
                            wait_sv(o.sem, o.sigval - 16)
                    ins = o.fn(eng)
                    if o.is_dma:
                        ins.then_inc(sems[o.sem], 16)
                    elif o.signaled:
                        ins.then_inc(sems[o.sem], 1)
                for sem, val in final.items():
                    wait_sv(sem, val)

            for e in self.ENGS:
                def mk(e):
                    def f(eng):
                        run(e, eng)
                    return f
                engobj[e](mk(e))
        self.ops = []
        self.phase += 1


class StopBuild(Exception):
    pass


def ck(n):
    import os
    lim = float(os.environ.get("KDBG_CK", "1000"))
    if n > lim:
        raise StopBuild()


class Buf:
    def __init__(self, h, name="", excl=False):
        self.h = h
        self.t = T(name)
        self.excl = excl

    def __getitem__(self, k):
        return V(self, self.h[k])


class V:
    def __init__(self, buf, ap):
        self.buf = buf
        self.ap = ap

    def __getitem__(self, k):
        return V(self.buf, self.ap[k])

    def re(self, pat_, **kw):
        return V(self.buf, self.ap.rearrange(pat_, **kw))

    def bc(self, shape):
        return V(self.buf, self.ap.to_broadcast(list(shape)))

    def un(self, ax):
        return V(self.buf, self.ap.unsqueeze(ax))

    def bitcast(self, dt):
        return V(self.buf, self.ap.bitcast(dt))


class Chunked:
    def __init__(self, buf, n, w):
        self.bufs = [Buf(buf.h, "%s_c%d" % (buf.t.name, i)) for i in range(n)]
        self.w = w

    def c(self, i, a=0, b=None):
        b = self.w if b is None else b
        return self.bufs[i][:, i * self.w + a:i * self.w + b]

    def inherit(self, buf):
        for b in self.bufs:
            b.t.last_writer = buf.t.last_writer


class Pool:
    def __init__(self, bufs):
        self.bufs = bufs
        self.i = 0

    def next(self):
        b = self.bufs[self.i % len(self.bufs)]
        self.i += 1
        return b


def _tr(*vs):
    return [v.buf.t for v in vs if isinstance(v, V) and v.buf is not None]


def _rw(reads, writes):
    r, w = [], []
    for v in reads:
        if isinstance(v, V) and v.buf is not None:
            (w if v.buf.excl else r).append(v.buf.t)
    for v in writes:
        if isinstance(v, V) and v.buf is not None:
            w.append(v.buf.t)
    return dict(reads=r, writes=w)


def _a(x):
    return x.ap if isinstance(x, V) else x


def _fs(v):
    n = 1
    for d in v.ap.shape[1:]:
        n *= int(d)
    return n


def _is_psum(v):
    return isinstance(v, V) and v.buf is not None and v.buf.excl


_ASET = {AF.Silu: "silu", AF.Exp: "lnexp", AF.Ln: "lnexp", AF.Tanh: "silu"}


class K:
    def __init__(self, nc, S):
        self.nc = nc
        self.S = S

    def mm(self, out, lhsT, rhs, start=True, stop=True):
        n = max(32, _fs(rhs))
        c = n / 2.37 * (4.0 if rhs.ap.dtype == F32 else 1.0) + 48.0
        self.S.op("pe", lambda e: e.matmul(out.ap, lhsT=lhsT.ap, rhs=rhs.ap, start=start, stop=stop),
                  cost=c, **_rw([lhsT, rhs], [out]))

    def tp(self, out, in_, ident):
        c = max(32, _fs(ident)) / 2.37 * (4.0 if in_.ap.dtype == F32 else 1.0) + 48.0
        self.S.op("pe", lambda e: e.transpose(out.ap, in_.ap, ident.ap), cost=c, **_rw([in_, ident], [out]))

    def act(self, out, in_, func, scale=1.0, bias=0.0, accum=None):
        def f(e):
            kw = dict(out=out.ap, in_=in_.ap, func=func, scale=_a(scale), bias=_a(bias))
            if accum is not None:
                kw["accum_out"] = accum.ap
            return e.activation(**kw)
        c = 190.0 + 0.6 * _fs(in_) + (90.0 if accum is not None else 0.0)
        self.S.op("act", f, cost=c, aset=_ASET.get(func), **_rw([in_, scale, bias], [out, accum]))

    def _vc(self, eng, *vs):
        f = max(_fs(v) for v in vs if isinstance(v, V))
        ps = any(_is_psum(v) for v in vs)
        if eng == "pool":
            return 1100.0 + 0.45 * f
        return (130.0 if ps else 90.0) + 1.25 * f

    def tt(self, eng, out, a, b, op):
        self.S.op(eng, lambda e: e.tensor_tensor(out=out.ap, in0=a.ap, in1=b.ap, op=op), cost=self._vc(eng, out, a, b),
                  **_rw([a, b], [out]))

    def ts(self, eng, out, a, s1, op0, s2=None, op1=None):
        def f(e):
            if s2 is None:
                return e.tensor_scalar(out=out.ap, in0=a.ap, scalar1=_a(s1), scalar2=None, op0=op0)
            return e.tensor_scalar(out=out.ap, in0=a.ap, scalar1=_a(s1), scalar2=_a(s2), op0=op0, op1=op1)
        self.S.op(eng, f, cost=self._vc(eng, out, a), **_rw([a, s1, s2], [out]))

    def stt(self, eng, out, a, s, b, op0, op1):
        self.S.op(eng, lambda e: e.scalar_tensor_tensor(out=out.ap, in0=a.ap, scalar=_a(s), in1=b.ap, op0=op0, op1=op1),
                  cost=self._vc(eng, out, a, b), **_rw([a, s, b], [out]))

    def cp(self, eng, out, a):
        if eng == "act":
            self.S.op("act", lambda e: e.copy(out=out.ap, in_=a.ap), cost=190.0 + 0.6 * _fs(a), **_rw([a], [out]))
        else:
            c = (250.0 + 0.3 * _fs(a)) if eng == "pool" else self._vc(eng, out, a)
            self.S.op(eng, lambda e: e.tensor_copy(out=out.ap, in_=a.ap), cost=c, **_rw([a], [out]))

    def recip(self, out, a):
        self.S.op("dve", lambda e: e.reciprocal(out=out.ap, in_=a.ap), cost=self._vc("dve", out, a), **_rw([a], [out]))

    def ms(self, eng, out, val):
        self.S.op(eng, lambda e: e.memset(out.ap, val), cost=250.0 + 0.3 * _fs(out), **_rw([], [out]))

    def dma(self, q, out, in_):
        nbytes = int(out.ap.shape[0]) * _fs(out) * 4
        c = 2000.0 + nbytes / 120.0
        return self.S.op(q, lambda e: e.dma_start(out=out.ap, in_=in_.ap), dma=True, cost=c, **_rw([in_], [out]))


def build_program():
    nc = bass.Bass("TRN2", target_bir_lowering=False)

    def din(name, shape):
        return V(None, nc.dram_tensor(name, list(shape), F32, kind="ExternalInput").ap())

    def dout(name, shape):
        return Buf(nc.dram_tensor(name, list(shape), F32, kind="ExternalOutput").ap(), name)

    class RowBufs:
        def __init__(self, ap):
            self.h = ap
            self.b = {}

        def rows(self, r0):
            if r0 not in self.b:
                self.b[r0] = Buf(self.h, "rb%d" % r0)
            return V(self.b[r0], self.h[r0:r0 + 128, :])

    x_d = din("x", [NTOK, D])
    p_d = din("p", [NTOK, 256])
    stdnc_d = din("st_dnc", [48, 1536])
    stdn_d = din("st_dn", [NSAMP, 4, 128, 128])
    stret_d = din("st_ret", [NSAMP, 4, 128, 128])
    stfc_d = din("st_fc", [32, 2 * DFF])
    w_in_d = din("w_in", [D, DIN])
    w_out_d = din("w_out", [D, D])
    w_up_d = din("w_up", [D, 2 * DFF])
    w_down_d = din("w_down", [DFF, D])
    w_gate_d = din("w_gate", [D, D])
    w_ple_d = din("w_ple", [256, D])
    anw8_d = din("anw8", [128, 8])
    fnw8_d = din("fnw8", [128, 8])
    pnw8_d = din("pnw8", [128, 8])
    finw_d = din("finw", [D])
    dncw_d = din("dncw", [128, 48])
    alog_d = din("alog", [4])
    dtb_d = din("dtb", [4])
    dnw4_d = din("dnw4", [512])
    retw_d = din("retw", [512])
    fcw_d = din("fcw", [128, 132])
    fcb_d = din("fcb", [128, 44])
    ident_d = din("ident", [128, 128])
    irep_d = din("irep", [128, 512])
    cos_d = din("cosT", [NTOK, 64])
    sin_d = din("sinT", [NTOK, 64])
    cvar = {}
    for v in ("P", "S"):
        cvar[v] = dict(CM=din("CM" + v, [128, 128]), UM=din("UM" + v, [128, 128]),
                       NEGs=din("NEGs" + v, [128, 128]), NEGT=din("NEGT" + v, [128, 128]),
                       DTr=din("DTr" + v, [128, 512]), rqd=din("rqd" + v, [128, 4]), rkt=din("rkt" + v, [128, 4]))
    e1_d = din("E1", [16 * 128])
    seq2_d = din("seq2", [128, 16])

    y_o = RowBufs(nc.dram_tensor("y", [NTOK, D], F32, kind="ExternalOutput").ap())
    dncp_o = dout("o_dnc_p", [3, 1536])
    dnp_o = dout("o_dn_p", [4, 128, 128])
    retp_o = dout("o_ret_p", [4, 128, 128])
    fcp_o = dout("o_fc_p", [2, 2 * DFF])
    dncs_o = dout("o_dnc_s", [48, 1536])
    dns_o = dout("o_dn_s", [NSAMP, 4, 128, 128])
    rets_o = dout("o_ret_s", [NSAMP, 4, 128, 128])
    fcs_o = dout("o_fc_s", [32, 2 * DFF])
    import os as _os
    _dbg = _os.environ.get("KDBG_OUT", "") == "1"
    _kind = dict(kind="ExternalOutput") if _dbg else {}
    h1_s = RowBufs(nc.dram_tensor("h1_scr", [NTOK, D], F32, **_kind).ap())
    h2_s = RowBufs(nc.dram_tensor("h2_scr", [NTOK, D], F32, **_kind).ap())

    lg = [float(np.log(1.0 - 2.0 ** (-5.0 - h))) for h in range(4)]

    with contextlib.ExitStack() as top:
        S = Sched(nc)
        S.open(top)
        k = K(nc, S)

        gcnt = [0]

        def mk_alloc(st):
            cnt = gcnt

            def sb(shape, dt=F32, n=0, name="t"):
                def one():
                    cnt[0] += 1
                    nm = "%s_%d" % (name, cnt[0])
                    return Buf(st.enter_context(nc.sbuf_tensor(nm, list(shape), dt)), nm)
                if n == 0:
                    return one()
                return Pool([one() for _ in range(n)])

            def psum_pool(**roles):
                assert sum(roles.values()) <= 8
                out = {}
                for role, n in roles.items():
                    bufs = []
                    for i in range(n):
                        cnt[0] += 1
                        nm = "ps_%d" % cnt[0]
                        bufs.append(Buf(st.enter_context(nc.psum_tensor(nm, [128, 512], F32)), nm, excl=True))
                    out[role] = Pool(bufs)
                return out
            return sb, psum_pool

        def load_w(st_sb, wd, kchunks, ncols, name):
            return load_cols(st_sb, wd, kchunks, 0, ncols, name)

        def rstd_act(ss_in, n_inv, small, ncol):
            a = small.next()
            k.act(a[:, 0:ncol], ss_in, AF.Ln, scale=n_inv, bias=epsc[:, 0:1])
            r = small.next()
            k.act(r[:, 0:ncol], a[:, 0:ncol], AF.Exp, scale=-0.5)
            return r[:, 0:ncol]

        def rstd_of(ss_in, n_inv, small, mhalf, ncol):
            if USE_ACT_RSTD[0]:
                return rstd_act(ss_in, n_inv, small, ncol)
            a = small.next()
            k.ts("dve", a[:, 0:ncol], ss_in, n_inv, ALU.mult, EPS, ALU.add)
            r = small.next()
            k.tt("pool", r[:, 0:ncol], a[:, 0:ncol], mhalf[:, 0:ncol], ALU.pow)
            return r[:, 0:ncol]

        USE_ACT_RSTD = [False]
        epsc = None

        def phase_a(samp):
            USE_ACT_RSTD[0] = True
            try:
                phase_a_body(samp)
            finally:
                USE_ACT_RSTD[0] = False

        def phase_a_body(samp):
            nonlocal epsc
            with contextlib.ExitStack() as st:
                sb, psum_pool = mk_alloc(st)
                pp = psum_pool(E=3, M=2, R=2, L=1)
                cv = cvar["S" if samp else "P"]
                nst = NSAMP if samp else 1
                nlev = 3 if samp else 7
                Cc = LS if samp else 128
                cdec = [float(np.exp(Cc * lg[h])) for h in range(4)]
                nb = 1 if samp else 1
                w_inq, w_inr, w_out = WA
                identf = sb([128, 128]); k.dma("sp", identf[:], ident_d)
                identb = sb([128, 128], BF16); k.dma("pool", identb[:], ident_d)
                irep = sb([128, 512], BF16); k.dma("pool", irep[:], irep_d)
                onesf = sb([128, 128]); k.ms("pool", onesf[:], 1.0)
                nonesf = sb([128, 128]); k.ms("pool", nonesf[:], -1.0)
                mhalf = sb([128, 8]); k.ms("pool", mhalf[:], -0.5)
                epsc = sb([128, 1]); k.ms("pool", epsc[:], EPS)
                ecvp = sb([128, 256], n=2, name="ecv")
                CM = sb([128, 128]); k.dma("sp", CM[:], cv["CM"])
                UM = sb([128, 128]); k.dma("sp", UM[:], cv["UM"])
                NEGs = sb([128, 128], BF16); k.dma("pool", NEGs[:], cv["NEGs"])
                NEGT = sb([128, 128], BF16); k.dma("pool", NEGT[:], cv["NEGT"])
                DTr = sb([128, 512]); k.dma("sp", DTr[:], cv["DTr"])
                rqd = sb([128, 4]); k.dma("sp", rqd[:], cv["rqd"])
                rkt = sb([128, 4]); k.dma("sp", rkt[:], cv["rkt"])
                anw8 = sb([128, 8]); k.dma("sp", anw8[:], anw8_d)
                cw = sb([128, 48]); k.dma("sp", cw[:], dncw_d)
                alogb = sb([128, 4]); k.dma("sp", alogb[:], V(None, alog_d.ap.partition_broadcast(128)))
                dtbb = sb([128, 4]); k.dma("sp", dtbb[:], V(None, dtb_d.ap.partition_broadcast(128)))
                negA = sb([128, 4])
                k.act(negA[:], alogb[:], AF.Exp)
                k.ts("dve", negA[:], negA[:], -1.0, ALU.mult)
                dnw = sb([128, 512]); k.dma("sp", dnw[:], V(None, dnw4_d.ap.partition_broadcast(128)))
                retw = sb([128, 512]); k.dma("sp", retw[:], V(None, retw_d.ap.partition_broadcast(128)))
                if samp:
                    E1 = sb([128, 16 * 128], BF16)
                    k.dma("pool", E1[:], V(None, e1_d.ap.partition_broadcast(128)))
                    seq2 = sb([128, 16]); k.dma("sp", seq2[:], seq2_d)
                NT = 128 if samp else 256
                nbuf = 1 if samp else 2
                xtp = sb([128, D], n=1, name="xt")
                xrp = None if samp else sb([128, D], n=1, name="xr")
                abfp = sb([128, D], BF16, n=1, name="abf")
                aTp = sb([128, 8 * NT], BF16, n=1, name="aT")
                qkvT = sb([128, 12 * NT], BF16, name="qkvT")
                xprep = sb([128, 264], n=1 if samp else 2, name="xpre")
                ycvp = sb([128, 256], n=1 if samp else 2, name="ycv")
                cx_all = sb([128, 12 * 48], name="cx"); cx = Chunked(cx_all, 12, 48)
                junk = sb([128, D], BF16, name="junk"); junk = V(None, junk.h[:])
                small = sb([128, 24], n=24 if samp else 32, name="small")
                zsp = sb([128, 512], BF16, n=1 if samp else 2, name="zs")
                rqkf = sb([128, 1024], name="rqkf")
                qkf = sb([128, 1024], BF16, name="qkf")
                rgsp = sb([128, 512], BF16, n=1 if samp else 2, name="rgs")
                tmpp = sb([128, 512], n=2, name="tmp")
                ebf = sb([128, 512], BF16, n=4, name="ebf")
                e2r = sb([128, 512], BF16, n=3 if samp else PE2R, name="e2r")
                rbf = sb([128, 512], BF16, n=2 if samp else 4, name="rbf")
                rbf2 = sb([128, 512], BF16, n=5 if samp else PRBF2, name="rbf2")
                kqT = sb([128, 1024], BF16, n=2 if samp else PKQT, name="kqT")
                rtk = sb([128, 1024], BF16, n=2 if samp else PRTK, name="rtk")
                MG = sb([128, 512], name="MG")
                Xp = sb([128, 512], BF16, n=2, name="X")
                XTp = sb([128, 512], BF16, n=2, name="XT")
                IXp = sb([128, 512], BF16, n=2, name="IX")
                PTp = sb([128, 512], BF16, n=2 if samp else 3, name="PT")
                dexp = sb([128, 512], BF16, n=3, name="dexp")
                ofp = sb([128, 512], n=POF, name="of")
                mix = sb([128, D], BF16, name="mix")
                mixT = sb([128, D], BF16, name="mixT")
                h1p = None if samp else sb([128, D], n=1, name="h1")
                csp = sb([128, 128], n=nbuf, name="cs")
                if samp:
                    sbigp = sb([128, 16 * 128], n=2, name="sbig")
                    sbigbp = sb([128, 16 * 128], BF16, n=2, name="sbigb")
                    expp = sb([128, 16 * 128], BF16, n=2, name="exp")
                    dncT = sb([128, 12 * 48], name="dncT")
                else:
                    Sdn = sb([128, 512], name="Sdn"); k.ms("pool", Sdn[:], 0.0)
                    Sdnb = sb([128, 512], BF16, name="Sdnb"); k.ms("pool", Sdnb[:], 0.0)
                    Srt = sb([128, 512], name="Srt"); k.ms("pool", Srt[:], 0.0)
                    Srtb = sb([128, 512], BF16, name="Srtb"); k.ms("pool", Srtb[:], 0.0)
                    k.ms("pool", cx_all[:], 0.0)
                    cx.inherit(cx_all)

                if samp:
                    blocks = [(SEQ, 128, NSAMP, LS)]
                else:
                    blocks = [(b * 256, 256, 1, 256) for b in range(SEQ // 256)]

                hsl = lambda h: slice(h * 128, (h + 1) * 128)
                b3 = lambda v: v.un(2).bc([128, 4, 128])
                r3 = lambda v: v.re("p (h d) -> p h d", h=4)

                if samp:
                    for gI in range(3):
                        tmp = tmpp.next()
                        k.dma("sp", tmp[0:48, :], stdnc_d[:, gI * 512:(gI + 1) * 512])
                        ps = pp["M"].next()
                        for i in range(4):
                            k.tp(ps[:, i * 48:(i + 1) * 48], tmp[0:48, i * 128:(i + 1) * 128], identf[0:48, 0:48])
                        k.cp("act", dncT[:, gI * 192:(gI + 1) * 192], ps[:, 0:192])

                last_xt = [None]

                def stage_a0(t0, NT, aT):
                    nsub = NT // 128
                    for j in range(nsub):
                        xt = xtp.next()
                        last_xt[0] = xt
                        k.dma("sp", xt[:], x_d[t0 + 128 * j:t0 + 128 * (j + 1), :])
                        ss = small.next()
                        k.act(junk, xt[:], AF.Square, accum=ss[:, 0:1])
                        rs = rstd_of(ss[:, 0:1], 1.0 / D, small, mhalf, 1)
                        abf = abfp.next()
                        k.ts("dve", abf[:], xt[:], rs, ALU.mult)
                        ps = pp["M"].next()
                        pb = ps[:].bitcast(BF16)
                        for kk in range(8):
                            k.tp(pb[:, hsl(kk)], abf[:, hsl(kk)], identb[:])
                        k.tt("dve", aT[:].re("p (k t) -> p k t", k=8)[:, :, 128 * j:128 * (j + 1)],
                             pb.re("p (k t) -> p k t", k=8), anw8[:].un(2).bc([128, 8, 128]), ALU.mult)

                try:
                  ck(0)
                  for bi, (t0, NT, nseq, L) in enumerate(blocks):
                    nsub = NT // 128
                    aT = aTp.next()
                    stage_a0(t0, NT, aT)
                    ck(1)
                    aT3 = aT[:].re("p (k t) -> p k t", k=8)
                    qkvT3 = qkvT[:].re("p (c t) -> p c t", c=12)
                    for fc in range(12):
                        ps = pp["M"].next()
                        for kk in range(8):
                            k.mm(ps[:, 0:NT], w_inq[kk][:, fc * 128:(fc + 1) * 128], aT3[:, kk, :], start=kk == 0, stop=kk == 7)
                        ck(1.1)
                        xp = xprep.next()
                        xv = xp[:, 0:nseq * (3 + L)].re("p (s l) -> p s l", s=nseq)
                        psv = ps[:, 0:NT].re("p (s l) -> p s l", s=nseq)
                        k.cp("act", xv[:, :, 3:3 + L], psv)
                        ck(1.2)
                        if samp:
                            k.cp("pool", xv[:, :, 0:3], dncT[:, fc * 48:(fc + 1) * 48].re("p (s j) -> p s j", s=nseq))
                        else:
                            k.cp("pool", xv[:, :, 0:3], cx.c(fc, 0, 3).un(1))
                        k.cp("pool", cx.c(fc, 0, 3 * nseq).re("p (s j) -> p s j", s=nseq), xv[:, :, L:L + 3])
                        ck(1.3)
                        y = ycvp.next()
                        yv = y[:, 0:NT].re("p (s l) -> p s l", s=nseq)
                        k.act(yv, psv, AF.Copy, scale=cw[:, fc * 4 + 3:fc * 4 + 4])
                        ck(1.4)
                        k.stt("dve", yv, xv[:, :, 2:2 + L], cw[:, fc * 4 + 2:fc * 4 + 3], yv, ALU.mult, ALU.add)
                        k.stt("dve", yv, xv[:, :, 1:1 + L], cw[:, fc * 4 + 1:fc * 4 + 2], yv, ALU.mult, ALU.add)
                        k.stt("dve", yv, xv[:, :, 0:L], cw[:, fc * 4 + 0:fc * 4 + 1], yv, ALU.mult, ALU.add)
                        ck(1.5)
                        ecv = ecvp.next()
                        k.act(ecv[:, 0:NT], y[:, 0:NT], AF.Exp, scale=-1.0)
                        k.act(ecv[:, 0:NT], ecv[:, 0:NT], AF.Ln, bias=1.0)
                        k.act(ecv[:, 0:NT], ecv[:, 0:NT], AF.Exp, scale=-1.0)
                        k.tt("dve", qkvT3[:, fc, :], y[:, 0:NT], ecv[:, 0:NT], ALU.mult)
                        ck(1.6)

                    ck(2)
                    for j in range(nsub):
                        js = slice(128 * j, 128 * (j + 1))
                        r0 = t0 + 128 * j
                        if samp:
                            xr = last_xt[0]
                        else:
                            xr = xrp.next()
                            k.dma("sp", xr[:], x_d[r0:r0 + 128, :])
                        cs = csp.next()
                        k.dma("sp", cs[:, 0:64], cos_d[r0:r0 + 128, :])
                        k.dma("sp", cs[:, 64:128], sin_d[r0:r0 + 128, :])

                        def proj(c0, n, role="M"):
                            ps = pp[role].next()
                            for kk in range(8):
                                k.mm(ps[:, 0:n], aT3[:, kk, js], w_inr[kk][:, c0 - 1536:c0 - 1536 + n], start=kk == 0, stop=kk == 7)
                            return ps

                        ck(2.1)
                        psqk = pp["E"].next(); pqk = psqk[:].bitcast(BF16)
                        for i in range(8):
                            k.tp(pqk[:, hsl(i)], qkvT3[:, i, js], identb[:])
                        psv_ = pp["E"].next(); pv = psv_[:].bitcast(BF16)
                        for h in range(4):
                            k.tp(pv[:, hsl(h)], qkvT3[:, 8 + h, js], identb[:])
                        ck(2.2)
                        k.cp("act", qkf[:], pqk)
                        ck(2.3)
                        st = small.next()
                        for i in range(8):
                            k.act(junk[:, 0:128], qkf[:, hsl(i)], AF.Square, accum=st[:, i:i + 1])
                        ck(2.4)
                        rs = rstd_of(st[:, 0:8], 1.0, small, mhalf, 8)
                        ck(3)
                        psba = proj(C_B, 8, "E")
                        sm = small.next()
                        k.act(sm[:, 0:4], psba[:, 0:4], AF.Exp, scale=-1.0)
                        k.ts("dve", sm[:, 0:4], sm[:, 0:4], 1.0, ALU.add)
                        beta_t = small.next(); beta = beta_t[:, 0:4]
                        k.recip(beta, sm[:, 0:4])
                        k.tt("dve", sm[:, 4:8], psba[:, 4:8], dtbb[:], ALU.add)
                        k.act(sm[:, 8:12], sm[:, 4:8], AF.Exp)
                        k.act(sm[:, 12:16], sm[:, 8:12], AF.Ln, bias=1.0)
                        g_t = small.next(); g = g_t[:, 0:4]
                        k.tt("dve", g, sm[:, 12:16], negA[:], ALU.mult)
                        ck(4)
                        psg = pp["E"].next()
                        k.mm(psg[:, 0:4], CM[:], g)
                        k.mm(psg[:, 4:8], UM[:], g)
                        if samp:
                            Gs = small.next() if False else tmpp.next()
                            k.tt("pool", Gs[:, 0:64].re("p (h s) -> p h s", h=4), g.un(2).bc([128, 4, 16]),
                                 seq2[:].un(1).bc([128, 4, 16]), ALU.mult)
                            k.mm(psg[:, 8:72], onesf[:], Gs[:, 0:64])
                        else:
                            k.mm(psg[:, 8:12], onesf[:], g)
                        nex = 8 + 4 * nst
                        ex_t = sb_ex.next()
                        ex = ex_t[:, 0:nex]
                        k.act(ex, psg[:, 0:nex], AF.Exp)
                        sc_t = small.next(); sc = sc_t
                        k.tt("dve", sc[:, 0:4], rs[:, 4:8], beta, ALU.mult)
                        k.tt("dve", sc[:, 4:8], rs[:, 4:8], ex[:, 4:8], ALU.mult)
                        k.ts("dve", sc[:, 8:12], rs[:, 0:4], 128.0 ** -0.5, ALU.mult)
                        k.tt("dve", sc[:, 12:16], sc[:, 8:12], ex[:, 0:4], ALU.mult)
                        k.stt("dve", sc[:, 16:20], beta, -1.0, ex[:, 0:4], ALU.mult, ALU.mult)
                        kf = r3(qkf[:, 512:1024]); qf = r3(qkf[:, 0:512])
                        Kn = ebf.next(); KB = ebf.next(); Qs = ebf.next(); Qd = ebf.next(); Kt = e2r.next(); Vb = e2r.next()
                        k.tt("pool", r3(Kn[:]), kf, b3(rs[:, 4:8]), ALU.mult)
                        k.tt("dve", r3(KB[:]), kf, b3(sc[:, 0:4]), ALU.mult)
                        k.tt("pool", r3(Kt[:]), kf, b3(sc[:, 4:8]), ALU.mult)
                        k.tt("dve", r3(Qs[:]), qf, b3(sc[:, 8:12]), ALU.mult)
                        k.tt("pool", r3(Qd[:]), qf, b3(sc[:, 12:16]), ALU.mult)
                        k.tt("dve", r3(Vb[:]), r3(pv[:, 0:512]), b3(beta), ALU.mult)
                        ck(5)
                        KKT = kqT.next(); QQT = kqT.next()
                        psA = pp["E"].next(); pA = psA[:].bitcast(BF16)
                        for h in range(4):
                            k.tp(pA[:, hsl(h)], Kn[:, hsl(h)], identb[:])
                        for h in range(4):
                            k.tp(pA[:, hsl(4 + h)], KB[:, hsl(h)], identb[:])
                        k.cp("dve" if "h" in BAL else "act", KKT[:], pA)
                        psB = pp["E"].next(); pB = psB[:].bitcast(BF16)
                        for h in range(4):
                            k.tp(pB[:, hsl(h)], Qs[:, hsl(h)], identb[:])
                        for h in range(4):
                            k.tp(pB[:, hsl(4 + h)], Qd[:, hsl(h)], identb[:])
                        k.cp("dve", QQT[:], pB)
                        KnT = lambda h: KKT[:, hsl(h)]
                        KBT = lambda h: KKT[:, hsl(4 + h)]
                        QsT = lambda h: QQT[:, hsl(h)]
                        QdT = lambda h: QQT[:, hsl(4 + h)]
                        ck(6)
                        k.tt("pool", r3(MG[:]), CM[:].un(1).bc([128, 4, 128]), g.un(2).bc([128, 4, 128]), ALU.mult)
                        psD = pp["E"].next()
                        for h in range(4):
                            k.mm(psD[:, hsl(h)], MG[:, hsl(h)], onesf[:], True, False)
                            k.mm(psD[:, hsl(h)], nonesf[:], MG[:, hsl(h)], False, False)
                            k.mm(psD[:, hsl(h)], identb[:], NEGs[:], False, True)
                        Ds = dexp.next(); DT = dexp.next(); DTs = dexp.next()
                        k.act(Ds[:], psD[:], AF.Exp)
                        psDT = pp["E"].next(); pDT = psDT[:].bitcast(BF16)
                        for h in range(4):
                            k.tp(pDT[:, hsl(h)], Ds[:, hsl(h)], identb[:])
                        k.cp("act", DTs[:], pDT[:, 0:512])
                        k.tt("dve", DT[:], pDT[:, 0:512], irep[:], ALU.add)
                        ck(7)
                        psA_ = pp["E"].next(); psAT = pp["E"].next(); psKQ = pp["E"].next()
                        for h in range(4):
                            k.mm(psA_[:, hsl(h)], KBT(h), KnT(h))
                        for h in range(4):
                            k.mm(psAT[:, hsl(h)], KnT(h), KBT(h))
                        for h in range(4):
                            k.mm(psKQ[:, hsl(h)], KnT(h), QsT(h))
                        X = Xp.next(); XT = XTp.next(); PT = PTp.next(); QKDT = e2r.next()
                        k.stt("dve", X[:], psA_[:], -1.0, Ds[:], ALU.mult, ALU.mult)
                        k.stt("dve", XT[:], psAT[:], -1.0, DTs[:], ALU.mult, ALU.mult)
                        k.tt("dve", QKDT[:], psKQ[:], DT[:], ALU.mult)
                        k.tt("pool", PT[:], XT[:], irep[:], ALU.add)
                        ck(8)
                        for lv in range(1, nlev):
                            psX = pp["E"].next()
                            for h in range(4):
                                k.mm(psX[:, hsl(h)], XT[:, hsl(h)], X[:, hsl(h)])
                            last = lv == nlev - 1
                            IX = IXp.next()
                            k.tt("dve", IX[:], psX[:], irep[:], ALU.add)
                            if not last:
                                Xn = Xp.next()
                                k.cp("dve" if "e" in BAL else "act", Xn[:], psX[:])
                                psXT = pp["E"].next()
                                for h in range(4):
                                    k.mm(psXT[:, hsl(h)], X[:, hsl(h)], XT[:, hsl(h)])
                                XTn = XTp.next()
                                k.cp("dve" if "f" in BAL else "act", XTn[:], psXT[:])
                            psP = pp["E"].next()
                            for h in range(4):
                                k.mm(psP[:, hsl(h)], IX[:, hsl(h)], PT[:, hsl(h)])
                            PTn = PTp.next()
                            k.cp("dve" if "g" in BAL else "act", PTn[:], psP[:])
                            PT = PTn
                            if not last:
                                X, XT = Xn, XTn
                        ck(9)
                        zs = zsp.next(); rgs = rgsp.next()
                        psz = proj(C_Z, 512)
                        sgt = tmpp.next()
                        k.act(sgt[:], psz[:], AF.Exp, scale=-1.0)
                        k.act(sgt[:], sgt[:], AF.Ln, bias=1.0)
                        k.act(sgt[:], sgt[:], AF.Exp, scale=-1.0)
                        k.tt("dve", zs[:], psz[:], sgt[:], ALU.mult)
                        psrq = proj(C_RQ, 512)
                        k.cp("act", rqkf[:, 0:512], psrq[:])
                        psrk = proj(C_RK, 512)
                        k.cp("act", rqkf[:, 512:1024], psrk[:])
                        RV = rbf2.next()
                        psrv = proj(C_RV, 512)
                        k.cp("act", RV[:], psrv[:])
                        psrg = proj(C_RG, 512)
                        sgt = tmpp.next()
                        k.act(sgt[:], psrg[:], AF.Exp, scale=-1.0)
                        k.act(sgt[:], sgt[:], AF.Ln, bias=1.0)
                        k.act(sgt[:], sgt[:], AF.Exp, scale=-1.0)
                        k.tt("dve", rgs[:], psrg[:], sgt[:], ALU.mult)

                        ck(10)
                        of = ofp.next()
                        R = rbf.next(); vn = rbf.next()
                        headsets = [[h] for h in range(4)] if samp else [[0, 1, 2, 3]]
                        for hs in headsets:
                            cols = slice(hs[0] * 128, (hs[-1] + 1) * 128)
                            if samp:
                                h = hs[0]
                                sbig = sbigp.next(); sbigb = sbigbp.next()
                                for _q in range(4):
                                    k.dma("sp", sbig[:, _q * 512:(_q + 1) * 512].re("p (s v) -> p s v", s=4), V(None, stdn_d.ap[4 * _q:4 * _q + 4, h].rearrange("s k v -> k s v")))
                                k.cp("pool", sbigb[:], sbig[:])
                                KnE = expp.next(); QdE = expp.next()
                                e3 = lambda v: v.re("p (s c) -> p s c", s=16)
                                k.tt("pool", e3(KnE[:]), KnT(h).un(1).bc([128, 16, 128]), e3(E1[:]), ALU.mult)
                                k.tt("dve", e3(QdE[:]), QdT(h).un(1).bc([128, 16, 128]), e3(E1[:]), ALU.mult)
                                Sf = lambda hh, s, sbig=sbig: sbig[:, hsl(s)]
                                Sb = lambda hh, s, sbigb=sbigb: sbigb[:, hsl(s)]
                                lK = lambda hh, s: KnE[:, hsl(s)]
                                lQ = lambda hh, s: QdE[:, hsl(s)]
                            else:
                                Sf = lambda hh, s: Sdn[:, hsl(hh)]
                                Sb = lambda hh, s: Sdnb[:, hsl(hh)]
                                lK = lambda hh, s: KnT(hh)
                                lQ = lambda hh, s: QdT(hh)
                                lT = lambda hh, s: Kt[:, hsl(hh)]
                            psKS = pp["R"].next()
                            for h in hs:
                                for s in range(nst):
                                    k.mm(psKS[:, hsl(h)], lK(h, s), Sb(h, s), s == 0, s == nst - 1)
                            if samp:
                                KtE = expp.next()
                                k.tt("pool", e3(KtE[:]), Kt[:, hsl(hs[0])].un(1).bc([128, 16, 128]), seq2[:].un(2).bc([128, 16, 128]), ALU.mult)
                                lT = lambda hh, s: KtE[:, hsl(s)]
                            for h in hs:
                                k.stt("dve", R[:, hsl(h)], psKS[:, hsl(h)], sc[:, 16 + h:17 + h], Vb[:, hsl(h)], ALU.mult, ALU.add)
                            psV = pp["R"].next()
                            for h in hs:
                                k.mm(psV[:, hsl(h)], PT[:, hsl(h)], R[:, hsl(h)])
                            k.cp("act", vn[:, cols], psV[:, cols])
                            psO = pp["R"].next()
                            for h in hs:
                                for s in range(nst):
                                    k.mm(psO[:, hsl(h)], lQ(h, s), Sb(h, s), s == 0, False)
                                k.mm(psO[:, hsl(h)], QKDT[:, hsl(h)], vn[:, hsl(h)], False, True)
                            k.cp("act", of[:, cols], psO[:, cols])
                            pairs = [(h, s) for h in hs for s in range(nst)]
                            for g0 in range(0, len(pairs), 4):
                                grp = pairs[g0:g0 + 4]
                                psS = pp["R"].next()
                                for i, (h, s) in enumerate(grp):
                                    k.mm(psS[:, hsl(i)], lT(h, s), vn[:, hsl(h)])
                                for i, (h, s) in enumerate(grp):
                                    k.stt("dve", Sf(h, s), Sf(h, s), ex[:, 8 + h * nst + s:9 + h * nst + s], psS[:, hsl(i)], ALU.mult, ALU.add)
                            if samp:
                                for _q in range(4):
                                    k.dma("sp", V(Buf(dns_o.h, "dns%d_%d" % (hs[0], _q)), dns_o.h[4 * _q:4 * _q + 4, hs[0]].rearrange("s k v -> k s v")), sbig[:, _q * 512:(_q + 1) * 512].re("p (s v) -> p s v", s=4))
                            else:
                                k.cp("act", Sdnb[:], Sdn[:])
                        ck(11)
                        st = small.next()
                        for h in range(4):
                            k.act(junk[:, 0:128], of[:, hsl(h)], AF.Square, accum=st[:, h:h + 1])
                        rso = rstd_of(st[:, 0:4], 1.0 / 128, small, mhalf, 4)
                        k.tt("pool" if "a" in BAL else "dve", r3(of[:]), r3(of[:]), b3(rso), ALU.mult)
                        k.tt("pool", of[:], of[:], dnw[:], ALU.mult)
                        k.tt("pool" if "b" in BAL else "dve", mix[:, 0:512], of[:], zs[:], ALU.mult)

                        ck(12)
                        rqkb = rtk.next()
                        g4 = lambda v: v.re("p (g i two) -> p g i two", g=8, two=2)
                        x1 = g4(rqkf[:])[:, :, :, 0]; x2 = g4(rqkf[:])[:, :, :, 1]
                        o1 = g4(rqkb[:])[:, :, :, 0]; o2 = g4(rqkb[:])[:, :, :, 1]
                        cosb = cs[:, 0:64].un(1).bc([128, 8, 64]); sinb = cs[:, 64:128].un(1).bc([128, 8, 64])
                        t8 = lambda v: v.re("p (g i) -> p g i", g=8)
                        ta = tmpp.next(); tb = tmpp.next()
                        k.tt("dve", t8(ta[:]), x1, cosb, ALU.mult)
                        k.tt("pool", t8(tb[:]), x2, sinb, ALU.mult)
                        k.tt("dve", o1, t8(ta[:]), t8(tb[:]), ALU.subtract)
                        ta = tmpp.next(); tb = tmpp.next()
                        k.tt("pool", t8(ta[:]), x1, sinb, ALU.mult)
                        k.tt("dve", t8(tb[:]), x2, cosb, ALU.mult)
                        k.tt("pool", o2, t8(ta[:]), t8(tb[:]), ALU.add)
                        RQd = rbf2.next(); RKt = rbf2.next()
                        k.tt("dve", r3(RQd[:]), r3(rqkb[:, 0:512]), b3(rqd[:]), ALU.mult)
                        k.tt("pool", r3(RKt[:]), r3(rqkb[:, 512:1024]), b3(rkt[:]), ALU.mult)
                        RQKT = rtk.next(); RQdT_t = rbf2.next()
                        psR1 = pp["M"].next(); pR1 = psR1[:].bitcast(BF16)
                        for i in range(8):
                            k.tp(pR1[:, hsl(i)], rqkb[:, hsl(i)], identb[:])
                        k.cp("act", RQKT[:], pR1)
                        psR2 = pp["M"].next(); pR2 = psR2[:].bitcast(BF16)
                        for h in range(4):
                            k.tp(pR2[:, hsl(h)], RQd[:, hsl(h)], identb[:])
                        k.cp("dve", RQdT_t[:], pR2[:, 0:512])
                        psKQr = pp["M"].next()
                        for h in range(4):
                            k.mm(psKQr[:, hsl(h)], RQKT[:, hsl(4 + h)], RQKT[:, hsl(h)])
                        QKDTr = rbf2.next()
                        k.tt("dve", QKDTr[:], psKQr[:], DTr[:], ALU.mult)
                        orf = ofp.next()
                        for hs in headsets:
                            cols = slice(hs[0] * 128, (hs[-1] + 1) * 128)
                            if samp:
                                h = hs[0]
                                sbig = sbigp.next(); sbigb = sbigbp.next()
                                for _q in range(4):
                                    k.dma("sp", sbig[:, _q * 512:(_q + 1) * 512].re("p (s v) -> p s v", s=4), V(None, stret_d.ap[4 * _q:4 * _q + 4, h].rearrange("s k v -> k s v")))
                                k.cp("pool", sbigb[:], sbig[:])
                                QdE = expp.next(); KtE = expp.next()
                                e3 = lambda v: v.re("p (s c) -> p s c", s=16)
                                k.tt("dve", e3(QdE[:]), RQdT_t[:, hsl(h)].un(1).bc([128, 16, 128]), e3(E1[:]), ALU.mult)
                                k.tt("pool", e3(KtE[:]), RKt[:, hsl(h)].un(1).bc([128, 16, 128]), seq2[:].un(2).bc([128, 16, 128]), ALU.mult)
                                Sf = lambda hh, s, sbig=sbig: sbig[:, hsl(s)]
                                Sb = lambda hh, s, sbigb=sbigb: sbigb[:, hsl(s)]
                                lQ = lambda hh, s: QdE[:, hsl(s)]
                                lT = lambda hh, s: KtE[:, hsl(s)]
                            else:
                                Sf = lambda hh, s: Srt[:, hsl(hh)]
                                Sb = lambda hh, s: Srtb[:, hsl(hh)]
                                lQ = lambda hh, s: RQdT_t[:, hsl(hh)]
                                lT = lambda hh, s: RKt[:, hsl(hh)]
                            psO = pp["R"].next()
                            for h in hs:
                                for s in range(nst):
                                    k.mm(psO[:, hsl(h)], lQ(h, s), Sb(h, s), s == 0, False)
                                k.mm(psO[:, hsl(h)], QKDTr[:, hsl(h)], RV[:, hsl(h)], False, True)
                            k.cp("act", orf[:, cols], psO[:, cols])
                            pairs = [(h, s) for h in hs for s in range(nst)]
                            for g0 in range(0, len(pairs), 4):
                                grp = pairs[g0:g0 + 4]
                                psS = pp["R"].next()
                                for i, (h, s) in enumerate(grp):
                                    k.mm(psS[:, hsl(i)], lT(h, s), RV[:, hsl(h)])
                                for i, (h, s) in enumerate(grp):
                                    k.stt("dve", Sf(h, s), Sf(h, s), cdec[h], psS[:, hsl(i)], ALU.mult, ALU.add)
                            if samp:
                                for _q in range(4):
                                    k.dma("sp", V(Buf(rets_o.h, "rets%d_%d" % (hs[0], _q)), rets_o.h[4 * _q:4 * _q + 4, hs[0]].rearrange("s k v -> k s v")), sbig[:, _q * 512:(_q + 1) * 512].re("p (s v) -> p s v", s=4))
                            else:
                                k.cp("act", Srtb[:], Srt[:])
                        ck(13)
                        st = small.next()
                        for h in range(4):
                            k.act(junk[:, 0:128], orf[:, hsl(h)], AF.Copy, accum=st[:, h:h + 1])
                        for h in range(4):
                            k.act(junk[:, 128:256], orf[:, hsl(h)], AF.Square, accum=st[:, 4 + h:5 + h])
                        s2 = small.next()
                        k.ts("dve", s2[:, 0:4], st[:, 0:4], 1.0 / 128, ALU.mult)
                        k.tt("dve", s2[:, 4:8], s2[:, 0:4], s2[:, 0:4], ALU.mult)
                        k.stt("dve", s2[:, 8:12], st[:, 4:8], 1.0 / 128, s2[:, 4:8], ALU.mult, ALU.subtract)
                        rsr = rstd_of(s2[:, 8:12], 1.0, small, mhalf, 4)
                        k.tt("pool" if "d" in BAL else "dve", r3(orf[:]), r3(orf[:]), b3(s2[:, 0:4]), ALU.subtract)
                        k.tt("pool", r3(orf[:]), r3(orf[:]), b3(rsr), ALU.mult)
                        k.tt("pool" if "c" in BAL else "dve", orf[:], orf[:], retw[:], ALU.mult)
                        k.tt("pool", mix[:, 512:1024], orf[:], rgs[:], ALU.mult)
                        ck(14)
                        psM = pp["L"].next(); pM = psM[:].bitcast(BF16)
                        for kk in range(8):
                            k.tp(pM[:, hsl(kk)], mix[:, hsl(kk)], identb[:])
                        k.cp("act", mixT[:], pM)
                        h1 = xr if samp else h1p.next()
                        for half in range(2):
                            psH = pp["L"].next()
                            for kk in range(8):
                                k.mm(psH[:], mixT[:, hsl(kk)], w_out[kk][:, half * 512:(half + 1) * 512], kk == 0, kk == 7)
                            k.tt("dve", h1[:, half * 512:(half + 1) * 512], psH[:], xr[:, half * 512:(half + 1) * 512], ALU.add)
                        k.dma("sp", h1_s.rows(r0), h1[:])

                except StopBuild:
                    pass
                n3 = 3 * (NSAMP if samp else 1)
                dnc_o = dncs_o if samp else dncp_o
                for gI in range(3):
                    ps = pp["L"].next()
                    for i in range(4):
                        fc = gI * 4 + i
                        k.tp(ps[0:n3, hsl(i)], cx.c(fc, 0, n3), identf[:])
                    tmp = tmpp.next()
                    k.cp("act", tmp[0:n3, :], ps[0:n3, :])
                    k.dma("sp", V(Buf(dnc_o.h, "dnc%d" % gI), dnc_o.h[:, gI * 512:(gI + 1) * 512]), tmp[0:n3, :])
                if not samp:
                    k.dma("sp", V(dnp_o, dnp_o.h.rearrange("h k v -> k h v")), Sdn[:].re("p (h v) -> p h v", h=4))
                    k.dma("sp", V(retp_o, retp_o.h.rearrange("h k v -> k h v")), Srt[:].re("p (h v) -> p h v", h=4))
                S.emit_phase()

        sb_ex = None

        WA = None

        def phase_a_wrap(samp):
            nonlocal sb_ex
            with contextlib.ExitStack() as st0:
                sb0, _ = mk_alloc(st0)
                sb_ex = sb0([128, 72], n=3, name="ex")
                phase_a(samp)

        class Sub:
            def __init__(self, buf, off, w):
                self.buf, self.off, self.w = buf, off, w

            def __getitem__(self, key):
                sl = key[1]
                a0 = 0 if sl.start is None else sl.start
                a1 = self.w if sl.stop is None else sl.stop
                return V(self.buf, self.buf.h[:, self.off + a0:self.off + a1])

        def load_cols(st_sb, wd, kchunks, c0, c1, name, kmax=11):
            out = []
            W = c1 - c0
            k0 = 0
            while k0 < kchunks:
                nk = min(kmax, kchunks - k0)
                b = st_sb([128, nk * W], BF16, name=name)
                src = V(None, wd.ap[k0 * 128:(k0 + nk) * 128, c0:c1].rearrange("(k p) c -> p k c", p=128))
                k.dma("pool", b[:].re("p (k c) -> p k c", k=nk), src)
                for j in range(nk):
                    out.append(Sub(b, j * W, W))
                k0 += nk
            return out

        import os
        _ph = os.environ.get("KDBG_PH", "ABCD")
        with contextlib.ExitStack() as stw:
            sbw, _ = mk_alloc(stw)
            if "A" in _ph or "B" in _ph:
                WA = (load_cols(sbw, w_in_d, 8, 0, 1536, "w_inq"), load_cols(sbw, w_in_d, 8, 1536, DIN, "w_inr"),
                      load_cols(sbw, w_out_d, 8, 0, D, "w_out"))
            if "A" in _ph:
                phase_a_wrap(False)
            if "B" in _ph:
                phase_a_wrap(True)

        def phase_b():
            with contextlib.ExitStack() as st:
                sb, psum_pool = mk_alloc(st)
                pp = psum_pool(tr=2, up=4, down=2)
                GP = [(0, 6), (6, 12), (12, 17), (17, 22)]
                w_up_g = []
                for (p0, p1) in GP:
                    ug = load_cols(sb, w_up_d, 8, p0 * 128, p1 * 128, "w_upg")
                    uv = load_cols(sb, w_up_d, 8, DFF + p0 * 128, DFF + p1 * 128, "w_upv")
                    w_up_g.append((p0, p1, ug, uv))

                def w_up_sl(kk, fc):
                    part, i = (0, fc) if fc < 22 else (1, fc - 22)
                    for (p0, p1, ug, uv) in w_up_g:
                        if p0 <= i < p1:
                            return (ug, uv)[part][kk][:, (i - p0) * 128:(i - p0 + 1) * 128]
                w_down = load_w(sb, w_down_d, 22, D, "w_down")
                identf = sb([128, 128]); k.dma("sp", identf[:], ident_d)
                identb = sb([128, 128], BF16); k.dma("pool", identb[:], ident_d)
                mhalf = sb([128, 8]); k.ms("pool", mhalf[:], -0.5)
                fnw8 = sb([128, 8]); k.dma("sp", fnw8[:], fnw8_d)
                fcw = sb([128, 132]); k.dma("sp", fcw[:], fcw_d)
                fcb = sb([128, 44]); k.dma("sp", fcb[:], fcb_d)
                NTm = 256
                import os as _o
                htp = sb([128, D], n=int(_o.environ.get("KB_HT", "4")), name="ht")
                mbfp = sb([128, D], BF16, n=2, name="mbf")
                import os as _o
                mTp = sb([128, 8 * NTm], BF16, n=int(_o.environ.get("KB_MT", "1")), name="mT")
                actTp = sb([128, 22 * NTm], BF16, n=int(_o.environ.get("KB_ACTT", "1")), name="actT")
                uprep = sb([128, 264], n=int(_o.environ.get("KB_UP", "4")), name="upre")
                ycp = sb([128, 256], n=int(_o.environ.get("KB_YC", "4")), name="yc")
                sgp = sb([128, 256], n=int(_o.environ.get("KB_SG", "2")), name="sg")
                cf_all = sb([128, 44 * 32], name="cf"); k.ms("pool", cf_all[:], 0.0); cf = Chunked(cf_all, 44, 32); cf.inherit(cf_all)
                fcT = sb([128, 44 * 32], name="fcT")
                junk = sb([128, D], BF16, name="junk"); junk = V(None, junk.h[:])
                small = sb([128, 8], n=8, name="small")
                h2p = sb([128, D], n=int(_o.environ.get("KB_H2", "2")), name="h2")
                tmpp = sb([128, 512], n=int(_o.environ.get("KB_TMP", "2")), name="tmp")
                hsl = lambda h: slice(h * 128, (h + 1) * 128)

                for gI in range(11):
                    tmp = tmpp.next()
                    k.dma("sp", tmp[0:32, :], stfc_d[:, gI * 512:(gI + 1) * 512])
                    ps = pp["tr"].next()
                    for i in range(4):
                        k.tp(ps[:, i * 32:(i + 1) * 32], tmp[0:32, hsl(i)], identf[0:32, 0:32])
                    k.cp("act", fcT[:, gI * 128:(gI + 1) * 128], ps[:, 0:128])

                blocks = [(b * 256, 256, 1, 256, False) for b in range(SEQ // 256)] + [(SEQ, 128, NSAMP, LS, True)]
                for (t0, NT, nseq, L, samp) in blocks:
                    nsub = NT // 128
                    mT = mTp.next(); actT = actTp.next()
                    mT3 = mT[:].re("p (k t) -> p k t", k=8)
                    hts = []
                    for j in range(nsub):
                        r0 = t0 + 128 * j
                        ht = htp.next(); hts.append(ht)
                        k.dma("sp", ht[:], h1_s.rows(r0))
                        ss = small.next()
                        k.act(junk, ht[:], AF.Square, accum=ss[:, 0:1])
                        rs = rstd_of(ss[:, 0:1], 1.0 / D, small, mhalf, 1)
                        mbf = mbfp.next()
                        k.ts("dve", mbf[:], ht[:], rs, ALU.mult)
                        ps = pp["tr"].next(); pb = ps[:].bitcast(BF16)
                        for kk in range(8):
                            k.tp(pb[:, hsl(kk)], mbf[:, hsl(kk)], identb[:])
                        k.tt("dve", mT3[:, :, 128 * j:128 * (j + 1)], pb.re("p (k t) -> p k t", k=8),
                             fnw8[:].un(2).bc([128, 8, 128]), ALU.mult)
                    actT3 = actT[:].re("p (c t) -> p c t", c=22)
                    for i in range(22):
                        ys = []
                        for fc in (i, 22 + i):
                            ps = pp["up"].next()
                            for kk in range(8):
                                k.mm(ps[:, 0:NT], w_up_sl(kk, fc), mT3[:, kk, 0:NT], kk == 0, kk == 7)
                            up = uprep.next()
                            uv = up[:, 0:nseq * (2 + L)].re("p (s l) -> p s l", s=nseq)
                            psv = ps[:, 0:NT].re("p (s l) -> p s l", s=nseq)
                            k.cp("act", uv[:, :, 2:2 + L], psv)
                            if samp:
                                k.cp("pool", uv[:, :, 0:2], fcT[:, fc * 32:(fc + 1) * 32].re("p (s j) -> p s j", s=nseq))
                            else:
                                k.cp("pool", uv[:, :, 0:2], cf.c(fc, 0, 2).un(1))
                            k.cp("pool", cf.c(fc, 0, 2 * nseq).re("p (s j) -> p s j", s=nseq), uv[:, :, L:L + 2])
                            y = ycp.next(); ys.append(y)
                            yv = y[:, 0:NT].re("p (s l) -> p s l", s=nseq)
                            k.act(yv, psv, AF.Identity, scale=fcw[:, fc * 3 + 2:fc * 3 + 3], bias=fcb[:, fc:fc + 1])
                            k.stt("dve", yv, uv[:, :, 1:1 + L], fcw[:, fc * 3 + 1:fc * 3 + 2], yv, ALU.mult, ALU.add)
                            k.stt("dve", yv, uv[:, :, 0:L], fcw[:, fc * 3 + 0:fc * 3 + 1], yv, ALU.mult, ALU.add)
                        sg = sgp.next()
                        k.act(sg[:, 0:NT], ys[0][:, 0:NT], AF.Silu)
                        k.tt("dve", actT3[:, i, 0:NT], sg[:, 0:NT], ys[1][:, 0:NT], ALU.mult)
                    for j in range(nsub):
                        r0 = t0 + 128 * j
                        h2 = h2p.next()
                        for half in range(2):
                            psH = pp["down"].next()
                            for c in range(22):
                                k.mm(psH[:], actT3[:, c, 128 * j:128 * (j + 1)], w_down[c][:, half * 512:(half + 1) * 512], c == 0, c == 21)
                            k.tt("dve", h2[:, half * 512:(half + 1) * 512], psH[:], hts[j][:, half * 512:(half + 1) * 512], ALU.add)
                        k.dma("sp", h2_s.rows(r0), h2[:])
                    last_prompt = (not samp) and t0 + NT == SEQ
                    if last_prompt or samp:
                        n2 = 2 * nseq
                        fo = fcs_o if samp else fcp_o
                        for gI in range(11):
                            ps = pp["tr"].next()
                            for i in range(4):
                                fc = gI * 4 + i
                                k.tp(ps[0:n2, hsl(i)], cf.c(fc, 0, n2), identf[:])
                            tmp = tmpp.next()
                            k.cp("act", tmp[0:n2, :], ps[0:n2, :])
                            k.dma("sp", V(Buf(fo.h, "fo%d" % gI), fo.h[:, gI * 512:(gI + 1) * 512]), tmp[0:n2, :])
                S.emit_phase()

        if "C" in _ph:
            phase_b()

        def phase_c():
            with contextlib.ExitStack() as st:
                sb, psum_pool = mk_alloc(st)
                pp = psum_pool(tr=2, mm=6)
                w_gate = load_w(sb, w_gate_d, 8, D, "w_gate")
                w_ple = load_w(sb, w_ple_d, 2, D, "w_ple")
                identb = sb([128, 128], BF16); k.dma("pool", identb[:], ident_d)
                mhalf = sb([128, 8]); k.ms("pool", mhalf[:], -0.5)
                pnw8 = sb([128, 8]); k.dma("sp", pnw8[:], pnw8_d)
                finw = sb([128, D]); k.dma("sp", finw[:], V(None, finw_d.ap.partition_broadcast(128)))
                h2p = sb([128, D], n=PCN, name="h2")
                ptp = sb([128, 256], n=PCN, name="pt")
                pbp = sb([128, 256], BF16, n=PCN, name="pb")
                nbfp = sb([128, D], BF16, n=PCN, name="nbf")
                nTp = sb([128, D], BF16, n=PCN, name="nT")
                pTp = sb([128, 256], BF16, n=PCN, name="pT")
                tgp = sb([128, D], n=PCN, name="tg")
                h3p = sb([128, D], n=PCN, name="h3")
                yp = sb([128, D], n=PCN, name="y")
                junk = sb([128, D], BF16, name="junk"); junk = V(None, junk.h[:])
                small = sb([128, 8], n=4 * PCN, name="small")
                hsl = lambda h: slice(h * 128, (h + 1) * 128)
                for it in range(NTOK // 128):
                    r0 = it * 128
                    h2 = h2p.next()
                    k.dma("sp", h2[:], h2_s.rows(r0))
                    pt = ptp.next()
                    k.dma("sp", pt[:], p_d[r0:r0 + 128, :])
                    ss = small.next()
                    k.act(junk, h2[:], AF.Square, accum=ss[:, 0:1])
                    rs = rstd_of(ss[:, 0:1], 1.0 / D, small, mhalf, 1)
                    nbf = nbfp.next()
                    k.act(nbf[:], h2[:], AF.Copy, scale=rs)
                    ps = pp["tr"].next(); pb = ps[:].bitcast(BF16)
                    for kk in range(8):
                        k.tp(pb[:, hsl(kk)], nbf[:, hsl(kk)], identb[:])
                    nT = nTp.next()
                    k.tt("dve", nT[:].re("p (k t) -> p k t", k=8), pb.re("p (k t) -> p k t", k=8),
                         pnw8[:].un(2).bc([128, 8, 128]), ALU.mult)
                    pbf = pbp.next()
                    k.cp("pool", pbf[:], pt[:])
                    ps2 = pp["tr"].next(); pb2 = ps2[:].bitcast(BF16)
                    for kk in range(2):
                        k.tp(pb2[:, hsl(kk)], pbf[:, hsl(kk)], identb[:])
                    pT = pTp.next()
                    k.cp("act", pT[:], pb2[:, 0:256])
                    tg = tgp.next(); h3 = h3p.next()
                    for half in range(2):
                        hs_ = slice(half * 512, (half + 1) * 512)
                        psG = pp["mm"].next()
                        for kk in range(8):
                            k.mm(psG[:], nT[:, hsl(kk)], w_gate[kk][:, hs_], kk == 0, kk == 7)
                        psP = pp["mm"].next()
                        for kk in range(2):
                            k.mm(psP[:], pT[:, hsl(kk)], w_ple[kk][:, hs_], kk == 0, kk == 1)
                        k.act(tg[:, hs_], psG[:], AF.Tanh, scale=0.5)
                        k.stt("dve", tg[:, hs_], tg[:, hs_], 1.0, psP[:], ALU.add, ALU.mult)
                        k.stt("dve", h3[:, hs_], tg[:, hs_], 0.5, h2[:, hs_], ALU.mult, ALU.add)
                    ss = small.next()
                    k.act(junk, h3[:], AF.Square, accum=ss[:, 0:1])
                    rs = rstd_of(ss[:, 0:1], 1.0 / D, small, mhalf, 1)
                    y = yp.next()
                    k.stt("dve", y[:], h3[:], rs, finw[:], ALU.mult, ALU.mult)
                    k.dma("sp", y_o.rows(r0), y[:])
                S.emit_phase()

        if "D" in _ph:
            phase_c()
    return nc


def _consts():
    c = {}
    idx = np.arange(128)
    c["ident"] = np.eye(128, dtype=np.float32)
    c["irep"] = np.tile(np.eye(128, dtype=np.float32), (1, 4))
    lg = np.log(1.0 - 2.0 ** (-5.0 - np.arange(4, dtype=np.float64)))
    sc = 128.0 ** -0.5
    for v, C in (("P", 128), ("S", LS)):
        seq = idx // C
        pos = idx % C
        same = seq[:, None] == seq[None, :]
        a = idx[:, None]
        b = idx[None, :]
        c["CM" + v] = (same & (a <= b)).astype(np.float32)
        c["UM" + v] = (same & (a > b)).astype(np.float32)
        c["NEGs" + v] = np.where(same & (a > b), 0.0, NEG).astype(np.float32)
        c["NEGT" + v] = np.where(same & (b >= a), 0.0, NEG).astype(np.float32)
        dtr = np.zeros((128, 4, 128), np.float64)
        for h in range(4):
            dtr[:, h, :] = np.where(same & (b >= a), sc * np.exp((b - a) * lg[h]), 0.0)
        c["DTr" + v] = dtr.reshape(128, 512).astype(np.float32)
        c["rqd" + v] = np.exp((pos[:, None] + 1.0) * lg[None, :]).astype(np.float32)
        c["rkt" + v] = (sc * np.exp((C - 1.0 - pos[:, None]) * lg[None, :])).astype(np.float32)
    s16 = np.arange(16)
    c["E1"] = (s16[:, None] == (idx[None, :] // LS)).astype(np.float32).reshape(-1)
    c["seq2"] = ((idx[:, None] // LS) == s16[None, :]).astype(np.float32)
    pos = np.concatenate([np.arange(SEQ), PAST + (np.arange(NSAMP * LS) % LS)]).astype(np.float32)
    inv = (np.float32(10000.0) ** (-np.arange(0, 128, 2, dtype=np.float32) / np.float32(128))).astype(np.float32)
    ang = (pos[:, None] * inv[None, :]).astype(np.float32)
    c["cosT"] = np.cos(ang.astype(np.float64)).astype(np.float32)
    c["sinT"] = np.sin(ang.astype(np.float64)).astype(np.float32)
    return c


_NC_CACHE = {}


def kernel(x_prompt, x_sample, p_prompt, p_sample, state_dn_conv, state_dn, state_ret,
           state_ffn_conv, attn_norm_w, w_in, dn_conv_w, dn_A_log, dn_dt_bias, dn_norm_w,
           ret_norm_w, w_out, ffn_norm_w, w_up, ffn_conv_w, ffn_conv_b, w_down, ple_norm_w,
           w_ple_gate, w_ple, final_norm_w):
    f = lambda a: np.ascontiguousarray(np.asarray(a), dtype=np.float32)
    x_prompt, x_sample, p_prompt, p_sample = f(x_prompt), f(x_sample), f(p_prompt), f(p_sample)
    state_dn_conv, state_dn, state_ret, state_ffn_conv = f(state_dn_conv), f(state_dn), f(state_ret), f(state_ffn_conv)
    col8 = lambda w: f(np.asarray(w).reshape(8, 128).T)
    shared = dict(
        w_in=f(w_in)[0], w_out=f(w_out)[0], w_up=f(w_up)[0], w_down=f(w_down)[0],
        w_gate=f(w_ple_gate)[0], w_ple=f(w_ple)[0],
        anw8=col8(f(attn_norm_w)[0]), fnw8=col8(f(ffn_norm_w)[0]), pnw8=col8(f(ple_norm_w)[0]),
        finw=f(final_norm_w),
        dncw=f(f(dn_conv_w)[0].T.reshape(12, 128, 4).transpose(1, 0, 2).reshape(128, 48)),
        alog=f(dn_A_log)[0], dtb=f(dn_dt_bias)[0],
        dnw4=f(np.tile(f(dn_norm_w)[0], 4)), retw=f(ret_norm_w)[0],
        fcw=f(f(ffn_conv_w)[0].T.reshape(44, 128, 3).transpose(1, 0, 2).reshape(128, 132)),
        fcb=f(f(ffn_conv_b)[0].reshape(44, 128).T),
    )
    shared.update(_consts())
    in_maps = []
    for i in range(NCORES):
        sl = slice(NSAMP * i, NSAMP * (i + 1))
        m = dict(shared)
        m["x"] = f(np.concatenate([x_prompt[i], x_sample[sl].reshape(NSAMP * LS, D)], axis=0))
        m["p"] = f(np.concatenate([p_prompt[0, i], p_sample[0, sl].reshape(NSAMP * LS, 256)], axis=0))
        m["st_dnc"] = f(state_dn_conv[0, sl].reshape(48, 1536))
        m["st_dn"] = f(state_dn[0, sl])
        m["st_ret"] = f(state_ret[0, sl])
        m["st_fc"] = f(state_ffn_conv[0, sl].reshape(32, 2 * DFF))
        in_maps.append(m)
    if "nc" not in _NC_CACHE:
        _NC_CACHE["nc"] = build_program()
    nc = _NC_CACHE["nc"]
    res = run_bass_kernel_spmd(nc, in_maps, core_ids=list(range(NCORES)))
    R = res.results
    _NC_CACHE["last"] = R
    g = lambda name: [np.asarray(R[i][name], dtype=np.float32) for i in range(NCORES)]
    y = g("y")
    y_prompt = np.stack([a[:SEQ] for a in y], axis=0)
    y_sample = np.concatenate([a[SEQ:].reshape(NSAMP, LS, D) for a in y], axis=0)
    dncp = np.stack(g("o_dnc_p"), axis=0)[None]
    dnp = np.stack(g("o_dn_p"), axis=0)[None]
    retp = np.stack(g("o_ret_p"), axis=0)[None]
    fcp = np.stack(g("o_fc_p"), axis=0)[None]
    dncs = np.concatenate([a.reshape(NSAMP, 3, 1536) for a in g("o_dnc_s")], axis=0)[None]
    dns = np.concatenate(g("o_dn_s"), axis=0)[None]
    rets = np.concatenate(g("o_ret_s"), axis=0)[None]
    fcs = np.concatenate([a.reshape(NSAMP, 2, 2 * DFF) for a in g("o_fc_s")], axis=0)[None]
    return (y_prompt, y_sample, dncp, dnp, retp, fcp, dncs, dns, rets, fcs)
```

```python
import contextlib
import os as _osb
import numpy as np
import concourse.bass as bass
import concourse.mybir as mybir
from concourse.bass_utils import run_bass_kernel_spmd

F32 = mybir.dt.float32
BF16 = mybir.dt.bfloat16
AF = mybir.ActivationFunctionType
ALU = mybir.AluOpType
AX = mybir.AxisListType

NCORES = 8
D = 1024
SEQ = 2048
NSAMP = 16
LS = 8
NTOK = SEQ + NSAMP * LS
DIN = 4104
DFF = 2816
EPS = 1e-6
PAST = 16384
NEG = -30000.0
PE2R, PRBF2, PKQT, PRTK, POF = 5, 5, 3, 3, 2
import os as _osb
BAL = _osb.environ.get('K_BAL', '')
PCN = int(_osb.environ.get('K_PCN', '6'))
C_QKV, C_Z, C_B, C_A, C_RQ, C_RK, C_RV, C_RG = 0, 1536, 2048, 2052, 2056, 2568, 3080, 3592


class T:
    __slots__ = ("name", "last_writer", "readers")

    def __init__(self, name=""):
        self.name = name
        self.last_writer = None
        self.readers = []


class Op:
    __slots__ = ("eng", "fn", "deps", "users", "ndep", "signaled", "sigval", "sem", "is_dma", "idx", "cost",
                 "aset", "phase", "finish", "pos", "rtime", "tag", "prio")

    def __init__(self, eng, fn, is_dma):
        self.eng = eng
        self.fn = fn
        self.deps = []
        self.users = []
        self.signaled = False
        self.sigval = None
        self.sem = None
        self.is_dma = is_dma
        self.finish = 0.0
        self.pos = -1


class Sched:
    ENGS = ("pe", "act", "dve", "pool", "sp")
    XLAT = float(_osb.environ.get('K_XLAT', '500'))
    SLAT = float(_osb.environ.get('K_SLAT', '60'))

    def __init__(self, nc, n_dma_sems=14):
        self.nc = nc
        self.ops = []
        self.n_dma_sems = n_dma_sems
        self.nops = 0
        self.phase = 0
        import os as _os
        self.prio_mode = int(_os.environ.get("KS_PRIO", "1"))
        self.prio_w = float(_os.environ.get("KS_PRIOW", "0.0"))

    def op(self, eng, fn, reads=(), writes=(), dma=False, cost=200.0, aset=None):
        o = Op(eng, fn, dma)
        o.idx = self.nops
        self.nops += 1
        o.cost = cost
        o.aset = aset
        o.phase = self.phase
        import sys as _sys
        fr = _sys._getframe(2)
        o.tag = fr.f_lineno if fr.f_code.co_name != "<lambda>" else fr.f_back.f_lineno
        deps = []
        for t in reads:
            if t.last_writer is not None:
                deps.append(t.last_writer)
        for t in writes:
            if t.last_writer is not None:
                deps.append(t.last_writer)
            deps.extend(t.readers)
        seen = set()
        for d in deps:
            if id(d) in seen or d is o or d.phase != o.phase:
                continue
            seen.add(id(d))
            o.deps.append(d)
            d.users.append(o)
        for t in reads:
            t.readers.append(o)
        for t in writes:
            t.last_writer = o
            t.readers = []
        self.ops.append(o)
        return o

    def open(self, stack):
        nc = self.nc
        self.sems = {}
        for e in ("pe", "act", "dve", "pool"):
            self.sems[e] = stack.enter_context(nc.semaphore("s_" + e))
        for e in ("sp", "act", "pool"):
            for k in range(self.n_dma_sems):
                self.sems[(e, k)] = stack.enter_context(nc.semaphore("d_%s_%d" % (e, k)))
        self.cnt = {}
        self.dma_n = {e: 0 for e in self.ENGS}

    def _schedule(self):
        import heapq
        ops = self.ops
        future = {e: [] for e in self.ENGS}
        avail = {e: [] for e in self.ENGS}
        free_at = {e: 0.0 for e in self.ENGS}
        cur_set = {e: None for e in self.ENGS}
        streams = {e: [] for e in self.ENGS}
        self._pipe = 0.0
        bl = {}
        for o in reversed(ops):
            m = 0.0
            for u in o.users:
                lat = self.XLAT if (u.eng != o.eng or o.is_dma) else self.SLAT
                v = bl[id(u)] + lat
                if v > m:
                    m = v
            bl[id(o)] = m + o.cost
        mode = self.prio_mode
        for o in ops:
            o.ndep = len(o.deps)
            o.rtime = 0.0
            if mode == 0:
                o.prio = o.idx
            else:
                o.prio = -bl[id(o)] + self.prio_w * o.idx
        for o in ops:
            if o.ndep == 0:
                heapq.heappush(future[o.eng], (0.0, o.prio, o.idx, o))
        left = len(ops)
        while left:
            best = None
            for e in self.ENGS:
                f, a = future[e], avail[e]
                while f and f[0][0] <= free_at[e]:
                    _, pr, i, o = heapq.heappop(f)
                    heapq.heappush(a, (pr, i, o))
                if a:
                    cand = (free_at[e], a[0][0], e, True)
                elif f:
                    cand = (f[0][0], f[0][1], e, False)
                else:
                    continue
                if best is None or cand < best:
                    best = cand
            start, _, e, from_avail = best
            if from_avail:
                _, _, o = heapq.heappop(avail[e])
            else:
                _, _, _, o = heapq.heappop(future[e])
            c = o.cost
            if o.aset is not None and o.aset != cur_set[e]:
                if cur_set[e] is not None:
                    c += 1300.0
                cur_set[e] = o.aset
            if o.is_dma:
                xfer = max(0.0, c - 2000.0) * (120.0 / 220.0)
                t0x = max(start + 1000.0, self._pipe)
                self._pipe = t0x + xfer
                o.finish = t0x + xfer + 1000.0
                free_at[e] = start + 60.0
            else:
                o.finish = start + c
                free_at[e] = o.finish
            o.pos = len(streams[e])
            streams[e].append(o)
            left -= 1
            for u in o.users:
                lat = self.XLAT if (u.eng != e or o.is_dma) else self.SLAT
                t = o.finish + lat
                if t > u.rtime:
                    u.rtime = t
                u.ndep -= 1
                if u.ndep == 0:
                    heapq.heappush(future[u.eng], (u.rtime, u.prio, u.idx, u))
        self.makespan = max(free_at.values())
        return streams

    def emit_phase(self):
        nc = self.nc
        sems = self.sems
        cnt = self.cnt
        streams = self._schedule()
        for e in self.ENGS:
            last_on_sem = {}
            for o in streams[e]:
                if o.is_dma:
                    kk = self.dma_n[e] % self.n_dma_sems
                    self.dma_n[e] += 1
                    o.sem = (e, kk)
                    c = cnt.get(o.sem, 0) + 16
                    cnt[o.sem] = c
                    o.sigval = c
        plan = {}
        for e in self.ENGS:
            wpos = {}
            wl = []
            for o in streams[e]:
                ws = []
                for d in o.deps:
                    if d.is_dma:
                        ws.append(d)
                        continue
                    if d.eng == e and e == "pe":
                        continue
                    if d.pos > wpos.get(d.eng, -1):
                        wpos[d.eng] = d.pos
                        d.signaled = True
                        ws.append(d)
                wl.append(ws)
            plan[e] = wl
        for e in ("pe", "act", "dve", "pool"):
            for o in reversed(streams[e]):
                if not o.is_dma:
                    o.signaled = True
                    break
        for e in self.ENGS:
            for o in streams[e]:
                if (not o.is_dma) and o.signaled:
                    c = cnt.get(e, 0) + 1
                    cnt[e] = c
                    o.sem = e
                    o.sigval = c
        final = dict(cnt)
        with nc.Block() as block:
            engobj = {"pe": block.tensor, "act": block.scalar, "dve": block.vector,
                      "pool": block.gpsimd, "sp": block.sync}

            def run(e, eng):
                waited = {}
                dma_prev = {}

                def wait_sv(sem, val):
                    if waited.get(sem, 0) >= val:
                        return
                    eng.wait_ge(sems[sem], val)
                    waited[sem] = val

                for o, ws in zip(streams[e], plan[e]):
                    for d in ws:
                        wait_sv(d.sem, d.sigval)
                    if o.is_dma:
                        if o.sigval > 16:
                            wait_sv(o.sem, o.sigval - 16)
                    ins = o.fn(eng)
                    if o.is_dma:
                        ins.then_inc(sems[o.sem], 16)
                    elif o.signaled:
                        ins.then_inc(sems[o.sem], 1)
                for sem, val in final.items():
                    wait_sv(sem, val)

            for e in self.ENGS:
                def mk(e):
                    def f(eng):
                        run(e, eng)
                    return f
                engobj[e](mk(e))
        self.ops = []
        self.phase += 1


class StopBuild(Exception):
    pass


def ck(n):
    import os
    lim = float(os.environ.get("KDBG_CK", "1000"))
    if n > lim:
        raise StopBuild()


class Buf:
    def __init__(self, h, name="", excl=False):
        self.h = h
        self.t = T(name)
        self.excl = excl

    def __getitem__(self, k):
        return V(self, self.h[k])


class V:
    def __init__(self, buf, ap):
        self.buf = buf
        self.ap = ap

    def __getitem__(self, k):
        return V(self.buf, self.ap[k])

    def re(self, pat_, **kw):
        return V(self.buf, self.ap.rearrange(pat_, **kw))

    def bc(self, shape):
        return V(self.buf, self.ap.to_broadcast(list(shape)))

    def un(self, ax):
        return V(self.buf, self.ap.unsqueeze(ax))

    def bitcast(self, dt):
        return V(self.buf, self.ap.bitcast(dt))


class Chunked:
    def __init__(self, buf, n, w):
        self.bufs = [Buf(buf.h, "%s_c%d" % (buf.t.name, i)) for i in range(n)]
        self.w = w

    def c(self, i, a=0, b=None):
        b = self.w if b is None else b
        return self.bufs[i][:, i * self.w + a:i * self.w + b]

    def inherit(self, buf):
        for b in self.bufs:
            b.t.last_writer = buf.t.last_writer


class Pool:
    def __init__(self, bufs):
        self.bufs = bufs
        self.i = 0

    def next(self):
        b = self.bufs[self.i % len(self.bufs)]
        self.i += 1
        return b


def _tr(*vs):
    return [v.buf.t for v in vs if isinstance(v, V) and v.buf is not None]


def _rw(reads, writes):
    r, w = [], []
    for v in reads:
        if isinstance(v, V) and v.buf is not None:
            (w if v.buf.excl else r).append(v.buf.t)
    for v in writes:
        if isinstance(v, V) and v.buf is not None:
            w.append(v.buf.t)
    return dict(reads=r, writes=w)


def _a(x):
    return x.ap if isinstance(x, V) else x


def _fs(v):
    n = 1
    for d in v.ap.shape[1:]:
        n *= int(d)
    return n


def _is_psum(v):
    return isinstance(v, V) and v.buf is not None and v.buf.excl


_ASET = {AF.Silu: "silu", AF.Exp: "lnexp", AF.Ln: "lnexp", AF.Tanh: "silu"}


class K:
    def __init__(self, nc, S):
        self.nc = nc
        self.S = S

    def mm(self, out, lhsT, rhs, start=True, stop=True):
        n = max(32, _fs(rhs))
        c = n / 2.37 * (4.0 if rhs.ap.dtype == F32 else 1.0) + 48.0
        self.S.op("pe", lambda e: e.matmul(out.ap, lhsT=lhsT.ap, rhs=rhs.ap, start=start, stop=stop),
                  cost=c, **_rw([lhsT, rhs], [out]))

    def tp(self, out, in_, ident):
        c = max(32, _fs(ident)) / 2.37 * (4.0 if in_.ap.dtype == F32 else 1.0) + 48.0
        self.S.op("pe", lambda e: e.transpose(out.ap, in_.ap, ident.ap), cost=c, **_rw([in_, ident], [out]))

    def act(self, out, in_, func, scale=1.0, bias=0.0, accum=None):
        def f(e):
            kw = dict(out=out.ap, in_=in_.ap, func=func, scale=_a(scale), bias=_a(bias))
            if accum is not None:
                kw["accum_out"] = accum.ap
            return e.activation(**kw)
        c = 190.0 + 0.6 * _fs(in_) + (90.0 if accum is not None else 0.0)
        self.S.op("act", f, cost=c, aset=_ASET.get(func), **_rw([in_, scale, bias], [out, accum]))

    def _vc(self, eng, *vs):
        f = max(_fs(v) for v in vs if isinstance(v, V))
        ps = any(_is_psum(v) for v in vs)
        if eng == "pool":
            return 1100.0 + 0.45 * f
        return (130.0 if ps else 90.0) + 1.25 * f

    def tt(self, eng, out, a, b, op):
        self.S.op(eng, lambda e: e.tensor_tensor(out=out.ap, in0=a.ap, in1=b.ap, op=op), cost=self._vc(eng, out, a, b),
                  **_rw([a, b], [out]))

    def ts(self, eng, out, a, s1, op0, s2=None, op1=None):
        def f(e):
            if s2 is None:
                return e.tensor_scalar(out=out.ap, in0=a.ap, scalar1=_a(s1), scalar2=None, op0=op0)
            return e.tensor_scalar(out=out.ap, in0=a.ap, scalar1=_a(s1), scalar2=_a(s2), op0=op0, op1=op1)
        self.S.op(eng, f, cost=self._vc(eng, out, a), **_rw([a, s1, s2], [out]))

    def stt(self, eng, out, a, s, b, op0, op1):
        self.S.op(eng, lambda e: e.scalar_tensor_tensor(out=out.ap, in0=a.ap, scalar=_a(s), in1=b.ap, op0=op0, op1=op1),
                  cost=self._vc(eng, out, a, b), **_rw([a, s, b], [out]))

    def cp(self, eng, out, a):
        if eng == "act":
            self.S.op("act", lambda e: e.copy(out=out.ap, in_=a.ap), cost=190.0 + 0.6 * _fs(a), **_rw([a], [out]))
        else:
            c = (250.0 + 0.3 * _fs(a)) if eng == "pool" else self._vc(eng, out, a)
            self.S.op(eng, lambda e: e.tensor_copy(out=out.ap, in_=a.ap), cost=c, **_rw([a], [out]))

    def recip(self, out, a):
        self.S.op("dve", lambda e: e.reciprocal(out=out.ap, in_=a.ap), cost=self._vc("dve", out, a), **_rw([a], [out]))

    def ms(self, eng, out, val):
        self.S.op(eng, lambda e: e.memset(out.ap, val), cost=250.0 + 0.3 * _fs(out), **_rw([], [out]))

    def dma(self, q, out, in_):
        nbytes = int(out.ap.shape[0]) * _fs(out) * 4
        c = 2000.0 + nbytes / 120.0
        return self.S.op(q, lambda e: e.dma_start(out=out.ap, in_=in_.ap), dma=True, cost=c, **_rw([in_], [out]))


def build_program():
    nc = bass.Bass("TRN2", target_bir_lowering=False)

    def din(name, shape):
        return V(None, nc.dram_tensor(name, list(shape), F32, kind="ExternalInput").ap())

    def dout(name, shape):
        return Buf(nc.dram_tensor(name, list(shape), F32, kind="ExternalOutput").ap(), name)

    class RowBufs:
        def __init__(self, ap):
            self.h = ap
            self.b = {}

        def rows(self, r0):
            if r0 not in self.b:
                self.b[r0] = Buf(self.h, "rb%d" % r0)
            return V(self.b[r0], self.h[r0:r0 + 128, :])

    x_d = din("x", [NTOK, D])
    p_d = din("p", [NTOK, 256])
    stdnc_d = din("st_dnc", [48, 1536])
    stdn_d = din("st_dn", [NSAMP, 4, 128, 128])
    stret_d = din("st_ret", [NSAMP, 4, 128, 128])
    stfc_d = din("st_fc", [32, 2 * DFF])
    w_in_d = din("w_in", [D, DIN])
    w_out_d = din("w_out", [D, D])
    w_up_d = din("w_up", [D, 2 * DFF])
    w_down_d = din("w_down", [DFF, D])
    w_gate_d = din("w_gate", [D, D])
    w_ple_d = din("w_ple", [256, D])
    anw8_d = din("anw8", [128, 8])
    fnw8_d = din("fnw8", [128, 8])
    pnw8_d = din("pnw8", [128, 8])
    finw_d = din("finw", [D])
    dncw_d = din("dncw", [128, 48])
    alog_d = din("alog", [4])
    dtb_d = din("dtb", [4])
    dnw4_d = din("dnw4", [512])
    retw_d = din("retw", [512])
    fcw_d = din("fcw", [128, 132])
    fcb_d = din("fcb", [128, 44])
    ident_d = din("ident", [128, 128])
    irep_d = din("irep", [128, 512])
    cos_d = din("cosT", [NTOK, 64])
    sin_d = din("sinT", [NTOK, 64])
    cvar = {}
    for v in ("P", "S"):
        cvar[v] = dict(CM=din("CM" + v, [128, 128]), UM=din("UM" + v, [128, 128]),
                       NEGs=din("NEGs" + v, [128, 128]), NEGT=din("NEGT" + v, [128, 128]),
                       DTr=din("DTr" + v, [128, 512]), rqd=din("rqd" + v, [128, 4]), rkt=din("rkt" + v, [128, 4]))
    e1_d = din("E1", [16 * 128])
    seq2_d = din("seq2", [128, 16])

    y_o = RowBufs(nc.dram_tensor("y", [NTOK, D], F32, kind="ExternalOutput").ap())
    dncp_o = dout("o_dnc_p", [3, 1536])
    dnp_o = dout("o_dn_p", [4, 128, 128])
    retp_o = dout("o_ret_p", [4, 128, 128])
    fcp_o = dout("o_fc_p", [2, 2 * DFF])
    dncs_o = dout("o_dnc_s", [48, 1536])
    dns_o = dout("o_dn_s", [NSAMP, 4, 128, 128])
    rets_o = dout("o_ret_s", [NSAMP, 4, 128, 128])
    fcs_o = dout("o_fc_s", [32, 2 * DFF])
    import os as _os
    _dbg = _os.environ.get("KDBG_OUT", "") == "1"
    _kind = dict(kind="ExternalOutput") if _dbg else {}
    h1_s = RowBufs(nc.dram_tensor("h1_scr", [NTOK, D], F32, **_kind).ap())
    h2_s = RowBufs(nc.dram_tensor("h2_scr", [NTOK, D], F32, **_kind).ap())

    lg = [float(np.log(1.0 - 2.0 ** (-5.0 - h))) for h in range(4)]

    with contextlib.ExitStack() as top:
        S = Sched(nc)
        S.open(top)
        k = K(nc, S)

        gcnt = [0]

        def mk_alloc(st):
            cnt = gcnt

            def sb(shape, dt=F32, n=0, name="t"):
                def one():
                    cnt[0] += 1
                    nm = "%s_%d" % (name, cnt[0])
                    return Buf(st.enter_context(nc.sbuf_tensor(nm, list(shape), dt)), nm)
                if n == 0:
                    return one()
                return Pool([one() for _ in range(n)])

            def psum_pool(**roles):
                assert sum(roles.values()) <= 8
                out = {}
                for role, n in roles.items():
                    bufs = []
                    for i in range(n):
                        cnt[0] += 1
                        nm = "ps_%d" % cnt[0]
                        bufs.append(Buf(st.enter_context(nc.psum_tensor(nm, [128, 512], F32)), nm, excl=True))
                    out[role] = Pool(bufs)
                return out
            return sb, psum_pool

        def load_w(st_sb, wd, kchunks, ncols, name):
            return load_cols(st_sb, wd, kchunks, 0, ncols, name)

        def rstd_act(ss_in, n_inv, small, ncol):
            a = small.next()
            k.act(a[:, 0:ncol], ss_in, AF.Ln, scale=n_inv, bias=epsc[:, 0:1])
            r = small.next()
            k.act(r[:, 0:ncol], a[:, 0:ncol], AF.Exp, scale=-0.5)
            return r[:, 0:ncol]

        def rstd_of(ss_in, n_inv, small, mhalf, ncol):
            if USE_ACT_RSTD[0]:
                return rstd_act(ss_in, n_inv, small, ncol)
            a = small.next()
            k.ts("dve", a[:, 0:ncol], ss_in, n_inv, ALU.mult, EPS, ALU.add)
            r = small.next()
            k.tt("pool", r[:, 0:ncol], a[:, 0:ncol], mhalf[:, 0:ncol], ALU.pow)
            return r[:, 0:ncol]

        USE_ACT_RSTD = [False]
        epsc = None

        def phase_a(samp):
            USE_ACT_RSTD[0] = True
            try:
                phase_a_body(samp)
            finally:
                USE_ACT_RSTD[0] = False

        def phase_a_body(samp):
            nonlocal epsc
            with contextlib.ExitStack() as st:
                sb, psum_pool = mk_alloc(st)
                pp = psum_pool(E=3, M=2, R=2, L=1)
                cv = cvar["S" if samp else "P"]
                nst = NSAMP if samp else 1
                nlev = 3 if samp else 7
                Cc = LS if samp else 128
                cdec = [float(np.exp(Cc * lg[h])) for h in range(4)]
                nb = 1 if samp else 1
                w_inq, w_inr, w_out = WA
                identf = sb([128, 128]); k.dma("sp", identf[:], ident_d)
                identb = sb([128, 128], BF16); k.dma("pool", identb[:], ident_d)
                irep = sb([128, 512], BF16); k.dma("pool", irep[:], irep_d)
                onesf = sb([128, 128]); k.ms("pool", onesf[:], 1.0)
                nonesf = sb([128, 128]); k.ms("pool", nonesf[:], -1.0)
                mhalf = sb([128, 8]); k.ms("pool", mhalf[:], -0.5)
                epsc = sb([128, 1]); k.ms("pool", epsc[:], EPS)
                ecvp = sb([128, 256], n=2, name="ecv")
                CM = sb([128, 128]); k.dma("sp", CM[:], cv["CM"])
                UM = sb([128, 128]); k.dma("sp", UM[:], cv["UM"])
                NEGs = sb([128, 128], BF16); k.dma("pool", NEGs[:], cv["NEGs"])
                NEGT = sb([128, 128], BF16); k.dma("pool", NEGT[:], cv["NEGT"])
                DTr = sb([128, 512]); k.dma("sp", DTr[:], cv["DTr"])
                rqd = sb([128, 4]); k.dma("sp", rqd[:], cv["rqd"])
                rkt = sb([128, 4]); k.dma("sp", rkt[:], cv["rkt"])
                anw8 = sb([128, 8]); k.dma("sp", anw8[:], anw8_d)
                cw = sb([128, 48]); k.dma("sp", cw[:], dncw_d)
                alogb = sb([128, 4]); k.dma("sp", alogb[:], V(None, alog_d.ap.partition_broadcast(128)))
                dtbb = sb([128, 4]); k.dma("sp", dtbb[:], V(None, dtb_d.ap.partition_broadcast(128)))
                negA = sb([128, 4])
                k.act(negA[:], alogb[:], AF.Exp)
                k.ts("dve", negA[:], negA[:], -1.0, ALU.mult)
                dnw = sb([128, 512]); k.dma("sp", dnw[:], V(None, dnw4_d.ap.partition_broadcast(128)))
                retw = sb([128, 512]); k.dma("sp", retw[:], V(None, retw_d.ap.partition_broadcast(128)))
                if samp:
                    E1 = sb([128, 16 * 128], BF16)
                    k.dma("pool", E1[:], V(None, e1_d.ap.partition_broadcast(128)))
                    seq2 = sb([128, 16]); k.dma("sp", seq2[:], seq2_d)
                NT = 128 if samp else 256
                nbuf = 1 if samp else 2
                xtp = sb([128, D], n=1, name="xt")
                xrp = None if samp else sb([128, D], n=1, name="xr")
                abfp = sb([128, D], BF16, n=1, name="abf")
                aTp = sb([128, 8 * NT], BF16, n=1, name="aT")
                qkvT = sb([128, 12 * NT], BF16, name="qkvT")
                xprep = sb([128, 264], n=1 if samp else 2, name="xpre")
                ycvp = sb([128, 256], n=1 if samp else 2, name="ycv")
                cx_all = sb([128, 12 * 48], name="cx"); cx = Chunked(cx_all, 12, 48)
                junk = sb([128, D], BF16, name="junk")
                junk_q = sb([128, 128], BF16, name="junkq")
                junk_o = sb([128, 128], BF16, name="junko")
                junk_r = sb([128, 256], BF16, name="junkr")
                small = sb([128, 24], n=14 if samp else 32, name="small")
                zsp = sb([128, 512], BF16, n=1 if samp else 2, name="zs")
                rqkf = sb([128, 1024], name="rqkf")
                qkf = sb([128, 1024], BF16, name="qkf")
                rgsp = sb([128, 512], BF16, n=1 if samp else 2, name="rgs")
                tmpp = sb([128, 512], n=2, name="tmp")
                ebf = sb([128, 512], BF16, n=4, name="ebf")
                e2r = sb([128, 512], BF16, n=3 if samp else PE2R, name="e2r")
                rbf = sb([128, 512], BF16, n=2 if samp else 4, name="rbf")
                rbf2 = sb([128, 512], BF16, n=5 if samp else PRBF2, name="rbf2")
                kqT = sb([128, 1024], BF16, n=2 if samp else PKQT, name="kqT")
                rtk = sb([128, 1024], BF16, n=2 if samp else PRTK, name="rtk")
                MG = sb([128, 512], name="MG")
                Xp = sb([128, 512], BF16, n=2, name="X")
                XTp = sb([128, 512], BF16, n=2, name="XT")
                IXp = sb([128, 512], BF16, n=2, name="IX")
                PTp = sb([128, 512], BF16, n=2 if samp else 3, name="PT")
                dexp = sb([128, 512], BF16, n=3, name="dexp")
                ofp = sb([128, 512], n=POF, name="of")
                mix = sb([128, D], BF16, name="mix")
                mixT = sb([128, D], BF16, name="mixT")
                h1p = None if samp else sb([128, D], n=1, name="h1")
                csp = sb([128, 128], n=nbuf, name="cs")
                if samp:
                    sbigp = sb([128, 16 * 128], n=2, name="sbig")
                    sbigbp = sb([128, 16 * 128], BF16, n=2, name="sbigb")
                    expp = sb([128, 16 * 128], BF16, n=2, name="exp")
                    dncT = sb([128, 12 * 48], name="dncT")
                else:
                    Sdn = sb([128, 512], name="Sdn"); k.ms("pool", Sdn[:], 0.0)
                    Sdnb = sb([128, 512], BF16, name="Sdnb"); k.ms("pool", Sdnb[:], 0.0)
                    Srt = sb([128, 512], name="Srt"); k.ms("pool", Srt[:], 0.0)
                    Srtb = sb([128, 512], BF16, name="Srtb"); k.ms("pool", Srtb[:], 0.0)
                    k.ms("pool", cx_all[:], 0.0)
                    cx.inherit(cx_all)

                if samp:
                    blocks = [(SEQ, 128, NSAMP, LS)]
                else:
                    blocks = [(b * 256, 256, 1, 256) for b in range(SEQ // 256)]

                hsl = lambda h: slice(h * 128, (h + 1) * 128)
                b3 = lambda v: v.un(2).bc([128, 4, 128])
                r3 = lambda v: v.re("p (h d) -> p h d", h=4)

                if samp:
                    for gI in range(3):
                        tmp = tmpp.next()
                        k.dma("sp", tmp[0:48, :], stdnc_d[:, gI * 512:(gI + 1) * 512])
                        ps = pp["M"].next()
                        for i in range(4):
                            k.tp(ps[:, i * 48:(i + 1) * 48], tmp[0:48, i * 128:(i + 1) * 128], identf[0:48, 0:48])
                        k.cp("act", dncT[:, gI * 192:(gI + 1) * 192], ps[:, 0:192])

                last_xt = [None]

                def stage_a0(t0, NT, aT):
                    nsub = NT // 128
                    for j in range(nsub):
                        xt = xtp.next()
                        last_xt[0] = xt
                        k.dma("sp", xt[:], x_d[t0 + 128 * j:t0 + 128 * (j + 1), :])
                        ss = small.next()
                        k.act(junk[:], xt[:], AF.Square, accum=ss[:, 0:1])
                        rs = rstd_of(ss[:, 0:1], 1.0 / D, small, mhalf, 1)
                        abf = abfp.next()
                        k.ts("dve", abf[:], xt[:], rs, ALU.mult)
                        ps = pp["M"].next()
                        pb = ps[:].bitcast(BF16)
                        for kk in range(8):
                            k.tp(pb[:, hsl(kk)], abf[:, hsl(kk)], identb[:])
                        k.tt("dve", aT[:].re("p (k t) -> p k t", k=8)[:, :, 128 * j:128 * (j + 1)],
                             pb.re("p (k t) -> p k t", k=8), anw8[:].un(2).bc([128, 8, 128]), ALU.mult)

                try:
                  ck(0)
                  for bi, (t0, NT, nseq, L) in enumerate(blocks):
                    nsub = NT // 128
                    aT = aTp.next()
                    stage_a0(t0, NT, aT)
                    ck(1)
                    aT3 = aT[:].re("p (k t) -> p k t", k=8)
                    qkvT3 = qkvT[:].re("p (c t) -> p c t", c=12)
                    for fc in range(12):
                        ps = pp["M"].next()
                        for kk in range(8):
                            k.mm(ps[:, 0:NT], w_inq[kk][:, fc * 128:(fc + 1) * 128], aT3[:, kk, :], start=kk == 0, stop=kk == 7)
                        ck(1.1)
                        xp = xprep.next()
                        xv = xp[:, 0:nseq * (3 + L)].re("p (s l) -> p s l", s=nseq)
                        psv = ps[:, 0:NT].re("p (s l) -> p s l", s=nseq)
                        k.cp("act", xv[:, :, 3:3 + L], psv)
                        ck(1.2)
                        if samp:
                            k.cp("pool", xv[:, :, 0:3], dncT[:, fc * 48:(fc + 1) * 48].re("p (s j) -> p s j", s=nseq))
                        else:
                            k.cp("pool", xv[:, :, 0:3], cx.c(fc, 0, 3).un(1))
                        k.cp("pool", cx.c(fc, 0, 3 * nseq).re("p (s j) -> p s j", s=nseq), xv[:, :, L:L + 3])
                        ck(1.3)
                        y = ycvp.next()
                        yv = y[:, 0:NT].re("p (s l) -> p s l", s=nseq)
                        k.act(yv, psv, AF.Copy, scale=cw[:, fc * 4 + 3:fc * 4 + 4])
                        ck(1.4)
                        k.stt("dve", yv, xv[:, :, 2:2 + L], cw[:, fc * 4 + 2:fc * 4 + 3], yv, ALU.mult, ALU.add)
                        k.stt("dve", yv, xv[:, :, 1:1 + L], cw[:, fc * 4 + 1:fc * 4 + 2], yv, ALU.mult, ALU.add)
                        k.stt("dve", yv, xv[:, :, 0:L], cw[:, fc * 4 + 0:fc * 4 + 1], yv, ALU.mult, ALU.add)
                        ck(1.5)
                        ecv = ecvp.next()
                        k.act(ecv[:, 0:NT], y[:, 0:NT], AF.Exp, scale=-1.0)
                        k.act(ecv[:, 0:NT], ecv[:, 0:NT], AF.Ln, bias=1.0)
                        k.act(ecv[:, 0:NT], ecv[:, 0:NT], AF.Exp, scale=-1.0)
                        k.tt("dve", qkvT3[:, fc, :], y[:, 0:NT], ecv[:, 0:NT], ALU.mult)
                        ck(1.6)

                    ck(2)
                    for j in range(nsub):
                        js = slice(128 * j, 128 * (j + 1))
                        r0 = t0 + 128 * j
                        if samp:
                            xr = last_xt[0]
                        else:
                            xr = xrp.next()
                            k.dma("sp", xr[:], x_d[r0:r0 + 128, :])
                        cs = csp.next()
                        k.dma("sp", cs[:, 0:64], cos_d[r0:r0 + 128, :])
                        k.dma("sp", cs[:, 64:128], sin_d[r0:r0 + 128, :])

                        def proj(c0, n, role="M"):
                            ps = pp[role].next()
                            for kk in range(8):
                                k.mm(ps[:, 0:n], aT3[:, kk, js], w_inr[kk][:, c0 - 1536:c0 - 1536 + n], start=kk == 0, stop=kk == 7)
                            return ps

                        ck(2.1)
                        psqk = pp["E"].next(); pqk = psqk[:].bitcast(BF16)
                        for i in range(8):
                            k.tp(pqk[:, hsl(i)], qkvT3[:, i, js], identb[:])
                        psv_ = pp["E"].next(); pv = psv_[:].bitcast(BF16)
                        for h in range(4):
                            k.tp(pv[:, hsl(h)], qkvT3[:, 8 + h, js], identb[:])
                        ck(2.2)
                        k.cp("act", qkf[:], pqk)
                        ck(2.3)
                        st = small.next()
                        for i in range(8):
                            k.act(junk_q[:], qkf[:, hsl(i)], AF.Square, accum=st[:, i:i + 1])
                        ck(2.4)
                        rs = rstd_of(st[:, 0:8], 1.0, small, mhalf, 8)
                        ck(3)
                        psba = proj(C_B, 8, "E")
                        sm = small.next()
                        k.act(sm[:, 0:4], psba[:, 0:4], AF.Exp, scale=-1.0)
                        k.ts("dve", sm[:, 0:4], sm[:, 0:4], 1.0, ALU.add)
                        beta_t = small.next(); beta = beta_t[:, 0:4]
                        k.recip(beta, sm[:, 0:4])
                        k.tt("dve", sm[:, 4:8], psba[:, 4:8], dtbb[:], ALU.add)
                        k.act(sm[:, 8:12], sm[:, 4:8], AF.Exp)
                        k.act(sm[:, 12:16], sm[:, 8:12], AF.Ln, bias=1.0)
                        g_t = small.next(); g = g_t[:, 0:4]
                        k.tt("dve", g, sm[:, 12:16], negA[:], ALU.mult)
                        ck(4)
                        psg = pp["E"].next()
                        k.mm(psg[:, 0:4], CM[:], g)
                        k.mm(psg[:, 4:8], UM[:], g)
                        if samp:
                            Gs = small.next() if False else tmpp.next()
                            k.tt("pool", Gs[:, 0:64].re("p (h s) -> p h s", h=4), g.un(2).bc([128, 4, 16]),
                                 seq2[:].un(1).bc([128, 4, 16]), ALU.mult)
                            k.mm(psg[:, 8:72], onesf[:], Gs[:, 0:64])
                        else:
                            k.mm(psg[:, 8:12], onesf[:], g)
                        nex = 8 + 4 * nst
                        ex_t = sb_ex.next()
                        ex = ex_t[:, 0:nex]
                        k.act(ex, psg[:, 0:nex], AF.Exp)
                        sc_t = small.next(); sc = sc_t
                        k.tt("dve", sc[:, 0:4], rs[:, 4:8], beta, ALU.mult)
                        k.tt("dve", sc[:, 4:8], rs[:, 4:8], ex[:, 4:8], ALU.mult)
                        k.ts("dve", sc[:, 8:12], rs[:, 0:4], 128.0 ** -0.5, ALU.mult)
                        k.tt("dve", sc[:, 12:16], sc[:, 8:12], ex[:, 0:4], ALU.mult)
                        k.stt("dve", sc[:, 16:20], beta, -1.0, ex[:, 0:4], ALU.mult, ALU.mult)
                        kf = r3(qkf[:, 512:1024]); qf = r3(qkf[:, 0:512])
                        Kn = ebf.next(); KB = ebf.next(); Qs = ebf.next(); Qd = ebf.next(); Kt = e2r.next(); Vb = e2r.next()
                        k.tt("pool", r3(Kn[:]), kf, b3(rs[:, 4:8]), ALU.mult)
                        k.tt("dve", r3(KB[:]), kf, b3(sc[:, 0:4]), ALU.mult)
                        k.tt("pool", r3(Kt[:]), kf, b3(sc[:, 4:8]), ALU.mult)
                        k.tt("dve", r3(Qs[:]), qf, b3(sc[:, 8:12]), ALU.mult)
                        k.tt("pool", r3(Qd[:]), qf, b3(sc[:, 12:16]), ALU.mult)
                        k.tt("dve", r3(Vb[:]), r3(pv[:, 0:512]), b3(beta), ALU.mult)
                        ck(5)
                        KKT = kqT.next(); QQT = kqT.next()
                        psA = pp["E"].next(); pA = psA[:].bitcast(BF16)
                        for h in range(4):
                            k.tp(pA[:, hsl(h)], Kn[:, hsl(h)], identb[:])
                        for h in range(4):
                            k.tp(pA[:, hsl(4 + h)], KB[:, hsl(h)], identb[:])
                        k.cp("dve" if "h" in BAL else "act", KKT[:], pA)
                        psB = pp["E"].next(); pB = psB[:].bitcast(BF16)
                        for h in range(4):
                            k.tp(pB[:, hsl(h)], Qs[:, hsl(h)], identb[:])
                        for h in range(4):
                            k.tp(pB[:, hsl(4 + h)], Qd[:, hsl(h)], identb[:])
                        k.cp("dve", QQT[:], pB)
                        KnT = lambda h: KKT[:, hsl(h)]
                        KBT = lambda h: KKT[:, hsl(4 + h)]
                        QsT = lambda h: QQT[:, hsl(h)]
                        QdT = lambda h: QQT[:, hsl(4 + h)]
                        ck(6)
                        k.tt("pool", r3(MG[:]), CM[:].un(1).bc([128, 4, 128]), g.un(2).bc([128, 4, 128]), ALU.mult)
                        psD = pp["E"].next()
                        for h in range(4):
                            k.mm(psD[:, hsl(h)], MG[:, hsl(h)], onesf[:], True, False)
                            k.mm(psD[:, hsl(h)], nonesf[:], MG[:, hsl(h)], False, False)
                            k.mm(psD[:, hsl(h)], identb[:], NEGs[:], False, True)
                        Ds = dexp.next(); DT = dexp.next(); DTs = dexp.next()
                        k.act(Ds[:], psD[:], AF.Exp)
                        psDT = pp["E"].next(); pDT = psDT[:].bitcast(BF16)
                        for h in range(4):
                            k.tp(pDT[:, hsl(h)], Ds[:, hsl(h)], identb[:])
                        k.cp("act", DTs[:], pDT[:, 0:512])
                        k.tt("dve", DT[:], pDT[:, 0:512], irep[:], ALU.add)
                        ck(7)
                        psA_ = pp["E"].next(); psAT = pp["E"].next(); psKQ = pp["E"].next()
                        for h in range(4):
                            k.mm(psA_[:, hsl(h)], KBT(h), KnT(h))
                        for h in range(4):
                            k.mm(psAT[:, hsl(h)], KnT(h), KBT(h))
                        for h in range(4):
                            k.mm(psKQ[:, hsl(h)], KnT(h), QsT(h))
                        X = Xp.next(); XT = XTp.next(); PT = PTp.next(); QKDT = e2r.next()
                        k.stt("dve", X[:], psA_[:], -1.0, Ds[:], ALU.mult, ALU.mult)
                        k.stt("dve", XT[:], psAT[:], -1.0, DTs[:], ALU.mult, ALU.mult)
                        k.tt("dve", QKDT[:], psKQ[:], DT[:], ALU.mult)
                        k.tt("pool", PT[:], XT[:], irep[:], ALU.add)
                        ck(8)
                        for lv in range(1, nlev):
                            psX = pp["E"].next()
                            for h in range(4):
                                k.mm(psX[:, hsl(h)], XT[:, hsl(h)], X[:, hsl(h)])
                            last = lv == nlev - 1
                            IX = IXp.next()
                            k.tt("dve", IX[:], psX[:], irep[:], ALU.add)
                            if not last:
                                Xn = Xp.next()
                                k.cp("dve" if "e" in BAL else "act", Xn[:], psX[:])
                                psXT = pp["E"].next()
                                for h in range(4):
                                    k.mm(psXT[:, hsl(h)], X[:, hsl(h)], XT[:, hsl(h)])
                                XTn = XTp.next()
                                k.cp("dve" if "f" in BAL else "act", XTn[:], psXT[:])
                            psP = pp["E"].next()
                            for h in range(4):
                                k.mm(psP[:, hsl(h)], IX[:, hsl(h)], PT[:, hsl(h)])
                            PTn = PTp.next()
                            k.cp("dve" if "g" in BAL else "act", PTn[:], psP[:])
                            PT = PTn
                            if not last:
                                X, XT = Xn, XTn
                        ck(9)
                        zs = zsp.next(); rgs = rgsp.next()
                        psz = proj(C_Z, 512)
                        sgt = tmpp.next()
                        k.act(sgt[:], psz[:], AF.Exp, scale=-1.0)
                        k.act(sgt[:], sgt[:], AF.Ln, bias=1.0)
                        k.act(sgt[:], sgt[:], AF.Exp, scale=-1.0)
                        k.tt("dve", zs[:], psz[:], sgt[:], ALU.mult)
                        psrq = proj(C_RQ, 512)
                        k.cp("act", rqkf[:, 0:512], psrq[:])
                        psrk = proj(C_RK, 512)
                        k.cp("act", rqkf[:, 512:1024], psrk[:])
                        RV = rbf2.next()
                        psrv = proj(C_RV, 512)
                        k.cp("act", RV[:], psrv[:])
                        psrg = proj(C_RG, 512)
                        sgt = tmpp.next()
                        k.act(sgt[:], psrg[:], AF.Exp, scale=-1.0)
                        k.act(sgt[:], sgt[:], AF.Ln, bias=1.0)
                        k.act(sgt[:], sgt[:], AF.Exp, scale=-1.0)
                        k.tt("dve", rgs[:], psrg[:], sgt[:], ALU.mult)

                        ck(10)
                        of = ofp.next()
                        R = rbf.next(); vn = rbf.next()
                        headsets = [[h] for h in range(4)] if samp else [[0, 1, 2, 3]]
                        for hs in headsets:
                            cols = slice(hs[0] * 128, (hs[-1] + 1) * 128)
                            if samp:
                                h = hs[0]
                                sbig = sbigp.next(); sbigb = sbigbp.next()
                                for _q in range(4):
                                    k.dma("sp", sbig[:, _q * 512:(_q + 1) * 512].re("p (s v) -> p s v", s=4), V(None, stdn_d.ap[4 * _q:4 * _q + 4, h].rearrange("s k v -> k s v")))
                                k.cp("pool", sbigb[:], sbig[:])
                                KnE = expp.next(); QdE = expp.next()
                                e3 = lambda v: v.re("p (s c) -> p s c", s=16)
                                k.tt("pool", e3(KnE[:]), KnT(h).un(1).bc([128, 16, 128]), e3(E1[:]), ALU.mult)
                                k.tt("dve", e3(QdE[:]), QdT(h).un(1).bc([128, 16, 128]), e3(E1[:]), ALU.mult)
                                Sf = lambda hh, s, sbig=sbig: sbig[:, hsl(s)]
                                Sb = lambda hh, s, sbigb=sbigb: sbigb[:, hsl(s)]
                                lK = lambda hh, s: KnE[:, hsl(s)]
                                lQ = lambda hh, s: QdE[:, hsl(s)]
                            else:
                                Sf = lambda hh, s: Sdn[:, hsl(hh)]
                                Sb = lambda hh, s: Sdnb[:, hsl(hh)]
                                lK = lambda hh, s: KnT(hh)
                                lQ = lambda hh, s: QdT(hh)
                                lT = lambda hh, s: Kt[:, hsl(hh)]
                            psKS = pp["R"].next()
                            for h in hs:
                                for s in range(nst):
                                    k.mm(psKS[:, hsl(h)], lK(h, s), Sb(h, s), s == 0, s == nst - 1)
                            if samp:
                                KtE = expp.next()
                                k.tt("pool", e3(KtE[:]), Kt[:, hsl(hs[0])].un(1).bc([128, 16, 128]), seq2[:].un(2).bc([128, 16, 128]), ALU.mult)
                                lT = lambda hh, s: KtE[:, hsl(s)]
                            for h in hs:
                                k.stt("dve", R[:, hsl(h)], psKS[:, hsl(h)], sc[:, 16 + h:17 + h], Vb[:, hsl(h)], ALU.mult, ALU.add)
                            psV = pp["R"].next()
                            for h in hs:
                                k.mm(psV[:, hsl(h)], PT[:, hsl(h)], R[:, hsl(h)])
                            k.cp("act", vn[:, cols], psV[:, cols])
                            psO = pp["R"].next()
                            for h in hs:
                                for s in range(nst):
                                    k.mm(psO[:, hsl(h)], lQ(h, s), Sb(h, s), s == 0, False)
                                k.mm(psO[:, hsl(h)], QKDT[:, hsl(h)], vn[:, hsl(h)], False, True)
                            k.cp("act", of[:, cols], psO[:, cols])
                            pairs = [(h, s) for h in hs for s in range(nst)]
                            for g0 in range(0, len(pairs), 4):
                                grp = pairs[g0:g0 + 4]
                                psS = pp["R"].next()
                                for i, (h, s) in enumerate(grp):
                                    k.mm(psS[:, hsl(i)], lT(h, s), vn[:, hsl(h)])
                                for i, (h, s) in enumerate(grp):
                                    k.stt("dve", Sf(h, s), Sf(h, s), ex[:, 8 + h * nst + s:9 + h * nst + s], psS[:, hsl(i)], ALU.mult, ALU.add)
                            if samp:
                                for _q in range(4):
                                    k.dma("sp", V(Buf(dns_o.h, "dns%d_%d" % (hs[0], _q)), dns_o.h[4 * _q:4 * _q + 4, hs[0]].rearrange("s k v -> k s v")), sbig[:, _q * 512:(_q + 1) * 512].re("p (s v) -> p s v", s=4))
                            else:
                                k.cp("act", Sdnb[:], Sdn[:])
                        ck(11)
                        st = small.next()
                        for h in range(4):
                            k.act(junk_o[:], of[:, hsl(h)], AF.Square, accum=st[:, h:h + 1])
                        rso = rstd_of(st[:, 0:4], 1.0 / 128, small, mhalf, 4)
                        k.tt("pool" if "a" in BAL else "dve", r3(of[:]), r3(of[:]), b3(rso), ALU.mult)
                        k.tt("pool", of[:], of[:], dnw[:], ALU.mult)
                        k.tt("pool" if "b" in BAL else "dve", mix[:, 0:512], of[:], zs[:], ALU.mult)

                        ck(12)
                        rqkb = rtk.next()
                        g4 = lambda v: v.re("p (g i two) -> p g i two", g=8, two=2)
                        x1 = g4(rqkf[:])[:, :, :, 0]; x2 = g4(rqkf[:])[:, :, :, 1]
                        o1 = g4(rqkb[:])[:, :, :, 0]; o2 = g4(rqkb[:])[:, :, :, 1]
                        cosb = cs[:, 0:64].un(1).bc([128, 8, 64]); sinb = cs[:, 64:128].un(1).bc([128, 8, 64])
                        t8 = lambda v: v.re("p (g i) -> p g i", g=8)
                        ta = tmpp.next(); tb = tmpp.next()
                        k.tt("dve", t8(ta[:]), x1, cosb, ALU.mult)
                        k.tt("pool", t8(tb[:]), x2, sinb, ALU.mult)
                        k.tt("dve", o1, t8(ta[:]), t8(tb[:]), ALU.subtract)
                        ta = tmpp.next(); tb = tmpp.next()
                        k.tt("pool", t8(ta[:]), x1, sinb, ALU.mult)
                        k.tt("dve", t8(tb[:]), x2, cosb, ALU.mult)
                        k.tt("pool", o2, t8(ta[:]), t8(tb[:]), ALU.add)
                        RQd = rbf2.next(); RKt = rbf2.next()
                        k.tt("dve", r3(RQd[:]), r3(rqkb[:, 0:512]), b3(rqd[:]), ALU.mult)
                        k.tt("pool", r3(RKt[:]), r3(rqkb[:, 512:1024]), b3(rkt[:]), ALU.mult)
                        RQKT = rtk.next(); RQdT_t = rbf2.next()
                        psR1 = pp["M"].next(); pR1 = psR1[:].bitcast(BF16)
                        for i in range(8):
                            k.tp(pR1[:, hsl(i)], rqkb[:, hsl(i)], identb[:])
                        k.cp("act", RQKT[:], pR1)
                        psR2 = pp["M"].next(); pR2 = psR2[:].bitcast(BF16)
                        for h in range(4):
                            k.tp(pR2[:, hsl(h)], RQd[:, hsl(h)], identb[:])
                        k.cp("dve", RQdT_t[:], pR2[:, 0:512])
                        psKQr = pp["M"].next()
                        for h in range(4):
                            k.mm(psKQr[:, hsl(h)], RQKT[:, hsl(4 + h)], RQKT[:, hsl(h)])
                        QKDTr = rbf2.next()
                        k.tt("dve", QKDTr[:], psKQr[:], DTr[:], ALU.mult)
                        orf = ofp.next()
                        for hs in headsets:
                            cols = slice(hs[0] * 128, (hs[-1] + 1) * 128)
                            if samp:
                                h = hs[0]
                                sbig = sbigp.next(); sbigb = sbigbp.next()
                                for _q in range(4):
                                    k.dma("sp", sbig[:, _q * 512:(_q + 1) * 512].re("p (s v) -> p s v", s=4), V(None, stret_d.ap[4 * _q:4 * _q + 4, h].rearrange("s k v -> k s v")))
                                k.cp("pool", sbigb[:], sbig[:])
                                QdE = expp.next(); KtE = expp.next()
                                e3 = lambda v: v.re("p (s c) -> p s c", s=16)
                                k.tt("dve", e3(QdE[:]), RQdT_t[:, hsl(h)].un(1).bc([128, 16, 128]), e3(E1[:]), ALU.mult)
                                k.tt("pool", e3(KtE[:]), RKt[:, hsl(h)].un(1).bc([128, 16, 128]), seq2[:].un(2).bc([128, 16, 128]), ALU.mult)
                                Sf = lambda hh, s, sbig=sbig: sbig[:, hsl(s)]
                                Sb = lambda hh, s, sbigb=sbigb: sbigb[:, hsl(s)]
                                lQ = lambda hh, s: QdE[:, hsl(s)]
                                lT = lambda hh, s: KtE[:, hsl(s)]
                            else:
                                Sf = lambda hh, s: Srt[:, hsl(hh)]
                                Sb = lambda hh, s: Srtb[:, hsl(hh)]
                                lQ = lambda hh, s: RQdT_t[:, hsl(hh)]
                                lT = lambda hh, s: RKt[:, hsl(hh)]
                            psO = pp["R"].next()
                            for h in hs:
                                for s in range(nst):
                                    k.mm(psO[:, hsl(h)], lQ(h, s), Sb(h, s), s == 0, False)
                                k.mm(psO[:, hsl(h)], QKDTr[:, hsl(h)], RV[:, hsl(h)], False, True)
                            k.cp("act", orf[:, cols], psO[:, cols])
                            pairs = [(h, s) for h in hs for s in range(nst)]
                            for g0 in range(0, len(pairs), 4):
                                grp = pairs[g0:g0 + 4]
                                psS = pp["R"].next()
                                for i, (h, s) in enumerate(grp):
                                    k.mm(psS[:, hsl(i)], lT(h, s), RV[:, hsl(h)])
                                for i, (h, s) in enumerate(grp):
                                    k.stt("dve", Sf(h, s), Sf(h, s), cdec[h], psS[:, hsl(i)], ALU.mult, ALU.add)
                            if samp:
                                for _q in range(4):
                                    k.dma("sp", V(Buf(rets_o.h, "rets%d_%d" % (hs[0], _q)), rets_o.h[4 * _q:4 * _q + 4, hs[0]].rearrange("s k v -> k s v")), sbig[:, _q * 512:(_q + 1) * 512].re("p (s v) -> p s v", s=4))
                            else:
                                k.cp("act", Srtb[:], Srt[:])
                        ck(13)
                        st = small.next()
                        for h in range(4):
                            k.act(junk_r[:, 0:128], orf[:, hsl(h)], AF.Copy, accum=st[:, h:h + 1])
                        for h in range(4):
                            k.act(junk_r[:, 128:256], orf[:, hsl(h)], AF.Square, accum=st[:, 4 + h:5 + h])
                        s2 = small.next()
                        k.ts("dve", s2[:, 0:4], st[:, 0:4], 1.0 / 128, ALU.mult)
                        k.tt("dve", s2[:, 4:8], s2[:, 0:4], s2[:, 0:4], ALU.mult)
                        k.stt("dve", s2[:, 8:12], st[:, 4:8], 1.0 / 128, s2[:, 4:8], ALU.mult, ALU.subtract)
                        rsr = rstd_of(s2[:, 8:12], 1.0, small, mhalf, 4)
                        k.tt("pool" if "d" in BAL else "dve", r3(orf[:]), r3(orf[:]), b3(s2[:, 0:4]), ALU.subtract)
                        k.tt("pool", r3(orf[:]), r3(orf[:]), b3(rsr), ALU.mult)
                        k.tt("pool" if "c" in BAL else "dve", orf[:], orf[:], retw[:], ALU.mult)
                        k.tt("pool", mix[:, 512:1024], orf[:], rgs[:], ALU.mult)
                        ck(14)
                        psM = pp["L"].next(); pM = psM[:].bitcast(BF16)
                        for kk in range(8):
                            k.tp(pM[:, hsl(kk)], mix[:, hsl(kk)], identb[:])
                        k.cp("act", mixT[:], pM)
                        h1 = xr if samp else h1p.next()
                        for half in range(2):
                            psH = pp["L"].next()
                            for kk in range(8):
                                k.mm(psH[:], mixT[:, hsl(kk)], w_out[kk][:, half * 512:(half + 1) * 512], kk == 0, kk == 7)
                            k.tt("dve", h1[:, half * 512:(half + 1) * 512], psH[:], xr[:, half * 512:(half + 1) * 512], ALU.add)
                        k.dma("sp", h1_s.rows(r0), h1[:])

                except StopBuild:
                    pass
                n3 = 3 * (NSAMP if samp else 1)
                dnc_o = dncs_o if samp else dncp_o
                for gI in range(3):
                    ps = pp["L"].next()
                    for i in range(4):
                        fc = gI * 4 + i
                        k.tp(ps[0:n3, hsl(i)], cx.c(fc, 0, n3), identf[:])
                    tmp = tmpp.next()
                    k.cp("act", tmp[0:n3, :], ps[0:n3, :])
                    k.dma("sp", V(Buf(dnc_o.h, "dnc%d" % gI), dnc_o.h[:, gI * 512:(gI + 1) * 512]), tmp[0:n3, :])
                if not samp:
                    k.dma("sp", V(dnp_o, dnp_o.h.rearrange("h k v -> k h v")), Sdn[:].re("p (h v) -> p h v", h=4))
                    k.dma("sp", V(retp_o, retp_o.h.rearrange("h k v -> k h v")), Srt[:].re("p (h v) -> p h v", h=4))
                S.emit_phase()

        sb_ex = None

        WA = None

        def phase_a_wrap(samp):
            nonlocal sb_ex
            with contextlib.ExitStack() as st0:
                sb0, _ = mk_alloc(st0)
                sb_ex = sb0([128, 72], n=3, name="ex")
                phase_a(samp)

        class Sub:
            def __init__(self, buf, off, w):
                self.buf, self.off, self.w = buf, off, w

            def __getitem__(self, key):
                sl = key[1]
                a0 = 0 if sl.start is None else sl.start
                a1 = self.w if sl.stop is None else sl.stop
                return V(self.buf, self.buf.h[:, self.off + a0:self.off + a1])

        def load_cols(st_sb, wd, kchunks, c0, c1, name, kmax=11):
            out = []
            W = c1 - c0
            k0 = 0
            while k0 < kchunks:
                nk = min(kmax, kchunks - k0)
                b = st_sb([128, nk * W], BF16, name=name)
                src = V(None, wd.ap[k0 * 128:(k0 + nk) * 128, c0:c1].rearrange("(k p) c -> p k c", p=128))
                k.dma("pool", b[:].re("p (k c) -> p k c", k=nk), src)
                for j in range(nk):
                    out.append(Sub(b, j * W, W))
                k0 += nk
            return out

        import os
        _ph = os.environ.get("KDBG_PH", "ABCD")
        with contextlib.ExitStack() as stw:
            sbw, _ = mk_alloc(stw)
            if "A" in _ph or "B" in _ph:
                WA = (load_cols(sbw, w_in_d, 8, 0, 1536, "w_inq"), load_cols(sbw, w_in_d, 8, 1536, DIN, "w_inr"),
                      load_cols(sbw, w_out_d, 8, 0, D, "w_out"))
            if "A" in _ph:
                phase_a_wrap(False)
            if "B" in _ph:
                phase_a_wrap(True)

        def phase_b():
            with contextlib.ExitStack() as st:
                sb, psum_pool = mk_alloc(st)
                pp = psum_pool(tr=2, up=4, down=2)
                GP = [(0, 6), (6, 12), (12, 17), (17, 22)]
                w_up_g = []
                for (p0, p1) in GP:
                    ug = load_cols(sb, w_up_d, 8, p0 * 128, p1 * 128, "w_upg")
                    uv = load_cols(sb, w_up_d, 8, DFF + p0 * 128, DFF + p1 * 128, "w_upv")
                    w_up_g.append((p0, p1, ug, uv))

                def w_up_sl(kk, fc):
                    part, i = (0, fc) if fc < 22 else (1, fc - 22)
                    for (p0, p1, ug, uv) in w_up_g:
                        if p0 <= i < p1:
                            return (ug, uv)[part][kk][:, (i - p0) * 128:(i - p0 + 1) * 128]
                w_down = load_w(sb, w_down_d, 22, D, "w_down")
                identf = sb([128, 128]); k.dma("sp", identf[:], ident_d)
                identb = sb([128, 128], BF16); k.dma("pool", identb[:], ident_d)
                mhalf = sb([128, 8]); k.ms("pool", mhalf[:], -0.5)
                fnw8 = sb([128, 8]); k.dma("sp", fnw8[:], fnw8_d)
                fcw = sb([128, 132]); k.dma("sp", fcw[:], fcw_d)
                fcb = sb([128, 44]); k.dma("sp", fcb[:], fcb_d)
                NTm = 256
                import os as _o
                htp = sb([128, D], n=int(_o.environ.get("KB_HT", "4")), name="ht")
                mbfp = sb([128, D], BF16, n=2, name="mbf")
                import os as _o
                mTp = sb([128, 8 * NTm], BF16, n=int(_o.environ.get("KB_MT", "1")), name="mT")
                actTp = sb([128, 22 * NTm], BF16, n=int(_o.environ.get("KB_ACTT", "1")), name="actT")
                uprep = sb([128, 264], n=int(_o.environ.get("KB_UP", "4")), name="upre")
                ycp = sb([128, 256], n=int(_o.environ.get("KB_YC", "4")), name="yc")
                sgp = sb([128, 256], n=int(_o.environ.get("KB_SG", "2")), name="sg")
                cf_all = sb([128, 44 * 32], name="cf"); k.ms("pool", cf_all[:], 0.0); cf = Chunked(cf_all, 44, 32); cf.inherit(cf_all)
                fcT = sb([128, 44 * 32], name="fcT")
                junk = sb([128, D], BF16, name="junk"); junk2 = sb([128, D], BF16, name="junk2")
                small = sb([128, 8], n=8, name="small")
                h2p = sb([128, D], n=int(_o.environ.get("KB_H2", "2")), name="h2")
                tmpp = sb([128, 512], n=int(_o.environ.get("KB_TMP", "2")), name="tmp")
                hsl = lambda h: slice(h * 128, (h + 1) * 128)

                for gI in range(11):
                    tmp = tmpp.next()
                    k.dma("sp", tmp[0:32, :], stfc_d[:, gI * 512:(gI + 1) * 512])
                    ps = pp["tr"].next()
                    for i in range(4):
                        k.tp(ps[:, i * 32:(i + 1) * 32], tmp[0:32, hsl(i)], identf[0:32, 0:32])
                    k.cp("act", fcT[:, gI * 128:(gI + 1) * 128], ps[:, 0:128])

                blocks = [(b * 256, 256, 1, 256, False) for b in range(SEQ // 256)] + [(SEQ, 128, NSAMP, LS, True)]
                for (t0, NT, nseq, L, samp) in blocks:
                    nsub = NT // 128
                    mT = mTp.next(); actT = actTp.next()
                    mT3 = mT[:].re("p (k t) -> p k t", k=8)
                    hts = []
                    for j in range(nsub):
                        r0 = t0 + 128 * j
                        ht = htp.next(); hts.append(ht)
                        k.dma("sp", ht[:], h1_s.rows(r0))
                        ss = small.next()
                        k.act(junk[:], ht[:], AF.Square, accum=ss[:, 0:1])
                        rs = rstd_of(ss[:, 0:1], 1.0 / D, small, mhalf, 1)
                        mbf = mbfp.next()
                        k.ts("dve", mbf[:], ht[:], rs, ALU.mult)
                        ps = pp["tr"].next(); pb = ps[:].bitcast(BF16)
                        for kk in range(8):
                            k.tp(pb[:, hsl(kk)], mbf[:, hsl(kk)], identb[:])
                        k.tt("dve", mT3[:, :, 128 * j:128 * (j + 1)], pb.re("p (k t) -> p k t", k=8),
                             fnw8[:].un(2).bc([128, 8, 128]), ALU.mult)
                    actT3 = actT[:].re("p (c t) -> p c t", c=22)
                    for i in range(22):
                        ys = []
                        for fc in (i, 22 + i):
                            ps = pp["up"].next()
                            for kk in range(8):
                                k.mm(ps[:, 0:NT], w_up_sl(kk, fc), mT3[:, kk, 0:NT], kk == 0, kk == 7)
                            up = uprep.next()
                            uv = up[:, 0:nseq * (2 + L)].re("p (s l) -> p s l", s=nseq)
                            psv = ps[:, 0:NT].re("p (s l) -> p s l", s=nseq)
                            k.cp("act", uv[:, :, 2:2 + L], psv)
                            if samp:
                                k.cp("pool", uv[:, :, 0:2], fcT[:, fc * 32:(fc + 1) * 32].re("p (s j) -> p s j", s=nseq))
                            else:
                                k.cp("pool", uv[:, :, 0:2], cf.c(fc, 0, 2).un(1))
                            k.cp("pool", cf.c(fc, 0, 2 * nseq).re("p (s j) -> p s j", s=nseq), uv[:, :, L:L + 2])
                            y = ycp.next(); ys.append(y)
                            yv = y[:, 0:NT].re("p (s l) -> p s l", s=nseq)
                            k.act(yv, psv, AF.Identity, scale=fcw[:, fc * 3 + 2:fc * 3 + 3], bias=fcb[:, fc:fc + 1])
                            k.stt("dve", yv, uv[:, :, 1:1 + L], fcw[:, fc * 3 + 1:fc * 3 + 2], yv, ALU.mult, ALU.add)
                            k.stt("dve", yv, uv[:, :, 0:L], fcw[:, fc * 3 + 0:fc * 3 + 1], yv, ALU.mult, ALU.add)
                        sg = sgp.next()
                        k.act(sg[:, 0:NT], ys[0][:, 0:NT], AF.Silu)
                        k.tt("dve", actT3[:, i, 0:NT], sg[:, 0:NT], ys[1][:, 0:NT], ALU.mult)
                    for j in range(nsub):
                        r0 = t0 + 128 * j
                        h2 = h2p.next()
                        for half in range(2):
                            psH = pp["down"].next()
                            for c in range(22):
                                k.mm(psH[:], actT3[:, c, 128 * j:128 * (j + 1)], w_down[c][:, half * 512:(half + 1) * 512], c == 0, c == 21)
                            k.tt("dve", h2[:, half * 512:(half + 1) * 512], psH[:], hts[j][:, half * 512:(half + 1) * 512], ALU.add)
                        k.dma("sp", h2_s.rows(r0), h2[:])
                    last_prompt = (not samp) and t0 + NT == SEQ
                    if last_prompt or samp:
                        n2 = 2 * nseq
                        fo = fcs_o if samp else fcp_o
                        for gI in range(11):
                            ps = pp["tr"].next()
                            for i in range(4):
                                fc = gI * 4 + i
                                k.tp(ps[0:n2, hsl(i)], cf.c(fc, 0, n2), identf[:])
                            tmp = tmpp.next()
                            k.cp("act", tmp[0:n2, :], ps[0:n2, :])
                            k.dma("sp", V(Buf(fo.h, "fo%d" % gI), fo.h[:, gI * 512:(gI + 1) * 512]), tmp[0:n2, :])
                S.emit_phase()

        if "C" in _ph:
            phase_b()

        def phase_c():
            with contextlib.ExitStack() as st:
                sb, psum_pool = mk_alloc(st)
                pp = psum_pool(tr=2, mm=6)
                w_gate = load_w(sb, w_gate_d, 8, D, "w_gate")
                w_ple = load_w(sb, w_ple_d, 2, D, "w_ple")
                identb = sb([128, 128], BF16); k.dma("pool", identb[:], ident_d)
                mhalf = sb([128, 8]); k.ms("pool", mhalf[:], -0.5)
                pnw8 = sb([128, 8]); k.dma("sp", pnw8[:], pnw8_d)
                finw = sb([128, D]); k.dma("sp", finw[:], V(None, finw_d.ap.partition_broadcast(128)))
                h2p = sb([128, D], n=PCN, name="h2")
                ptp = sb([128, 256], n=PCN, name="pt")
                pbp = sb([128, 256], BF16, n=PCN, name="pb")
                nbfp = sb([128, D], BF16, n=PCN, name="nbf")
                nTp = sb([128, D], BF16, n=PCN, name="nT")
                pTp = sb([128, 256], BF16, n=PCN, name="pT")
                tgp = sb([128, D], n=PCN, name="tg")
                h3p = sb([128, D], n=PCN, name="h3")
                yp = sb([128, D], n=PCN, name="y")
                junk = sb([128, D], BF16, name="junk"); junk2 = sb([128, D], BF16, name="junk2")
                small = sb([128, 8], n=4 * PCN, name="small")
                hsl = lambda h: slice(h * 128, (h + 1) * 128)
                for it in range(NTOK // 128):
                    r0 = it * 128
                    h2 = h2p.next()
                    k.dma("sp", h2[:], h2_s.rows(r0))
                    pt = ptp.next()
                    k.dma("sp", pt[:], p_d[r0:r0 + 128, :])
                    ss = small.next()
                    k.act(junk[:], h2[:], AF.Square, accum=ss[:, 0:1])
                    rs = rstd_of(ss[:, 0:1], 1.0 / D, small, mhalf, 1)
                    nbf = nbfp.next()
                    k.act(nbf[:], h2[:], AF.Copy, scale=rs)
                    ps = pp["tr"].next(); pb = ps[:].bitcast(BF16)
                    for kk in range(8):
                        k.tp(pb[:, hsl(kk)], nbf[:, hsl(kk)], identb[:])
                    nT = nTp.next()
                    k.tt("dve", nT[:].re("p (k t) -> p k t", k=8), pb.re("p (k t) -> p k t", k=8),
                         pnw8[:].un(2).bc([128, 8, 128]), ALU.mult)
                    pbf = pbp.next()
                    k.cp("pool", pbf[:], pt[:])
                    ps2 = pp["tr"].next(); pb2 = ps2[:].bitcast(BF16)
                    for kk in range(2):
                        k.tp(pb2[:, hsl(kk)], pbf[:, hsl(kk)], identb[:])
                    pT = pTp.next()
                    k.cp("act", pT[:], pb2[:, 0:256])
                    tg = tgp.next(); h3 = h3p.next()
                    for half in range(2):
                        hs_ = slice(half * 512, (half + 1) * 512)
                        psG = pp["mm"].next()
                        for kk in range(8):
                            k.mm(psG[:], nT[:, hsl(kk)], w_gate[kk][:, hs_], kk == 0, kk == 7)
                        psP = pp["mm"].next()
                        for kk in range(2):
                            k.mm(psP[:], pT[:, hsl(kk)], w_ple[kk][:, hs_], kk == 0, kk == 1)
                        k.act(tg[:, hs_], psG[:], AF.Tanh, scale=0.5)
                        k.stt("dve", tg[:, hs_], tg[:, hs_], 1.0, psP[:], ALU.add, ALU.mult)
                        k.stt("dve", h3[:, hs_], tg[:, hs_], 0.5, h2[:, hs_], ALU.mult, ALU.add)
                    ss = small.next()
                    k.act(junk2[:], h3[:], AF.Square, accum=ss[:, 0:1])
                    rs = rstd_of(ss[:, 0:1], 1.0 / D, small, mhalf, 1)
                    y = yp.next()
                    k.stt("dve", y[:], h3[:], rs, finw[:], ALU.mult, ALU.mult)
                    k.dma("sp", y_o.rows(r0), y[:])
                S.emit_phase()

        if "D" in _ph:
            phase_c()
    return nc


def _consts():
    c = {}
    idx = np.arange(128)
    c["ident"] = np.eye(128, dtype=np.float32)
    c["irep"] = np.tile(np.eye(128, dtype=np.float32), (1, 4))
    lg = np.log(1.0 - 2.0 ** (-5.0 - np.arange(4, dtype=np.float64)))
    sc = 128.0 ** -0.5
    for v, C in (("P", 128), ("S", LS)):
        seq = idx // C
        pos = idx % C
        same = seq[:, None] == seq[None, :]
        a = idx[:, None]
        b = idx[None, :]
        c["CM" + v] = (same & (a <= b)).astype(np.float32)
        c["UM" + v] = (same & (a > b)).astype(np.float32)
        c["NEGs" + v] = np.where(same & (a > b), 0.0, NEG).astype(np.float32)
        c["NEGT" + v] = np.where(same & (b >= a), 0.0, NEG).astype(np.float32)
        dtr = np.zeros((128, 4, 128), np.float64)
        for h in range(4):
            dtr[:, h, :] = np.where(same & (b >= a), sc * np.exp((b - a) * lg[h]), 0.0)
        c["DTr" + v] = dtr.reshape(128, 512).astype(np.float32)
        c["rqd" + v] = np.exp((pos[:, None] + 1.0) * lg[None, :]).astype(np.float32)
        c["rkt" + v] = (sc * np.exp((C - 1.0 - pos[:, None]) * lg[None, :])).astype(np.float32)
    s16 = np.arange(16)
    c["E1"] = (s16[:, None] == (idx[None, :] // LS)).astype(np.float32).reshape(-1)
    c["seq2"] = ((idx[:, None] // LS) == s16[None, :]).astype(np.float32)
    pos = np.concatenate([np.arange(SEQ), PAST + (np.arange(NSAMP * LS) % LS)]).astype(np.float32)
    inv = (np.float32(10000.0) ** (-np.arange(0, 128, 2, dtype=np.float32) / np.float32(128))).astype(np.float32)
    ang = (pos[:, None] * inv[None, :]).astype(np.float32)
    c["cosT"] = np.cos(ang.astype(np.float64)).astype(np.float32)
    c["sinT"] = np.sin(ang.astype(np.float64)).astype(np.float32)
    return c


_NC_CACHE = {}


def kernel(x_prompt, x_sample, p_prompt, p_sample, state_dn_conv, state_dn, state_ret,
           state_ffn_conv, attn_norm_w, w_in, dn_conv_w, dn_A_log, dn_dt_bias, dn_norm_w,
           ret_norm_w, w_out, ffn_norm_w, w_up, ffn_conv_w, ffn_conv_b, w_down, ple_norm_w,
           w_ple_gate, w_ple, final_norm_w):
    f = lambda a: np.ascontiguousarray(np.asarray(a), dtype=np.float32)
    x_prompt, x_sample, p_prompt, p_sample = f(x_prompt), f(x_sample), f(p_prompt), f(p_sample)
    state_dn_conv, state_dn, state_ret, state_ffn_conv = f(state_dn_conv), f(state_dn), f(state_ret), f(state_ffn_conv)
    col8 = lambda w: f(np.asarray(w).reshape(8, 128).T)
    shared = dict(
        w_in=f(w_in)[0], w_out=f(w_out)[0], w_up=f(w_up)[0], w_down=f(w_down)[0],
        w_gate=f(w_ple_gate)[0], w_ple=f(w_ple)[0],
        anw8=col8(f(attn_norm_w)[0]), fnw8=col8(f(ffn_norm_w)[0]), pnw8=col8(f(ple_norm_w)[0]),
        finw=f(final_norm_w),
        dncw=f(f(dn_conv_w)[0].T.reshape(12, 128, 4).transpose(1, 0, 2).reshape(128, 48)),
        alog=f(dn_A_log)[0], dtb=f(dn_dt_bias)[0],
        dnw4=f(np.tile(f(dn_norm_w)[0], 4)), retw=f(ret_norm_w)[0],
        fcw=f(f(ffn_conv_w)[0].T.reshape(44, 128, 3).transpose(1, 0, 2).reshape(128, 132)),
        fcb=f(f(ffn_conv_b)[0].reshape(44, 128).T),
    )
    shared.update(_consts())
    in_maps = []
    for i in range(NCORES):
        sl = slice(NSAMP * i, NSAMP * (i + 1))
        m = dict(shared)
        m["x"] = f(np.concatenate([x_prompt[i], x_sample[sl].reshape(NSAMP * LS, D)], axis=0))
        m["p"] = f(np.concatenate([p_prompt[0, i], p_sample[0, sl].reshape(NSAMP * LS, 256)], axis=0))
        m["st_dnc"] = f(state_dn_conv[0, sl].reshape(48, 1536))
        m["st_dn"] = f(state_dn[0, sl])
        m["st_ret"] = f(state_ret[0, sl])
        m["st_fc"] = f(state_ffn_conv[0, sl].reshape(32, 2 * DFF))
        in_maps.append(m)
    if "nc" not in _NC_CACHE:
        _NC_CACHE["nc"] = build_program()
    nc = _NC_CACHE["nc"]
    res = run_bass_kernel_spmd(nc, in_maps, core_ids=list(range(NCORES)))
    R = res.results
    _NC_CACHE["last"] = R
    g = lambda name: [np.asarray(R[i][name], dtype=np.float32) for i in range(NCORES)]
    y = g("y")
    y_prompt = np.stack([a[:SEQ] for a in y], axis=0)
    y_sample = np.concatenate([a[SEQ:].reshape(NSAMP, LS, D) for a in y], axis=0)
    dncp = np.stack(g("o_dnc_p"), axis=0)[None]
    dnp = np.stack(g("o_dn_p"), axis=0)[None]
    retp = np.stack(g("o_ret_p"), axis=0)[None]
    fcp = np.stack(g("o_fc_p"), axis=0)[None]
    dncs = np.concatenate([a.reshape(NSAMP, 3, 1536) for a in g("o_dnc_s")], axis=0)[None]
    dns = np.concatenate(g("o_dn_s"), axis=0)[None]
    rets = np.concatenate(g("o_ret_s"), axis=0)[None]
    fcs = np.concatenate([a.reshape(NSAMP, 2, 2 * DFF) for a in g("o_fc_s")], axis=0)[None]
    return (y_prompt, y_sample, dncp, dnp, retp, fcp, dncs, dns, rets, fcs)
```
